# Optimizing a Trainium2 kernel written in Bass

```python
import math
import jax, jax.numpy as jnp
from jax import lax
import numpy as np

D_MODEL = 1024
BATCH = 32
SEQ = 2048
DEPTH = 4

SSM_WIDTH = D_MODEL // 2
SSM_GROUP = 16
SSM_GROUPS = SSM_WIDTH // SSM_GROUP
SSM_STATE = 64
HEAD_DIM = 64
ATTN_WIDTH = D_MODEL - SSM_WIDTH
N_Q_HEADS = ATTN_WIDTH // HEAD_DIM
N_KV_HEADS = 2
Q_PER_KV = N_Q_HEADS // N_KV_HEADS
KV_WIDTH = N_KV_HEADS * HEAD_DIM
IN_WIDTH = SSM_WIDTH + ATTN_WIDTH + 2 * KV_WIDTH
WINDOW = 128
BLOCK = 128
N_BUCKETS = 32
MAX_DISTANCE = 128
D_FF = 2816
CONV_WIDTH = 3
EPS = 1e-6
DT_MIN = 1e-3
DT_MAX = 1e-1
NEG_INF = -1e30

kernel_name = 'hybrid_s5_swa_convffn_encoder'


def rms_norm(x, g):
    xf = x.astype(jnp.float32)
    y = xf * lax.rsqrt(jnp.mean(xf * xf, axis=-1, keepdims=True) + EPS)
    return (y * g.astype(jnp.float32)).astype(x.dtype)


def t5_bucket(rel):
    half = N_BUCKETS // 2
    max_exact = half // 2
    ret = jnp.where(rel > 0, half, 0)
    n = jnp.abs(rel)
    nf = jnp.maximum(n, 1).astype(jnp.float32)
    large = max_exact + (jnp.log(nf / max_exact) / math.log(MAX_DISTANCE / max_exact)
                         * (half - max_exact)).astype(jnp.int32)
    large = jnp.minimum(large, half - 1)
    return ret + jnp.where(n < max_exact, n, large)


def s5_scan(u, lam_re, lam_im, log_step, b_re, b_im, c_re, c_im, reverse):
    f32 = jnp.float32
    lam_re, lam_im = lam_re.astype(f32), lam_im.astype(f32)
    dt = jnp.exp(log_step.astype(f32))[:, None]
    mag = jnp.exp(lam_re * dt)
    ang = lam_im * dt
    lb_re, lb_im = mag * jnp.cos(ang), mag * jnp.sin(ang)
    den = lam_re * lam_re + lam_im * lam_im
    nr, ni = lb_re - 1.0, lb_im
    coef_re = (nr * lam_re + ni * lam_im) / den
    coef_im = (ni * lam_re - nr * lam_im) / den
    b_re, b_im = b_re.astype(f32), b_im.astype(f32)
    bb_re = coef_re[..., None] * b_re - coef_im[..., None] * b_im
    bb_im = coef_re[..., None] * b_im + coef_im[..., None] * b_re
    bu_re = jnp.einsum('blgp,gnp->blgn', u, bb_re)
    bu_im = jnp.einsum('blgp,gnp->blgn', u, bb_im)
    a_re = jnp.broadcast_to(lb_re, bu_re.shape)
    a_im = jnp.broadcast_to(lb_im, bu_im.shape)

    def combine(e1, e2):
        a1r, a1i, b1r, b1i = e1
        a2r, a2i, b2r, b2i = e2
        return (a1r * a2r - a1i * a2i,
                a1r * a2i + a1i * a2r,
                a2r * b1r - a2i * b1i + b2r,
                a2r * b1i + a2i * b1r + b2i)

    _, _, s_re, s_im = lax.associative_scan(combine, (a_re, a_im, bu_re, bu_im),
                                            reverse=reverse, axis=1)
    return (jnp.einsum('blgn,gpn->blgp', s_re, c_re.astype(f32))
            - jnp.einsum('blgn,gpn->blgp', s_im, c_im.astype(f32)))


def s5_mixer(u, lam_re, lam_im, log_step, b_re, b_im, c_re, c_im, d, w_glu, b_glu):
    bsz, seq, _ = u.shape
    uf = u.astype(jnp.float32).reshape(bsz, seq, SSM_GROUPS, SSM_GROUP)
    y = (s5_scan(uf, lam_re[0], lam_im[0], log_step[0], b_re[0], b_im[0], c_re[0], c_im[0], False)
         + s5_scan(uf, lam_re[1], lam_im[1], log_step[1], b_re[1], b_im[1], c_re[1], c_im[1], True)
         + d.astype(jnp.float32).reshape(SSM_GROUPS, SSM_GROUP) * uf)
    z = jax.nn.gelu(y.reshape(bsz, seq, SSM_WIDTH), approximate=True)
    out = z * jax.nn.sigmoid(z @ w_glu.astype(jnp.float32) + b_glu.astype(jnp.float32))
    return out.astype(u.dtype)


def banded_attention(q, k, v, sink, rel_bias):
    bsz, seq, _ = q.shape
    nblk = seq // BLOCK
    q = q.reshape(bsz, nblk, BLOCK, N_KV_HEADS, Q_PER_KV, HEAD_DIM)

    def band(t):
        t = t.reshape(bsz, seq, N_KV_HEADS, HEAD_DIM)
        tp = jnp.pad(t, ((0, 0), (BLOCK, BLOCK), (0, 0), (0, 0)))
        tp = tp.reshape(bsz, nblk + 2, BLOCK, N_KV_HEADS, HEAD_DIM)
        return jnp.concatenate([tp[:, :-2], tp[:, 1:-1], tp[:, 2:]], axis=2)

    kb, vb = band(k), band(v)
    s = jnp.einsum('bnqkgd,bnskd->bnkgqs', q, kb).astype(jnp.float32) * (HEAD_DIM ** -0.5)
    qi = jnp.arange(BLOCK, dtype=jnp.int32)[:, None]
    sj = jnp.arange(3 * BLOCK, dtype=jnp.int32)[None, :]
    rel = sj - BLOCK - qi
    bias = rel_bias.astype(jnp.float32)[t5_bucket(rel)]
    bias = bias.transpose(2, 0, 1).reshape(N_KV_HEADS, Q_PER_KV, BLOCK, 3 * BLOCK)
    key_pos = jnp.arange(nblk, dtype=jnp.int32)[:, None] * BLOCK - BLOCK + sj
    valid = (jnp.abs(rel) <= WINDOW)[None] & ((key_pos >= 0) & (key_pos < seq))[:, None, :]
    s = jnp.where(valid[None, :, None, None], s + bias, NEG_INF)
    sink_l = sink.astype(jnp.float32).reshape(1, 1, N_KV_HEADS, Q_PER_KV, 1, 1)
    m = jnp.maximum(jnp.max(s, axis=-1, keepdims=True), sink_l)
    e = jnp.exp(s - m)
    p = e / (jnp.sum(e, axis=-1, keepdims=True) + jnp.exp(sink_l - m))
    o = jnp.einsum('bnkgqs,bnskd->bnqkgd', p.astype(vb.dtype), vb)
    return o.reshape(bsz, seq, ATTN_WIDTH)


def conv_ffn(h, w_up, conv_w, conv_b, w_down):
    a = h @ w_up
    ap = jnp.pad(a, ((0, 0), (1, 1), (0, 0)))
    a = ap[:, :-2] * conv_w[0] + ap[:, 1:-1] * conv_w[1] + ap[:, 2:] * conv_w[2] + conv_b
    value, gate = a[..., :D_FF], a[..., D_FF:]
    return (jax.nn.gelu(gate, approximate=True) * value) @ w_down


def setup_inputs(seed: int = 0) -> dict:
    key = jax.random.key(seed)
    ks = jax.random.split(key, 28)
    nrm = jax.random.normal
    G, N, P = SSM_GROUPS, SSM_STATE, SSM_GROUP
    n_idx = jnp.arange(N, dtype=jnp.float32)

    def gain(k, shape):
        return 1.0 + 0.05 * nrm(k, shape, jnp.float32)

    return {
        'x': nrm(ks[0], (BATCH, SEQ, D_MODEL), jnp.float32),
        'rel_bias': 0.5 * nrm(ks[1], (N_BUCKETS, N_Q_HEADS), jnp.float32),
        'pre_mix_norm': gain(ks[2], (DEPTH, D_MODEL)),
        'w_in': nrm(ks[3], (DEPTH, D_MODEL, IN_WIDTH), jnp.float32) * D_MODEL ** -0.5,
        'lam_re': -0.5 + 0.01 * nrm(ks[4], (DEPTH, 2, G, N), jnp.float32),
        'lam_im': math.pi * n_idx + 0.01 * nrm(ks[5], (DEPTH, 2, G, N), jnp.float32),
        'log_step': jax.random.uniform(ks[6], (DEPTH, 2, G), jnp.float32,
                                       minval=math.log(DT_MIN), maxval=math.log(DT_MAX)),
        'b_re': nrm(ks[7], (DEPTH, 2, G, N, P), jnp.float32) * (2 * P) ** -0.5,
        'b_im': nrm(ks[8], (DEPTH, 2, G, N, P), jnp.float32) * (2 * P) ** -0.5,
        'c_re': nrm(ks[9], (DEPTH, 2, G, P, N), jnp.float32) * (2 * N) ** -0.5,
        'c_im': nrm(ks[10], (DEPTH, 2, G, P, N), jnp.float32) * (2 * N) ** -0.5,
        'ssm_d': nrm(ks[11], (DEPTH, SSM_WIDTH), jnp.float32),
        'w_glu': nrm(ks[12], (DEPTH, SSM_WIDTH, SSM_WIDTH), jnp.float32) * SSM_WIDTH ** -0.5,
        'b_glu': 0.01 * nrm(ks[13], (DEPTH, SSM_WIDTH), jnp.float32),
        'attn_sink': 0.5 * nrm(ks[14], (DEPTH, N_Q_HEADS), jnp.float32),
        'ssm_out_norm': gain(ks[15], (DEPTH, SSM_WIDTH)),
        'attn_out_norm': gain(ks[16], (DEPTH, ATTN_WIDTH)),
        'w_out': nrm(ks[17], (DEPTH, D_MODEL, D_MODEL), jnp.float32) * D_MODEL ** -0.5,
        'post_mix_norm': gain(ks[18], (DEPTH, D_MODEL)),
        'pre_ffn_norm': gain(ks[19], (DEPTH, D_MODEL)),
        'w_up': nrm(ks[20], (DEPTH, D_MODEL, 2 * D_FF), jnp.float32) * D_MODEL ** -0.5,
        'conv_w': nrm(ks[21], (DEPTH, CONV_WIDTH, 2 * D_FF), jnp.float32) * CONV_WIDTH ** -0.5,
        'conv_b': 0.01 * nrm(ks[22], (DEPTH, 2 * D_FF), jnp.float32),
        'w_down': nrm(ks[23], (DEPTH, D_FF, D_MODEL), jnp.float32) * D_FF ** -0.5,
        'post_ffn_norm': gain(ks[24], (DEPTH, D_MODEL)),
    }


def reference(x, rel_bias, pre_mix_norm, w_in, lam_re, lam_im, log_step, b_re, b_im, c_re, c_im,
              ssm_d, w_glu, b_glu, attn_sink, ssm_out_norm, attn_out_norm, w_out, post_mix_norm,
              pre_ffn_norm, w_up, conv_w, conv_b, w_down, post_ffn_norm):
    for l in range(DEPTH):
        h = rms_norm(x, pre_mix_norm[l])
        proj = h @ w_in[l]
        u = proj[..., :SSM_WIDTH]
        q = proj[..., SSM_WIDTH:SSM_WIDTH + ATTN_WIDTH]
        k = proj[..., SSM_WIDTH + ATTN_WIDTH:SSM_WIDTH + ATTN_WIDTH + KV_WIDTH]
        v = proj[..., SSM_WIDTH + ATTN_WIDTH + KV_WIDTH:]
        y_ssm = s5_mixer(u, lam_re[l], lam_im[l], log_step[l], b_re[l], b_im[l], c_re[l], c_im[l],
                         ssm_d[l], w_glu[l], b_glu[l])
        y_att = banded_attention(q, k, v, attn_sink[l], rel_bias)
        merged = jnp.concatenate([rms_norm(y_ssm, ssm_out_norm[l]),
                                  rms_norm(y_att, attn_out_norm[l])], axis=-1) @ w_out[l]
        x = x + rms_norm(merged, post_mix_norm[l])
        h = rms_norm(x, pre_ffn_norm[l])
        x = x + rms_norm(conv_ffn(h, w_up[l], conv_w[l], conv_b[l], w_down[l]), post_ffn_norm[l])
    return x
```

```python
import math
from contextlib import ExitStack
import numpy as np
import jax
import jax.numpy as jnp
import concourse.bass as bass
import concourse.mybir as mybir
from concourse.bass_utils import run_bass_kernel_spmd

F32 = mybir.dt.float32
BF16 = mybir.dt.bfloat16
I32 = mybir.dt.int32
AF = mybir.ActivationFunctionType
ALU = mybir.AluOpType

D_MODEL = 1024
SEQ = 2048
DEPTH = 4
SSM_W = 512
NG = 32
NST = 64
D_FF = 2816
NFB = 22
IN_W = 1280
EPS = 1e-6
JC = 4
LCH = 8 * JC
NSUB = SEQ // 8
NCH = NSUB // JC
NT = SEQ // 128
TWO_PI = 2.0 * math.pi

EC_TB = 0
EC_TC = EC_TB + 8
EC_SB = EC_TC + 8 * JC
EC_CC = EC_SB + 8 * JC
EC_L = EC_CC + 8 * JC
NEC = EC_L + 1
ANG_SHIFT = 64

MA_COLS = 2 * JC * 128
MB_COLS = 4 * JC * 128


class Buf:
    __slots__ = ("w", "r", "name")

    def __init__(self, name=""):
        self.w = None
        self.r = {}
        self.name = name


class Eng:
    def __init__(self, K, name, eng, is_pe=False):
        self.K = K
        self.name = name
        self.eng = eng
        self.is_pe = is_pe
        self.sems = []
        self.ep = -1
        self.cnt = 0
        self.seen = {}
        self.pending = []
        self._new_epoch()

    def _new_epoch(self):
        self.ep += 1
        self.cnt = 0
        self.sems.append(self.K.nc.alloc_semaphore(f"s_{self.name}_{self.ep}"))

class Kern:
    def __init__(self, nc):
        self.nc = nc
        self.pe = Eng(self, "pe", nc.tensor, True)
        self.act = Eng(self, "act", nc.scalar)
        self.dve = Eng(self, "dve", nc.vector)
        self.pool = Eng(self, "pool", nc.gpsimd)
        self.sp = Eng(self, "sp", nc.sync)
        self.engs = [self.pe, self.act, self.dve, self.pool, self.sp]
        self.slots = {}
        for e, n in ((self.sp, 20), (self.pool, 6), (self.act, 6)):
            self.slots[e.name] = [[nc.alloc_semaphore(f"d_{e.name}_{i}"), 0, ("dma", e.name, i)] for i in range(n)]
        self.slot_i = {k: 0 for k in self.slots}
        self.n_ins = 0

    def _wait(self, E, tick):
        sem, val, key = tick
        if key[0] == E.name and E.is_pe:
            return
        if E.seen.get(key, 0) >= val:
            return
        if key[0] != "dma":
            for (k2, v2) in E.seen.items():
                if k2[0] == key[0] and k2[0] != "dma" and k2[1] > key[1]:
                    return
        E.eng.wait_ge(sem, val)
        E.seen[key] = val
        self.n_ins += 1

    def _deps(self, E, reads, writes):
        need = []
        for b in reads:
            if b.w is not None:
                need.append(b.w)
        for b in writes:
            if b.w is not None:
                need.append(b.w)
            need.extend(b.r.values())
        for t in need:
            self._wait(E, t)

    def op(self, E, fn, reads=(), writes=(), inc=True):
        if E.cnt >= 28000 and not E.pending:
            E._new_epoch()
        self._deps(E, reads, writes)
        tick = (E.sems[E.ep], E.cnt + 1, (E.name, E.ep))
        ins = fn()
        self.n_ins += 1
        if inc:
            ins.then_inc(E.sems[E.ep], 1)
            E.cnt += 1
            E.pending = []
        else:
            E.pending.append(1)
        for b in reads:
            b.r[E.name] = tick
        for b in writes:
            b.w = tick
            b.r = {}
        return ins

    def dma(self, E, out, in_, reads=(), writes=(), **kw):
        slots = self.slots[E.name]
        i = self.slot_i[E.name]
        self.slot_i[E.name] = (i + 1) % len(slots)
        s = slots[i]
        if s[1] > 0:
            self._wait(E, (s[0], s[1], s[2]))
        self._deps(E, reads, writes)
        ins = E.eng.dma_start(out=out, in_=in_, **kw)
        s[1] += 16
        ins.then_inc(s[0], 16)
        self.n_ins += 1
        tick = (s[0], s[1], s[2])
        for b in reads:
            b.r[("dma", E.name, i)] = tick
        for b in writes:
            b.w = tick
            b.r = {}
        return ins

    def barrier(self):
        sp = self.sp
        for E in self.engs:
            if E is sp:
                continue
            if E.cnt > 0:
                self._wait(sp, (E.sems[E.ep], E.cnt, (E.name, E.ep)))
        for lst in self.slots.values():
            for s in lst:
                if s[1] > 0:
                    self._wait(sp, (s[0], s[1], s[2]))
        self.op(sp, lambda: sp.eng.nop())
        t = (sp.sems[sp.ep], sp.cnt, (sp.name, sp.ep))
        for E in self.engs:
            if E is not sp:
                self._wait(E, t)


def _t5_bucket(rel):
    n_buckets, max_distance = 32, 128
    half = n_buckets // 2
    max_exact = half // 2
    ret = jnp.where(rel > 0, half, 0)
    n = jnp.abs(rel)
    nf = jnp.maximum(n, 1).astype(jnp.float32)
    large = max_exact + (jnp.log(nf / max_exact) / math.log(max_distance / max_exact)
                         * (half - max_exact)).astype(jnp.int32)
    large = jnp.minimum(large, half - 1)
    return ret + jnp.where(n < max_exact, n, large)


def _host_consts():
    import ml_dtypes
    c = {}
    c["ident_f"] = np.eye(128, dtype=np.float32)
    ex = np.zeros((2, NEC), np.float32)
    for s in range(8):
        ex[0, EC_TB + s] = -s
        ex[1, EC_TB + s] = s
    for d in range(JC):
        for s in range(8):
            ex[0, EC_TC + d * 8 + s] = 8 * d + s
            ex[1, EC_TC + d * 8 + s] = 8 * d - s
            ex[0, EC_SB + d * 8 + s] = LCH - 1 - 8 * d - s
            ex[1, EC_SB + d * 8 + s] = 8 * d + s
            ex[0, EC_CC + d * 8 + s] = 8 * d + s + 1
            ex[1, EC_CC + d * 8 + s] = LCH - 8 * d - s
    ex[:, EC_L] = LCH
    c["extab"] = np.repeat(ex, 64, axis=0).astype(np.float32)
    sp = np.arange(128)[:, None] // 16
    s = np.arange(128)[None, :] // 16
    tm = np.stack([(s >= sp), (sp >= s)], axis=1).astype(np.float32)
    c["tmask"] = np.ascontiguousarray(tm)
    k = np.arange(128)[:, None, None]
    off = (np.arange(3) - 1)[None, :, None]
    q = np.arange(128)[None, None, :]
    rel = (k + 128 * off - q).astype(np.int32)
    with jax.default_device(jax.devices("cpu")[0]):
        bk = np.asarray(_t5_bucket(jnp.asarray(rel)))
    oh = (bk[:, None, :, :] == np.arange(32)[None, :, None, None]).astype(np.float32)
    c["onehot"] = oh.reshape(128, 32, 384).astype(ml_dtypes.bfloat16)
    c["vmask"] = (np.abs(rel) <= 128).astype(np.float32).reshape(128, 384)
    return c


def build(nseq=4, nlayers=DEPTH, debug=None):
    nc = bass.Bass("TRN2", target_bir_lowering=False)
    K = Kern(nc)
    pe, act, dve, pool, sp = K.pe, K.act, K.dve, K.pool, K.sp
    NTOK = nseq * SEQ

    def din(name, shape, dt=F32):
        return nc.dram_tensor(name, list(shape), dt, kind="ExternalInput").ap()

    x_in = din("x", (NTOK, D_MODEL))
    P = {}
    P["rel_bias"] = din("rel_bias", (32, 8))
    P["pre_mix_norm"] = din("pre_mix_norm", (DEPTH, D_MODEL))
    P["w_in"] = din("w_in", (DEPTH, D_MODEL, IN_W))
    P["lam_re"] = din("lam_re", (DEPTH, 2, NG, NST))
    P["lam_im"] = din("lam_im", (DEPTH, 2, NG, NST))
    P["log_step"] = din("log_step", (DEPTH, 2, NG))
    P["b_re"] = din("b_re", (DEPTH, 2, NG, NST, 16))
    P["b_im"] = din("b_im", (DEPTH, 2, NG, NST, 16))
    P["c_re"] = din("c_re", (DEPTH, 2, NG, 16, NST))
    P["c_im"] = din("c_im", (DEPTH, 2, NG, 16, NST))
    P["ssm_d"] = din("ssm_d", (DEPTH, SSM_W))
    P["w_glu"] = din("w_glu", (DEPTH, SSM_W, SSM_W))
    P["b_glu"] = din("b_glu", (DEPTH, SSM_W))
    P["attn_sink"] = din("attn_sink", (DEPTH, 8))
    P["ssm_out_norm"] = din("ssm_out_norm", (DEPTH, SSM_W))
    P["attn_out_norm"] = din("attn_out_norm", (DEPTH, SSM_W))
    P["w_out"] = din("w_out", (DEPTH, D_MODEL, D_MODEL))
    P["post_mix_norm"] = din("post_mix_norm", (DEPTH, D_MODEL))
    P["pre_ffn_norm"] = din("pre_ffn_norm", (DEPTH, D_MODEL))
    P["w_up"] = din("w_up", (DEPTH, D_MODEL, 2 * D_FF))
    P["conv_w"] = din("conv_w", (DEPTH, 3, 2 * D_FF))
    P["conv_b"] = din("conv_b", (DEPTH, 2 * D_FF))
    P["w_down"] = din("w_down", (DEPTH, D_FF, D_MODEL))
    P["post_ffn_norm"] = din("post_ffn_norm", (DEPTH, D_MODEL))
    c_ident = din("ident_f", (128, 128))
    c_extab = din("extab", (128, NEC))
    c_tmask = din("tmask", (128, 2, 128))
    c_onehot = din("onehot", (128, 32, 384), BF16)
    c_vmask = din("vmask", (128, 384))

    out = nc.dram_tensor("out", [NTOK, D_MODEL], F32, kind="ExternalOutput").ap()
    matsA = nc.dram_tensor("matsA", [NG, 128, MA_COLS], BF16).ap()
    matsB = nc.dram_tensor("matsB", [NG, 128, MB_COLS], BF16).ap()
    u_scr = nc.dram_tensor("u_scr", [NG * 16, SEQ], BF16).ap()
    z_scr = nc.dram_tensor("z_scr", [NG * 16, SEQ], BF16).ap()
    eb_scr = nc.dram_tensor("eb_scr", [128, 3 * 2 * 4 * 128], BF16).ap()
    matsA_b = [Buf() for _ in range(NG)]
    matsB_b = [Buf() for _ in range(NG)]
    u_scr_b, z_scr_b = Buf(), Buf()
    out_b = [[Buf() for _ in range(NT)] for _ in range(nseq)]
    dbg = {}
    if debug:
        for nm, (shp, dt_) in debug.items():
          if shp is not None:
            dbg[nm] = nc.dram_tensor("dbg_" + nm, list(shp), dt_, kind="ExternalOutput").ap()

    top = ExitStack()
    with top:
        uid = [0]

        def sb(es, name, shape, dt=F32):
            uid[0] += 1
            return es.enter_context(nc.sbuf_tensor(f"{name}_{uid[0]}", list(shape), dt))

        psum = top.enter_context(nc.psum_tensor("psum", [128, 8, 512], F32))
        ps_b = [Buf(f"ps{i}") for i in range(8)]
        ps_rr = [0]

        def next_ps(n=1):
            if n == 1:
                i = ps_rr[0] % 8
                ps_rr[0] += 1
                return i
            i = ps_rr[0] % 8
            if i % 2:
                i = (i + 1) % 8
            ps_rr[0] = i + 2
            return i

        ident = sb(top, "ident", (128, 128))
        identb = sb(top, "identb", (128, 128), BF16)
        ones_bf = sb(top, "ones_bf", (128, 1), BF16)
        epsb = sb(top, "epsb", (128, 1))
        aL = sb(top, "aL", (128, 2, 2, 32))
        aL_b = Buf("aL")
        cb = Buf("consts")
        ebB = Buf("eb")
        K.dma(sp, ident[:], c_ident, writes=[cb])
        K.op(dve, lambda: nc.vector.tensor_copy(out=identb[:], in_=ident[:]), reads=[cb], writes=[cb])
        K.op(dve, lambda: nc.vector.memset(ones_bf[:], 1.0), writes=[cb])
        K.op(dve, lambda: nc.vector.memset(epsb[:], EPS), writes=[cb])

        with ExitStack() as es:
            oh = sb(es, "oh", (128, 32, 384), BF16)
            eb = sb(es, "eb0", (128, 3, 2, 4, 128), BF16)
            rb = sb(es, "rb", (128, 256))
            vm = sb(es, "vm", (128, 384))
            acc = sb(es, "acc", (128, 8, 384))
            tb = Buf()
            K.dma(sp, oh[:], c_onehot, writes=[tb])
            K.dma(sp, vm[:], c_vmask, writes=[tb])
            K.dma(sp, rb[:], P["rel_bias"].rearrange("b h -> (b h)").partition_broadcast(128), writes=[tb])
            accb = [Buf() for _ in range(8)]
            for h in range(8):
                E = dve
                K.op(E, lambda h=h: nc.vector.tensor_scalar(out=acc[:, h, :], in0=oh[:, 0, :], scalar1=rb[:, h:h + 1],
                                                            scalar2=None, op0=ALU.mult), reads=[tb], writes=[accb[h]])
                for b in range(1, 32):
                    K.op(E, lambda h=h, b=b: nc.vector.scalar_tensor_tensor(
                        out=acc[:, h, :], in0=oh[:, b, :], scalar=rb[:, b * 8 + h:b * 8 + h + 1], in1=acc[:, h, :],
                        op0=ALU.mult, op1=ALU.add), reads=[tb, accb[h]], writes=[accb[h]])
                K.op(act, lambda h=h: nc.scalar.activation(out=acc[:, h, :], in_=acc[:, h, :], func=AF.Exp),
                     reads=[accb[h]], writes=[accb[h]])
                kv, c = h // 4, h % 4
                K.op(dve, lambda h=h, kv=kv, c=c: nc.vector.tensor_tensor(
                    out=eb[:, :, kv, c, :], in0=acc[:, h, :].rearrange("p (o q) -> p o q", o=3),
                    in1=vm[:].rearrange("p (o q) -> p o q", o=3), op=ALU.mult), reads=[accb[h], tb], writes=[ebB])
            K.dma(sp, eb_scr, eb[:].rearrange("p o k c q -> p (o k c q)"), reads=[ebB], writes=[ebB])
            K.barrier()

        for l in range(nlayers):
            x_src = x_in if l == 0 else out
            with ExitStack() as es:
                s5_prologue(nc, K, es, sb, psum, ps_b, next_ps, P, l, ident, c_extab, c_tmask,
                            aL, aL_b, matsA, matsB, matsA_b, matsB_b, dbg)
                K.barrier()
            with ExitStack() as es:
                mixer_phase(nc, K, es, sb, psum, ps_b, next_ps, P, l, nseq, x_src, out, out_b, ident, identb, ones_bf,
                            eb_scr, ebB, epsb, cb, aL, aL_b, matsA, matsB, matsA_b, matsB_b, u_scr, z_scr, u_scr_b, z_scr_b, dbg)
                K.barrier()
            with ExitStack() as es:
              if not (debug and "skip_ffn" in debug):
                ffn_phase(nc, K, es, sb, psum, ps_b, next_ps, P, l, nseq, out, out_b, ident, identb, epsb, cb, dbg)
                K.barrier()
        K.barrier()
    return nc, K


def make_staging(es, sb, tag, n=6, ch=2048):
    return ([sb(es, f"stg_{tag}{i}", (128, ch)) for i in range(n)], [Buf() for _ in range(n)], [0])


def load_cast_weight(nc, K, stage, dst, dst_b, src_rows, ncols, gain, gain_b, engines):
    stg, stg_b, itr = stage
    NS = len(stg)
    CH = 2048
    dq = [K.sp, K.act]
    it = itr[0]
    for r, src in enumerate(src_rows):
        for c0 in range(0, ncols, CH):
            cw = min(CH, ncols - c0)
            i = it % NS
            E = engines[it % len(engines)]
            Q = dq[it % 2]
            it += 1
            K.dma(Q, stg[i][:, 0:cw], src[:, c0:c0 + cw], writes=[stg_b[i]])
            if gain is None:
                if E is K.act:
                    K.op(E, lambda i=i, r=r, c0=c0, cw=cw: nc.scalar.copy(out=dst[:, r, c0:c0 + cw], in_=stg[i][:, 0:cw]),
                         reads=[stg_b[i]], writes=[dst_b])
                else:
                    K.op(E, lambda i=i, r=r, c0=c0, cw=cw, E=E: E.eng.tensor_copy(out=dst[:, r, c0:c0 + cw], in_=stg[i][:, 0:cw]),
                         reads=[stg_b[i]], writes=[dst_b])
            else:
                if E is K.act:
                    K.op(E, lambda i=i, r=r, c0=c0, cw=cw: nc.scalar.activation(
                        out=dst[:, r, c0:c0 + cw], in_=stg[i][:, 0:cw], func=AF.Copy, scale=gain[:, r:r + 1]),
                        reads=[stg_b[i], gain_b], writes=[dst_b])
                else:
                    K.op(E, lambda i=i, r=r, c0=c0, cw=cw, E=E: E.eng.tensor_scalar(
                        out=dst[:, r, c0:c0 + cw], in0=stg[i][:, 0:cw], scalar1=gain[:, r:r + 1], scalar2=None,
                        op0=ALU.mult), reads=[stg_b[i], gain_b], writes=[dst_b])
    itr[0] = it


def load_cols(nc, K, psum, ps_b, next_ps, ident, cb, ld, ld_b, dst, dst_b, src1d, n):
    K.dma(K.sp, ld[0:n, :], src1d.rearrange("(r p) -> r p", p=128), writes=[ld_b])
    b = next_ps()
    K.op(K.pe, lambda: nc.tensor.transpose(psum[:, b, 0:n], ld[0:n, :], ident[0:n, 0:n]), reads=[ld_b, cb], writes=[ps_b[b]])
    K.op(K.dve, lambda: nc.vector.tensor_copy(out=dst, in_=psum[:, b, 0:n]), reads=[ps_b[b]], writes=[dst_b])


def rstd_from_ssq(nc, K, ssq, rs, n, width, epsb, bufs_r, bufs_w):
    K.op(K.act, lambda: nc.scalar.activation(out=rs[:, 0:n], in_=ssq[:, 0:n], func=AF.Sqrt, bias=epsb[:, 0:1],
                                             scale=1.0 / width), reads=bufs_r, writes=bufs_w)
    K.op(K.dve, lambda: nc.vector.reciprocal(out=rs[:, 0:n], in_=rs[:, 0:n]), reads=bufs_w, writes=bufs_w)


def s5_prologue(nc, K, es, sb, psum, ps_b, next_ps, P, l, ident, c_extab, c_tmask,
                aL, aL_b, matsA, matsB, matsA_b, matsB_b, dbg):
    pe, act, dve, pool, sp = K.pe, K.act, K.dve, K.pool, K.sp
    V = nc.vector
    G = NG
    tb = Buf("s5tab")

    def t(name, shape, dt=F32):
        return sb(es, name, shape, dt)

    lamld = t("lamld", (32, 2, 128))
    lam = t("lam", (128, 2, 32))
    ls = t("ls", (128, 32))
    braw = t("braw", (128, 2, 32, 16))
    cld = t("cld", (128, 2, 4, 128))
    craw = t("craw", (128, 2, 32, 16))
    dvec = t("dvec", (128, 32))
    extab = t("extab_sb", (128, NEC))
    tmask = t("tmask_sb", (128, 2, 128))
    K.dma(sp, extab[:], c_extab, writes=[tb])
    K.dma(sp, tmask[:], c_tmask, writes=[tb])
    with nc.allow_non_contiguous_dma(reason="tiny param loads"):
        for ri, nm in enumerate(("lam_re", "lam_im")):
            K.dma(sp, lamld[:, ri, :].rearrange("g (d n) -> g d n", d=2), P[nm][l].rearrange("d g n -> g d n"), writes=[tb])
        for d in range(2):
            K.dma(sp, ls[d * 64:(d + 1) * 64, :], P["log_step"][l, d].partition_broadcast(64), writes=[tb])
            for ri, nm in enumerate(("b_re", "b_im")):
                K.dma(sp, braw[d * 64:(d + 1) * 64, ri, :, :], P[nm][l, d].rearrange("g n q -> n g q"), writes=[tb])
        for ri, nm in enumerate(("c_re", "c_im")):
            for d in range(2):
                K.dma(sp, cld[:, ri, :, d * 64:(d + 1) * 64],
                      P[nm][l, d].rearrange("(t gl) p n -> (gl p) t n", t=4), writes=[tb])
        for s in range(8):
            K.dma(sp, dvec[s * 16:(s + 1) * 16, :], P["ssm_d"][l].rearrange("(g q) -> q g", q=16), writes=[tb])
    for ri in range(2):
        b = next_ps()
        K.op(pe, lambda ri=ri, b=b: nc.tensor.transpose(psum[:, b, 0:32], lamld[:, ri, :], ident[0:32, 0:32]),
             reads=[tb], writes=[ps_b[b]])
        K.op(dve, lambda ri=ri, b=b: V.tensor_copy(out=lam[:, ri, :], in_=psum[:, b, 0:32]), reads=[ps_b[b]], writes=[tb])
        b = next_ps()
        for tt in range(4):
            K.op(pe, lambda ri=ri, b=b, tt=tt: nc.tensor.transpose(psum[:, b, tt * 128:(tt + 1) * 128], cld[:, ri, tt, :], ident[:]),
                 reads=[tb], writes=[ps_b[b]], inc=(tt == 3))
        K.op(dve, lambda ri=ri, b=b: V.tensor_copy(out=craw[:, ri, :, :].rearrange("p g q -> p (g q)"), in_=psum[:, b, :]),
             reads=[ps_b[b]], writes=[tb])

    dt_ = t("dt", (128, 32))
    ar = t("ar", (128, 32))
    ang = t("ang", (128, 32))
    lb = t("lb", (128, 2, 32))
    coef = t("coef", (128, 2, 32))
    tmp = t("tmpa", (128, 4, 32))

    def dv(fn, inc=True):
        K.op(dve, fn, reads=[tb], writes=[tb], inc=inc)

    def ac(fn):
        K.op(act, fn, reads=[tb], writes=[tb])

    ac(lambda: nc.scalar.activation(out=dt_[:], in_=ls[:], func=AF.Exp))
    dv(lambda: V.tensor_tensor(out=ar[:], in0=lam[:, 0, :], in1=dt_[:], op=ALU.mult))
    dv(lambda: V.tensor_tensor(out=ang[:], in0=lam[:, 1, :], in1=dt_[:], op=ALU.mult))

    PR = t("PR", (128, 32, NEC))
    PI = t("PI", (128, 32, NEC))
    A1 = t("A1", (128, 32, NEC))
    A2 = t("A2", (128, 32, NEC))
    A3i = t("A3i", (128, 32, NEC), I32)
    A4 = t("A4", (128, 32, NEC))
    ex_b = extab[:].unsqueeze(1).to_broadcast([128, 32, NEC])

    def bc_g(x):
        return x.unsqueeze(2).to_broadcast([128, 32, NEC])

    dv(lambda: V.tensor_tensor(out=A1[:], in0=ex_b, in1=bc_g(ar[:]), op=ALU.mult))
    ac(lambda: nc.scalar.activation(out=A4[:], in_=A1[:], func=AF.Exp))
    dv(lambda: V.tensor_tensor(out=A1[:], in0=ex_b, in1=bc_g(ang[:]), op=ALU.mult))

    def sin_of(dst, shift):
        dv(lambda: V.tensor_scalar(out=A2[:], in0=A1[:], scalar1=shift + ANG_SHIFT * TWO_PI, scalar2=1.0 / TWO_PI,
                                   op0=ALU.add, op1=ALU.mult))
        dv(lambda: V.tensor_copy(out=A3i[:], in_=A2[:]))
        dv(lambda: V.tensor_copy(out=dst[:], in_=A3i[:]))
        dv(lambda: V.tensor_tensor(out=A2[:], in0=A2[:], in1=dst[:], op=ALU.subtract))
        dv(lambda: V.tensor_scalar(out=dst[:], in0=A2[:], scalar1=0.5, scalar2=None, op0=ALU.is_gt))
        dv(lambda: V.tensor_tensor(out=A2[:], in0=A2[:], in1=dst[:], op=ALU.subtract))
        dv(lambda: V.tensor_scalar(out=dst[:], in0=A2[:], scalar1=-0.5, scalar2=None, op0=ALU.is_lt))
        dv(lambda: V.tensor_tensor(out=A2[:], in0=A2[:], in1=dst[:], op=ALU.add))
        dv(lambda: V.tensor_scalar(out=A2[:], in0=A2[:], scalar1=TWO_PI, scalar2=3.14159, op0=ALU.mult, op1=ALU.min))
        dv(lambda: V.tensor_scalar(out=A2[:], in0=A2[:], scalar1=-3.14159, scalar2=None, op0=ALU.max))
        ac(lambda: nc.scalar.activation(out=dst[:], in_=A2[:], func=AF.Sin))

    sin_of(PI, 0.0)
    sin_of(PR, math.pi / 2)
    dv(lambda: V.tensor_tensor(out=PR[:], in0=PR[:], in1=A4[:], op=ALU.mult))
    dv(lambda: V.tensor_tensor(out=PI[:], in0=PI[:], in1=A4[:], op=ALU.mult))

    ac(lambda: nc.scalar.activation(out=tmp[:, 0, :], in_=ar[:], func=AF.Exp))
    dv(lambda: V.tensor_copy(out=lb[0:64, 0, :], in_=PR[0:64, :, EC_TC + 1]))
    dv(lambda: V.tensor_copy(out=lb[0:64, 1, :], in_=PI[0:64, :, EC_TC + 1]))
    dv(lambda: V.tensor_copy(out=lb[64:128, 0, :], in_=PR[64:128, :, EC_TB + 1]))
    dv(lambda: V.tensor_copy(out=lb[64:128, 1, :], in_=PI[64:128, :, EC_TB + 1]))
    dv(lambda: V.tensor_tensor(out=tmp[:, 0, :], in0=lam[:, 0, :], in1=lam[:, 0, :], op=ALU.mult))
    dv(lambda: V.tensor_tensor(out=tmp[:, 1, :], in0=lam[:, 1, :], in1=lam[:, 1, :], op=ALU.mult))
    dv(lambda: V.tensor_tensor(out=tmp[:, 0, :], in0=tmp[:, 0, :], in1=tmp[:, 1, :], op=ALU.add))
    dv(lambda: V.reciprocal(out=tmp[:, 0, :], in_=tmp[:, 0, :]))
    dv(lambda: V.tensor_scalar(out=tmp[:, 1, :], in0=lb[:, 0, :], scalar1=-1.0, scalar2=None, op0=ALU.add))
    dv(lambda: V.tensor_tensor(out=tmp[:, 2, :], in0=tmp[:, 1, :], in1=lam[:, 0, :], op=ALU.mult))
    dv(lambda: V.tensor_tensor(out=tmp[:, 3, :], in0=lb[:, 1, :], in1=lam[:, 1, :], op=ALU.mult))
    dv(lambda: V.tensor_tensor(out=tmp[:, 2, :], in0=tmp[:, 2, :], in1=tmp[:, 3, :], op=ALU.add))
    dv(lambda: V.tensor_tensor(out=coef[:, 0, :], in0=tmp[:, 2, :], in1=tmp[:, 0, :], op=ALU.mult))
    dv(lambda: V.tensor_tensor(out=tmp[:, 2, :], in0=lb[:, 1, :], in1=lam[:, 0, :], op=ALU.mult))
    dv(lambda: V.tensor_tensor(out=tmp[:, 3, :], in0=tmp[:, 1, :], in1=lam[:, 1, :], op=ALU.mult))
    dv(lambda: V.tensor_tensor(out=tmp[:, 2, :], in0=tmp[:, 2, :], in1=tmp[:, 3, :], op=ALU.subtract))
    dv(lambda: V.tensor_tensor(out=coef[:, 1, :], in0=tmp[:, 2, :], in1=tmp[:, 0, :], op=ALU.mult))
    bb = t("bb", (128, 2, 32, 16))
    t16 = t("t16", (128, 32, 16))

    def bq(x):
        return x.unsqueeze(2).to_broadcast([128, 32, 16])

    dv(lambda: V.tensor_tensor(out=bb[:, 0], in0=braw[:, 0], in1=bq(coef[:, 0, :]), op=ALU.mult))
    dv(lambda: V.tensor_tensor(out=t16[:], in0=braw[:, 1], in1=bq(coef[:, 1, :]), op=ALU.mult))
    dv(lambda: V.tensor_tensor(out=bb[:, 0], in0=bb[:, 0], in1=t16[:], op=ALU.subtract))
    dv(lambda: V.tensor_tensor(out=bb[:, 1], in0=braw[:, 1], in1=bq(coef[:, 0, :]), op=ALU.mult))
    dv(lambda: V.tensor_tensor(out=t16[:], in0=braw[:, 0], in1=bq(coef[:, 1, :]), op=ALU.mult))
    dv(lambda: V.tensor_tensor(out=bb[:, 1], in0=bb[:, 1], in1=t16[:], op=ALU.add))

    if "s5tab" in dbg:
        K.dma(sp, dbg["s5tab"][:, 0:NEC], PR[:, 0, :], reads=[tb])
        K.dma(sp, dbg["s5tab"][:, NEC:2 * NEC], PI[:, 0, :], reads=[tb])
        K.dma(sp, dbg["s5tab"][:, 2 * NEC:2 * NEC + 32], bb[:, 0, 0:2, :].rearrange("p g q -> p (g q)"), reads=[tb])
        K.dma(sp, dbg["s5tab"][:, 2 * NEC + 32:2 * NEC + 64], bb[:, 1, 0:2, :].rearrange("p g q -> p (g q)"), reads=[tb])

    K.op(dve, lambda: V.tensor_copy(out=aL[:, 0, 0, :], in_=PR[:, :, EC_L]), reads=[tb], writes=[aL_b])
    K.op(dve, lambda: V.tensor_copy(out=aL[:, 0, 1, :], in_=PR[:, :, EC_L]), reads=[tb], writes=[aL_b])
    K.op(dve, lambda: V.tensor_scalar(out=aL[:, 1, 0, :], in0=PI[:, :, EC_L], scalar1=-1.0, scalar2=None, op0=ALU.mult), reads=[tb], writes=[aL_b])
    K.op(dve, lambda: V.tensor_copy(out=aL[:, 1, 1, :], in_=PI[:, :, EC_L]), reads=[tb], writes=[aL_b])

    NB = 8 + 8 * JC
    NC8 = 8 * JC
    GB = 2
    xb = [t(f"xb{i}", (128, 2, GB, NB, 16)) for i in range(2)]
    xc = [t(f"xc{i}", (128, 2, GB, NC8, 16)) for i in range(2)]
    xcc = [t(f"xcc{i}", (128, 2, GB, NC8, 16)) for i in range(2)]
    tq = [t(f"tq{i}", (128, GB, NB, 16)) for i in range(2)]
    mA = [t(f"mA{i}", (128, MA_COLS), BF16) for i in range(2)]
    mB = [t(f"mB{i}", (128, MB_COLS), BF16) for i in range(2)]
    dmat = [t(f"dmat{i}", (128, 128)) for i in range(2)]
    t0m = [t(f"t0m{i}", (128, 128)) for i in range(2)]
    xb_b = [Buf() for _ in range(2)]
    xc_b = [Buf() for _ in range(2)]
    xcc_b = [Buf() for _ in range(2)]
    tq_b = [Buf() for _ in range(2)]
    mA_b = [Buf() for _ in range(2)]
    mB_b = [Buf() for _ in range(2)]
    dm_b = [Buf() for _ in range(2)]

    def pwB(x, g0, c0, n):
        return x[:, g0:g0 + GB, c0:c0 + n].unsqueeze(3).to_broadcast([128, GB, n, 16])

    def bbB(ri, g0, n):
        return bb[:, ri, g0:g0 + GB, :].unsqueeze(2).to_broadcast([128, GB, n, 16])

    def ccB(ri, g0, n):
        return craw[:, ri, g0:g0 + GB, :].unsqueeze(2).to_broadcast([128, GB, n, 16])

    for gb in range(G // GB):
        g0 = gb * GB
        i = gb % 2
        E2 = pool if (gb % 2) else dve
        EN = E2.eng

        def tt_(out, in0, in1, op, reads, writes):
            K.op(E2, lambda: EN.tensor_tensor(out=out, in0=in0, in1=in1, op=op), reads=reads, writes=writes)

        for (dst0, c0, n) in ((0, EC_TB, 8), (8, EC_SB, NC8)):
            o_re = xb[i][:, 0, :, dst0:dst0 + n, :]
            o_im = xb[i][:, 1, :, dst0:dst0 + n, :]
            t_ = tq[i][:, :, dst0:dst0 + n, :]
            tt_(o_re, pwB(PR, g0, c0, n), bbB(0, g0, n), ALU.mult, [tb], [xb_b[i]])
            tt_(t_, pwB(PI, g0, c0, n), bbB(1, g0, n), ALU.mult, [tb], [tq_b[i]])
            tt_(o_re, o_re, t_, ALU.subtract, [tq_b[i], xb_b[i]], [xb_b[i]])
            tt_(o_im, pwB(PR, g0, c0, n), bbB(1, g0, n), ALU.mult, [tb], [xb_b[i]])
            tt_(t_, pwB(PI, g0, c0, n), bbB(0, g0, n), ALU.mult, [tb, xb_b[i]], [tq_b[i]])
            tt_(o_im, o_im, t_, ALU.add, [tq_b[i], xb_b[i]], [xb_b[i]])
        for (dstt, dstb, c0) in ((xc, xc_b, EC_TC), (xcc, xcc_b, EC_CC)):
            o_re = dstt[i][:, 0]
            o_im = dstt[i][:, 1]
            t_ = tq[i][:, :, 0:NC8, :]
            tt_(o_re, pwB(PR, g0, c0, NC8), ccB(0, g0, NC8), ALU.mult, [tb], [dstb[i]])
            tt_(t_, pwB(PI, g0, c0, NC8), ccB(1, g0, NC8), ALU.mult, [tb], [tq_b[i]])
            tt_(o_re, o_re, t_, ALU.subtract, [tq_b[i], dstb[i]], [dstb[i]])
            tt_(o_im, pwB(PI, g0, c0, NC8), ccB(0, g0, NC8), ALU.mult, [tb], [dstb[i]])
            tt_(t_, pwB(PR, g0, c0, NC8), ccB(1, g0, NC8), ALU.mult, [tb, dstb[i]], [tq_b[i]])
            tt_(o_im, o_im, t_, ALU.add, [tq_b[i], dstb[i]], [dstb[i]])
            K.op(E2, lambda: EN.tensor_scalar(out=o_im, in0=o_im, scalar1=-1.0, scalar2=None, op0=ALU.mult),
                 reads=[dstb[i]], writes=[dstb[i]])

        for gl in range(GB):
            g = g0 + gl
            j2 = g % 2
            for d in range(2):
                b = next_ps()
                rows = slice(d * 64, (d + 1) * 64)
                for ri in range(2):
                    K.op(pe, lambda: nc.tensor.matmul(
                        psum[:, b, 0:JC * 128],
                        xb[i][rows, ri, gl, 0:8, :].rearrange("p s q -> p (s q)"),
                        xc[i][rows, ri, gl, :, :].rearrange("p c q -> p (c q)"),
                        start=(ri == 0), stop=(ri == 1)),
                        reads=[xb_b[i], xc_b[i]], writes=[ps_b[b]], inc=(ri == 1))
                base = d * JC * 128
                if d == 0:
                    K.op(dve, lambda: V.tensor_scalar(out=dmat[j2][:], in0=ident[:], scalar1=dvec[:, g:g + 1],
                                                      scalar2=None, op0=ALU.mult), reads=[tb], writes=[dm_b[j2]])
                    K.op(dve, lambda: V.tensor_tensor(out=t0m[j2][:], in0=psum[:, b, 0:128], in1=tmask[:, 0, :], op=ALU.mult),
                         reads=[ps_b[b], tb, dm_b[j2]], writes=[dm_b[j2]])
                    K.op(dve, lambda: V.tensor_tensor(out=mB[j2][:, base:base + 128], in0=t0m[j2][:], in1=dmat[j2][:], op=ALU.add),
                         reads=[dm_b[j2]], writes=[mB_b[j2]])
                else:
                    K.op(dve, lambda: V.tensor_tensor(out=mB[j2][:, base:base + 128], in0=psum[:, b, 0:128],
                                                      in1=tmask[:, 1, :], op=ALU.mult),
                         reads=[ps_b[b], tb], writes=[mB_b[j2]])
                if JC > 1:
                    K.op(act, lambda: nc.scalar.copy(out=mB[j2][:, base + 128:base + JC * 128], in_=psum[:, b, 128:JC * 128]),
                         reads=[ps_b[b]], writes=[mB_b[j2]])
            K.op(act, lambda: nc.scalar.copy(out=mB[j2][:, 2 * JC * 128:3 * JC * 128],
                                             in_=xcc[i][:, 0, gl].rearrange("p c q -> p (c q)")),
                 reads=[xcc_b[i]], writes=[mB_b[j2]])
            K.op(act, lambda: nc.scalar.copy(out=mB[j2][:, 3 * JC * 128:4 * JC * 128],
                                             in_=xcc[i][:, 1, gl].rearrange("p c q -> p (c q)")),
                 reads=[xcc_b[i]], writes=[mB_b[j2]])
            K.dma(sp, matsB[g], mB[j2][:], reads=[mB_b[j2]], writes=[matsB_b[g]])
            for ri in range(2):
                b = next_ps()
                for j in range(JC):
                    K.op(pe, lambda: nc.tensor.transpose(
                        psum[:, b, j * 128:(j + 1) * 128],
                        xb[i][:, ri, gl, 8 + 8 * j:16 + 8 * j, :].rearrange("p s q -> p (s q)"), ident[:]),
                        reads=[xb_b[i]], writes=[ps_b[b]], inc=(j == JC - 1))
                K.op(act, lambda: nc.scalar.copy(out=mA[j2][:, ri * JC * 128:(ri + 1) * JC * 128], in_=psum[:, b, 0:JC * 128]),
                     reads=[ps_b[b]], writes=[mA_b[j2]])
            K.dma(sp, matsA[g], mA[j2][:], reads=[mA_b[j2]], writes=[matsA_b[g]])
    if "matsB0" in dbg:
        pass


def mixer_phase(nc, K, es, sb, psum, ps_b, next_ps, P, l, nseq, x_src, out, out_b, ident, identb, ones_bf,
                eb_scr, ebB, epsb, cb, aL, aL_b, matsA, matsB, matsA_b, matsB_b, u_scr, z_scr, u_scr_b, z_scr_b, dbg):
    pe, act, dve, pool, sp = K.pe, K.act, K.dve, K.pool, K.sp
    V = nc.vector
    G = NG
    wi = sb(es, "wi", (128, 8, IN_W), BF16)
    wg = sb(es, "wg", (128, 4, 512), BF16)
    wo = sb(es, "wo", (128, 8, 1024), BF16)
    gpm = sb(es, "gpm", (128, 8))
    gso = sb(es, "gso", (128, 8))
    bglu = sb(es, "bglu", (128, 4))
    esink = sb(es, "esink", (128, 8))
    gpost = sb(es, "gpost", (128, 1024))
    wi_b, wg_b, wo_b, pb = Buf(), Buf(), Buf(), Buf()
    eb = sb(es, "eb", (128, 3, 2, 4, 128), BF16)
    K.dma(sp, eb[:].rearrange("p o k c q -> p (o k c q)"), eb_scr, reads=[ebB], writes=[pb])
    ebB = pb
    with ExitStack() as es1:
        ld = sb(es1, "ldm", (8, 128))
        ld_b = Buf()
        LC = lambda dst, src, n: load_cols(nc, K, psum, ps_b, next_ps, ident, cb, ld, ld_b, dst, pb, src, n)
        LC(gpm[:], P["pre_mix_norm"][l], 8)
        LC(gso[:, 0:4], P["ssm_out_norm"][l], 4)
        LC(gso[:, 4:8], P["attn_out_norm"][l], 4)
        LC(bglu[:], P["b_glu"][l], 4)
        K.barrier()
    K.dma(sp, esink[:], P["attn_sink"][l].partition_broadcast(128), writes=[pb])
    K.dma(sp, gpost[:], P["post_mix_norm"][l].partition_broadcast(128), writes=[pb])
    K.op(act, lambda: nc.scalar.activation(out=esink[:], in_=esink[:], func=AF.Exp), reads=[pb], writes=[pb])
    with ExitStack() as es2:
        engs = [act, dve, pool, act, dve]
        stage = make_staging(es2, sb, "m")
        load_cast_weight(nc, K, stage, wi, wi_b, [P["w_in"][l, k * 128:(k + 1) * 128, :] for k in range(8)], IN_W,
                         gpm, pb, engs)
        load_cast_weight(nc, K, stage, wg, wg_b, [P["w_glu"][l, k * 128:(k + 1) * 128, :] for k in range(4)], 512,
                         None, pb, engs)
        load_cast_weight(nc, K, stage, wo, wo_b, [P["w_out"][l, k * 128:(k + 1) * 128, :] for k in range(8)], 1024,
                         gso, pb, engs)
        K.barrier()

    X1 = sb(es, "X1", (128, 4, SEQ), BF16)
    hTlo = sb(es, "hTlo", (128, 4, SEQ), BF16)
    hThi = sb(es, "hThi", (128, 4, SEQ), BF16)
    qT = sb(es, "qT", (128, 4, SEQ), BF16)
    UY = sb(es, "UY", (128, 4, SEQ), BF16)
    kT = sb(es, "kT", (128, SEQ), BF16)
    vaug = sb(es, "vaug", (128, NT, 2, 66), BF16)
    xt = [sb(es, f"xt{i}", (128, 1024)) for i in range(3)]
    hs = xt
    xo = xt
    junk = sb(es, "junk", (128, 1024), BF16)
    ssq = sb(es, "ssq", (128, 4, NT))
    rs = sb(es, "rs", (128, 4, NT))
    B = lambda n=1: [Buf() for _ in range(n)]
    X1_b, hTlo_b, hThi_b, qT_b, UY_b, kT_b, va_b, H_b, Ein_b = (Buf() for _ in range(9))
    Hf_b, Hb_b = Buf(), Buf()
    xt_b, mAb_b, mBb_b, et_b, pt_b, oa_b, den_b, gate_b, mt_b = B(3), B(2), B(2), B(3), B(6), B(2), B(2), B(2), B(2)
    hs_b = xt_b
    xo_b = xt_b
    junk_b, ssq_b, rs_b, st_b = Buf(), [Buf() for _ in range(4)], [Buf() for _ in range(4)], [Buf(), Buf()]
    K.op(dve, lambda: V.memset(vaug[:, :, :, 64:66], 1.0), writes=[va_b])
    hT_b = [hTlo_b, hThi_b]
    hT = [hTlo, hThi]
    Uf = UY[:].rearrange("p c t -> p (c t)").rearrange("p (g s) -> p g s", g=NG)
    Zf = hTlo[:].rearrange("p c t -> p (c t)").rearrange("p (g s) -> p g s", g=NG)
    ysT, ysT_b = qT, qT_b
    yaT, yaT_b = hThi, hThi_b
    ysq, ysq_b = UY, UY_b

    for s in range(nseq):
        tok0 = s * SEQ
        for tt in range(NT):
            i = tt % 3
            r0 = tok0 + tt * 128
            rd = [out_b[s][tt]] if x_src is out else []
            K.dma(sp, xt[i][:], x_src[r0:r0 + 128, :], reads=rd, writes=[xt_b[i]])
            K.op(act, lambda: nc.scalar.activation(out=junk[:], in_=xt[i][:], func=AF.Square, accum_out=ssq[:, 0, tt:tt + 1]),
                 reads=[xt_b[i]], writes=[junk_b, ssq_b[0]])
            K.op(act, lambda: nc.scalar.activation(out=rs[:, 0, tt:tt + 1], in_=ssq[:, 0, tt:tt + 1], func=AF.Sqrt,
                                                   bias=epsb[:, 0:1], scale=1.0 / D_MODEL), reads=[ssq_b[0], cb], writes=[rs_b[0]])
            K.op(dve, lambda: V.reciprocal(out=rs[:, 0, tt:tt + 1], in_=rs[:, 0, tt:tt + 1]), reads=[rs_b[0]], writes=[rs_b[0]])
            K.op(pool, lambda: nc.gpsimd.tensor_scalar(out=hs[i][:], in0=xt[i][:], scalar1=rs[:, 0, tt:tt + 1], scalar2=None, op0=ALU.mult),
                 reads=[xt_b[i], rs_b[0]], writes=[hs_b[i]])
            b = next_ps(2)
            for c in range(8):
                K.op(pe, lambda: nc.tensor.transpose(psum[:, b + c // 4, (c % 4) * 128:(c % 4 + 1) * 128],
                                                     hs[i][:, c * 128:(c + 1) * 128], ident[:]),
                     reads=[hs_b[i], cb], writes=[ps_b[b], ps_b[b + 1]], inc=(c == 7))
            K.op(dve, lambda: V.tensor_copy(out=hTlo[:, :, tt * 128:(tt + 1) * 128],
                                            in_=psum[:, b, :].rearrange("p (c t) -> p c t", c=4)),
                 reads=[ps_b[b]], writes=[hTlo_b])
            K.op(act, lambda: nc.scalar.copy(out=hThi[:, :, tt * 128:(tt + 1) * 128],
                                             in_=psum[:, b + 1, :].rearrange("p (c t) -> p c t", c=4)),
                 reads=[ps_b[b + 1]], writes=[hThi_b])
        for m in range(9):
            for tg in range(4):
                b = next_ps()
                for k in range(8):
                    K.op(pe, lambda: nc.tensor.matmul(psum[:, b, :], wi[:, k, m * 128:(m + 1) * 128],
                                                      hT[k // 4][:, k % 4, tg * 512:(tg + 1) * 512],
                                                      start=(k == 0), stop=(k == 7)),
                         reads=[wi_b, hT_b[k // 4]], writes=[ps_b[b]], inc=(k == 7))
                if m < 4:
                    K.op(act, lambda: nc.scalar.copy(
                        out=X1[:, m, :].rearrange("p (s sub) -> p sub s", s=8)[:, tg * 64:(tg + 1) * 64, :],
                        in_=psum[:, b, :].rearrange("p (sub s) -> p sub s", s=8)), reads=[ps_b[b]], writes=[X1_b])
                elif m < 8:
                    K.op(dve, lambda: V.tensor_copy(out=qT[:, m - 4, tg * 512:(tg + 1) * 512], in_=psum[:, b, :]),
                         reads=[ps_b[b]], writes=[qT_b])
                else:
                    K.op(act, lambda: nc.scalar.copy(out=kT[:, tg * 512:(tg + 1) * 512], in_=psum[:, b, :]),
                         reads=[ps_b[b]], writes=[kT_b])
        for tt in range(NT):
            b = next_ps()
            for k in range(8):
                K.op(pe, lambda: nc.tensor.matmul(psum[:, b, 0:128], hT[k // 4][:, k % 4, tt * 128:(tt + 1) * 128],
                                                  wi[:, k, 1152:1280], start=(k == 0), stop=(k == 7)),
                     reads=[wi_b, hT_b[k // 4]], writes=[ps_b[b]], inc=(k == 7))
            K.op(dve, lambda: V.tensor_copy(out=vaug[:, tt, :, 0:64], in_=psum[:, b, 0:128].rearrange("p (k d) -> p k d", k=2)),
                 reads=[ps_b[b]], writes=[va_b])
        if s == 0 and "uT" in dbg:
            K.dma(sp, dbg["uT"], X1[:], reads=[X1_b])
            K.dma(sp, dbg["qT"], qT[:], reads=[qT_b])
            K.dma(sp, dbg["kT"], kT[:], reads=[kT_b])
            K.dma(sp, dbg["vaug"], vaug[:], reads=[va_b])
        K.dma(sp, u_scr.rearrange("(c r) t -> r c t", c=4), X1[:], reads=[X1_b], writes=[u_scr_b])
        for s8 in range(8):
            K.dma(sp, Uf[s8 * 16:(s8 + 1) * 16, :, :],
                  u_scr.rearrange("(g q) (s sub) -> s q g sub", q=16, s=8)[s8], reads=[u_scr_b], writes=[UY_b])

        e_s5 = ExitStack()
        H = sb(e_s5, "H", (128, 2, NG, NCH + 1))
        Ein = sb(e_s5, "Ein", (128, 2, NG, NCH), BF16)
        st1 = sb(e_s5, "st1", (128, 2, NG))
        st2 = sb(e_s5, "st2", (128, 2, NG))
        mAb = [sb(e_s5, f"mAb{i}", (128, MA_COLS), BF16) for i in range(2)]
        mBb = [sb(e_s5, f"mBb{i}", (128, MB_COLS), BF16) for i in range(2)]
        e_att = ExitStack()
        et = [sb(e_att, f"et{i}", (128, 512), BF16) for i in range(3)]
        pt = [sb(e_att, f"pt{i}", (128, 512), BF16) for i in range(6)]
        oa = [sb(e_att, f"oa{i}", (128, 512)) for i in range(2)]
        den = [sb(e_att, f"den{i}", (128, 4)) for i in range(2)]

        def att_block(qb):
            kbs = [kb for kb in (qb - 1, qb, qb + 1) if 0 <= kb < NT]
            for kv in range(2):
                rows = slice(kv * 64, (kv + 1) * 64)
                pts = []
                for kb in kbs:
                    ie = K_rr(K, "et", 3)
                    ip = K_rr(K, "pt", 6)
                    b = next_ps()
                    K.op(pe, lambda: nc.tensor.matmul(psum[:, b, :], kT[rows, kb * 128:(kb + 1) * 128],
                                                      qT[rows, :, qb * 128:(qb + 1) * 128], start=True, stop=True),
                         reads=[kT_b, qT_b], writes=[ps_b[b]])
                    K.op(act, lambda: nc.scalar.activation(out=et[ie][:], in_=psum[:, b, :], func=AF.Exp, scale=0.125),
                         reads=[ps_b[b]], writes=[et_b[ie]])
                    K.op(dve, lambda: V.tensor_tensor(out=pt[ip][:], in0=et[ie][:],
                                                      in1=eb[:, kb - qb + 1, kv].rearrange("p c q -> p (c q)"), op=ALU.mult),
                         reads=[et_b[ie], ebB], writes=[pt_b[ip]])
                    pts.append((ip, kb))
                b = next_ps()
                for c in range(4):
                    for n_, (ip, kb) in enumerate(pts):
                        K.op(pe, lambda: nc.tensor.matmul(psum[:, b, c * 65:(c + 1) * 65], pt[ip][:, c * 128:(c + 1) * 128],
                                                          vaug[:, kb, kv, 0:65], start=(n_ == 0), stop=(n_ == len(pts) - 1)),
                             reads=[pt_b[ip], va_b], writes=[ps_b[b]], inc=(c == 3 and n_ == len(pts) - 1))
                io = qb % 2
                ov = psum[:, b, 0:260].rearrange("p (c d) -> p c d", c=4)
                K.op(dve, lambda: V.tensor_tensor(out=den[kv][:], in0=ov[:, :, 64], in1=esink[:, kv * 4:(kv + 1) * 4], op=ALU.add),
                     reads=[ps_b[b], pb], writes=[den_b[kv]])
                K.op(dve, lambda: V.reciprocal(out=den[kv][:], in_=den[kv][:]), reads=[den_b[kv]], writes=[den_b[kv]])
                K.op(dve, lambda: V.tensor_tensor(out=oa[io][:, kv * 256:(kv + 1) * 256].rearrange("p (c d) -> p c d", c=4),
                                                  in0=ov[:, :, 0:64], in1=den[kv][:].unsqueeze(2).to_broadcast([128, 4, 64]),
                                                  op=ALU.mult), reads=[ps_b[b], den_b[kv]], writes=[oa_b[io]])
            K.op(act, lambda: nc.scalar.activation(out=junk[:, 0:512], in_=oa[io][:], func=AF.Square, accum_out=ssq[:, 2, qb:qb + 1]),
                 reads=[oa_b[io]], writes=[junk_b, ssq_b[2]])
            b = next_ps()
            for c in range(4):
                K.op(pe, lambda: nc.tensor.transpose(psum[:, b, c * 128:(c + 1) * 128], oa[io][:, c * 128:(c + 1) * 128], ident[:]),
                     reads=[oa_b[io], cb], writes=[ps_b[b]], inc=(c == 3))
            K.op(act, lambda: nc.scalar.copy(out=yaT[:, :, qb * 128:(qb + 1) * 128], in_=psum[:, b, :].rearrange("p (c t) -> p c t", c=4)),
                 reads=[ps_b[b]], writes=[yaT_b])


        def scan_step(c):
            r = slice(0, 64)
            K.op(dve, lambda: V.tensor_tensor(out=st1[r], in0=H[r, :, :, c], in1=aL[r, 0], op=ALU.mult), reads=[H_b, aL_b, Hf_b], writes=[Hf_b])
            K.op(dve, lambda: V.tensor_tensor(out=H[r, :, :, c + 1], in0=H[r, :, :, c + 1], in1=st1[r], op=ALU.add), reads=[H_b, aL_b, Hf_b], writes=[Hf_b])
            K.op(dve, lambda: V.tensor_tensor(out=st1[r], in0=H[r, ::-1, :, c], in1=aL[r, 1], op=ALU.mult), reads=[H_b, aL_b, Hf_b], writes=[Hf_b])
            K.op(dve, lambda: V.tensor_tensor(out=H[r, :, :, c + 1], in0=H[r, :, :, c + 1], in1=st1[r], op=ALU.add), reads=[H_b, aL_b, Hf_b], writes=[Hf_b])
            r = slice(64, 128)
            cp = NCH - 1 - c
            G_ = nc.gpsimd
            K.op(pool, lambda: G_.tensor_tensor(out=st2[r], in0=H[r, :, :, cp + 1], in1=aL[r, 0], op=ALU.mult), reads=[H_b, aL_b, Hb_b], writes=[Hb_b])
            K.op(pool, lambda: G_.tensor_tensor(out=H[r, :, :, cp], in0=H[r, :, :, cp], in1=st2[r], op=ALU.add), reads=[H_b, aL_b, Hb_b], writes=[Hb_b])
            K.op(pool, lambda: G_.tensor_tensor(out=st2[r], in0=H[r, ::-1, :, cp + 1], in1=aL[r, 1], op=ALU.mult), reads=[H_b, aL_b, Hb_b], writes=[Hb_b])
            K.op(pool, lambda: G_.tensor_tensor(out=H[r, :, :, cp], in0=H[r, :, :, cp], in1=st2[r], op=ALU.add), reads=[H_b, aL_b, Hb_b], writes=[Hb_b])


        K.op(dve, lambda: V.memset(H[0:64, :, :, 0:1], 0.0), writes=[H_b, Hf_b, Hb_b])
        K.op(dve, lambda: V.memset(H[64:128, :, :, NCH:NCH + 1], 0.0), writes=[H_b, Hf_b, Hb_b])
        for g0 in range(0, G, 4):
            b = next_ps()
            for gl in range(4):
                g = g0 + gl
                im = g % 2
                K.dma(sp, mAb[im][:], matsA[g], reads=[matsA_b[g]], writes=[mAb_b[im]])
                for ri in range(2):
                    for j in range(JC):
                        K.op(pe, lambda: nc.tensor.matmul(
                            psum[:, b, (ri * 4 + gl) * NCH:(ri * 4 + gl + 1) * NCH],
                            mAb[im][:, (ri * JC + j) * 128:(ri * JC + j + 1) * 128],
                            Uf[:, g, :].rearrange("p (c j) -> p c j", j=JC)[:, :, j],
                            start=(j == 0), stop=(j == JC - 1)),
                            reads=[mAb_b[im], UY_b], writes=[ps_b[b]], inc=(ri == 1 and j == JC - 1))
            pv = psum[:, b, :].rearrange("p (r g c) -> p r g c", r=2, g=4)
            K.op(act, lambda: nc.scalar.copy(out=H[0:64, :, g0:g0 + 4, 1:NCH + 1], in_=pv[0:64]), reads=[ps_b[b]], writes=[H_b, Hf_b, Hb_b])
            K.op(act, lambda: nc.scalar.copy(out=H[64:128, :, g0:g0 + 4, 0:NCH], in_=pv[64:128]), reads=[ps_b[b]], writes=[H_b, Hf_b, Hb_b])
        spq = NCH // NT
        for qb in range(NT):
            att_block(qb)
            for c in range(qb * spq, (qb + 1) * spq):
                scan_step(c)
        K.op(dve, lambda: V.tensor_copy(out=Ein[0:64], in_=H[0:64, :, :, 0:NCH]), reads=[H_b, Hf_b], writes=[Ein_b])
        K.op(pool, lambda: nc.gpsimd.tensor_copy(out=Ein[64:128], in_=H[64:128, :, :, 1:NCH + 1]), reads=[H_b, Hb_b], writes=[Ein_b])
        for g0 in range(0, G, 2):
            b = next_ps()
            K.op(dve, lambda: V.memset(psum[:, b, :], 0.0), writes=[ps_b[b]])
            for gl in range(2):
                g = g0 + gl
                im = g % 2
                K.dma(sp, mBb[im][:], matsB[g], reads=[matsB_b[g]], writes=[mBb_b[im]])
                Ug = Uf[:, g, :].rearrange("p (c j) -> p c j", j=JC)
                Yg = psum[:, b, gl * 256:(gl + 1) * 256].rearrange("p (c j) -> p c j", j=JC)
                mm = []
                for d in range(JC):
                    mm.append((Yg[:, :, d:JC], mBb[im][:, d * 128:(d + 1) * 128], Ug[:, :, 0:JC - d], [UY_b]))
                    mm.append((Yg[:, :, 0:JC - d], mBb[im][:, (JC + d) * 128:(JC + d + 1) * 128], Ug[:, :, d:JC], [UY_b]))
                for j in range(JC):
                    mm.append((Yg[:, :, j], mBb[im][:, (2 * JC + j) * 128:(2 * JC + j + 1) * 128], Ein[:, 0, g, :], [Ein_b]))
                    mm.append((Yg[:, :, j], mBb[im][:, (3 * JC + j) * 128:(3 * JC + j + 1) * 128], Ein[:, 1, g, :], [Ein_b]))
                for n_, (o_, l_, r_, rd_) in enumerate(mm):
                    K.op(pe, lambda: nc.tensor.matmul(o_, l_, r_, start=False, stop=(n_ == len(mm) - 1), skip_group_check=True),
                         reads=[mBb_b[im]] + rd_, writes=[ps_b[b]], inc=(n_ == len(mm) - 1))
            K.op(act, lambda: nc.scalar.activation(out=Zf[:, g0:g0 + 2, :], in_=psum[:, b, :].rearrange("p (g s) -> p g s", g=2),
                                                   func=AF.Gelu_apprx_tanh), reads=[ps_b[b]], writes=[hTlo_b])
        for s8 in range(8):
            K.dma(sp, z_scr.rearrange("(g q) (s sub) -> s q g sub", q=16, s=8)[s8], Zf[s8 * 16:(s8 + 1) * 16, :, :],
                  reads=[hTlo_b], writes=[z_scr_b])
        K.dma(sp, X1[:], z_scr.rearrange("(c r) t -> r c t", c=4), reads=[z_scr_b], writes=[X1_b])
        K.barrier()
        e_att.close()
        gate = [sb(e_s5, f"gate{i}", (128, 512)) for i in range(2)]
        for m in range(4):
            for tg in range(4):
                b = next_ps()
                ig = (m * 4 + tg) % 2
                for k in range(4):
                    K.op(pe, lambda: nc.tensor.matmul(psum[:, b, :], wg[:, k, m * 128:(m + 1) * 128], X1[:, k, tg * 512:(tg + 1) * 512],
                                                      start=(k == 0), stop=(k == 3)),
                         reads=[wg_b, X1_b], writes=[ps_b[b]], inc=(k == 3))
                K.op(act, lambda: nc.scalar.activation(out=gate[ig][:], in_=psum[:, b, :], func=AF.Sigmoid, bias=bglu[:, m:m + 1]),
                     reads=[ps_b[b], pb], writes=[gate_b[ig]])
                nat = lambda tns: tns[:, m, :].rearrange("p (sub s) -> p s sub", s=8)[:, 2 * tg:2 * tg + 2, :]
                K.op(dve, lambda: V.tensor_tensor(out=nat(ysT), in0=X1[:, m, tg * 512:(tg + 1) * 512].rearrange("p (s sub) -> p s sub", s=2),
                                                  in1=gate[ig][:].rearrange("p (s sub) -> p s sub", s=2), op=ALU.mult),
                     reads=[X1_b, gate_b[ig]], writes=[ysT_b])
        K.op(pool, lambda: nc.gpsimd.tensor_tensor(out=ysq[:], in0=ysT[:], in1=ysT[:], op=ALU.mult), reads=[ysT_b], writes=[ysq_b])

        K.barrier()
        e_s5.close()
        if s == 0 and "ysT" in dbg:
            K.dma(sp, dbg["ysT"], ysT[:], reads=[ysT_b])
            K.dma(sp, dbg["yaT"], yaT[:], reads=[yaT_b])
            K.dma(sp, dbg["zT"], X1[:], reads=[X1_b])
        e_m4 = ExitStack()
        mt = [sb(e_m4, f"mt{i}", (128, 1024)) for i in range(2)]
        b = next_ps()
        for tt in range(NT):
            for k in range(4):
                K.op(pe, lambda: nc.tensor.matmul(psum[:, b, tt:tt + 1], ysq[:, k, tt * 128:(tt + 1) * 128], ones_bf[:, 0:1],
                                                  start=(k == 0), stop=(k == 3)),
                     reads=[ysq_b, cb], writes=[ps_b[b]], inc=(k == 3 and tt == NT - 1))
        K.op(act, lambda: nc.scalar.activation(out=rs[:, 1, :], in_=psum[:, b, 0:NT], func=AF.Sqrt, bias=epsb[:, 0:1], scale=1.0 / SSM_W),
             reads=[ps_b[b], cb], writes=[rs_b[1]])
        K.op(dve, lambda: V.reciprocal(out=rs[:, 1, :], in_=rs[:, 1, :]), reads=[rs_b[1]], writes=[rs_b[1]])
        K.op(act, lambda: nc.scalar.activation(out=rs[:, 2, :], in_=ssq[:, 2, :], func=AF.Sqrt, bias=epsb[:, 0:1], scale=1.0 / SSM_W),
             reads=[ssq_b[2], cb], writes=[rs_b[2]])
        K.op(dve, lambda: V.reciprocal(out=rs[:, 2, :], in_=rs[:, 2, :]), reads=[rs_b[2]], writes=[rs_b[2]])
        for tt in range(NT):
            i = tt % 2
            r0 = tok0 + tt * 128
            ba = next_ps(2)
            for hf in range(2):
                for k in range(4):
                    K.op(pe, lambda: nc.tensor.matmul(psum[:, ba + hf, :], ysT[:, k, tt * 128:(tt + 1) * 128], wo[:, k, hf * 512:(hf + 1) * 512],
                                                      start=(k == 0), stop=(k == 3)),
                         reads=[ysT_b, wo_b], writes=[ps_b[ba + hf]], inc=(k == 3))
            bb_ = next_ps(2)
            for hf in range(2):
                for k in range(4):
                    K.op(pe, lambda: nc.tensor.matmul(psum[:, bb_ + hf, :], yaT[:, k, tt * 128:(tt + 1) * 128], wo[:, 4 + k, hf * 512:(hf + 1) * 512],
                                                      start=(k == 0), stop=(k == 3)),
                         reads=[yaT_b, wo_b], writes=[ps_b[bb_ + hf]], inc=(k == 3))
            K.op(act, lambda: nc.scalar.activation(out=mt[i][:], in_=psum[:, ba:ba + 2, :].rearrange("p a n -> p (a n)"), func=AF.Copy,
                                                   scale=rs[:, 1, tt:tt + 1]), reads=[ps_b[ba], ps_b[ba + 1], rs_b[1]], writes=[mt_b[i]])
            K.op(dve, lambda: V.scalar_tensor_tensor(out=mt[i][:], in0=psum[:, bb_:bb_ + 2, :].rearrange("p a n -> p (a n)"),
                                                     scalar=rs[:, 2, tt:tt + 1], in1=mt[i][:], op0=ALU.mult, op1=ALU.add),
                 reads=[ps_b[bb_], ps_b[bb_ + 1], rs_b[2], mt_b[i]], writes=[mt_b[i]])
            K.op(act, lambda: nc.scalar.activation(out=junk[:], in_=mt[i][:], func=AF.Square, accum_out=ssq[:, 3, tt:tt + 1]),
                 reads=[mt_b[i]], writes=[junk_b, ssq_b[3]])
            K.op(act, lambda: nc.scalar.activation(out=rs[:, 3, tt:tt + 1], in_=ssq[:, 3, tt:tt + 1], func=AF.Sqrt,
                                                   bias=epsb[:, 0:1], scale=1.0 / D_MODEL), reads=[ssq_b[3], cb], writes=[rs_b[3]])
            K.op(dve, lambda: V.reciprocal(out=rs[:, 3, tt:tt + 1], in_=rs[:, 3, tt:tt + 1]), reads=[rs_b[3]], writes=[rs_b[3]])
            K.op(pool, lambda: nc.gpsimd.tensor_tensor(out=mt[i][:], in0=mt[i][:], in1=gpost[:], op=ALU.mult),
                 reads=[mt_b[i], pb], writes=[mt_b[i]])
            rd = [out_b[s][tt]] if x_src is out else []
            K.dma(sp, xt[i][:], x_src[r0:r0 + 128, :], reads=rd, writes=[xt_b[i]])
            K.op(dve, lambda: V.scalar_tensor_tensor(out=xo[i][:], in0=mt[i][:], scalar=rs[:, 3, tt:tt + 1], in1=xt[i][:],
                                                     op0=ALU.mult, op1=ALU.add),
                 reads=[mt_b[i], rs_b[3], xt_b[i]], writes=[xo_b[i]])
            K.dma(sp, out[r0:r0 + 128, :], xo[i][:], reads=[xo_b[i]], writes=[out_b[s][tt]])
        K.barrier()
        e_m4.close()


_RR = {}


def K_rr(K, key, n):
    v = _RR.get(key, 0)
    _RR[key] = (v + 1) % n
    return v


def ffn_phase(nc, K, es, sb, psum, ps_b, next_ps, P, l, nseq, out, out_b, ident, identb, epsb, cb, dbg):
    pe, act, dve, pool, sp = K.pe, K.act, K.dve, K.pool, K.sp
    V = nc.vector
    wu = sb(es, "wu", (128, 8, 2 * D_FF), BF16)
    wd = sb(es, "wd", (128, NFB, 1024), BF16)
    gpf = sb(es, "gpf", (128, 8))
    cw = sb(es, "cw", (128, 4, 2 * NFB))
    gpost = sb(es, "gpost2", (128, 1024))
    wu_b, wd_b, pb = Buf(), Buf(), Buf()
    with ExitStack() as es1:
        ld = sb(es1, "ldf", (2 * NFB, 128))
        ld_b = Buf()
        LC = lambda dst, src, n: load_cols(nc, K, psum, ps_b, next_ps, ident, cb, ld, ld_b, dst, pb, src, n)
        LC(gpf[:], P["pre_ffn_norm"][l], 8)
        for j in range(3):
            LC(cw[:, j, :], P["conv_w"][l, j], 2 * NFB)
        LC(cw[:, 3, :], P["conv_b"][l], 2 * NFB)
        K.barrier()
    K.dma(sp, gpost[:], P["post_ffn_norm"][l].partition_broadcast(128), writes=[pb])
    with ExitStack() as es2:
        engs = [act, dve, pool, act, dve]
        stage = make_staging(es2, sb, "f")
        load_cast_weight(nc, K, stage, wu, wu_b, [P["w_up"][l, k * 128:(k + 1) * 128, :] for k in range(8)], 2 * D_FF,
                         gpf, pb, engs)
        load_cast_weight(nc, K, stage, wd, wd_b, [P["w_down"][l, k * 128:(k + 1) * 128, :] for k in range(NFB)], 1024,
                         None, pb, engs)
        K.barrier()

    HW = 1026
    h2T = sb(es, "h2T", (128, 8, HW), BF16)
    gbuf = sb(es, "gbuf", (128, NFB, 384), BF16)
    halo = sb(es, "halo", (128, 8, 2), BF16)
    xt = [sb(es, f"fxt{i}", (128, 1024)) for i in range(3)]
    hs = xt
    junk = sb(es, "fjunk", (128, 1024), BF16)
    cv = [sb(es, f"cv{i}", (128, 384)) for i in range(3)]
    cg = [sb(es, f"cg{i}", (128, 384)) for i in range(3)]
    gg = [sb(es, f"gg{i}", (128, 384)) for i in range(3)]
    yt = [sb(es, f"yt{i}", (128, 1024)) for i in range(1)] * 2
    xo = xt
    st = sb(es, "fst", (128, 4))
    B = lambda n=1: [Buf() for _ in range(n)]
    h2T_b, g_b, junk_b, st_b = Buf(), Buf(), Buf(), Buf()
    xt_b, cv_b, cg_b, gg_b = B(3), B(3), B(3), B(3)
    yt_b = B(1) * 2
    hs_b = xt_b
    xo_b = xt_b
    it = [0]

    for s in range(nseq):
        tok0 = s * SEQ
        for half in range(2):
            hs0 = half * 1024
            t_lo = hs0 // 128 - 1
            if half == 1:
                K.op(dve, lambda: V.tensor_copy(out=h2T[:, :, 0:1], in_=halo[:, :, 0:1]), reads=[h2T_b], writes=[h2T_b])
            for tt in range(t_lo, t_lo + 10):
                if tt < 0 or tt >= NT or (half == 1 and tt == t_lo):
                    continue
                i = it[0] % 3
                it[0] += 1
                r0 = tok0 + tt * 128
                K.dma(sp, xt[i][:], out[r0:r0 + 128, :], reads=[out_b[s][tt]], writes=[xt_b[i]])
                K.op(act, lambda: nc.scalar.activation(out=junk[:], in_=xt[i][:], func=AF.Square, accum_out=st[:, 0:1]),
                     reads=[xt_b[i]], writes=[junk_b, st_b])
                K.op(act, lambda: nc.scalar.activation(out=st[:, 1:2], in_=st[:, 0:1], func=AF.Sqrt, bias=epsb[:, 0:1], scale=1.0 / D_MODEL),
                     reads=[st_b, cb], writes=[st_b])
                K.op(dve, lambda: V.reciprocal(out=st[:, 1:2], in_=st[:, 1:2]), reads=[st_b], writes=[st_b])
                K.op(pool, lambda: nc.gpsimd.tensor_scalar(out=hs[i][:], in0=xt[i][:], scalar1=st[:, 1:2], scalar2=None, op0=ALU.mult),
                     reads=[xt_b[i], st_b], writes=[hs_b[i]])
                b = next_ps(2)
                for c in range(8):
                    K.op(pe, lambda: nc.tensor.transpose(psum[:, b + c // 4, (c % 4) * 128:(c % 4 + 1) * 128],
                                                         hs[i][:, c * 128:(c + 1) * 128], ident[:]),
                         reads=[hs_b[i], cb], writes=[ps_b[b], ps_b[b + 1]], inc=(c == 7))
                j0 = tt * 128 - hs0 + 1
                lo, hi = max(j0, 0), min(j0 + 128, HW)
                for hh in range(2):
                    E = dve if hh == 0 else act
                    src = psum[:, b + hh, :].rearrange("p (c t) -> p c t", c=4)[:, :, lo - j0:hi - j0]
                    dst = h2T[:, hh * 4:(hh + 1) * 4, lo:hi]
                    if E is dve:
                        K.op(dve, lambda: V.tensor_copy(out=dst, in_=src), reads=[ps_b[b + hh]], writes=[h2T_b])
                    else:
                        K.op(act, lambda: nc.scalar.copy(out=dst, in_=src), reads=[ps_b[b + hh]], writes=[h2T_b])
            if half == 0:
                K.op(dve, lambda: V.tensor_copy(out=halo[:, :, 0:1], in_=h2T[:, :, 1024:1025]), reads=[h2T_b], writes=[h2T_b])
            for (t0, ln) in ((0, 384), (384, 384), (768, 256)):
                g_first = (hs0 + t0 == 0)
                g_last = (hs0 + t0 + ln == SEQ)
                lo = 1 if g_first else 0
                hi = 1 if g_last else 0
                w0c, w1c = t0 + lo, t0 + ln + 2 - hi
                nW = w1c - w0c
                ctr = 1 - lo
                for fb in range(NFB):
                    i = fb % 3
                    res = []
                    for vg in range(2):
                        rb = vg * NFB + fb
                        b = next_ps()
                        for k in range(8):
                            K.op(pe, lambda: nc.tensor.matmul(psum[:, b, 0:nW], wu[:, k, rb * 128:(rb + 1) * 128],
                                                              h2T[:, k, w0c:w1c], start=(k == 0), stop=(k == 7)),
                                 reads=[wu_b, h2T_b], writes=[ps_b[b]], inc=(k == 7))
                        dst, dst_b = (cv[i], cv_b[i]) if vg == 0 else (cg[i], cg_b[i])
                        K.op(act, lambda: nc.scalar.activation(out=dst[:, 0:ln], in_=psum[:, b, ctr:ctr + ln], func=AF.Identity,
                                                               bias=cw[:, 3, rb:rb + 1], scale=cw[:, 1, rb:rb + 1]),
                             reads=[ps_b[b], pb], writes=[dst_b])
                        K.op(dve, lambda: V.scalar_tensor_tensor(out=dst[:, lo:ln], in0=psum[:, b, ctr + lo - 1:ctr + ln - 1],
                                                                 scalar=cw[:, 0, rb:rb + 1], in1=dst[:, lo:ln], op0=ALU.mult, op1=ALU.add),
                             reads=[ps_b[b], pb, dst_b], writes=[dst_b])
                        K.op(dve, lambda: V.scalar_tensor_tensor(out=dst[:, 0:ln - hi], in0=psum[:, b, ctr + 1:ctr + ln - hi + 1],
                                                                 scalar=cw[:, 2, rb:rb + 1], in1=dst[:, 0:ln - hi], op0=ALU.mult, op1=ALU.add),
                             reads=[ps_b[b], pb, dst_b], writes=[dst_b])
                    K.op(act, lambda: nc.scalar.activation(out=gg[i][:, 0:ln], in_=cg[i][:, 0:ln], func=AF.Gelu_apprx_tanh),
                         reads=[cg_b[i]], writes=[gg_b[i]])
                    K.op(pool, lambda: nc.gpsimd.tensor_tensor(out=gbuf[:, fb, 0:ln], in0=gg[i][:, 0:ln], in1=cv[i][:, 0:ln], op=ALU.mult),
                         reads=[gg_b[i], cv_b[i]], writes=[g_b])
                for ti in range(ln // 128):
                    tt = (hs0 + t0) // 128 + ti
                    r0 = tok0 + tt * 128
                    i = tt % 2
                    b = next_ps(2)
                    for hf in range(2):
                        for fb in range(NFB):
                            K.op(pe, lambda: nc.tensor.matmul(psum[:, b + hf, :], gbuf[:, fb, ti * 128:(ti + 1) * 128],
                                                              wd[:, fb, hf * 512:(hf + 1) * 512], start=(fb == 0), stop=(fb == NFB - 1)),
                                 reads=[g_b, wd_b], writes=[ps_b[b + hf]], inc=(fb == NFB - 1))
                    pv = psum[:, b:b + 2, :].rearrange("p a n -> p (a n)")
                    K.op(act, lambda: nc.scalar.activation(out=junk[:], in_=pv, func=AF.Square, accum_out=st[:, 2:3]),
                         reads=[ps_b[b], ps_b[b + 1]], writes=[junk_b, st_b])
                    K.op(act, lambda: nc.scalar.activation(out=st[:, 3:4], in_=st[:, 2:3], func=AF.Sqrt, bias=epsb[:, 0:1], scale=1.0 / D_MODEL),
                         reads=[st_b, cb], writes=[st_b])
                    K.op(dve, lambda: V.reciprocal(out=st[:, 3:4], in_=st[:, 3:4]), reads=[st_b], writes=[st_b])
                    K.op(dve, lambda: V.tensor_tensor(out=yt[i][:], in0=pv, in1=gpost[:], op=ALU.mult),
                         reads=[ps_b[b], ps_b[b + 1], pb], writes=[yt_b[i]])
                    K.dma(sp, xt[i][:], out[r0:r0 + 128, :], reads=[out_b[s][tt]], writes=[xt_b[i]])
                    K.op(dve, lambda: V.scalar_tensor_tensor(out=xo[i][:], in0=yt[i][:], scalar=st[:, 3:4], in1=xt[i][:],
                                                             op0=ALU.mult, op1=ALU.add),
                         reads=[yt_b[i], st_b, xt_b[i]], writes=[xo_b[i]])
                    K.dma(sp, out[r0:r0 + 128, :], xo[i][:], reads=[xo_b[i]], writes=[out_b[s][tt]])


_CACHE = {}


def _q_perm():
    cols = list(range(512))
    for c in range(4):
        for h in (c, 4 + c):
            cols.extend(range(512 + h * 64, 512 + (h + 1) * 64))
    cols.extend(range(1024, 1280))
    return np.asarray(cols)


def kernel(**inputs):
    n_cores = 8
    x = np.ascontiguousarray(np.asarray(inputs["x"], dtype=np.float32))
    nseq = x.shape[0] // n_cores
    if "nc" not in _CACHE:
        _CACHE["nc"] = build(nseq=nseq)[0]
        _CACHE["consts"] = _host_consts()
    nc = _CACHE["nc"]
    consts = _CACHE["consts"]
    shared = {}
    for k, v in inputs.items():
        if k == "x":
            continue
        a = np.ascontiguousarray(np.asarray(v, dtype=np.float32))
        if k == "w_in":
            a = np.ascontiguousarray(a[:, :, _q_perm()])
        shared[k] = a
    shared.update(consts)
    in_maps = []
    for i in range(n_cores):
        m = dict(shared)
        m["x"] = x[i * nseq:(i + 1) * nseq].reshape(nseq * SEQ, D_MODEL)
        in_maps.append(m)
    res = run_bass_kernel_spmd(nc, in_maps, core_ids=list(range(n_cores)))
    outs = [np.asarray(r["out"]).reshape(nseq, SEQ, D_MODEL) for r in res.results]
    return np.concatenate(outs, axis=0).astype(np.float32)
```

```python
import math
from contextlib import ExitStack
import numpy as np
import jax
import jax.numpy as jnp
import concourse.bass as bass
import concourse.mybir as mybir
from concourse.bass_utils import run_bass_kernel_spmd

F32 = mybir.dt.float32
BF16 = mybir.dt.bfloat16
I32 = mybir.dt.int32
AF = mybir.ActivationFunctionType
ALU = mybir.AluOpType

D_MODEL = 1024
SEQ = 2048
DEPTH = 4
SSM_W = 512
NG = 32
NST = 64
D_FF = 2816
NFB = 22
IN_W = 1280
EPS = 1e-6
JC = 4
LCH = 8 * JC
NSUB = SEQ // 8
NCH = NSUB // JC
NT = SEQ // 128
TWO_PI = 2.0 * math.pi

EC_TB = 0
EC_TC = EC_TB + 8
EC_SB = EC_TC + 8 * JC
EC_CC = EC_SB + 8 * JC
EC_L = EC_CC + 8 * JC
NEC = EC_L + 1
ANG_SHIFT = 64

MA_COLS = 2 * JC * 128
MB_COLS = 4 * JC * 128


class Buf:
    __slots__ = ("w", "r", "name")

    def __init__(self, name=""):
        self.w = None
        self.r = {}
        self.name = name


class Eng:
    def __init__(self, K, name, eng, is_pe=False):
        self.K = K
        self.name = name
        self.eng = eng
        self.is_pe = is_pe
        self.sems = []
        self.ep = -1
        self.cnt = 0
        self.seen = {}
        self.pending = []
        self._new_epoch()

    def _new_epoch(self):
        self.ep += 1
        self.cnt = 0
        self.sems.append(self.K.nc.alloc_semaphore(f"s_{self.name}_{self.ep}"))

class Kern:
    def __init__(self, nc):
        self.nc = nc
        self.pe = Eng(self, "pe", nc.tensor, True)
        self.act = Eng(self, "act", nc.scalar)
        self.dve = Eng(self, "dve", nc.vector)
        self.pool = Eng(self, "pool", nc.gpsimd)
        self.sp = Eng(self, "sp", nc.sync)
        self.engs = [self.pe, self.act, self.dve, self.pool, self.sp]
        self.slots = {}
        for e, n in ((self.sp, 20), (self.pool, 6), (self.act, 6)):
            self.slots[e.name] = [[nc.alloc_semaphore(f"d_{e.name}_{i}"), 0, ("dma", e.name, i)] for i in range(n)]
        self.slot_i = {k: 0 for k in self.slots}
        self.n_ins = 0

    def _wait(self, E, tick):
        sem, val, key = tick
        if key[0] == E.name and E.is_pe:
            return
        if E.seen.get(key, 0) >= val:
            return
        if key[0] != "dma":
            for (k2, v2) in E.seen.items():
                if k2[0] == key[0] and k2[0] != "dma" and k2[1] > key[1]:
                    return
        E.eng.wait_ge(sem, val)
        E.seen[key] = val
        self.n_ins += 1

    def _deps(self, E, reads, writes):
        need = []
        for b in reads:
            if b.w is not None:
                need.append(b.w)
        for b in writes:
            if b.w is not None:
                need.append(b.w)
            need.extend(b.r.values())
        for t in need:
            self._wait(E, t)

    def op(self, E, fn, reads=(), writes=(), inc=True):
        if E.cnt >= 28000 and not E.pending:
            E._new_epoch()
        self._deps(E, reads, writes)
        tick = (E.sems[E.ep], E.cnt + 1, (E.name, E.ep))
        ins = fn()
        self.n_ins += 1
        if inc:
            ins.then_inc(E.sems[E.ep], 1)
            E.cnt += 1
            E.pending = []
        else:
            E.pending.append(1)
        for b in reads:
            b.r[E.name] = tick
        for b in writes:
            b.w = tick
            b.r = {}
        return ins

    def dma(self, E, out, in_, reads=(), writes=(), **kw):
        slots = self.slots[E.name]
        i = self.slot_i[E.name]
        self.slot_i[E.name] = (i + 1) % len(slots)
        s = slots[i]
        if s[1] > 0:
            self._wait(E, (s[0], s[1], s[2]))
        self._deps(E, reads, writes)
        ins = E.eng.dma_start(out=out, in_=in_, **kw)
        s[1] += 16
        ins.then_inc(s[0], 16)
        self.n_ins += 1
        tick = (s[0], s[1], s[2])
        for b in reads:
            b.r[("dma", E.name, i)] = tick
        for b in writes:
            b.w = tick
            b.r = {}
        return ins

    def barrier(self):
        sp = self.sp
        for E in self.engs:
            if E is sp:
                continue
            if E.cnt > 0:
                self._wait(sp, (E.sems[E.ep], E.cnt, (E.name, E.ep)))
        for lst in self.slots.values():
            for s in lst:
                if s[1] > 0:
                    self._wait(sp, (s[0], s[1], s[2]))
        self.op(sp, lambda: sp.eng.nop())
        t = (sp.sems[sp.ep], sp.cnt, (sp.name, sp.ep))
        for E in self.engs:
            if E is not sp:
                self._wait(E, t)


def _t5_bucket(rel):
    n_buckets, max_distance = 32, 128
    half = n_buckets // 2
    max_exact = half // 2
    ret = jnp.where(rel > 0, half, 0)
    n = jnp.abs(rel)
    nf = jnp.maximum(n, 1).astype(jnp.float32)
    large = max_exact + (jnp.log(nf / max_exact) / math.log(max_distance / max_exact)
                         * (half - max_exact)).astype(jnp.int32)
    large = jnp.minimum(large, half - 1)
    return ret + jnp.where(n < max_exact, n, large)


def _host_consts():
    import ml_dtypes
    c = {}
    c["ident_f"] = np.eye(128, dtype=np.float32)
    ex = np.zeros((2, NEC), np.float32)
    for s in range(8):
        ex[0, EC_TB + s] = -s
        ex[1, EC_TB + s] = s
    for d in range(JC):
        for s in range(8):
            ex[0, EC_TC + d * 8 + s] = 8 * d + s
            ex[1, EC_TC + d * 8 + s] = 8 * d - s
            ex[0, EC_SB + d * 8 + s] = LCH - 1 - 8 * d - s
            ex[1, EC_SB + d * 8 + s] = 8 * d + s
            ex[0, EC_CC + d * 8 + s] = 8 * d + s + 1
            ex[1, EC_CC + d * 8 + s] = LCH - 8 * d - s
    ex[:, EC_L] = LCH
    c["extab"] = np.repeat(ex, 64, axis=0).astype(np.float32)
    sp = np.arange(128)[:, None] // 16
    s = np.arange(128)[None, :] // 16
    tm = np.stack([(s >= sp), (sp >= s)], axis=1).astype(np.float32)
    c["tmask"] = np.ascontiguousarray(tm)
    k = np.arange(128)[:, None, None]
    off = (np.arange(3) - 1)[None, :, None]
    q = np.arange(128)[None, None, :]
    rel = (k + 128 * off - q).astype(np.int32)
    with jax.default_device(jax.devices("cpu")[0]):
        bk = np.asarray(_t5_bucket(jnp.asarray(rel)))
    oh = (bk[:, None, :, :] == np.arange(32)[None, :, None, None]).astype(np.float32)
    c["onehot"] = oh.reshape(128, 32, 384).astype(ml_dtypes.bfloat16)
    c["vmask"] = (np.abs(rel) <= 128).astype(np.float32).reshape(128, 384)
    return c


def build(nseq=4, nlayers=DEPTH, debug=None):
    nc = bass.Bass("TRN2", target_bir_lowering=False)
    K = Kern(nc)
    pe, act, dve, pool, sp = K.pe, K.act, K.dve, K.pool, K.sp
    NTOK = nseq * SEQ

    def din(name, shape, dt=F32):
        return nc.dram_tensor(name, list(shape), dt, kind="ExternalInput").ap()

    x_in = din("x", (NTOK, D_MODEL))
    P = {}
    P["rel_bias"] = din("rel_bias", (32, 8))
    P["pre_mix_norm"] = din("pre_mix_norm", (DEPTH, D_MODEL))
    P["w_in"] = din("w_in", (DEPTH, D_MODEL, IN_W))
    P["lam_re"] = din("lam_re", (DEPTH, 2, NG, NST))
    P["lam_im"] = din("lam_im", (DEPTH, 2, NG, NST))
    P["log_step"] = din("log_step", (DEPTH, 2, NG))
    P["b_re"] = din("b_re", (DEPTH, 2, NG, NST, 16))
    P["b_im"] = din("b_im", (DEPTH, 2, NG, NST, 16))
    P["c_re"] = din("c_re", (DEPTH, 2, NG, 16, NST))
    P["c_im"] = din("c_im", (DEPTH, 2, NG, 16, NST))
    P["ssm_d"] = din("ssm_d", (DEPTH, SSM_W))
    P["w_glu"] = din("w_glu", (DEPTH, SSM_W, SSM_W))
    P["b_glu"] = din("b_glu", (DEPTH, SSM_W))
    P["attn_sink"] = din("attn_sink", (DEPTH, 8))
    P["ssm_out_norm"] = din("ssm_out_norm", (DEPTH, SSM_W))
    P["attn_out_norm"] = din("attn_out_norm", (DEPTH, SSM_W))
    P["w_out"] = din("w_out", (DEPTH, D_MODEL, D_MODEL))
    P["post_mix_norm"] = din("post_mix_norm", (DEPTH, D_MODEL))
    P["pre_ffn_norm"] = din("pre_ffn_norm", (DEPTH, D_MODEL))
    P["w_up"] = din("w_up", (DEPTH, D_MODEL, 2 * D_FF))
    P["conv_w"] = din("conv_w", (DEPTH, 3, 2 * D_FF))
    P["conv_b"] = din("conv_b", (DEPTH, 2 * D_FF))
    P["w_down"] = din("w_down", (DEPTH, D_FF, D_MODEL))
    P["post_ffn_norm"] = din("post_ffn_norm", (DEPTH, D_MODEL))
    c_ident = din("ident_f", (128, 128))
    c_extab = din("extab", (128, NEC))
    c_tmask = din("tmask", (128, 2, 128))
    c_onehot = din("onehot", (128, 32, 384), BF16)
    c_vmask = din("vmask", (128, 384))

    out = nc.dram_tensor("out", [NTOK, D_MODEL], F32, kind="ExternalOutput").ap()
    matsA = nc.dram_tensor("matsA", [NG, 128, MA_COLS], BF16).ap()
    matsB = nc.dram_tensor("matsB", [NG, 128, MB_COLS], BF16).ap()
    u_scr = nc.dram_tensor("u_scr", [NG * 16, SEQ], BF16).ap()
    z_scr = nc.dram_tensor("z_scr", [NG * 16, SEQ], BF16).ap()
    eb_scr = nc.dram_tensor("eb_scr", [128, 3 * 2 * 4 * 128], BF16).ap()
    matsA_b = [Buf() for _ in range(NG)]
    matsB_b = [Buf() for _ in range(NG)]
    u_scr_b, z_scr_b = Buf(), Buf()
    out_b = [[Buf() for _ in range(NT)] for _ in range(nseq)]
    dbg = {}
    if debug:
        for nm, (shp, dt_) in debug.items():
          if shp is not None:
            dbg[nm] = nc.dram_tensor("dbg_" + nm, list(shp), dt_, kind="ExternalOutput").ap()

    top = ExitStack()
    with top:
        uid = [0]

        def sb(es, name, shape, dt=F32):
            uid[0] += 1
            return es.enter_context(nc.sbuf_tensor(f"{name}_{uid[0]}", list(shape), dt))

        psum = top.enter_context(nc.psum_tensor("psum", [128, 8, 512], F32))
        ps_b = [Buf(f"ps{i}") for i in range(8)]
        ps_rr = [0]

        def next_ps(n=1):
            if n == 1:
                i = ps_rr[0] % 8
                ps_rr[0] += 1
                return i
            i = ps_rr[0] % 8
            if i % 2:
                i = (i + 1) % 8
            ps_rr[0] = i + 2
            return i

        ident = sb(top, "ident", (128, 128))
        identb = sb(top, "identb", (128, 128), BF16)
        ones_bf = sb(top, "ones_bf", (128, 1), BF16)
        epsb = sb(top, "epsb", (128, 1))
        aL = sb(top, "aL", (128, 2, 2, 32))
        aL_b = Buf("aL")
        cb = Buf("consts")
        ebB = Buf("eb")
        K.dma(sp, ident[:], c_ident, writes=[cb])
        K.op(dve, lambda: nc.vector.tensor_copy(out=identb[:], in_=ident[:]), reads=[cb], writes=[cb])
        K.op(dve, lambda: nc.vector.memset(ones_bf[:], 1.0), writes=[cb])
        K.op(dve, lambda: nc.vector.memset(epsb[:], EPS), writes=[cb])

        with ExitStack() as es:
            oh = sb(es, "oh", (128, 32, 384), BF16)
            eb = sb(es, "eb0", (128, 3, 2, 4, 128), BF16)
            rb = sb(es, "rb", (128, 256))
            vm = sb(es, "vm", (128, 384))
            acc = sb(es, "acc", (128, 8, 384))
            tb = Buf()
            K.dma(sp, oh[:], c_onehot, writes=[tb])
            K.dma(sp, vm[:], c_vmask, writes=[tb])
            K.dma(sp, rb[:], P["rel_bias"].rearrange("b h -> (b h)").partition_broadcast(128), writes=[tb])
            accb = [Buf() for _ in range(8)]
            for h in range(8):
                E = dve
                K.op(E, lambda h=h: nc.vector.tensor_scalar(out=acc[:, h, :], in0=oh[:, 0, :], scalar1=rb[:, h:h + 1],
                                                            scalar2=None, op0=ALU.mult), reads=[tb], writes=[accb[h]])
                for b in range(1, 32):
                    K.op(E, lambda h=h, b=b: nc.vector.scalar_tensor_tensor(
                        out=acc[:, h, :], in0=oh[:, b, :], scalar=rb[:, b * 8 + h:b * 8 + h + 1], in1=acc[:, h, :],
                        op0=ALU.mult, op1=ALU.add), reads=[tb, accb[h]], writes=[accb[h]])
                K.op(act, lambda h=h: nc.scalar.activation(out=acc[:, h, :], in_=acc[:, h, :], func=AF.Exp),
                     reads=[accb[h]], writes=[accb[h]])
                kv, c = h // 4, h % 4
                K.op(dve, lambda h=h, kv=kv, c=c: nc.vector.tensor_tensor(
                    out=eb[:, :, kv, c, :], in0=acc[:, h, :].rearrange("p (o q) -> p o q", o=3),
                    in1=vm[:].rearrange("p (o q) -> p o q", o=3), op=ALU.mult), reads=[accb[h], tb], writes=[ebB])
            K.dma(sp, eb_scr, eb[:].rearrange("p o k c q -> p (o k c q)"), reads=[ebB], writes=[ebB])
            K.barrier()

        for l in range(nlayers):
            x_src = x_in if l == 0 else out
            with ExitStack() as es:
                s5_prologue(nc, K, es, sb, psum, ps_b, next_ps, P, l, ident, c_extab, c_tmask,
                            aL, aL_b, matsA, matsB, matsA_b, matsB_b, dbg)
                K.barrier()
            with ExitStack() as es:
                mixer_phase(nc, K, es, sb, psum, ps_b, next_ps, P, l, nseq, x_src, out, out_b, ident, identb, ones_bf,
                            eb_scr, ebB, epsb, cb, aL, aL_b, matsA, matsB, matsA_b, matsB_b, u_scr, z_scr, u_scr_b, z_scr_b, dbg)
                K.barrier()
            with ExitStack() as es:
              if not (debug and "skip_ffn" in debug):
                ffn_phase(nc, K, es, sb, psum, ps_b, next_ps, P, l, nseq, out, out_b, ident, identb, epsb, cb, dbg)
                K.barrier()
        K.barrier()
    return nc, K


def make_staging(es, sb, tag, n=6, ch=2048):
    return ([sb(es, f"stg_{tag}{i}", (128, ch)) for i in range(n)], [Buf() for _ in range(n)], [0])


def load_cast_weight(nc, K, stage, dst, dst_b, src_rows, ncols, gain, gain_b, engines):
    stg, stg_b, itr = stage
    NS = len(stg)
    CH = 2048
    dq = [K.sp, K.act]
    it = itr[0]
    for r, src in enumerate(src_rows):
        for c0 in range(0, ncols, CH):
            cw = min(CH, ncols - c0)
            i = it % NS
            E = engines[it % len(engines)]
            Q = dq[it % 2]
            it += 1
            K.dma(Q, stg[i][:, 0:cw], src[:, c0:c0 + cw], writes=[stg_b[i]])
            if gain is None:
                if E is K.act:
                    K.op(E, lambda i=i, r=r, c0=c0, cw=cw: nc.scalar.copy(out=dst[:, r, c0:c0 + cw], in_=stg[i][:, 0:cw]),
                         reads=[stg_b[i]], writes=[dst_b])
                else:
                    K.op(E, lambda i=i, r=r, c0=c0, cw=cw, E=E: E.eng.tensor_copy(out=dst[:, r, c0:c0 + cw], in_=stg[i][:, 0:cw]),
                         reads=[stg_b[i]], writes=[dst_b])
            else:
                if E is K.act:
                    K.op(E, lambda i=i, r=r, c0=c0, cw=cw: nc.scalar.activation(
                        out=dst[:, r, c0:c0 + cw], in_=stg[i][:, 0:cw], func=AF.Copy, scale=gain[:, r:r + 1]),
                        reads=[stg_b[i], gain_b], writes=[dst_b])
                else:
                    K.op(E, lambda i=i, r=r, c0=c0, cw=cw, E=E: E.eng.tensor_scalar(
                        out=dst[:, r, c0:c0 + cw], in0=stg[i][:, 0:cw], scalar1=gain[:, r:r + 1], scalar2=None,
                        op0=ALU.mult), reads=[stg_b[i], gain_b], writes=[dst_b])
    itr[0] = it


def load_cols(nc, K, psum, ps_b, next_ps, ident, cb, ld, ld_b, dst, dst_b, src1d, n):
    K.dma(K.sp, ld[0:n, :], src1d.rearrange("(r p) -> r p", p=128), writes=[ld_b])
    b = next_ps()
    K.op(K.pe, lambda: nc.tensor.transpose(psum[:, b, 0:n], ld[0:n, :], ident[0:n, 0:n]), reads=[ld_b, cb], writes=[ps_b[b]])
    K.op(K.dve, lambda: nc.vector.tensor_copy(out=dst, in_=psum[:, b, 0:n]), reads=[ps_b[b]], writes=[dst_b])


def rstd_from_ssq(nc, K, ssq, rs, n, width, epsb, bufs_r, bufs_w):
    K.op(K.act, lambda: nc.scalar.activation(out=rs[:, 0:n], in_=ssq[:, 0:n], func=AF.Sqrt, bias=epsb[:, 0:1],
                                             scale=1.0 / width), reads=bufs_r, writes=bufs_w)
    K.op(K.dve, lambda: nc.vector.reciprocal(out=rs[:, 0:n], in_=rs[:, 0:n]), reads=bufs_w, writes=bufs_w)


def s5_prologue(nc, K, es, sb, psum, ps_b, next_ps, P, l, ident, c_extab, c_tmask,
                aL, aL_b, matsA, matsB, matsA_b, matsB_b, dbg):
    pe, act, dve, pool, sp = K.pe, K.act, K.dve, K.pool, K.sp
    V = nc.vector
    G = NG
    tb = Buf("s5tab")

    def t(name, shape, dt=F32):
        return sb(es, name, shape, dt)

    lamld = t("lamld", (32, 2, 128))
    lam = t("lam", (128, 2, 32))
    ls = t("ls", (128, 32))
    braw = t("braw", (128, 2, 32, 16))
    cld = t("cld", (128, 2, 4, 128))
    craw = t("craw", (128, 2, 32, 16))
    dvec = t("dvec", (128, 32))
    extab = t("extab_sb", (128, NEC))
    tmask = t("tmask_sb", (128, 2, 128))
    K.dma(sp, extab[:], c_extab, writes=[tb])
    K.dma(sp, tmask[:], c_tmask, writes=[tb])
    with nc.allow_non_contiguous_dma(reason="tiny param loads"):
        for ri, nm in enumerate(("lam_re", "lam_im")):
            K.dma(sp, lamld[:, ri, :].rearrange("g (d n) -> g d n", d=2), P[nm][l].rearrange("d g n -> g d n"), writes=[tb])
        for d in range(2):
            K.dma(sp, ls[d * 64:(d + 1) * 64, :], P["log_step"][l, d].partition_broadcast(64), writes=[tb])
            for ri, nm in enumerate(("b_re", "b_im")):
                K.dma(sp, braw[d * 64:(d + 1) * 64, ri, :, :], P[nm][l, d].rearrange("g n q -> n g q"), writes=[tb])
        for ri, nm in enumerate(("c_re", "c_im")):
            for d in range(2):
                K.dma(sp, cld[:, ri, :, d * 64:(d + 1) * 64],
                      P[nm][l, d].rearrange("(t gl) p n -> (gl p) t n", t=4), writes=[tb])
        for s in range(8):
            K.dma(sp, dvec[s * 16:(s + 1) * 16, :], P["ssm_d"][l].rearrange("(g q) -> q g", q=16), writes=[tb])
    for ri in range(2):
        b = next_ps()
        K.op(pe, lambda ri=ri, b=b: nc.tensor.transpose(psum[:, b, 0:32], lamld[:, ri, :], ident[0:32, 0:32]),
             reads=[tb], writes=[ps_b[b]])
        K.op(dve, lambda ri=ri, b=b: V.tensor_copy(out=lam[:, ri, :], in_=psum[:, b, 0:32]), reads=[ps_b[b]], writes=[tb])
        b = next_ps()
        for tt in range(4):
            K.op(pe, lambda ri=ri, b=b, tt=tt: nc.tensor.transpose(psum[:, b, tt * 128:(tt + 1) * 128], cld[:, ri, tt, :], ident[:]),
                 reads=[tb], writes=[ps_b[b]], inc=(tt == 3))
        K.op(dve, lambda ri=ri, b=b: V.tensor_copy(out=craw[:, ri, :, :].rearrange("p g q -> p (g q)"), in_=psum[:, b, :]),
             reads=[ps_b[b]], writes=[tb])

    dt_ = t("dt", (128, 32))
    ar = t("ar", (128, 32))
    ang = t("ang", (128, 32))
    lb = t("lb", (128, 2, 32))
    coef = t("coef", (128, 2, 32))
    tmp = t("tmpa", (128, 4, 32))

    def dv(fn, inc=True):
        K.op(dve, fn, reads=[tb], writes=[tb], inc=inc)

    def ac(fn):
        K.op(act, fn, reads=[tb], writes=[tb])

    ac(lambda: nc.scalar.activation(out=dt_[:], in_=ls[:], func=AF.Exp))
    dv(lambda: V.tensor_tensor(out=ar[:], in0=lam[:, 0, :], in1=dt_[:], op=ALU.mult))
    dv(lambda: V.tensor_tensor(out=ang[:], in0=lam[:, 1, :], in1=dt_[:], op=ALU.mult))

    PR = t("PR", (128, 32, NEC))
    PI = t("PI", (128, 32, NEC))
    A1 = t("A1", (128, 32, NEC))
    A2 = t("A2", (128, 32, NEC))
    A3i = t("A3i", (128, 32, NEC), I32)
    A4 = t("A4", (128, 32, NEC))
    ex_b = extab[:].unsqueeze(1).to_broadcast([128, 32, NEC])

    def bc_g(x):
        return x.unsqueeze(2).to_broadcast([128, 32, NEC])

    dv(lambda: V.tensor_tensor(out=A1[:], in0=ex_b, in1=bc_g(ar[:]), op=ALU.mult))
    ac(lambda: nc.scalar.activation(out=A4[:], in_=A1[:], func=AF.Exp))
    dv(lambda: V.tensor_tensor(out=A1[:], in0=ex_b, in1=bc_g(ang[:]), op=ALU.mult))

    def sin_of(dst, shift):
        dv(lambda: V.tensor_scalar(out=A2[:], in0=A1[:], scalar1=shift + ANG_SHIFT * TWO_PI, scalar2=1.0 / TWO_PI,
                                   op0=ALU.add, op1=ALU.mult))
        dv(lambda: V.tensor_copy(out=A3i[:], in_=A2[:]))
        dv(lambda: V.tensor_copy(out=dst[:], in_=A3i[:]))
        dv(lambda: V.tensor_tensor(out=A2[:], in0=A2[:], in1=dst[:], op=ALU.subtract))
        dv(lambda: V.tensor_scalar(out=dst[:], in0=A2[:], scalar1=0.5, scalar2=None, op0=ALU.is_gt))
        dv(lambda: V.tensor_tensor(out=A2[:], in0=A2[:], in1=dst[:], op=ALU.subtract))
        dv(lambda: V.tensor_scalar(out=dst[:], in0=A2[:], scalar1=-0.5, scalar2=None, op0=ALU.is_lt))
        dv(lambda: V.tensor_tensor(out=A2[:], in0=A2[:], in1=dst[:], op=ALU.add))
        dv(lambda: V.tensor_scalar(out=A2[:], in0=A2[:], scalar1=TWO_PI, scalar2=3.14159, op0=ALU.mult, op1=ALU.min))
        dv(lambda: V.tensor_scalar(out=A2[:], in0=A2[:], scalar1=-3.14159, scalar2=None, op0=ALU.max))
        ac(lambda: nc.scalar.activation(out=dst[:], in_=A2[:], func=AF.Sin))

    sin_of(PI, 0.0)
    sin_of(PR, math.pi / 2)
    dv(lambda: V.tensor_tensor(out=PR[:], in0=PR[:], in1=A4[:], op=ALU.mult))
    dv(lambda: V.tensor_tensor(out=PI[:], in0=PI[:], in1=A4[:], op=ALU.mult))

    ac(lambda: nc.scalar.activation(out=tmp[:, 0, :], in_=ar[:], func=AF.Exp))
    dv(lambda: V.tensor_copy(out=lb[0:64, 0, :], in_=PR[0:64, :, EC_TC + 1]))
    dv(lambda: V.tensor_copy(out=lb[0:64, 1, :], in_=PI[0:64, :, EC_TC + 1]))
    dv(lambda: V.tensor_copy(out=lb[64:128, 0, :], in_=PR[64:128, :, EC_TB + 1]))
    dv(lambda: V.tensor_copy(out=lb[64:128, 1, :], in_=PI[64:128, :, EC_TB + 1]))
    dv(lambda: V.tensor_tensor(out=tmp[:, 0, :], in0=lam[:, 0, :], in1=lam[:, 0, :], op=ALU.mult))
    dv(lambda: V.tensor_tensor(out=tmp[:, 1, :], in0=lam[:, 1, :], in1=lam[:, 1, :], op=ALU.mult))
    dv(lambda: V.tensor_tensor(out=tmp[:, 0, :], in0=tmp[:, 0, :], in1=tmp[:, 1, :], op=ALU.add))
    dv(lambda: V.reciprocal(out=tmp[:, 0, :], in_=tmp[:, 0, :]))
    dv(lambda: V.tensor_scalar(out=tmp[:, 1, :], in0=lb[:, 0, :], scalar1=-1.0, scalar2=None, op0=ALU.add))
    dv(lambda: V.tensor_tensor(out=tmp[:, 2, :], in0=tmp[:, 1, :], in1=lam[:, 0, :], op=ALU.mult))
    dv(lambda: V.tensor_tensor(out=tmp[:, 3, :], in0=lb[:, 1, :], in1=lam[:, 1, :], op=ALU.mult))
    dv(lambda: V.tensor_tensor(out=tmp[:, 2, :], in0=tmp[:, 2, :], in1=tmp[:, 3, :], op=ALU.add))
    dv(lambda: V.tensor_tensor(out=coef[:, 0, :], in0=tmp[:, 2, :], in1=tmp[:, 0, :], op=ALU.mult))
    dv(lambda: V.tensor_tensor(out=tmp[:, 2, :], in0=lb[:, 1, :], in1=lam[:, 0, :], op=ALU.mult))
    dv(lambda: V.tensor_tensor(out=tmp[:, 3, :], in0=tmp[:, 1, :], in1=lam[:, 1, :], op=ALU.mult))
    dv(lambda: V.tensor_tensor(out=tmp[:, 2, :], in0=tmp[:, 2, :], in1=tmp[:, 3, :], op=ALU.subtract))
    dv(lambda: V.tensor_tensor(out=coef[:, 1, :], in0=tmp[:, 2, :], in1=tmp[:, 0, :], op=ALU.mult))
    bb = t("bb", (128, 2, 32, 16))
    t16 = t("t16", (128, 32, 16))

    def bq(x):
        return x.unsqueeze(2).to_broadcast([128, 32, 16])

    dv(lambda: V.tensor_tensor(out=bb[:, 0], in0=braw[:, 0], in1=bq(coef[:, 0, :]), op=ALU.mult))
    dv(lambda: V.tensor_tensor(out=t16[:], in0=braw[:, 1], in1=bq(coef[:, 1, :]), op=ALU.mult))
    dv(lambda: V.tensor_tensor(out=bb[:, 0], in0=bb[:, 0], in1=t16[:], op=ALU.subtract))
    dv(lambda: V.tensor_tensor(out=bb[:, 1], in0=braw[:, 1], in1=bq(coef[:, 0, :]), op=ALU.mult))
    dv(lambda: V.tensor_tensor(out=t16[:], in0=braw[:, 0], in1=bq(coef[:, 1, :]), op=ALU.mult))
    dv(lambda: V.tensor_tensor(out=bb[:, 1], in0=bb[:, 1], in1=t16[:], op=ALU.add))

    if "s5tab" in dbg:
        K.dma(sp, dbg["s5tab"][:, 0:NEC], PR[:, 0, :], reads=[tb])
        K.dma(sp, dbg["s5tab"][:, NEC:2 * NEC], PI[:, 0, :], reads=[tb])
        K.dma(sp, dbg["s5tab"][:, 2 * NEC:2 * NEC + 32], bb[:, 0, 0:2, :].rearrange("p g q -> p (g q)"), reads=[tb])
        K.dma(sp, dbg["s5tab"][:, 2 * NEC + 32:2 * NEC + 64], bb[:, 1, 0:2, :].rearrange("p g q -> p (g q)"), reads=[tb])

    K.op(dve, lambda: V.tensor_copy(out=aL[:, 0, 0, :], in_=PR[:, :, EC_L]), reads=[tb], writes=[aL_b])
    K.op(dve, lambda: V.tensor_copy(out=aL[:, 0, 1, :], in_=PR[:, :, EC_L]), reads=[tb], writes=[aL_b])
    K.op(dve, lambda: V.tensor_scalar(out=aL[:, 1, 0, :], in0=PI[:, :, EC_L], scalar1=-1.0, scalar2=None, op0=ALU.mult), reads=[tb], writes=[aL_b])
    K.op(dve, lambda: V.tensor_copy(out=aL[:, 1, 1, :], in_=PI[:, :, EC_L]), reads=[tb], writes=[aL_b])

    NB = 8 + 8 * JC
    NC8 = 8 * JC
    GB = 2
    xb = [t(f"xb{i}", (128, 2, GB, NB, 16)) for i in range(2)]
    xc = [t(f"xc{i}", (128, 2, GB, NC8, 16)) for i in range(2)]
    xcc = [t(f"xcc{i}", (128, 2, GB, NC8, 16)) for i in range(2)]
    tq = [t(f"tq{i}", (128, GB, NB, 16)) for i in range(2)]
    mA = [t(f"mA{i}", (128, MA_COLS), BF16) for i in range(2)]
    mB = [t(f"mB{i}", (128, MB_COLS), BF16) for i in range(2)]
    dmat = [t(f"dmat{i}", (128, 128)) for i in range(2)]
    t0m = [t(f"t0m{i}", (128, 128)) for i in range(2)]
    xb_b = [Buf() for _ in range(2)]
    xc_b = [Buf() for _ in range(2)]
    xcc_b = [Buf() for _ in range(2)]
    tq_b = [Buf() for _ in range(2)]
    mA_b = [Buf() for _ in range(2)]
    mB_b = [Buf() for _ in range(2)]
    dm_b = [Buf() for _ in range(2)]

    def pwB(x, g0, c0, n):
        return x[:, g0:g0 + GB, c0:c0 + n].unsqueeze(3).to_broadcast([128, GB, n, 16])

    def bbB(ri, g0, n):
        return bb[:, ri, g0:g0 + GB, :].unsqueeze(2).to_broadcast([128, GB, n, 16])

    def ccB(ri, g0, n):
        return craw[:, ri, g0:g0 + GB, :].unsqueeze(2).to_broadcast([128, GB, n, 16])

    for gb in range(G // GB):
        g0 = gb * GB
        i = gb % 2
        E2 = pool if (gb % 2) else dve
        EN = E2.eng

        def tt_(out, in0, in1, op, reads, writes):
            K.op(E2, lambda: EN.tensor_tensor(out=out, in0=in0, in1=in1, op=op), reads=reads, writes=writes)

        for (dst0, c0, n) in ((0, EC_TB, 8), (8, EC_SB, NC8)):
            o_re = xb[i][:, 0, :, dst0:dst0 + n, :]
            o_im = xb[i][:, 1, :, dst0:dst0 + n, :]
            t_ = tq[i][:, :, dst0:dst0 + n, :]
            tt_(o_re, pwB(PR, g0, c0, n), bbB(0, g0, n), ALU.mult, [tb], [xb_b[i]])
            tt_(t_, pwB(PI, g0, c0, n), bbB(1, g0, n), ALU.mult, [tb], [tq_b[i]])
            tt_(o_re, o_re, t_, ALU.subtract, [tq_b[i], xb_b[i]], [xb_b[i]])
            tt_(o_im, pwB(PR, g0, c0, n), bbB(1, g0, n), ALU.mult, [tb], [xb_b[i]])
            tt_(t_, pwB(PI, g0, c0, n), bbB(0, g0, n), ALU.mult, [tb, xb_b[i]], [tq_b[i]])
            tt_(o_im, o_im, t_, ALU.add, [tq_b[i], xb_b[i]], [xb_b[i]])
        for (dstt, dstb, c0) in ((xc, xc_b, EC_TC), (xcc, xcc_b, EC_CC)):
            o_re = dstt[i][:, 0]
            o_im = dstt[i][:, 1]
            t_ = tq[i][:, :, 0:NC8, :]
            tt_(o_re, pwB(PR, g0, c0, NC8), ccB(0, g0, NC8), ALU.mult, [tb], [dstb[i]])
            tt_(t_, pwB(PI, g0, c0, NC8), ccB(1, g0, NC8), ALU.mult, [tb], [tq_b[i]])
            tt_(o_re, o_re, t_, ALU.subtract, [tq_b[i], dstb[i]], [dstb[i]])
            tt_(o_im, pwB(PI, g0, c0, NC8), ccB(0, g0, NC8), ALU.mult, [tb], [dstb[i]])
            tt_(t_, pwB(PR, g0, c0, NC8), ccB(1, g0, NC8), ALU.mult, [tb, dstb[i]], [tq_b[i]])
            tt_(o_im, o_im, t_, ALU.add, [tq_b[i], dstb[i]], [dstb[i]])
            K.op(E2, lambda: EN.tensor_scalar(out=o_im, in0=o_im, scalar1=-1.0, scalar2=None, op0=ALU.mult),
                 reads=[dstb[i]], writes=[dstb[i]])

        for gl in range(GB):
            g = g0 + gl
            j2 = g % 2
            for d in range(2):
                b = next_ps()
                rows = slice(d * 64, (d + 1) * 64)
                for ri in range(2):
                    K.op(pe, lambda: nc.tensor.matmul(
                        psum[:, b, 0:JC * 128],
                        xb[i][rows, ri, gl, 0:8, :].rearrange("p s q -> p (s q)"),
                        xc[i][rows, ri, gl, :, :].rearrange("p c q -> p (c q)"),
                        start=(ri == 0), stop=(ri == 1)),
                        reads=[xb_b[i], xc_b[i]], writes=[ps_b[b]], inc=(ri == 1))
                base = d * JC * 128
                if d == 0:
                    K.op(dve, lambda: V.tensor_scalar(out=dmat[j2][:], in0=ident[:], scalar1=dvec[:, g:g + 1],
                                                      scalar2=None, op0=ALU.mult), reads=[tb], writes=[dm_b[j2]])
                    K.op(dve, lambda: V.tensor_tensor(out=t0m[j2][:], in0=psum[:, b, 0:128], in1=tmask[:, 0, :], op=ALU.mult),
                         reads=[ps_b[b], tb, dm_b[j2]], writes=[dm_b[j2]])
                    K.op(dve, lambda: V.tensor_tensor(out=mB[j2][:, base:base + 128], in0=t0m[j2][:], in1=dmat[j2][:], op=ALU.add),
                         reads=[dm_b[j2]], writes=[mB_b[j2]])
                else:
                    K.op(dve, lambda: V.tensor_tensor(out=mB[j2][:, base:base + 128], in0=psum[:, b, 0:128],
                                                      in1=tmask[:, 1, :], op=ALU.mult),
                         reads=[ps_b[b], tb], writes=[mB_b[j2]])
                if JC > 1:
                    K.op(act, lambda: nc.scalar.copy(out=mB[j2][:, base + 128:base + JC * 128], in_=psum[:, b, 128:JC * 128]),
                         reads=[ps_b[b]], writes=[mB_b[j2]])
            K.op(act, lambda: nc.scalar.copy(out=mB[j2][:, 2 * JC * 128:3 * JC * 128],
                                             in_=xcc[i][:, 0, gl].rearrange("p c q -> p (c q)")),
                 reads=[xcc_b[i]], writes=[mB_b[j2]])
            K.op(act, lambda: nc.scalar.copy(out=mB[j2][:, 3 * JC * 128:4 * JC * 128],
                                             in_=xcc[i][:, 1, gl].rearrange("p c q -> p (c q)")),
                 reads=[xcc_b[i]], writes=[mB_b[j2]])
            K.dma(sp, matsB[g], mB[j2][:], reads=[mB_b[j2]], writes=[matsB_b[g]])
            for ri in range(2):
                b = next_ps()
                for j in range(JC):
                    K.op(pe, lambda: nc.tensor.transpose(
                        psum[:, b, j * 128:(j + 1) * 128],
                        xb[i][:, ri, gl, 8 + 8 * j:16 + 8 * j, :].rearrange("p s q -> p (s q)"), ident[:]),
                        reads=[xb_b[i]], writes=[ps_b[b]], inc=(j == JC - 1))
                K.op(act, lambda: nc.scalar.copy(out=mA[j2][:, ri * JC * 128:(ri + 1) * JC * 128], in_=psum[:, b, 0:JC * 128]),
                     reads=[ps_b[b]], writes=[mA_b[j2]])
            K.dma(sp, matsA[g], mA[j2][:], reads=[mA_b[j2]], writes=[matsA_b[g]])
    if "matsB0" in dbg:
        pass


def mixer_phase(nc, K, es, sb, psum, ps_b, next_ps, P, l, nseq, x_src, out, out_b, ident, identb, ones_bf,
                eb_scr, ebB, epsb, cb, aL, aL_b, matsA, matsB, matsA_b, matsB_b, u_scr, z_scr, u_scr_b, z_scr_b, dbg):
    pe, act, dve, pool, sp = K.pe, K.act, K.dve, K.pool, K.sp
    V = nc.vector
    G = NG
    wi = sb(es, "wi", (128, 8, IN_W), BF16)
    wg = sb(es, "wg", (128, 4, 512), BF16)
    wo = sb(es, "wo", (128, 8, 1024), BF16)
    gpm = sb(es, "gpm", (128, 8))
    gso = sb(es, "gso", (128, 8))
    bglu = sb(es, "bglu", (128, 4))
    esink = sb(es, "esink", (128, 8))
    gpost = sb(es, "gpost", (128, 1024))
    wi_b, wg_b, wo_b, pb = Buf(), Buf(), Buf(), Buf()
    eb = sb(es, "eb", (128, 3, 2, 4, 128), BF16)
    K.dma(sp, eb[:].rearrange("p o k c q -> p (o k c q)"), eb_scr, reads=[ebB], writes=[pb])
    ebB = pb
    with ExitStack() as es1:
        ld = sb(es1, "ldm", (8, 128))
        ld_b = Buf()
        LC = lambda dst, src, n: load_cols(nc, K, psum, ps_b, next_ps, ident, cb, ld, ld_b, dst, pb, src, n)
        LC(gpm[:], P["pre_mix_norm"][l], 8)
        LC(gso[:, 0:4], P["ssm_out_norm"][l], 4)
        LC(gso[:, 4:8], P["attn_out_norm"][l], 4)
        LC(bglu[:], P["b_glu"][l], 4)
        K.barrier()
    K.dma(sp, esink[:], P["attn_sink"][l].partition_broadcast(128), writes=[pb])
    K.dma(sp, gpost[:], P["post_mix_norm"][l].partition_broadcast(128), writes=[pb])
    K.op(act, lambda: nc.scalar.activation(out=esink[:], in_=esink[:], func=AF.Exp), reads=[pb], writes=[pb])
    with ExitStack() as es2:
        engs = [act, dve]
        stage = make_staging(es2, sb, "m")
        load_cast_weight(nc, K, stage, wi, wi_b, [P["w_in"][l, k * 128:(k + 1) * 128, :] for k in range(8)], IN_W,
                         gpm, pb, engs)
        load_cast_weight(nc, K, stage, wg, wg_b, [P["w_glu"][l, k * 128:(k + 1) * 128, :] for k in range(4)], 512,
                         None, pb, engs)
        load_cast_weight(nc, K, stage, wo, wo_b, [P["w_out"][l, k * 128:(k + 1) * 128, :] for k in range(8)], 1024,
                         gso, pb, engs)
        K.barrier()

    X1 = sb(es, "X1", (128, 4, SEQ), BF16)
    hTlo = sb(es, "hTlo", (128, 4, SEQ), BF16)
    hThi = sb(es, "hThi", (128, 4, SEQ), BF16)
    qT = sb(es, "qT", (128, 4, SEQ), BF16)
    UY = sb(es, "UY", (128, 4, SEQ), BF16)
    kT = sb(es, "kT", (128, SEQ), BF16)
    vaug = sb(es, "vaug", (128, NT, 2, 66), BF16)
    xt = [sb(es, f"xt{i}", (128, 1024)) for i in range(3)]
    hs = xt
    xo = xt
    junk = sb(es, "junk", (128, 1024), BF16)
    ssq = sb(es, "ssq", (128, 4, NT))
    rs = sb(es, "rs", (128, 4, NT))
    B = lambda n=1: [Buf() for _ in range(n)]
    X1_b, hTlo_b, hThi_b, qT_b, UY_b, kT_b, va_b, H_b, Ein_b = (Buf() for _ in range(9))
    Hf_b, Hb_b = Buf(), Buf()
    xt_b, mAb_b, mBb_b, et_b, pt_b, oa_b, den_b, gate_b, mt_b = B(3), B(2), B(2), B(3), B(6), B(2), B(2), B(2), B(2)
    hs_b = xt_b
    xo_b = xt_b
    junk_b, ssq_b, rs_b, st_b = Buf(), [Buf() for _ in range(4)], [Buf() for _ in range(4)], [Buf(), Buf()]
    m1s_b = [(Buf(), Buf()) for _ in range(4)]
    K.op(dve, lambda: V.memset(vaug[:, :, :, 64:66], 1.0), writes=[va_b])
    hT_b = [hTlo_b, hThi_b]
    hT = [hTlo, hThi]
    Uf = UY[:].rearrange("p c t -> p (c t)").rearrange("p (g s) -> p g s", g=NG)
    Zf = hTlo[:].rearrange("p c t -> p (c t)").rearrange("p (g s) -> p g s", g=NG)
    ysT, ysT_b = qT, qT_b
    yaT, yaT_b = hThi, hThi_b
    ysq, ysq_b = UY, UY_b

    for s in range(nseq):
        tok0 = s * SEQ
        for tt in range(NT):
            i = tt % 3
            r0 = tok0 + tt * 128
            rd = [out_b[s][tt]] if x_src is out else []
            K.dma(sp, xt[i][:], x_src[r0:r0 + 128, :], reads=rd, writes=[xt_b[i]])
            sq_b, r_b = m1s_b[tt % 4]
            K.op(act, lambda: nc.scalar.activation(out=junk[:], in_=xt[i][:], func=AF.Square, accum_out=ssq[:, 0, tt:tt + 1]),
                 reads=[xt_b[i]], writes=[junk_b, sq_b])
            K.op(act, lambda: nc.scalar.activation(out=rs[:, 0, tt:tt + 1], in_=ssq[:, 0, tt:tt + 1], func=AF.Sqrt,
                                                   bias=epsb[:, 0:1], scale=1.0 / D_MODEL), reads=[sq_b, cb], writes=[r_b])
            K.op(dve, lambda: V.reciprocal(out=rs[:, 0, tt:tt + 1], in_=rs[:, 0, tt:tt + 1]), reads=[r_b], writes=[r_b])
            K.op(dve, lambda: V.tensor_scalar(out=hs[i][:], in0=xt[i][:], scalar1=rs[:, 0, tt:tt + 1], scalar2=None, op0=ALU.mult),
                 reads=[xt_b[i], r_b], writes=[hs_b[i]])
            b = next_ps(2)
            for c in range(8):
                K.op(pe, lambda: nc.tensor.transpose(psum[:, b + c // 4, (c % 4) * 128:(c % 4 + 1) * 128],
                                                     hs[i][:, c * 128:(c + 1) * 128], ident[:]),
                     reads=[hs_b[i], cb], writes=[ps_b[b], ps_b[b + 1]], inc=(c == 7))
            K.op(dve, lambda: V.tensor_copy(out=hTlo[:, :, tt * 128:(tt + 1) * 128],
                                            in_=psum[:, b, :].rearrange("p (c t) -> p c t", c=4)),
                 reads=[ps_b[b]], writes=[hTlo_b])
            K.op(act, lambda: nc.scalar.copy(out=hThi[:, :, tt * 128:(tt + 1) * 128],
                                             in_=psum[:, b + 1, :].rearrange("p (c t) -> p c t", c=4)),
                 reads=[ps_b[b + 1]], writes=[hThi_b])
        for m in range(9):
            for tg in range(4):
                b = next_ps()
                for k in range(8):
                    K.op(pe, lambda: nc.tensor.matmul(psum[:, b, :], wi[:, k, m * 128:(m + 1) * 128],
                                                      hT[k // 4][:, k % 4, tg * 512:(tg + 1) * 512],
                                                      start=(k == 0), stop=(k == 7)),
                         reads=[wi_b, hT_b[k // 4]], writes=[ps_b[b]], inc=(k == 7))
                if m < 4:
                    K.op(act, lambda: nc.scalar.copy(
                        out=X1[:, m, :].rearrange("p (s sub) -> p sub s", s=8)[:, tg * 64:(tg + 1) * 64, :],
                        in_=psum[:, b, :].rearrange("p (sub s) -> p sub s", s=8)), reads=[ps_b[b]], writes=[X1_b])
                elif m < 8:
                    K.op(dve, lambda: V.tensor_copy(out=qT[:, m - 4, tg * 512:(tg + 1) * 512], in_=psum[:, b, :]),
                         reads=[ps_b[b]], writes=[qT_b])
                else:
                    K.op(act, lambda: nc.scalar.copy(out=kT[:, tg * 512:(tg + 1) * 512], in_=psum[:, b, :]),
                         reads=[ps_b[b]], writes=[kT_b])
        for tt in range(NT):
            b = next_ps()
            for k in range(8):
                K.op(pe, lambda: nc.tensor.matmul(psum[:, b, 0:128], hT[k // 4][:, k % 4, tt * 128:(tt + 1) * 128],
                                                  wi[:, k, 1152:1280], start=(k == 0), stop=(k == 7)),
                     reads=[wi_b, hT_b[k // 4]], writes=[ps_b[b]], inc=(k == 7))
            K.op(dve, lambda: V.tensor_copy(out=vaug[:, tt, :, 0:64], in_=psum[:, b, 0:128].rearrange("p (k d) -> p k d", k=2)),
                 reads=[ps_b[b]], writes=[va_b])
        if s == 0 and "uT" in dbg:
            K.dma(sp, dbg["uT"], X1[:], reads=[X1_b])
            K.dma(sp, dbg["qT"], qT[:], reads=[qT_b])
            K.dma(sp, dbg["kT"], kT[:], reads=[kT_b])
            K.dma(sp, dbg["vaug"], vaug[:], reads=[va_b])
        K.dma(sp, u_scr.rearrange("(c r) t -> r c t", c=4), X1[:], reads=[X1_b], writes=[u_scr_b])
        for s8 in range(8):
            K.dma(sp, Uf[s8 * 16:(s8 + 1) * 16, :, :],
                  u_scr.rearrange("(g q) (s sub) -> s q g sub", q=16, s=8)[s8], reads=[u_scr_b], writes=[UY_b])

        e_s5 = ExitStack()
        H = sb(e_s5, "H", (128, 2, NG, NCH + 1))
        Ein = sb(e_s5, "Ein", (128, 2, NG, NCH), BF16)
        st1 = sb(e_s5, "st1", (128, 2, NG))
        st2 = sb(e_s5, "st2", (128, 2, NG))
        mAb = [sb(e_s5, f"mAb{i}", (128, MA_COLS), BF16) for i in range(2)]
        mBb = [sb(e_s5, f"mBb{i}", (128, MB_COLS), BF16) for i in range(2)]
        e_att = ExitStack()
        et = [sb(e_att, f"et{i}", (128, 512), BF16) for i in range(3)]
        pt = [sb(e_att, f"pt{i}", (128, 512), BF16) for i in range(6)]
        oa = [sb(e_att, f"oa{i}", (128, 512)) for i in range(2)]
        den = [sb(e_att, f"den{i}", (128, 4)) for i in range(2)]

        def att_block(qb):
            kbs = [kb for kb in (qb - 1, qb, qb + 1) if 0 <= kb < NT]
            for kv in range(2):
                rows = slice(kv * 64, (kv + 1) * 64)
                pts = []
                for kb in kbs:
                    ie = K_rr(K, "et", 3)
                    ip = K_rr(K, "pt", 6)
                    b = next_ps()
                    K.op(pe, lambda: nc.tensor.matmul(psum[:, b, :], kT[rows, kb * 128:(kb + 1) * 128],
                                                      qT[rows, :, qb * 128:(qb + 1) * 128], start=True, stop=True),
                         reads=[kT_b, qT_b], writes=[ps_b[b]])
                    K.op(act, lambda: nc.scalar.activation(out=et[ie][:], in_=psum[:, b, :], func=AF.Exp, scale=0.125),
                         reads=[ps_b[b]], writes=[et_b[ie]])
                    K.op(dve, lambda: V.tensor_tensor(out=pt[ip][:], in0=et[ie][:],
                                                      in1=eb[:, kb - qb + 1, kv].rearrange("p c q -> p (c q)"), op=ALU.mult),
                         reads=[et_b[ie], ebB], writes=[pt_b[ip]])
                    pts.append((ip, kb))
                b = next_ps()
                for c in range(4):
                    for n_, (ip, kb) in enumerate(pts):
                        K.op(pe, lambda: nc.tensor.matmul(psum[:, b, c * 65:(c + 1) * 65], pt[ip][:, c * 128:(c + 1) * 128],
                                                          vaug[:, kb, kv, 0:65], start=(n_ == 0), stop=(n_ == len(pts) - 1)),
                             reads=[pt_b[ip], va_b], writes=[ps_b[b]], inc=(c == 3 and n_ == len(pts) - 1))
                io = qb % 2
                ov = psum[:, b, 0:260].rearrange("p (c d) -> p c d", c=4)
                K.op(dve, lambda: V.tensor_tensor(out=den[kv][:], in0=ov[:, :, 64], in1=esink[:, kv * 4:(kv + 1) * 4], op=ALU.add),
                     reads=[ps_b[b], pb], writes=[den_b[kv]])
                K.op(dve, lambda: V.reciprocal(out=den[kv][:], in_=den[kv][:]), reads=[den_b[kv]], writes=[den_b[kv]])
                K.op(dve, lambda: V.tensor_tensor(out=oa[io][:, kv * 256:(kv + 1) * 256].rearrange("p (c d) -> p c d", c=4),
                                                  in0=ov[:, :, 0:64], in1=den[kv][:].unsqueeze(2).to_broadcast([128, 4, 64]),
                                                  op=ALU.mult), reads=[ps_b[b], den_b[kv]], writes=[oa_b[io]])
            K.op(act, lambda: nc.scalar.activation(out=junk[:, 0:512], in_=oa[io][:], func=AF.Square, accum_out=ssq[:, 2, qb:qb + 1]),
                 reads=[oa_b[io]], writes=[junk_b, ssq_b[2]])
            b = next_ps()
            for c in range(4):
                K.op(pe, lambda: nc.tensor.transpose(psum[:, b, c * 128:(c + 1) * 128], oa[io][:, c * 128:(c + 1) * 128], ident[:]),
                     reads=[oa_b[io], cb], writes=[ps_b[b]], inc=(c == 3))
            K.op(act, lambda: nc.scalar.copy(out=yaT[:, :, qb * 128:(qb + 1) * 128], in_=psum[:, b, :].rearrange("p (c t) -> p c t", c=4)),
                 reads=[ps_b[b]], writes=[yaT_b])


        def scan_step(c):
            r = slice(0, 64)
            K.op(dve, lambda: V.tensor_tensor(out=st1[r], in0=H[r, :, :, c], in1=aL[r, 0], op=ALU.mult), reads=[H_b, aL_b, Hf_b], writes=[Hf_b])
            K.op(dve, lambda: V.tensor_tensor(out=H[r, :, :, c + 1], in0=H[r, :, :, c + 1], in1=st1[r], op=ALU.add), reads=[H_b, aL_b, Hf_b], writes=[Hf_b])
            K.op(dve, lambda: V.tensor_tensor(out=st1[r], in0=H[r, ::-1, :, c], in1=aL[r, 1], op=ALU.mult), reads=[H_b, aL_b, Hf_b], writes=[Hf_b])
            K.op(dve, lambda: V.tensor_tensor(out=H[r, :, :, c + 1], in0=H[r, :, :, c + 1], in1=st1[r], op=ALU.add), reads=[H_b, aL_b, Hf_b], writes=[Hf_b])
            r = slice(64, 128)
            cp = NCH - 1 - c
            G_ = nc.gpsimd
            K.op(pool, lambda: G_.tensor_tensor(out=st2[r], in0=H[r, :, :, cp + 1], in1=aL[r, 0], op=ALU.mult), reads=[H_b, aL_b, Hb_b], writes=[Hb_b])
            K.op(pool, lambda: G_.tensor_tensor(out=H[r, :, :, cp], in0=H[r, :, :, cp], in1=st2[r], op=ALU.add), reads=[H_b, aL_b, Hb_b], writes=[Hb_b])
            K.op(pool, lambda: G_.tensor_tensor(out=st2[r], in0=H[r, ::-1, :, cp + 1], in1=aL[r, 1], op=ALU.mult), reads=[H_b, aL_b, Hb_b], writes=[Hb_b])
            K.op(pool, lambda: G_.tensor_tensor(out=H[r, :, :, cp], in0=H[r, :, :, cp], in1=st2[r], op=ALU.add), reads=[H_b, aL_b, Hb_b], writes=[Hb_b])


        K.op(dve, lambda: V.memset(H[0:64, :, :, 0:1], 0.0), writes=[H_b, Hf_b, Hb_b])
        K.op(dve, lambda: V.memset(H[64:128, :, :, NCH:NCH + 1], 0.0), writes=[H_b, Hf_b, Hb_b])
        for g0 in range(0, G, 4):
            b = next_ps()
            for gl in range(4):
                g = g0 + gl
                im = g % 2
                K.dma(sp, mAb[im][:], matsA[g], reads=[matsA_b[g]], writes=[mAb_b[im]])
                for ri in range(2):
                    for j in range(JC):
                        K.op(pe, lambda: nc.tensor.matmul(
                            psum[:, b, (ri * 4 + gl) * NCH:(ri * 4 + gl + 1) * NCH],
                            mAb[im][:, (ri * JC + j) * 128:(ri * JC + j + 1) * 128],
                            Uf[:, g, :].rearrange("p (c j) -> p c j", j=JC)[:, :, j],
                            start=(j == 0), stop=(j == JC - 1)),
                            reads=[mAb_b[im], UY_b], writes=[ps_b[b]], inc=(ri == 1 and j == JC - 1))
            pv = psum[:, b, :].rearrange("p (r g c) -> p r g c", r=2, g=4)
            K.op(act, lambda: nc.scalar.copy(out=H[0:64, :, g0:g0 + 4, 1:NCH + 1], in_=pv[0:64]), reads=[ps_b[b]], writes=[H_b, Hf_b, Hb_b])
            K.op(act, lambda: nc.scalar.copy(out=H[64:128, :, g0:g0 + 4, 0:NCH], in_=pv[64:128]), reads=[ps_b[b]], writes=[H_b, Hf_b, Hb_b])
        spq = NCH // NT
        for qb in range(NT):
            att_block(qb)
            for c in range(qb * spq, (qb + 1) * spq):
                scan_step(c)
        K.op(dve, lambda: V.tensor_copy(out=Ein[0:64], in_=H[0:64, :, :, 0:NCH]), reads=[H_b, Hf_b], writes=[Ein_b])
        K.op(pool, lambda: nc.gpsimd.tensor_copy(out=Ein[64:128], in_=H[64:128, :, :, 1:NCH + 1]), reads=[H_b, Hb_b], writes=[Ein_b])
        for g0 in range(0, G, 2):
            b = next_ps()
            K.op(dve, lambda: V.memset(psum[:, b, :], 0.0), writes=[ps_b[b]])
            for gl in range(2):
                g = g0 + gl
                im = g % 2
                K.dma(sp, mBb[im][:], matsB[g], reads=[matsB_b[g]], writes=[mBb_b[im]])
                Ug = Uf[:, g, :].rearrange("p (c j) -> p c j", j=JC)
                Yg = psum[:, b, gl * 256:(gl + 1) * 256].rearrange("p (c j) -> p c j", j=JC)
                mm = []
                for d in range(JC):
                    mm.append((Yg[:, :, d:JC], mBb[im][:, d * 128:(d + 1) * 128], Ug[:, :, 0:JC - d], [UY_b]))
                    mm.append((Yg[:, :, 0:JC - d], mBb[im][:, (JC + d) * 128:(JC + d + 1) * 128], Ug[:, :, d:JC], [UY_b]))
                for j in range(JC):
                    mm.append((Yg[:, :, j], mBb[im][:, (2 * JC + j) * 128:(2 * JC + j + 1) * 128], Ein[:, 0, g, :], [Ein_b]))
                    mm.append((Yg[:, :, j], mBb[im][:, (3 * JC + j) * 128:(3 * JC + j + 1) * 128], Ein[:, 1, g, :], [Ein_b]))
                for n_, (o_, l_, r_, rd_) in enumerate(mm):
                    K.op(pe, lambda: nc.tensor.matmul(o_, l_, r_, start=False, stop=(n_ == len(mm) - 1), skip_group_check=True),
                         reads=[mBb_b[im]] + rd_, writes=[ps_b[b]], inc=(n_ == len(mm) - 1))
            K.op(act, lambda: nc.scalar.activation(out=Zf[:, g0:g0 + 2, :], in_=psum[:, b, :].rearrange("p (g s) -> p g s", g=2),
                                                   func=AF.Gelu_apprx_tanh), reads=[ps_b[b]], writes=[hTlo_b])
        for s8 in range(8):
            K.dma(sp, z_scr.rearrange("(g q) (s sub) -> s q g sub", q=16, s=8)[s8], Zf[s8 * 16:(s8 + 1) * 16, :, :],
                  reads=[hTlo_b], writes=[z_scr_b])
        K.dma(sp, X1[:], z_scr.rearrange("(c r) t -> r c t", c=4), reads=[z_scr_b], writes=[X1_b])
        K.barrier()
        e_att.close()
        gate = [sb(e_s5, f"gate{i}", (128, 512)) for i in range(2)]
        for m in range(4):
            for tg in range(4):
                b = next_ps()
                ig = (m * 4 + tg) % 2
                for k in range(4):
                    K.op(pe, lambda: nc.tensor.matmul(psum[:, b, :], wg[:, k, m * 128:(m + 1) * 128], X1[:, k, tg * 512:(tg + 1) * 512],
                                                      start=(k == 0), stop=(k == 3)),
                         reads=[wg_b, X1_b], writes=[ps_b[b]], inc=(k == 3))
                K.op(act, lambda: nc.scalar.activation(out=gate[ig][:], in_=psum[:, b, :], func=AF.Sigmoid, bias=bglu[:, m:m + 1]),
                     reads=[ps_b[b], pb], writes=[gate_b[ig]])
                nat = lambda tns: tns[:, m, :].rearrange("p (sub s) -> p s sub", s=8)[:, 2 * tg:2 * tg + 2, :]
                K.op(dve, lambda: V.tensor_tensor(out=nat(ysT), in0=X1[:, m, tg * 512:(tg + 1) * 512].rearrange("p (s sub) -> p s sub", s=2),
                                                  in1=gate[ig][:].rearrange("p (s sub) -> p s sub", s=2), op=ALU.mult),
                     reads=[X1_b, gate_b[ig]], writes=[ysT_b])
        K.op(act, lambda: nc.scalar.activation(out=ysq[:], in_=ysT[:], func=AF.Square), reads=[ysT_b], writes=[ysq_b])

        K.barrier()
        e_s5.close()
        if s == 0 and "ysT" in dbg:
            K.dma(sp, dbg["ysT"], ysT[:], reads=[ysT_b])
            K.dma(sp, dbg["yaT"], yaT[:], reads=[yaT_b])
            K.dma(sp, dbg["zT"], X1[:], reads=[X1_b])
        e_m4 = ExitStack()
        mt = [sb(e_m4, f"mt{i}", (128, 1024)) for i in range(2)]
        b = next_ps()
        for tt in range(NT):
            for k in range(4):
                K.op(pe, lambda: nc.tensor.matmul(psum[:, b, tt:tt + 1], ysq[:, k, tt * 128:(tt + 1) * 128], ones_bf[:, 0:1],
                                                  start=(k == 0), stop=(k == 3)),
                     reads=[ysq_b, cb], writes=[ps_b[b]], inc=(k == 3 and tt == NT - 1))
        K.op(act, lambda: nc.scalar.activation(out=rs[:, 1, :], in_=psum[:, b, 0:NT], func=AF.Sqrt, bias=epsb[:, 0:1], scale=1.0 / SSM_W),
             reads=[ps_b[b], cb], writes=[rs_b[1]])
        K.op(dve, lambda: V.reciprocal(out=rs[:, 1, :], in_=rs[:, 1, :]), reads=[rs_b[1]], writes=[rs_b[1]])
        K.op(act, lambda: nc.scalar.activation(out=rs[:, 2, :], in_=ssq[:, 2, :], func=AF.Sqrt, bias=epsb[:, 0:1], scale=1.0 / SSM_W),
             reads=[ssq_b[2], cb], writes=[rs_b[2]])
        K.op(dve, lambda: V.reciprocal(out=rs[:, 2, :], in_=rs[:, 2, :]), reads=[rs_b[2]], writes=[rs_b[2]])
        for tt in range(NT):
            i = tt % 2
            r0 = tok0 + tt * 128
            ba = next_ps(2)
            for hf in range(2):
                for k in range(4):
                    K.op(pe, lambda: nc.tensor.matmul(psum[:, ba + hf, :], ysT[:, k, tt * 128:(tt + 1) * 128], wo[:, k, hf * 512:(hf + 1) * 512],
                                                      start=(k == 0), stop=(k == 3)),
                         reads=[ysT_b, wo_b], writes=[ps_b[ba + hf]], inc=(k == 3))
            bb_ = next_ps(2)
            for hf in range(2):
                for k in range(4):
                    K.op(pe, lambda: nc.tensor.matmul(psum[:, bb_ + hf, :], yaT[:, k, tt * 128:(tt + 1) * 128], wo[:, 4 + k, hf * 512:(hf + 1) * 512],
                                                      start=(k == 0), stop=(k == 3)),
                         reads=[yaT_b, wo_b], writes=[ps_b[bb_ + hf]], inc=(k == 3))
            K.op(act, lambda: nc.scalar.activation(out=mt[i][:], in_=psum[:, ba:ba + 2, :].rearrange("p a n -> p (a n)"), func=AF.Copy,
                                                   scale=rs[:, 1, tt:tt + 1]), reads=[ps_b[ba], ps_b[ba + 1], rs_b[1]], writes=[mt_b[i]])
            K.op(dve, lambda: V.scalar_tensor_tensor(out=mt[i][:], in0=psum[:, bb_:bb_ + 2, :].rearrange("p a n -> p (a n)"),
                                                     scalar=rs[:, 2, tt:tt + 1], in1=mt[i][:], op0=ALU.mult, op1=ALU.add),
                 reads=[ps_b[bb_], ps_b[bb_ + 1], rs_b[2], mt_b[i]], writes=[mt_b[i]])
            K.op(act, lambda: nc.scalar.activation(out=junk[:], in_=mt[i][:], func=AF.Square, accum_out=ssq[:, 3, tt:tt + 1]),
                 reads=[mt_b[i]], writes=[junk_b, ssq_b[3]])
            K.op(act, lambda: nc.scalar.activation(out=rs[:, 3, tt:tt + 1], in_=ssq[:, 3, tt:tt + 1], func=AF.Sqrt,
                                                   bias=epsb[:, 0:1], scale=1.0 / D_MODEL), reads=[ssq_b[3], cb], writes=[rs_b[3]])
            K.op(dve, lambda: V.reciprocal(out=rs[:, 3, tt:tt + 1], in_=rs[:, 3, tt:tt + 1]), reads=[rs_b[3]], writes=[rs_b[3]])
            K.op(dve, lambda: V.tensor_tensor(out=mt[i][:], in0=mt[i][:], in1=gpost[:], op=ALU.mult),
                 reads=[mt_b[i], pb], writes=[mt_b[i]])
            rd = [out_b[s][tt]] if x_src is out else []
            K.dma(sp, xt[i][:], x_src[r0:r0 + 128, :], reads=rd, writes=[xt_b[i]])
            K.op(dve, lambda: V.scalar_tensor_tensor(out=xo[i][:], in0=mt[i][:], scalar=rs[:, 3, tt:tt + 1], in1=xt[i][:],
                                                     op0=ALU.mult, op1=ALU.add),
                 reads=[mt_b[i], rs_b[3], xt_b[i]], writes=[xo_b[i]])
            K.dma(sp, out[r0:r0 + 128, :], xo[i][:], reads=[xo_b[i]], writes=[out_b[s][tt]])
        K.barrier()
        e_m4.close()


_RR = {}


def K_rr(K, key, n):
    v = _RR.get(key, 0)
    _RR[key] = (v + 1) % n
    return v


def ffn_phase(nc, K, es, sb, psum, ps_b, next_ps, P, l, nseq, out, out_b, ident, identb, epsb, cb, dbg):
    pe, act, dve, pool, sp = K.pe, K.act, K.dve, K.pool, K.sp
    V = nc.vector
    wu = sb(es, "wu", (128, 8, 2 * D_FF), BF16)
    wd = sb(es, "wd", (128, NFB, 1024), BF16)
    gpf = sb(es, "gpf", (128, 8))
    cw = sb(es, "cw", (128, 4, 2 * NFB))
    gpost = sb(es, "gpost2", (128, 1024))
    wu_b, wd_b, pb = Buf(), Buf(), Buf()
    with ExitStack() as es1:
        ld = sb(es1, "ldf", (2 * NFB, 128))
        ld_b = Buf()
        LC = lambda dst, src, n: load_cols(nc, K, psum, ps_b, next_ps, ident, cb, ld, ld_b, dst, pb, src, n)
        LC(gpf[:], P["pre_ffn_norm"][l], 8)
        for j in range(3):
            LC(cw[:, j, :], P["conv_w"][l, j], 2 * NFB)
        LC(cw[:, 3, :], P["conv_b"][l], 2 * NFB)
        K.barrier()
    K.dma(sp, gpost[:], P["post_ffn_norm"][l].partition_broadcast(128), writes=[pb])
    with ExitStack() as es2:
        engs = [act, dve]
        stage = make_staging(es2, sb, "f")
        load_cast_weight(nc, K, stage, wu, wu_b, [P["w_up"][l, k * 128:(k + 1) * 128, :] for k in range(8)], 2 * D_FF,
                         gpf, pb, engs)
        load_cast_weight(nc, K, stage, wd, wd_b, [P["w_down"][l, k * 128:(k + 1) * 128, :] for k in range(NFB)], 1024,
                         None, pb, engs)
        K.barrier()

    HW = 1026
    h2T = sb(es, "h2T", (128, 8, HW), BF16)
    gbuf = sb(es, "gbuf", (128, NFB, 384), BF16)
    halo = sb(es, "halo", (128, 8, 2), BF16)
    xt = [sb(es, f"fxt{i}", (128, 1024)) for i in range(3)]
    hs = xt
    junk = sb(es, "fjunk", (128, 1024), BF16)
    cv = [sb(es, f"cv{i}", (128, 384)) for i in range(3)]
    cg = [sb(es, f"cg{i}", (128, 384)) for i in range(3)]
    gg = [sb(es, f"gg{i}", (128, 384)) for i in range(3)]
    yt = [sb(es, f"yt{i}", (128, 1024)) for i in range(1)] * 2
    xo = xt
    st = sb(es, "fst", (128, 4))
    stf = sb(es, "fstf", (128, 2, 4))
    stf_b = [Buf() for _ in range(4)]
    std = sb(es, "fstd", (128, 2, 2))
    std_b = [Buf() for _ in range(2)]
    B = lambda n=1: [Buf() for _ in range(n)]
    h2T_b, g_b, junk_b, st_b = Buf(), Buf(), Buf(), Buf()
    xt_b, cv_b, cg_b, gg_b = B(3), B(3), B(3), B(3)
    yt_b = B(1) * 2
    hs_b = xt_b
    xo_b = xt_b
    it = [0]

    for s in range(nseq):
        tok0 = s * SEQ
        for half in range(2):
            hs0 = half * 1024
            t_lo = hs0 // 128 - 1
            if half == 1:
                K.op(dve, lambda: V.tensor_copy(out=h2T[:, :, 0:1], in_=halo[:, :, 0:1]), reads=[h2T_b], writes=[h2T_b])
            for tt in range(t_lo, t_lo + 10):
                if tt < 0 or tt >= NT or (half == 1 and tt == t_lo):
                    continue
                i = it[0] % 3
                it[0] += 1
                r0 = tok0 + tt * 128
                K.dma(sp, xt[i][:], out[r0:r0 + 128, :], reads=[out_b[s][tt]], writes=[xt_b[i]])
                js = (it[0] - 1) % 4
                K.op(act, lambda: nc.scalar.activation(out=junk[:], in_=xt[i][:], func=AF.Square, accum_out=stf[:, 0, js:js + 1]),
                     reads=[xt_b[i]], writes=[junk_b, stf_b[js]])
                K.op(act, lambda: nc.scalar.activation(out=stf[:, 1, js:js + 1], in_=stf[:, 0, js:js + 1], func=AF.Sqrt, bias=epsb[:, 0:1], scale=1.0 / D_MODEL),
                     reads=[stf_b[js], cb], writes=[stf_b[js]])
                K.op(dve, lambda: V.reciprocal(out=stf[:, 1, js:js + 1], in_=stf[:, 1, js:js + 1]), reads=[stf_b[js]], writes=[stf_b[js]])
                K.op(dve, lambda: V.tensor_scalar(out=hs[i][:], in0=xt[i][:], scalar1=stf[:, 1, js:js + 1], scalar2=None, op0=ALU.mult),
                     reads=[xt_b[i], stf_b[js]], writes=[hs_b[i]])
                b = next_ps(2)
                for c in range(8):
                    K.op(pe, lambda: nc.tensor.transpose(psum[:, b + c // 4, (c % 4) * 128:(c % 4 + 1) * 128],
                                                         hs[i][:, c * 128:(c + 1) * 128], ident[:]),
                         reads=[hs_b[i], cb], writes=[ps_b[b], ps_b[b + 1]], inc=(c == 7))
                j0 = tt * 128 - hs0 + 1
                lo, hi = max(j0, 0), min(j0 + 128, HW)
                for hh in range(2):
                    E = dve if hh == 0 else act
                    src = psum[:, b + hh, :].rearrange("p (c t) -> p c t", c=4)[:, :, lo - j0:hi - j0]
                    dst = h2T[:, hh * 4:(hh + 1) * 4, lo:hi]
                    if E is dve:
                        K.op(dve, lambda: V.tensor_copy(out=dst, in_=src), reads=[ps_b[b + hh]], writes=[h2T_b])
                    else:
                        K.op(act, lambda: nc.scalar.copy(out=dst, in_=src), reads=[ps_b[b + hh]], writes=[h2T_b])
            if half == 0:
                K.op(dve, lambda: V.tensor_copy(out=halo[:, :, 0:1], in_=h2T[:, :, 1024:1025]), reads=[h2T_b], writes=[h2T_b])
            for (t0, ln) in ((0, 384), (384, 384), (768, 256)):
                g_first = (hs0 + t0 == 0)
                g_last = (hs0 + t0 + ln == SEQ)
                lo = 1 if g_first else 0
                hi = 1 if g_last else 0
                w0c, w1c = t0 + lo, t0 + ln + 2 - hi
                nW = w1c - w0c
                ctr = 1 - lo
                for fb in range(NFB):
                    i = fb % 3
                    res = []
                    for vg in range(2):
                        rb = vg * NFB + fb
                        b = next_ps()
                        for k in range(8):
                            K.op(pe, lambda: nc.tensor.matmul(psum[:, b, 0:nW], wu[:, k, rb * 128:(rb + 1) * 128],
                                                              h2T[:, k, w0c:w1c], start=(k == 0), stop=(k == 7)),
                                 reads=[wu_b, h2T_b], writes=[ps_b[b]], inc=(k == 7))
                        dst, dst_b = (cv[i], cv_b[i]) if vg == 0 else (cg[i], cg_b[i])
                        K.op(act, lambda: nc.scalar.activation(out=dst[:, 0:ln], in_=psum[:, b, ctr:ctr + ln], func=AF.Identity,
                                                               bias=cw[:, 3, rb:rb + 1], scale=cw[:, 1, rb:rb + 1]),
                             reads=[ps_b[b], pb], writes=[dst_b])
                        K.op(dve, lambda: V.scalar_tensor_tensor(out=dst[:, lo:ln], in0=psum[:, b, ctr + lo - 1:ctr + ln - 1],
                                                                 scalar=cw[:, 0, rb:rb + 1], in1=dst[:, lo:ln], op0=ALU.mult, op1=ALU.add),
                             reads=[ps_b[b], pb, dst_b], writes=[dst_b])
                        K.op(dve, lambda: V.scalar_tensor_tensor(out=dst[:, 0:ln - hi], in0=psum[:, b, ctr + 1:ctr + ln - hi + 1],
                                                                 scalar=cw[:, 2, rb:rb + 1], in1=dst[:, 0:ln - hi], op0=ALU.mult, op1=ALU.add),
                             reads=[ps_b[b], pb, dst_b], writes=[dst_b])
                    K.op(act, lambda: nc.scalar.activation(out=gg[i][:, 0:ln], in_=cg[i][:, 0:ln], func=AF.Gelu_apprx_tanh),
                         reads=[cg_b[i]], writes=[gg_b[i]])
                    K.op(pool, lambda: nc.gpsimd.tensor_tensor(out=gbuf[:, fb, 0:ln], in0=gg[i][:, 0:ln], in1=cv[i][:, 0:ln], op=ALU.mult),
                         reads=[gg_b[i], cv_b[i]], writes=[g_b])
                for ti in range(ln // 128):
                    tt = (hs0 + t0) // 128 + ti
                    r0 = tok0 + tt * 128
                    i = tt % 2
                    b = next_ps(2)
                    for hf in range(2):
                        for fb in range(NFB):
                            K.op(pe, lambda: nc.tensor.matmul(psum[:, b + hf, :], gbuf[:, fb, ti * 128:(ti + 1) * 128],
                                                              wd[:, fb, hf * 512:(hf + 1) * 512], start=(fb == 0), stop=(fb == NFB - 1)),
                                 reads=[g_b, wd_b], writes=[ps_b[b + hf]], inc=(fb == NFB - 1))
                    pv = psum[:, b:b + 2, :].rearrange("p a n -> p (a n)")
                    K.op(act, lambda: nc.scalar.activation(out=junk[:], in_=pv, func=AF.Square, accum_out=std[:, 0, i:i + 1]),
                         reads=[ps_b[b], ps_b[b + 1]], writes=[junk_b, std_b[i]])
                    K.op(act, lambda: nc.scalar.activation(out=std[:, 1, i:i + 1], in_=std[:, 0, i:i + 1], func=AF.Sqrt, bias=epsb[:, 0:1], scale=1.0 / D_MODEL),
                         reads=[std_b[i], cb], writes=[std_b[i]])
                    K.op(dve, lambda: V.reciprocal(out=std[:, 1, i:i + 1], in_=std[:, 1, i:i + 1]), reads=[std_b[i]], writes=[std_b[i]])
                    K.op(dve, lambda: V.tensor_tensor(out=yt[i][:], in0=pv, in1=gpost[:], op=ALU.mult),
                         reads=[ps_b[b], ps_b[b + 1], pb], writes=[yt_b[i]])
                    K.dma(sp, xt[i][:], out[r0:r0 + 128, :], reads=[out_b[s][tt]], writes=[xt_b[i]])
                    K.op(dve, lambda: V.scalar_tensor_tensor(out=xo[i][:], in0=yt[i][:], scalar=std[:, 1, i:i + 1], in1=xt[i][:],
                                                             op0=ALU.mult, op1=ALU.add),
                         reads=[yt_b[i], std_b[i], xt_b[i]], writes=[xo_b[i]])
                    K.dma(sp, out[r0:r0 + 128, :], xo[i][:], reads=[xo_b[i]], writes=[out_b[s][tt]])


_CACHE = {}


def _q_perm():
    cols = list(range(512))
    for c in range(4):
        for h in (c, 4 + c):
            cols.extend(range(512 + h * 64, 512 + (h + 1) * 64))
    cols.extend(range(1024, 1280))
    return np.asarray(cols)


def kernel(**inputs):
    n_cores = 8
    x = np.ascontiguousarray(np.asarray(inputs["x"], dtype=np.float32))
    nseq = x.shape[0] // n_cores
    if "nc" not in _CACHE:
        _CACHE["nc"] = build(nseq=nseq)[0]
        _CACHE["consts"] = _host_consts()
    nc = _CACHE["nc"]
    consts = _CACHE["consts"]
    shared = {}
    for k, v in inputs.items():
        if k == "x":
            continue
        a = np.ascontiguousarray(np.asarray(v, dtype=np.float32))
        if k == "w_in":
            a = np.ascontiguousarray(a[:, :, _q_perm()])
        shared[k] = a
    shared.update(consts)
    in_maps = []
    for i in range(n_cores):
        m = dict(shared)
        m["x"] = x[i * nseq:(i + 1) * nseq].reshape(nseq * SEQ, D_MODEL)
        in_maps.append(m)
    res = run_bass_kernel_spmd(nc, in_maps, core_ids=list(range(n_cores)))
    outs = [np.asarray(r["out"]).reshape(nseq, SEQ, D_MODEL) for r in res.results]
    return np.concatenate(outs, axis=0).astype(np.float32)
```

```python
import math
from contextlib import ExitStack
import numpy as np
import jax
import jax.numpy as jnp
import concourse.bass as bass
import concourse.mybir as mybir
from concourse.bass_utils import run_bass_kernel_spmd

F32 = mybir.dt.float32
BF16 = mybir.dt.bfloat16
I32 = mybir.dt.int32
AF = mybir.ActivationFunctionType
ALU = mybir.AluOpType

D_MODEL = 1024
SEQ = 2048
DEPTH = 4
SSM_W = 512
NG = 32
NST = 64
D_FF = 2816
NFB = 22
IN_W = 1280
EPS = 1e-6
JC = 4
LCH = 8 * JC
NSUB = SEQ // 8
NCH = NSUB // JC
NT = SEQ // 128
TWO_PI = 2.0 * math.pi

EC_TB = 0
EC_TC = EC_TB + 8
EC_SB = EC_TC + 8 * JC
EC_CC = EC_SB + 8 * JC
EC_L = EC_CC + 8 * JC
NEC = EC_L + 1
ANG_SHIFT = 64

MA_COLS = 2 * JC * 128
MB_COLS = 4 * JC * 128


class Buf:
    __slots__ = ("w", "r", "name")

    def __init__(self, name=""):
        self.w = None
        self.r = {}
        self.name = name


class Eng:
    def __init__(self, K, name, eng, is_pe=False):
        self.K = K
        self.name = name
        self.eng = eng
        self.is_pe = is_pe
        self.sems = []
        self.ep = -1
        self.cnt = 0
        self.seen = {}
        self.pending = []
        self._new_epoch()

    def _new_epoch(self):
        self.ep += 1
        self.cnt = 0
        self.sems.append(self.K.nc.alloc_semaphore(f"s_{self.name}_{self.ep}"))

class Kern:
    def __init__(self, nc):
        self.nc = nc
        self.pe = Eng(self, "pe", nc.tensor, True)
        self.act = Eng(self, "act", nc.scalar)
        self.dve = Eng(self, "dve", nc.vector)
        self.pool = Eng(self, "pool", nc.gpsimd)
        self.sp = Eng(self, "sp", nc.sync)
        self.engs = [self.pe, self.act, self.dve, self.pool, self.sp]
        self.slots = {}
        for e, n in ((self.sp, 20), (self.pool, 6), (self.act, 6)):
            self.slots[e.name] = [[nc.alloc_semaphore(f"d_{e.name}_{i}"), 0, ("dma", e.name, i)] for i in range(n)]
        self.slot_i = {k: 0 for k in self.slots}
        self.n_ins = 0

    def _wait(self, E, tick):
        sem, val, key = tick
        if key[0] == E.name and E.is_pe:
            return
        if E.seen.get(key, 0) >= val:
            return
        if key[0] != "dma":
            for (k2, v2) in E.seen.items():
                if k2[0] == key[0] and k2[0] != "dma" and k2[1] > key[1]:
                    return
        E.eng.wait_ge(sem, val)
        E.seen[key] = val
        self.n_ins += 1

    def _deps(self, E, reads, writes):
        need = []
        for b in reads:
            if b.w is not None:
                need.append(b.w)
        for b in writes:
            if b.w is not None:
                need.append(b.w)
            need.extend(b.r.values())
        for t in need:
            self._wait(E, t)

    def op(self, E, fn, reads=(), writes=(), inc=True):
        if E.cnt >= 28000 and not E.pending:
            E._new_epoch()
        self._deps(E, reads, writes)
        tick = (E.sems[E.ep], E.cnt + 1, (E.name, E.ep))
        ins = fn()
        self.n_ins += 1
        if inc:
            ins.then_inc(E.sems[E.ep], 1)
            E.cnt += 1
            E.pending = []
        else:
            E.pending.append(1)
        for b in reads:
            b.r[E.name] = tick
        for b in writes:
            b.w = tick
            b.r = {}
        return ins

    def dma(self, E, out, in_, reads=(), writes=(), **kw):
        slots = self.slots[E.name]
        i = self.slot_i[E.name]
        self.slot_i[E.name] = (i + 1) % len(slots)
        s = slots[i]
        if s[1] > 0:
            self._wait(E, (s[0], s[1], s[2]))
        self._deps(E, reads, writes)
        ins = E.eng.dma_start(out=out, in_=in_, **kw)
        s[1] += 16
        ins.then_inc(s[0], 16)
        self.n_ins += 1
        tick = (s[0], s[1], s[2])
        for b in reads:
            b.r[("dma", E.name, i)] = tick
        for b in writes:
            b.w = tick
            b.r = {}
        return ins

    def barrier(self):
        sp = self.sp
        for E in self.engs:
            if E is sp:
                continue
            if E.cnt > 0:
                self._wait(sp, (E.sems[E.ep], E.cnt, (E.name, E.ep)))
        for lst in self.slots.values():
            for s in lst:
                if s[1] > 0:
                    self._wait(sp, (s[0], s[1], s[2]))
        self.op(sp, lambda: sp.eng.nop())
        t = (sp.sems[sp.ep], sp.cnt, (sp.name, sp.ep))
        for E in self.engs:
            if E is not sp:
                self._wait(E, t)


def _t5_bucket(rel):
    n_buckets, max_distance = 32, 128
    half = n_buckets // 2
    max_exact = half // 2
    ret = jnp.where(rel > 0, half, 0)
    n = jnp.abs(rel)
    nf = jnp.maximum(n, 1).astype(jnp.float32)
    large = max_exact + (jnp.log(nf / max_exact) / math.log(max_distance / max_exact)
                         * (half - max_exact)).astype(jnp.int32)
    large = jnp.minimum(large, half - 1)
    return ret + jnp.where(n < max_exact, n, large)


def _host_consts():
    import ml_dtypes
    c = {}
    c["ident_f"] = np.eye(128, dtype=np.float32)
    ex = np.zeros((2, NEC), np.float32)
    for s in range(8):
        ex[0, EC_TB + s] = -s
        ex[1, EC_TB + s] = s
    for d in range(JC):
        for s in range(8):
            ex[0, EC_TC + d * 8 + s] = 8 * d + s
            ex[1, EC_TC + d * 8 + s] = 8 * d - s
            ex[0, EC_SB + d * 8 + s] = LCH - 1 - 8 * d - s
            ex[1, EC_SB + d * 8 + s] = 8 * d + s
            ex[0, EC_CC + d * 8 + s] = 8 * d + s + 1
            ex[1, EC_CC + d * 8 + s] = LCH - 8 * d - s
    ex[:, EC_L] = LCH
    c["extab"] = np.repeat(ex, 64, axis=0).astype(np.float32)
    sp = np.arange(128)[:, None] // 16
    s = np.arange(128)[None, :] // 16
    tm = np.stack([(s >= sp), (sp >= s)], axis=1).astype(np.float32)
    c["tmask"] = np.ascontiguousarray(tm)
    k = np.arange(128)[:, None, None]
    off = (np.arange(3) - 1)[None, :, None]
    q = np.arange(128)[None, None, :]
    rel = (k + 128 * off - q).astype(np.int32)
    with jax.default_device(jax.devices("cpu")[0]):
        bk = np.asarray(_t5_bucket(jnp.asarray(rel)))
    oh = (bk[:, None, :, :] == np.arange(32)[None, :, None, None]).astype(np.float32)
    c["onehot"] = oh.reshape(128, 32, 384).astype(ml_dtypes.bfloat16)
    c["vmask"] = (np.abs(rel) <= 128).astype(np.float32).reshape(128, 384)
    return c


def build(nseq=4, nlayers=DEPTH, debug=None):
    nc = bass.Bass("TRN2", target_bir_lowering=False)
    K = Kern(nc)
    pe, act, dve, pool, sp = K.pe, K.act, K.dve, K.pool, K.sp
    NTOK = nseq * SEQ

    def din(name, shape, dt=F32):
        return nc.dram_tensor(name, list(shape), dt, kind="ExternalInput").ap()

    x_in = din("x", (NTOK, D_MODEL))
    P = {}
    P["rel_bias"] = din("rel_bias", (32, 8))
    P["pre_mix_norm"] = din("pre_mix_norm", (DEPTH, D_MODEL))
    P["w_in"] = din("w_in", (DEPTH, D_MODEL, IN_W))
    P["lam_re"] = din("lam_re", (DEPTH, 2, NG, NST))
    P["lam_im"] = din("lam_im", (DEPTH, 2, NG, NST))
    P["log_step"] = din("log_step", (DEPTH, 2, NG))
    P["b_re"] = din("b_re", (DEPTH, 2, NG, NST, 16))
    P["b_im"] = din("b_im", (DEPTH, 2, NG, NST, 16))
    P["c_re"] = din("c_re", (DEPTH, 2, NG, 16, NST))
    P["c_im"] = din("c_im", (DEPTH, 2, NG, 16, NST))
    P["ssm_d"] = din("ssm_d", (DEPTH, SSM_W))
    P["w_glu"] = din("w_glu", (DEPTH, SSM_W, SSM_W))
    P["b_glu"] = din("b_glu", (DEPTH, SSM_W))
    P["attn_sink"] = din("attn_sink", (DEPTH, 8))
    P["ssm_out_norm"] = din("ssm_out_norm", (DEPTH, SSM_W))
    P["attn_out_norm"] = din("attn_out_norm", (DEPTH, SSM_W))
    P["w_out"] = din("w_out", (DEPTH, D_MODEL, D_MODEL))
    P["post_mix_norm"] = din("post_mix_norm", (DEPTH, D_MODEL))
    P["pre_ffn_norm"] = din("pre_ffn_norm", (DEPTH, D_MODEL))
    P["w_up"] = din("w_up", (DEPTH, D_MODEL, 2 * D_FF))
    P["conv_w"] = din("conv_w", (DEPTH, 3, 2 * D_FF))
    P["conv_b"] = din("conv_b", (DEPTH, 2 * D_FF))
    P["w_down"] = din("w_down", (DEPTH, D_FF, D_MODEL))
    P["post_ffn_norm"] = din("post_ffn_norm", (DEPTH, D_MODEL))
    c_ident = din("ident_f", (128, 128))
    c_extab = din("extab", (128, NEC))
    c_tmask = din("tmask", (128, 2, 128))
    c_onehot = din("onehot", (128, 32, 384), BF16)
    c_vmask = din("vmask", (128, 384))

    out = nc.dram_tensor("out", [NTOK, D_MODEL], F32, kind="ExternalOutput").ap()
    matsA = nc.dram_tensor("matsA", [NG, 128, MA_COLS], BF16).ap()
    matsB = nc.dram_tensor("matsB", [NG, 128, MB_COLS], BF16).ap()
    u_scr = nc.dram_tensor("u_scr", [NG * 16, SEQ], BF16).ap()
    z_scr = nc.dram_tensor("z_scr", [NG * 16, SEQ], BF16).ap()
    eb_scr = nc.dram_tensor("eb_scr", [128, 3 * 2 * 4 * 128], BF16).ap()
    matsA_b = [Buf() for _ in range(NG)]
    matsB_b = [Buf() for _ in range(NG)]
    u_scr_b, z_scr_b = Buf(), Buf()
    out_b = [[Buf() for _ in range(NT)] for _ in range(nseq)]
    dbg = {}
    if debug:
        for nm, (shp, dt_) in debug.items():
          if shp is not None:
            dbg[nm] = nc.dram_tensor("dbg_" + nm, list(shp), dt_, kind="ExternalOutput").ap()

    top = ExitStack()
    with top:
        uid = [0]

        def sb(es, name, shape, dt=F32):
            uid[0] += 1
            return es.enter_context(nc.sbuf_tensor(f"{name}_{uid[0]}", list(shape), dt))

        psum = top.enter_context(nc.psum_tensor("psum", [128, 8, 512], F32))
        ps_b = [Buf(f"ps{i}") for i in range(8)]
        ps_rr = [0]

        def next_ps(n=1):
            if n == 1:
                i = ps_rr[0] % 8
                ps_rr[0] += 1
                return i
            i = ps_rr[0] % 8
            if i % 2:
                i = (i + 1) % 8
            ps_rr[0] = i + 2
            return i

        ident = sb(top, "ident", (128, 128))
        identb = sb(top, "identb", (128, 128), BF16)
        ones_bf = sb(top, "ones_bf", (128, 1), BF16)
        epsb = sb(top, "epsb", (128, 1))
        aL = sb(top, "aL", (128, 2, 2, 32))
        aL_b = Buf("aL")
        cb = Buf("consts")
        ebB = Buf("eb")
        K.dma(sp, ident[:], c_ident, writes=[cb])
        K.op(dve, lambda: nc.vector.tensor_copy(out=identb[:], in_=ident[:]), reads=[cb], writes=[cb])
        K.op(dve, lambda: nc.vector.memset(ones_bf[:], 1.0), writes=[cb])
        K.op(dve, lambda: nc.vector.memset(epsb[:], EPS), writes=[cb])

        with ExitStack() as es:
            oh = sb(es, "oh", (128, 32, 384), BF16)
            eb = sb(es, "eb0", (128, 3, 2, 4, 128), BF16)
            rb = sb(es, "rb", (128, 256))
            vm = sb(es, "vm", (128, 384))
            acc = sb(es, "acc", (128, 8, 384))
            tb = Buf()
            K.dma(sp, oh[:], c_onehot, writes=[tb])
            K.dma(sp, vm[:], c_vmask, writes=[tb])
            K.dma(sp, rb[:], P["rel_bias"].rearrange("b h -> (b h)").partition_broadcast(128), writes=[tb])
            accb = [Buf() for _ in range(8)]
            for h in range(8):
                E = dve
                K.op(E, lambda h=h: nc.vector.tensor_scalar(out=acc[:, h, :], in0=oh[:, 0, :], scalar1=rb[:, h:h + 1],
                                                            scalar2=None, op0=ALU.mult), reads=[tb], writes=[accb[h]])
                for b in range(1, 32):
                    K.op(E, lambda h=h, b=b: nc.vector.scalar_tensor_tensor(
                        out=acc[:, h, :], in0=oh[:, b, :], scalar=rb[:, b * 8 + h:b * 8 + h + 1], in1=acc[:, h, :],
                        op0=ALU.mult, op1=ALU.add), reads=[tb, accb[h]], writes=[accb[h]])
                K.op(act, lambda h=h: nc.scalar.activation(out=acc[:, h, :], in_=acc[:, h, :], func=AF.Exp),
                     reads=[accb[h]], writes=[accb[h]])
                kv, c = h // 4, h % 4
                K.op(dve, lambda h=h, kv=kv, c=c: nc.vector.tensor_tensor(
                    out=eb[:, :, kv, c, :], in0=acc[:, h, :].rearrange("p (o q) -> p o q", o=3),
                    in1=vm[:].rearrange("p (o q) -> p o q", o=3), op=ALU.mult), reads=[accb[h], tb], writes=[ebB])
            K.dma(sp, eb_scr, eb[:].rearrange("p o k c q -> p (o k c q)"), reads=[ebB], writes=[ebB])
            K.barrier()

        for l in range(nlayers):
            x_src = x_in if l == 0 else out
            with ExitStack() as es:
                s5_prologue(nc, K, es, sb, psum, ps_b, next_ps, P, l, ident, c_extab, c_tmask,
                            aL, aL_b, matsA, matsB, matsA_b, matsB_b, dbg)
                K.barrier()
            with ExitStack() as es:
                mixer_phase(nc, K, es, sb, psum, ps_b, next_ps, P, l, nseq, x_src, out, out_b, ident, identb, ones_bf,
                            eb_scr, ebB, epsb, cb, aL, aL_b, matsA, matsB, matsA_b, matsB_b, u_scr, z_scr, u_scr_b, z_scr_b, dbg)
                K.barrier()
            with ExitStack() as es:
              if not (debug and "skip_ffn" in debug):
                ffn_phase(nc, K, es, sb, psum, ps_b, next_ps, P, l, nseq, out, out_b, ident, identb, epsb, cb, dbg)
                K.barrier()
        K.barrier()
    return nc, K


def make_staging(es, sb, tag, n=6, ch=2048):
    return ([sb(es, f"stg_{tag}{i}", (128, ch)) for i in range(n)], [Buf() for _ in range(n)], [0])


def load_cast_weight(nc, K, stage, dst, dst_b, src_rows, ncols, gain, gain_b, engines):
    stg, stg_b, itr = stage
    NS = len(stg)
    CH = 2048
    dq = [K.sp, K.act]
    it = itr[0]
    for r, src in enumerate(src_rows):
        for c0 in range(0, ncols, CH):
            cw = min(CH, ncols - c0)
            i = it % NS
            E = engines[it % len(engines)]
            Q = dq[it % 2]
            it += 1
            K.dma(Q, stg[i][:, 0:cw], src[:, c0:c0 + cw], writes=[stg_b[i]])
            if gain is None:
                if E is K.act:
                    K.op(E, lambda i=i, r=r, c0=c0, cw=cw: nc.scalar.copy(out=dst[:, r, c0:c0 + cw], in_=stg[i][:, 0:cw]),
                         reads=[stg_b[i]], writes=[dst_b])
                else:
                    K.op(E, lambda i=i, r=r, c0=c0, cw=cw, E=E: E.eng.tensor_copy(out=dst[:, r, c0:c0 + cw], in_=stg[i][:, 0:cw]),
                         reads=[stg_b[i]], writes=[dst_b])
            else:
                if E is K.act:
                    K.op(E, lambda i=i, r=r, c0=c0, cw=cw: nc.scalar.activation(
                        out=dst[:, r, c0:c0 + cw], in_=stg[i][:, 0:cw], func=AF.Copy, scale=gain[:, r:r + 1]),
                        reads=[stg_b[i], gain_b], writes=[dst_b])
                else:
                    K.op(E, lambda i=i, r=r, c0=c0, cw=cw, E=E: E.eng.tensor_scalar(
                        out=dst[:, r, c0:c0 + cw], in0=stg[i][:, 0:cw], scalar1=gain[:, r:r + 1], scalar2=None,
                        op0=ALU.mult), reads=[stg_b[i], gain_b], writes=[dst_b])
    itr[0] = it


def load_cols(nc, K, psum, ps_b, next_ps, ident, cb, ld, ld_b, dst, dst_b, src1d, n):
    K.dma(K.sp, ld[0:n, :], src1d.rearrange("(r p) -> r p", p=128), writes=[ld_b])
    b = next_ps()
    K.op(K.pe, lambda: nc.tensor.transpose(psum[:, b, 0:n], ld[0:n, :], ident[0:n, 0:n]), reads=[ld_b, cb], writes=[ps_b[b]])
    K.op(K.dve, lambda: nc.vector.tensor_copy(out=dst, in_=psum[:, b, 0:n]), reads=[ps_b[b]], writes=[dst_b])


def rstd_from_ssq(nc, K, ssq, rs, n, width, epsb, bufs_r, bufs_w):
    K.op(K.act, lambda: nc.scalar.activation(out=rs[:, 0:n], in_=ssq[:, 0:n], func=AF.Sqrt, bias=epsb[:, 0:1],
                                             scale=1.0 / width), reads=bufs_r, writes=bufs_w)
    K.op(K.dve, lambda: nc.vector.reciprocal(out=rs[:, 0:n], in_=rs[:, 0:n]), reads=bufs_w, writes=bufs_w)


def s5_prologue(nc, K, es, sb, psum, ps_b, next_ps, P, l, ident, c_extab, c_tmask,
                aL, aL_b, matsA, matsB, matsA_b, matsB_b, dbg):
    pe, act, dve, pool, sp = K.pe, K.act, K.dve, K.pool, K.sp
    V = nc.vector
    G = NG
    tb = Buf("s5tab")

    def t(name, shape, dt=F32):
        return sb(es, name, shape, dt)

    lamld = t("lamld", (32, 2, 128))
    lam = t("lam", (128, 2, 32))
    ls = t("ls", (128, 32))
    braw = t("braw", (128, 2, 32, 16))
    cld = t("cld", (128, 2, 4, 128))
    craw = t("craw", (128, 2, 32, 16))
    dvec = t("dvec", (128, 32))
    extab = t("extab_sb", (128, NEC))
    tmask = t("tmask_sb", (128, 2, 128))
    K.dma(sp, extab[:], c_extab, writes=[tb])
    K.dma(sp, tmask[:], c_tmask, writes=[tb])
    with nc.allow_non_contiguous_dma(reason="tiny param loads"):
        for ri, nm in enumerate(("lam_re", "lam_im")):
            K.dma(sp, lamld[:, ri, :].rearrange("g (d n) -> g d n", d=2), P[nm][l].rearrange("d g n -> g d n"), writes=[tb])
        for d in range(2):
            K.dma(sp, ls[d * 64:(d + 1) * 64, :], P["log_step"][l, d].partition_broadcast(64), writes=[tb])
            for ri, nm in enumerate(("b_re", "b_im")):
                K.dma(sp, braw[d * 64:(d + 1) * 64, ri, :, :], P[nm][l, d].rearrange("g n q -> n g q"), writes=[tb])
        for ri, nm in enumerate(("c_re", "c_im")):
            for d in range(2):
                K.dma(sp, cld[:, ri, :, d * 64:(d + 1) * 64],
                      P[nm][l, d].rearrange("(t gl) p n -> (gl p) t n", t=4), writes=[tb])
        for s in range(8):
            K.dma(sp, dvec[s * 16:(s + 1) * 16, :], P["ssm_d"][l].rearrange("(g q) -> q g", q=16), writes=[tb])
    for ri in range(2):
        b = next_ps()
        K.op(pe, lambda ri=ri, b=b: nc.tensor.transpose(psum[:, b, 0:32], lamld[:, ri, :], ident[0:32, 0:32]),
             reads=[tb], writes=[ps_b[b]])
        K.op(dve, lambda ri=ri, b=b: V.tensor_copy(out=lam[:, ri, :], in_=psum[:, b, 0:32]), reads=[ps_b[b]], writes=[tb])
        b = next_ps()
        for tt in range(4):
            K.op(pe, lambda ri=ri, b=b, tt=tt: nc.tensor.transpose(psum[:, b, tt * 128:(tt + 1) * 128], cld[:, ri, tt, :], ident[:]),
                 reads=[tb], writes=[ps_b[b]], inc=(tt == 3))
        K.op(dve, lambda ri=ri, b=b: V.tensor_copy(out=craw[:, ri, :, :].rearrange("p g q -> p (g q)"), in_=psum[:, b, :]),
             reads=[ps_b[b]], writes=[tb])

    dt_ = t("dt", (128, 32))
    ar = t("ar", (128, 32))
    ang = t("ang", (128, 32))
    lb = t("lb", (128, 2, 32))
    coef = t("coef", (128, 2, 32))
    tmp = t("tmpa", (128, 4, 32))

    def dv(fn, inc=True):
        K.op(dve, fn, reads=[tb], writes=[tb], inc=inc)

    def ac(fn):
        K.op(act, fn, reads=[tb], writes=[tb])

    ac(lambda: nc.scalar.activation(out=dt_[:], in_=ls[:], func=AF.Exp))
    dv(lambda: V.tensor_tensor(out=ar[:], in0=lam[:, 0, :], in1=dt_[:], op=ALU.mult))
    dv(lambda: V.tensor_tensor(out=ang[:], in0=lam[:, 1, :], in1=dt_[:], op=ALU.mult))

    PR = t("PR", (128, 32, NEC))
    PI = t("PI", (128, 32, NEC))
    A1 = t("A1", (128, 32, NEC))
    A2 = t("A2", (128, 32, NEC))
    A3i = t("A3i", (128, 32, NEC), I32)
    A4 = t("A4", (128, 32, NEC))
    ex_b = extab[:].unsqueeze(1).to_broadcast([128, 32, NEC])

    def bc_g(x):
        return x.unsqueeze(2).to_broadcast([128, 32, NEC])

    dv(lambda: V.tensor_tensor(out=A1[:], in0=ex_b, in1=bc_g(ar[:]), op=ALU.mult))
    ac(lambda: nc.scalar.activation(out=A4[:], in_=A1[:], func=AF.Exp))
    dv(lambda: V.tensor_tensor(out=A1[:], in0=ex_b, in1=bc_g(ang[:]), op=ALU.mult))

    def sin_of(dst, shift):
        dv(lambda: V.tensor_scalar(out=A2[:], in0=A1[:], scalar1=shift + ANG_SHIFT * TWO_PI, scalar2=1.0 / TWO_PI,
                                   op0=ALU.add, op1=ALU.mult))
        dv(lambda: V.tensor_copy(out=A3i[:], in_=A2[:]))
        dv(lambda: V.tensor_copy(out=dst[:], in_=A3i[:]))
        dv(lambda: V.tensor_tensor(out=A2[:], in0=A2[:], in1=dst[:], op=ALU.subtract))
        dv(lambda: V.tensor_scalar(out=dst[:], in0=A2[:], scalar1=0.5, scalar2=None, op0=ALU.is_gt))
        dv(lambda: V.tensor_tensor(out=A2[:], in0=A2[:], in1=dst[:], op=ALU.subtract))
        dv(lambda: V.tensor_scalar(out=dst[:], in0=A2[:], scalar1=-0.5, scalar2=None, op0=ALU.is_lt))
        dv(lambda: V.tensor_tensor(out=A2[:], in0=A2[:], in1=dst[:], op=ALU.add))
        dv(lambda: V.tensor_scalar(out=A2[:], in0=A2[:], scalar1=TWO_PI, scalar2=3.14159, op0=ALU.mult, op1=ALU.min))
        dv(lambda: V.tensor_scalar(out=A2[:], in0=A2[:], scalar1=-3.14159, scalar2=None, op0=ALU.max))
        ac(lambda: nc.scalar.activation(out=dst[:], in_=A2[:], func=AF.Sin))

    sin_of(PI, 0.0)
    sin_of(PR, math.pi / 2)
    dv(lambda: V.tensor_tensor(out=PR[:], in0=PR[:], in1=A4[:], op=ALU.mult))
    dv(lambda: V.tensor_tensor(out=PI[:], in0=PI[:], in1=A4[:], op=ALU.mult))

    ac(lambda: nc.scalar.activation(out=tmp[:, 0, :], in_=ar[:], func=AF.Exp))
    dv(lambda: V.tensor_copy(out=lb[0:64, 0, :], in_=PR[0:64, :, EC_TC + 1]))
    dv(lambda: V.tensor_copy(out=lb[0:64, 1, :], in_=PI[0:64, :, EC_TC + 1]))
    dv(lambda: V.tensor_copy(out=lb[64:128, 0, :], in_=PR[64:128, :, EC_TB + 1]))
    dv(lambda: V.tensor_copy(out=lb[64:128, 1, :], in_=PI[64:128, :, EC_TB + 1]))
    dv(lambda: V.tensor_tensor(out=tmp[:, 0, :], in0=lam[:, 0, :], in1=lam[:, 0, :], op=ALU.mult))
    dv(lambda: V.tensor_tensor(out=tmp[:, 1, :], in0=lam[:, 1, :], in1=lam[:, 1, :], op=ALU.mult))
    dv(lambda: V.tensor_tensor(out=tmp[:, 0, :], in0=tmp[:, 0, :], in1=tmp[:, 1, :], op=ALU.add))
    dv(lambda: V.reciprocal(out=tmp[:, 0, :], in_=tmp[:, 0, :]))
    dv(lambda: V.tensor_scalar(out=tmp[:, 1, :], in0=lb[:, 0, :], scalar1=-1.0, scalar2=None, op0=ALU.add))
    dv(lambda: V.tensor_tensor(out=tmp[:, 2, :], in0=tmp[:, 1, :], in1=lam[:, 0, :], op=ALU.mult))
    dv(lambda: V.tensor_tensor(out=tmp[:, 3, :], in0=lb[:, 1, :], in1=lam[:, 1, :], op=ALU.mult))
    dv(lambda: V.tensor_tensor(out=tmp[:, 2, :], in0=tmp[:, 2, :], in1=tmp[:, 3, :], op=ALU.add))
    dv(lambda: V.tensor_tensor(out=coef[:, 0, :], in0=tmp[:, 2, :], in1=tmp[:, 0, :], op=ALU.mult))
    dv(lambda: V.tensor_tensor(out=tmp[:, 2, :], in0=lb[:, 1, :], in1=lam[:, 0, :], op=ALU.mult))
    dv(lambda: V.tensor_tensor(out=tmp[:, 3, :], in0=tmp[:, 1, :], in1=lam[:, 1, :], op=ALU.mult))
    dv(lambda: V.tensor_tensor(out=tmp[:, 2, :], in0=tmp[:, 2, :], in1=tmp[:, 3, :], op=ALU.subtract))
    dv(lambda: V.tensor_tensor(out=coef[:, 1, :], in0=tmp[:, 2, :], in1=tmp[:, 0, :], op=ALU.mult))
    bb = t("bb", (128, 2, 32, 16))
    t16 = t("t16", (128, 32, 16))

    def bq(x):
        return x.unsqueeze(2).to_broadcast([128, 32, 16])

    dv(lambda: V.tensor_tensor(out=bb[:, 0], in0=braw[:, 0], in1=bq(coef[:, 0, :]), op=ALU.mult))
    dv(lambda: V.tensor_tensor(out=t16[:], in0=braw[:, 1], in1=bq(coef[:, 1, :]), op=ALU.mult))
    dv(lambda: V.tensor_tensor(out=bb[:, 0], in0=bb[:, 0], in1=t16[:], op=ALU.subtract))
    dv(lambda: V.tensor_tensor(out=bb[:, 1], in0=braw[:, 1], in1=bq(coef[:, 0, :]), op=ALU.mult))
    dv(lambda: V.tensor_tensor(out=t16[:], in0=braw[:, 0], in1=bq(coef[:, 1, :]), op=ALU.mult))
    dv(lambda: V.tensor_tensor(out=bb[:, 1], in0=bb[:, 1], in1=t16[:], op=ALU.add))

    if "s5tab" in dbg:
        K.dma(sp, dbg["s5tab"][:, 0:NEC], PR[:, 0, :], reads=[tb])
        K.dma(sp, dbg["s5tab"][:, NEC:2 * NEC], PI[:, 0, :], reads=[tb])
        K.dma(sp, dbg["s5tab"][:, 2 * NEC:2 * NEC + 32], bb[:, 0, 0:2, :].rearrange("p g q -> p (g q)"), reads=[tb])
        K.dma(sp, dbg["s5tab"][:, 2 * NEC + 32:2 * NEC + 64], bb[:, 1, 0:2, :].rearrange("p g q -> p (g q)"), reads=[tb])

    K.op(dve, lambda: V.tensor_copy(out=aL[:, 0, 0, :], in_=PR[:, :, EC_L]), reads=[tb], writes=[aL_b])
    K.op(dve, lambda: V.tensor_copy(out=aL[:, 0, 1, :], in_=PR[:, :, EC_L]), reads=[tb], writes=[aL_b])
    K.op(dve, lambda: V.tensor_scalar(out=aL[:, 1, 0, :], in0=PI[:, :, EC_L], scalar1=-1.0, scalar2=None, op0=ALU.mult), reads=[tb], writes=[aL_b])
    K.op(dve, lambda: V.tensor_copy(out=aL[:, 1, 1, :], in_=PI[:, :, EC_L]), reads=[tb], writes=[aL_b])

    NB = 8 + 8 * JC
    NC8 = 8 * JC
    GB = 2
    xb = [t(f"xb{i}", (128, 2, GB, NB, 16)) for i in range(2)]
    xc = [t(f"xc{i}", (128, 2, GB, NC8, 16)) for i in range(2)]
    xcc = [t(f"xcc{i}", (128, 2, GB, NC8, 16)) for i in range(2)]
    tq = [t(f"tq{i}", (128, GB, NB, 16)) for i in range(2)]
    mA = [t(f"mA{i}", (128, MA_COLS), BF16) for i in range(2)]
    mB = [t(f"mB{i}", (128, MB_COLS), BF16) for i in range(2)]
    dmat = [t(f"dmat{i}", (128, 128)) for i in range(2)]
    t0m = [t(f"t0m{i}", (128, 128)) for i in range(2)]
    xb_b = [Buf() for _ in range(2)]
    xc_b = [Buf() for _ in range(2)]
    xcc_b = [Buf() for _ in range(2)]
    tq_b = [Buf() for _ in range(2)]
    mA_b = [Buf() for _ in range(2)]
    mB_b = [Buf() for _ in range(2)]
    dm_b = [Buf() for _ in range(2)]

    def pwB(x, g0, c0, n):
        return x[:, g0:g0 + GB, c0:c0 + n].unsqueeze(3).to_broadcast([128, GB, n, 16])

    def bbB(ri, g0, n):
        return bb[:, ri, g0:g0 + GB, :].unsqueeze(2).to_broadcast([128, GB, n, 16])

    def ccB(ri, g0, n):
        return craw[:, ri, g0:g0 + GB, :].unsqueeze(2).to_broadcast([128, GB, n, 16])

    for gb in range(G // GB):
        g0 = gb * GB
        i = gb % 2
        E2 = pool if (gb % 4 == 3) else dve
        EN = E2.eng

        def tt_(out, in0, in1, op, reads, writes):
            K.op(E2, lambda: EN.tensor_tensor(out=out, in0=in0, in1=in1, op=op), reads=reads, writes=writes)

        for (dst0, c0, n) in ((0, EC_TB, 8), (8, EC_SB, NC8)):
            o_re = xb[i][:, 0, :, dst0:dst0 + n, :]
            o_im = xb[i][:, 1, :, dst0:dst0 + n, :]
            t_ = tq[i][:, :, dst0:dst0 + n, :]
            tt_(o_re, pwB(PR, g0, c0, n), bbB(0, g0, n), ALU.mult, [tb], [xb_b[i]])
            tt_(t_, pwB(PI, g0, c0, n), bbB(1, g0, n), ALU.mult, [tb], [tq_b[i]])
            tt_(o_re, o_re, t_, ALU.subtract, [tq_b[i], xb_b[i]], [xb_b[i]])
            tt_(o_im, pwB(PR, g0, c0, n), bbB(1, g0, n), ALU.mult, [tb], [xb_b[i]])
            tt_(t_, pwB(PI, g0, c0, n), bbB(0, g0, n), ALU.mult, [tb, xb_b[i]], [tq_b[i]])
            tt_(o_im, o_im, t_, ALU.add, [tq_b[i], xb_b[i]], [xb_b[i]])
        for (dstt, dstb, c0) in ((xc, xc_b, EC_TC), (xcc, xcc_b, EC_CC)):
            o_re = dstt[i][:, 0]
            o_im = dstt[i][:, 1]
            t_ = tq[i][:, :, 0:NC8, :]
            tt_(o_re, pwB(PR, g0, c0, NC8), ccB(0, g0, NC8), ALU.mult, [tb], [dstb[i]])
            tt_(t_, pwB(PI, g0, c0, NC8), ccB(1, g0, NC8), ALU.mult, [tb], [tq_b[i]])
            tt_(o_re, o_re, t_, ALU.subtract, [tq_b[i], dstb[i]], [dstb[i]])
            tt_(o_im, pwB(PI, g0, c0, NC8), ccB(0, g0, NC8), ALU.mult, [tb], [dstb[i]])
            tt_(t_, pwB(PR, g0, c0, NC8), ccB(1, g0, NC8), ALU.mult, [tb, dstb[i]], [tq_b[i]])
            tt_(o_im, o_im, t_, ALU.add, [tq_b[i], dstb[i]], [dstb[i]])
            K.op(E2, lambda: EN.tensor_scalar(out=o_im, in0=o_im, scalar1=-1.0, scalar2=None, op0=ALU.mult),
                 reads=[dstb[i]], writes=[dstb[i]])

        for gl in range(GB):
            g = g0 + gl
            j2 = g % 2
            for d in range(2):
                b = next_ps()
                rows = slice(d * 64, (d + 1) * 64)
                for ri in range(2):
                    K.op(pe, lambda: nc.tensor.matmul(
                        psum[:, b, 0:JC * 128],
                        xb[i][rows, ri, gl, 0:8, :].rearrange("p s q -> p (s q)"),
                        xc[i][rows, ri, gl, :, :].rearrange("p c q -> p (c q)"),
                        start=(ri == 0), stop=(ri == 1)),
                        reads=[xb_b[i], xc_b[i]], writes=[ps_b[b]], inc=(ri == 1))
                base = d * JC * 128
                if d == 0:
                    K.op(dve, lambda: V.tensor_scalar(out=dmat[j2][:], in0=ident[:], scalar1=dvec[:, g:g + 1],
                                                      scalar2=None, op0=ALU.mult), reads=[tb], writes=[dm_b[j2]])
                    K.op(dve, lambda: V.tensor_tensor(out=t0m[j2][:], in0=psum[:, b, 0:128], in1=tmask[:, 0, :], op=ALU.mult),
                         reads=[ps_b[b], tb, dm_b[j2]], writes=[dm_b[j2]])
                    K.op(dve, lambda: V.tensor_tensor(out=mB[j2][:, base:base + 128], in0=t0m[j2][:], in1=dmat[j2][:], op=ALU.add),
                         reads=[dm_b[j2]], writes=[mB_b[j2]])
                else:
                    K.op(dve, lambda: V.tensor_tensor(out=mB[j2][:, base:base + 128], in0=psum[:, b, 0:128],
                                                      in1=tmask[:, 1, :], op=ALU.mult),
                         reads=[ps_b[b], tb], writes=[mB_b[j2]])
                if JC > 1:
                    K.op(act, lambda: nc.scalar.copy(out=mB[j2][:, base + 128:base + JC * 128], in_=psum[:, b, 128:JC * 128]),
                         reads=[ps_b[b]], writes=[mB_b[j2]])
            K.op(act, lambda: nc.scalar.copy(out=mB[j2][:, 2 * JC * 128:3 * JC * 128],
                                             in_=xcc[i][:, 0, gl].rearrange("p c q -> p (c q)")),
                 reads=[xcc_b[i]], writes=[mB_b[j2]])
            K.op(act, lambda: nc.scalar.copy(out=mB[j2][:, 3 * JC * 128:4 * JC * 128],
                                             in_=xcc[i][:, 1, gl].rearrange("p c q -> p (c q)")),
                 reads=[xcc_b[i]], writes=[mB_b[j2]])
            K.dma(sp, matsB[g], mB[j2][:], reads=[mB_b[j2]], writes=[matsB_b[g]])
            for ri in range(2):
                b = next_ps()
                for j in range(JC):
                    K.op(pe, lambda: nc.tensor.transpose(
                        psum[:, b, j * 128:(j + 1) * 128],
                        xb[i][:, ri, gl, 8 + 8 * j:16 + 8 * j, :].rearrange("p s q -> p (s q)"), ident[:]),
                        reads=[xb_b[i]], writes=[ps_b[b]], inc=(j == JC - 1))
                K.op(act, lambda: nc.scalar.copy(out=mA[j2][:, ri * JC * 128:(ri + 1) * JC * 128], in_=psum[:, b, 0:JC * 128]),
                     reads=[ps_b[b]], writes=[mA_b[j2]])
            K.dma(sp, matsA[g], mA[j2][:], reads=[mA_b[j2]], writes=[matsA_b[g]])
    if "matsB0" in dbg:
        pass


def mixer_phase(nc, K, es, sb, psum, ps_b, next_ps, P, l, nseq, x_src, out, out_b, ident, identb, ones_bf,
                eb_scr, ebB, epsb, cb, aL, aL_b, matsA, matsB, matsA_b, matsB_b, u_scr, z_scr, u_scr_b, z_scr_b, dbg):
    pe, act, dve, pool, sp = K.pe, K.act, K.dve, K.pool, K.sp
    V = nc.vector
    G = NG
    wi = sb(es, "wi", (128, 8, IN_W), BF16)
    wg = sb(es, "wg", (128, 4, 512), BF16)
    wo = sb(es, "wo", (128, 8, 1024), BF16)
    gpm = sb(es, "gpm", (128, 8))
    gso = sb(es, "gso", (128, 8))
    bglu = sb(es, "bglu", (128, 4))
    esink = sb(es, "esink", (128, 8))
    gpost = sb(es, "gpost", (128, 1024))
    wi_b, wg_b, wo_b, pb = Buf(), Buf(), Buf(), Buf()
    eb = sb(es, "eb", (128, 3, 2, 4, 128), BF16)
    K.dma(sp, eb[:].rearrange("p o k c q -> p (o k c q)"), eb_scr, reads=[ebB], writes=[pb])
    ebB = pb
    with ExitStack() as es1:
        ld = sb(es1, "ldm", (8, 128))
        ld_b = Buf()
        LC = lambda dst, src, n: load_cols(nc, K, psum, ps_b, next_ps, ident, cb, ld, ld_b, dst, pb, src, n)
        LC(gpm[:], P["pre_mix_norm"][l], 8)
        LC(gso[:, 0:4], P["ssm_out_norm"][l], 4)
        LC(gso[:, 4:8], P["attn_out_norm"][l], 4)
        LC(bglu[:], P["b_glu"][l], 4)
        K.barrier()
    K.dma(sp, esink[:], P["attn_sink"][l].partition_broadcast(128), writes=[pb])
    K.dma(sp, gpost[:], P["post_mix_norm"][l].partition_broadcast(128), writes=[pb])
    K.op(act, lambda: nc.scalar.activation(out=esink[:], in_=esink[:], func=AF.Exp), reads=[pb], writes=[pb])
    with ExitStack() as es2:
        engs = [act, dve]
        stage = make_staging(es2, sb, "m")
        load_cast_weight(nc, K, stage, wi, wi_b, [P["w_in"][l, k * 128:(k + 1) * 128, :] for k in range(8)], IN_W,
                         gpm, pb, engs)
        load_cast_weight(nc, K, stage, wg, wg_b, [P["w_glu"][l, k * 128:(k + 1) * 128, :] for k in range(4)], 512,
                         None, pb, engs)
        load_cast_weight(nc, K, stage, wo, wo_b, [P["w_out"][l, k * 128:(k + 1) * 128, :] for k in range(8)], 1024,
                         gso, pb, engs)
        K.barrier()

    X1 = sb(es, "X1", (128, 4, SEQ), BF16)
    hTlo = sb(es, "hTlo", (128, 4, SEQ), BF16)
    hThi = sb(es, "hThi", (128, 4, SEQ), BF16)
    qT = sb(es, "qT", (128, 4, SEQ), BF16)
    UY = sb(es, "UY", (128, 4, SEQ), BF16)
    kT = sb(es, "kT", (128, SEQ), BF16)
    vaug = sb(es, "vaug", (128, NT, 2, 66), BF16)
    xt = [sb(es, f"xt{i}", (128, 1024)) for i in range(3)]
    hs = xt
    xo = xt
    junk = sb(es, "junk", (128, 1024), BF16)
    ssq = sb(es, "ssq", (128, 4, NT))
    rs = sb(es, "rs", (128, 4, NT))
    B = lambda n=1: [Buf() for _ in range(n)]
    X1_b, hTlo_b, hThi_b, qT_b, UY_b, kT_b, va_b, H_b, Ein_b = (Buf() for _ in range(9))
    Hf_b, Hb_b = Buf(), Buf()
    xt_b, mAb_b, mBb_b, et_b, pt_b, oa_b, den_b, gate_b, mt_b = B(3), B(2), B(2), B(3), B(6), B(2), B(2), B(2), B(2)
    hs_b = xt_b
    xo_b = xt_b
    junk_b, ssq_b, rs_b, st_b = Buf(), [Buf() for _ in range(4)], [Buf() for _ in range(4)], [Buf(), Buf()]
    m1s_b = [(Buf(), Buf()) for _ in range(4)]
    K.op(dve, lambda: V.memset(vaug[:, :, :, 64:66], 1.0), writes=[va_b])
    hT_b = [hTlo_b, hThi_b]
    hT = [hTlo, hThi]
    Uf = UY[:].rearrange("p c t -> p (c t)").rearrange("p (g s) -> p g s", g=NG)
    Zf = hTlo[:].rearrange("p c t -> p (c t)").rearrange("p (g s) -> p g s", g=NG)
    ysT, ysT_b = qT, qT_b
    yaT, yaT_b = hThi, hThi_b
    ysq, ysq_b = UY, UY_b

    for s in range(nseq):
        tok0 = s * SEQ
        for tt in range(NT):
            i = tt % 3
            r0 = tok0 + tt * 128
            rd = [out_b[s][tt]] if x_src is out else []
            K.dma(sp, xt[i][:], x_src[r0:r0 + 128, :], reads=rd, writes=[xt_b[i]])
            sq_b, r_b = m1s_b[tt % 4]
            K.op(act, lambda: nc.scalar.activation(out=junk[:], in_=xt[i][:], func=AF.Square, accum_out=ssq[:, 0, tt:tt + 1]),
                 reads=[xt_b[i]], writes=[junk_b, sq_b])
            K.op(act, lambda: nc.scalar.activation(out=rs[:, 0, tt:tt + 1], in_=ssq[:, 0, tt:tt + 1], func=AF.Sqrt,
                                                   bias=epsb[:, 0:1], scale=1.0 / D_MODEL), reads=[sq_b, cb], writes=[r_b])
            K.op(dve, lambda: V.reciprocal(out=rs[:, 0, tt:tt + 1], in_=rs[:, 0, tt:tt + 1]), reads=[r_b], writes=[r_b])
            K.op(dve, lambda: V.tensor_scalar(out=hs[i][:], in0=xt[i][:], scalar1=rs[:, 0, tt:tt + 1], scalar2=None, op0=ALU.mult),
                 reads=[xt_b[i], r_b], writes=[hs_b[i]])
            b = next_ps(2)
            for c in range(8):
                K.op(pe, lambda: nc.tensor.transpose(psum[:, b + c // 4, (c % 4) * 128:(c % 4 + 1) * 128],
                                                     hs[i][:, c * 128:(c + 1) * 128], ident[:]),
                     reads=[hs_b[i], cb], writes=[ps_b[b], ps_b[b + 1]], inc=(c == 7))
            K.op(dve, lambda: V.tensor_copy(out=hTlo[:, :, tt * 128:(tt + 1) * 128],
                                            in_=psum[:, b, :].rearrange("p (c t) -> p c t", c=4)),
                 reads=[ps_b[b]], writes=[hTlo_b])
            K.op(act, lambda: nc.scalar.copy(out=hThi[:, :, tt * 128:(tt + 1) * 128],
                                             in_=psum[:, b + 1, :].rearrange("p (c t) -> p c t", c=4)),
                 reads=[ps_b[b + 1]], writes=[hThi_b])
        for m in range(9):
            for tg in range(4):
                b = next_ps()
                for k in range(8):
                    K.op(pe, lambda: nc.tensor.matmul(psum[:, b, :], wi[:, k, m * 128:(m + 1) * 128],
                                                      hT[k // 4][:, k % 4, tg * 512:(tg + 1) * 512],
                                                      start=(k == 0), stop=(k == 7)),
                         reads=[wi_b, hT_b[k // 4]], writes=[ps_b[b]], inc=(k == 7))
                if m < 4:
                    K.op(act, lambda: nc.scalar.copy(
                        out=X1[:, m, :].rearrange("p (s sub) -> p sub s", s=8)[:, tg * 64:(tg + 1) * 64, :],
                        in_=psum[:, b, :].rearrange("p (sub s) -> p sub s", s=8)), reads=[ps_b[b]], writes=[X1_b])
                elif m < 8:
                    K.op(dve, lambda: V.tensor_copy(out=qT[:, m - 4, tg * 512:(tg + 1) * 512], in_=psum[:, b, :]),
                         reads=[ps_b[b]], writes=[qT_b])
                else:
                    K.op(act, lambda: nc.scalar.copy(out=kT[:, tg * 512:(tg + 1) * 512], in_=psum[:, b, :]),
                         reads=[ps_b[b]], writes=[kT_b])
        for tt in range(NT):
            b = next_ps()
            for k in range(8):
                K.op(pe, lambda: nc.tensor.matmul(psum[:, b, 0:128], hT[k // 4][:, k % 4, tt * 128:(tt + 1) * 128],
                                                  wi[:, k, 1152:1280], start=(k == 0), stop=(k == 7)),
                     reads=[wi_b, hT_b[k // 4]], writes=[ps_b[b]], inc=(k == 7))
            K.op(dve, lambda: V.tensor_copy(out=vaug[:, tt, :, 0:64], in_=psum[:, b, 0:128].rearrange("p (k d) -> p k d", k=2)),
                 reads=[ps_b[b]], writes=[va_b])
        if s == 0 and "uT" in dbg:
            K.dma(sp, dbg["uT"], X1[:], reads=[X1_b])
            K.dma(sp, dbg["qT"], qT[:], reads=[qT_b])
            K.dma(sp, dbg["kT"], kT[:], reads=[kT_b])
            K.dma(sp, dbg["vaug"], vaug[:], reads=[va_b])
        K.dma(sp, u_scr.rearrange("(c r) t -> r c t", c=4), X1[:], reads=[X1_b], writes=[u_scr_b])
        for s8 in range(8):
            K.dma(sp, Uf[s8 * 16:(s8 + 1) * 16, :, :],
                  u_scr.rearrange("(g q) (s sub) -> s q g sub", q=16, s=8)[s8], reads=[u_scr_b], writes=[UY_b])

        e_s5 = ExitStack()
        H = sb(e_s5, "H", (128, 2, NG, NCH + 1))
        Ein = sb(e_s5, "Ein", (128, 2, NG, NCH), BF16)
        st1 = sb(e_s5, "st1", (128, 2, NG))
        st2 = sb(e_s5, "st2", (128, 2, NG))
        mAb = [sb(e_s5, f"mAb{i}", (128, MA_COLS), BF16) for i in range(2)]
        mBb = [sb(e_s5, f"mBb{i}", (128, MB_COLS), BF16) for i in range(2)]
        e_att = ExitStack()
        et = [sb(e_att, f"et{i}", (128, 512), BF16) for i in range(3)]
        pt = [sb(e_att, f"pt{i}", (128, 512), BF16) for i in range(6)]
        oa = [sb(e_att, f"oa{i}", (128, 512)) for i in range(2)]
        den = [sb(e_att, f"den{i}", (128, 4)) for i in range(2)]

        def att_block(qb):
            kbs = [kb for kb in (qb - 1, qb, qb + 1) if 0 <= kb < NT]
            for kv in range(2):
                rows = slice(kv * 64, (kv + 1) * 64)
                pts = []
                for kb in kbs:
                    ie = K_rr(K, "et", 3)
                    ip = K_rr(K, "pt", 6)
                    b = next_ps()
                    K.op(pe, lambda: nc.tensor.matmul(psum[:, b, :], kT[rows, kb * 128:(kb + 1) * 128],
                                                      qT[rows, :, qb * 128:(qb + 1) * 128], start=True, stop=True),
                         reads=[kT_b, qT_b], writes=[ps_b[b]])
                    K.op(act, lambda: nc.scalar.activation(out=et[ie][:], in_=psum[:, b, :], func=AF.Exp, scale=0.125),
                         reads=[ps_b[b]], writes=[et_b[ie]])
                    K.op(dve, lambda: V.tensor_tensor(out=pt[ip][:], in0=et[ie][:],
                                                      in1=eb[:, kb - qb + 1, kv].rearrange("p c q -> p (c q)"), op=ALU.mult),
                         reads=[et_b[ie], ebB], writes=[pt_b[ip]])
                    pts.append((ip, kb))
                b = next_ps()
                for c in range(4):
                    for n_, (ip, kb) in enumerate(pts):
                        K.op(pe, lambda: nc.tensor.matmul(psum[:, b, c * 65:(c + 1) * 65], pt[ip][:, c * 128:(c + 1) * 128],
                                                          vaug[:, kb, kv, 0:65], start=(n_ == 0), stop=(n_ == len(pts) - 1)),
                             reads=[pt_b[ip], va_b], writes=[ps_b[b]], inc=(c == 3 and n_ == len(pts) - 1))
                io = qb % 2
                ov = psum[:, b, 0:260].rearrange("p (c d) -> p c d", c=4)
                K.op(dve, lambda: V.tensor_tensor(out=den[kv][:], in0=ov[:, :, 64], in1=esink[:, kv * 4:(kv + 1) * 4], op=ALU.add),
                     reads=[ps_b[b], pb], writes=[den_b[kv]])
                K.op(dve, lambda: V.reciprocal(out=den[kv][:], in_=den[kv][:]), reads=[den_b[kv]], writes=[den_b[kv]])
                K.op(dve, lambda: V.tensor_tensor(out=oa[io][:, kv * 256:(kv + 1) * 256].rearrange("p (c d) -> p c d", c=4),
                                                  in0=ov[:, :, 0:64], in1=den[kv][:].unsqueeze(2).to_broadcast([128, 4, 64]),
                                                  op=ALU.mult), reads=[ps_b[b], den_b[kv]], writes=[oa_b[io]])
            K.op(act, lambda: nc.scalar.activation(out=junk[:, 0:512], in_=oa[io][:], func=AF.Square, accum_out=ssq[:, 2, qb:qb + 1]),
                 reads=[oa_b[io]], writes=[junk_b, ssq_b[2]])
            b = next_ps()
            for c in range(4):
                K.op(pe, lambda: nc.tensor.transpose(psum[:, b, c * 128:(c + 1) * 128], oa[io][:, c * 128:(c + 1) * 128], ident[:]),
                     reads=[oa_b[io], cb], writes=[ps_b[b]], inc=(c == 3))
            K.op(act, lambda: nc.scalar.copy(out=yaT[:, :, qb * 128:(qb + 1) * 128], in_=psum[:, b, :].rearrange("p (c t) -> p c t", c=4)),
                 reads=[ps_b[b]], writes=[yaT_b])


        def scan_step(c):
            r = slice(0, 64)
            K.op(dve, lambda: V.tensor_tensor(out=st1[r], in0=H[r, :, :, c], in1=aL[r, 0], op=ALU.mult), reads=[H_b, aL_b, Hf_b], writes=[Hf_b])
            K.op(dve, lambda: V.tensor_tensor(out=H[r, :, :, c + 1], in0=H[r, :, :, c + 1], in1=st1[r], op=ALU.add), reads=[H_b, aL_b, Hf_b], writes=[Hf_b])
            K.op(dve, lambda: V.tensor_tensor(out=st1[r], in0=H[r, ::-1, :, c], in1=aL[r, 1], op=ALU.mult), reads=[H_b, aL_b, Hf_b], writes=[Hf_b])
            K.op(dve, lambda: V.tensor_tensor(out=H[r, :, :, c + 1], in0=H[r, :, :, c + 1], in1=st1[r], op=ALU.add), reads=[H_b, aL_b, Hf_b], writes=[Hf_b])
            r = slice(64, 128)
            cp = NCH - 1 - c
            G_ = nc.gpsimd
            K.op(pool, lambda: G_.tensor_tensor(out=st2[r], in0=H[r, :, :, cp + 1], in1=aL[r, 0], op=ALU.mult), reads=[H_b, aL_b, Hb_b], writes=[Hb_b])
            K.op(pool, lambda: G_.tensor_tensor(out=H[r, :, :, cp], in0=H[r, :, :, cp], in1=st2[r], op=ALU.add), reads=[H_b, aL_b, Hb_b], writes=[Hb_b])
            K.op(pool, lambda: G_.tensor_tensor(out=st2[r], in0=H[r, ::-1, :, cp + 1], in1=aL[r, 1], op=ALU.mult), reads=[H_b, aL_b, Hb_b], writes=[Hb_b])
            K.op(pool, lambda: G_.tensor_tensor(out=H[r, :, :, cp], in0=H[r, :, :, cp], in1=st2[r], op=ALU.add), reads=[H_b, aL_b, Hb_b], writes=[Hb_b])


        K.op(dve, lambda: V.memset(H[0:64, :, :, 0:1], 0.0), writes=[H_b, Hf_b, Hb_b])
        K.op(dve, lambda: V.memset(H[64:128, :, :, NCH:NCH + 1], 0.0), writes=[H_b, Hf_b, Hb_b])
        for g0 in range(0, G, 4):
            b = next_ps()
            for gl in range(4):
                g = g0 + gl
                im = g % 2
                K.dma(sp, mAb[im][:], matsA[g], reads=[matsA_b[g]], writes=[mAb_b[im]])
                for ri in range(2):
                    for j in range(JC):
                        K.op(pe, lambda: nc.tensor.matmul(
                            psum[:, b, (ri * 4 + gl) * NCH:(ri * 4 + gl + 1) * NCH],
                            mAb[im][:, (ri * JC + j) * 128:(ri * JC + j + 1) * 128],
                            Uf[:, g, :].rearrange("p (c j) -> p c j", j=JC)[:, :, j],
                            start=(j == 0), stop=(j == JC - 1)),
                            reads=[mAb_b[im], UY_b], writes=[ps_b[b]], inc=(ri == 1 and j == JC - 1))
            pv = psum[:, b, :].rearrange("p (r g c) -> p r g c", r=2, g=4)
            K.op(act, lambda: nc.scalar.copy(out=H[0:64, :, g0:g0 + 4, 1:NCH + 1], in_=pv[0:64]), reads=[ps_b[b]], writes=[H_b, Hf_b, Hb_b])
            K.op(act, lambda: nc.scalar.copy(out=H[64:128, :, g0:g0 + 4, 0:NCH], in_=pv[64:128]), reads=[ps_b[b]], writes=[H_b, Hf_b, Hb_b])
        spq = NCH // NT
        for qb in range(NT):
            att_block(qb)
            for c in range(qb * spq, (qb + 1) * spq):
                scan_step(c)
        K.op(dve, lambda: V.tensor_copy(out=Ein[0:64], in_=H[0:64, :, :, 0:NCH]), reads=[H_b, Hf_b], writes=[Ein_b])
        K.op(pool, lambda: nc.gpsimd.tensor_copy(out=Ein[64:128], in_=H[64:128, :, :, 1:NCH + 1]), reads=[H_b, Hb_b], writes=[Ein_b])
        for g0 in range(0, G, 2):
            b = next_ps()
            K.op(dve, lambda: V.memset(psum[:, b, :], 0.0), writes=[ps_b[b]])
            for gl in range(2):
                g = g0 + gl
                im = g % 2
                K.dma(sp, mBb[im][:], matsB[g], reads=[matsB_b[g]], writes=[mBb_b[im]])
                Ug = Uf[:, g, :].rearrange("p (c j) -> p c j", j=JC)
                Yg = psum[:, b, gl * 256:(gl + 1) * 256].rearrange("p (c j) -> p c j", j=JC)
                mm = []
                for d in range(JC):
                    mm.append((Yg[:, :, d:JC], mBb[im][:, d * 128:(d + 1) * 128], Ug[:, :, 0:JC - d], [UY_b]))
                    mm.append((Yg[:, :, 0:JC - d], mBb[im][:, (JC + d) * 128:(JC + d + 1) * 128], Ug[:, :, d:JC], [UY_b]))
                for j in range(JC):
                    mm.append((Yg[:, :, j], mBb[im][:, (2 * JC + j) * 128:(2 * JC + j + 1) * 128], Ein[:, 0, g, :], [Ein_b]))
                    mm.append((Yg[:, :, j], mBb[im][:, (3 * JC + j) * 128:(3 * JC + j + 1) * 128], Ein[:, 1, g, :], [Ein_b]))
                for n_, (o_, l_, r_, rd_) in enumerate(mm):
                    K.op(pe, lambda: nc.tensor.matmul(o_, l_, r_, start=False, stop=(n_ == len(mm) - 1), skip_group_check=True),
                         reads=[mBb_b[im]] + rd_, writes=[ps_b[b]], inc=(n_ == len(mm) - 1))
            K.op(act, lambda: nc.scalar.activation(out=Zf[:, g0:g0 + 2, :], in_=psum[:, b, :].rearrange("p (g s) -> p g s", g=2),
                                                   func=AF.Gelu_apprx_tanh), reads=[ps_b[b]], writes=[hTlo_b])
        for s8 in range(8):
            K.dma(sp, z_scr.rearrange("(g q) (s sub) -> s q g sub", q=16, s=8)[s8], Zf[s8 * 16:(s8 + 1) * 16, :, :],
                  reads=[hTlo_b], writes=[z_scr_b])
        K.dma(sp, X1[:], z_scr.rearrange("(c r) t -> r c t", c=4), reads=[z_scr_b], writes=[X1_b])
        K.barrier()
        e_att.close()
        gate = [sb(e_s5, f"gate{i}", (128, 512)) for i in range(2)]
        for m in range(4):
            for tg in range(4):
                b = next_ps()
                ig = (m * 4 + tg) % 2
                for k in range(4):
                    K.op(pe, lambda: nc.tensor.matmul(psum[:, b, :], wg[:, k, m * 128:(m + 1) * 128], X1[:, k, tg * 512:(tg + 1) * 512],
                                                      start=(k == 0), stop=(k == 3)),
                         reads=[wg_b, X1_b], writes=[ps_b[b]], inc=(k == 3))
                K.op(act, lambda: nc.scalar.activation(out=gate[ig][:], in_=psum[:, b, :], func=AF.Sigmoid, bias=bglu[:, m:m + 1]),
                     reads=[ps_b[b], pb], writes=[gate_b[ig]])
                nat = lambda tns: tns[:, m, :].rearrange("p (sub s) -> p s sub", s=8)[:, 2 * tg:2 * tg + 2, :]
                K.op(dve, lambda: V.tensor_tensor(out=nat(ysT), in0=X1[:, m, tg * 512:(tg + 1) * 512].rearrange("p (s sub) -> p s sub", s=2),
                                                  in1=gate[ig][:].rearrange("p (s sub) -> p s sub", s=2), op=ALU.mult),
                     reads=[X1_b, gate_b[ig]], writes=[ysT_b])
        K.op(act, lambda: nc.scalar.activation(out=ysq[:], in_=ysT[:], func=AF.Square), reads=[ysT_b], writes=[ysq_b])

        K.barrier()
        e_s5.close()
        if s == 0 and "ysT" in dbg:
            K.dma(sp, dbg["ysT"], ysT[:], reads=[ysT_b])
            K.dma(sp, dbg["yaT"], yaT[:], reads=[yaT_b])
            K.dma(sp, dbg["zT"], X1[:], reads=[X1_b])
        e_m4 = ExitStack()
        mt = [sb(e_m4, f"mt{i}", (128, 1024)) for i in range(2)]
        b = next_ps()
        for tt in range(NT):
            for k in range(4):
                K.op(pe, lambda: nc.tensor.matmul(psum[:, b, tt:tt + 1], ysq[:, k, tt * 128:(tt + 1) * 128], ones_bf[:, 0:1],
                                                  start=(k == 0), stop=(k == 3)),
                     reads=[ysq_b, cb], writes=[ps_b[b]], inc=(k == 3 and tt == NT - 1))
        K.op(act, lambda: nc.scalar.activation(out=rs[:, 1, :], in_=psum[:, b, 0:NT], func=AF.Sqrt, bias=epsb[:, 0:1], scale=1.0 / SSM_W),
             reads=[ps_b[b], cb], writes=[rs_b[1]])
        K.op(dve, lambda: V.reciprocal(out=rs[:, 1, :], in_=rs[:, 1, :]), reads=[rs_b[1]], writes=[rs_b[1]])
        K.op(act, lambda: nc.scalar.activation(out=rs[:, 2, :], in_=ssq[:, 2, :], func=AF.Sqrt, bias=epsb[:, 0:1], scale=1.0 / SSM_W),
             reads=[ssq_b[2], cb], writes=[rs_b[2]])
        K.op(dve, lambda: V.reciprocal(out=rs[:, 2, :], in_=rs[:, 2, :]), reads=[rs_b[2]], writes=[rs_b[2]])
        for tt in range(NT):
            i = tt % 2
            r0 = tok0 + tt * 128
            ba = next_ps(2)
            for hf in range(2):
                for k in range(4):
                    K.op(pe, lambda: nc.tensor.matmul(psum[:, ba + hf, :], ysT[:, k, tt * 128:(tt + 1) * 128], wo[:, k, hf * 512:(hf + 1) * 512],
                                                      start=(k == 0), stop=(k == 3)),
                         reads=[ysT_b, wo_b], writes=[ps_b[ba + hf]], inc=(k == 3))
            bb_ = next_ps(2)
            for hf in range(2):
                for k in range(4):
                    K.op(pe, lambda: nc.tensor.matmul(psum[:, bb_ + hf, :], yaT[:, k, tt * 128:(tt + 1) * 128], wo[:, 4 + k, hf * 512:(hf + 1) * 512],
                                                      start=(k == 0), stop=(k == 3)),
                         reads=[yaT_b, wo_b], writes=[ps_b[bb_ + hf]], inc=(k == 3))
            K.op(act, lambda: nc.scalar.activation(out=mt[i][:], in_=psum[:, ba:ba + 2, :].rearrange("p a n -> p (a n)"), func=AF.Copy,
                                                   scale=rs[:, 1, tt:tt + 1]), reads=[ps_b[ba], ps_b[ba + 1], rs_b[1]], writes=[mt_b[i]])
            K.op(dve, lambda: V.scalar_tensor_tensor(out=mt[i][:], in0=psum[:, bb_:bb_ + 2, :].rearrange("p a n -> p (a n)"),
                                                     scalar=rs[:, 2, tt:tt + 1], in1=mt[i][:], op0=ALU.mult, op1=ALU.add),
                 reads=[ps_b[bb_], ps_b[bb_ + 1], rs_b[2], mt_b[i]], writes=[mt_b[i]])
            K.op(act, lambda: nc.scalar.activation(out=junk[:], in_=mt[i][:], func=AF.Square, accum_out=ssq[:, 3, tt:tt + 1]),
                 reads=[mt_b[i]], writes=[junk_b, ssq_b[3]])
            K.op(act, lambda: nc.scalar.activation(out=rs[:, 3, tt:tt + 1], in_=ssq[:, 3, tt:tt + 1], func=AF.Sqrt,
                                                   bias=epsb[:, 0:1], scale=1.0 / D_MODEL), reads=[ssq_b[3], cb], writes=[rs_b[3]])
            K.op(dve, lambda: V.reciprocal(out=rs[:, 3, tt:tt + 1], in_=rs[:, 3, tt:tt + 1]), reads=[rs_b[3]], writes=[rs_b[3]])
            K.op(dve, lambda: V.tensor_tensor(out=mt[i][:], in0=mt[i][:], in1=gpost[:], op=ALU.mult),
                 reads=[mt_b[i], pb], writes=[mt_b[i]])
            rd = [out_b[s][tt]] if x_src is out else []
            K.dma(sp, xt[i][:], x_src[r0:r0 + 128, :], reads=rd, writes=[xt_b[i]])
            K.op(dve, lambda: V.scalar_tensor_tensor(out=xo[i][:], in0=mt[i][:], scalar=rs[:, 3, tt:tt + 1], in1=xt[i][:],
                                                     op0=ALU.mult, op1=ALU.add),
                 reads=[mt_b[i], rs_b[3], xt_b[i]], writes=[xo_b[i]])
            K.dma(sp, out[r0:r0 + 128, :], xo[i][:], reads=[xo_b[i]], writes=[out_b[s][tt]])
        K.barrier()
        e_m4.close()


_RR = {}


def K_rr(K, key, n):
    v = _RR.get(key, 0)
    _RR[key] = (v + 1) % n
    return v


def ffn_phase(nc, K, es, sb, psum, ps_b, next_ps, P, l, nseq, out, out_b, ident, identb, epsb, cb, dbg):
    pe, act, dve, pool, sp = K.pe, K.act, K.dve, K.pool, K.sp
    V = nc.vector
    wu = sb(es, "wu", (128, 8, 2 * D_FF), BF16)
    wd = sb(es, "wd", (128, NFB, 1024), BF16)
    gpf = sb(es, "gpf", (128, 8))
    cw = sb(es, "cw", (128, 4, 2 * NFB))
    gpost = sb(es, "gpost2", (128, 1024))
    wu_b, wd_b, pb = Buf(), Buf(), Buf()
    with ExitStack() as es1:
        ld = sb(es1, "ldf", (2 * NFB, 128))
        ld_b = Buf()
        LC = lambda dst, src, n: load_cols(nc, K, psum, ps_b, next_ps, ident, cb, ld, ld_b, dst, pb, src, n)
        LC(gpf[:], P["pre_ffn_norm"][l], 8)
        for j in range(3):
            LC(cw[:, j, :], P["conv_w"][l, j], 2 * NFB)
        LC(cw[:, 3, :], P["conv_b"][l], 2 * NFB)
        K.barrier()
    K.dma(sp, gpost[:], P["post_ffn_norm"][l].partition_broadcast(128), writes=[pb])
    with ExitStack() as es2:
        engs = [act, dve]
        stage = make_staging(es2, sb, "f")
        load_cast_weight(nc, K, stage, wu, wu_b, [P["w_up"][l, k * 128:(k + 1) * 128, :] for k in range(8)], 2 * D_FF,
                         gpf, pb, engs)
        load_cast_weight(nc, K, stage, wd, wd_b, [P["w_down"][l, k * 128:(k + 1) * 128, :] for k in range(NFB)], 1024,
                         None, pb, engs)
        K.barrier()

    HW = 1026
    h2T = sb(es, "h2T", (128, 8, HW), BF16)
    NPRE = 4
    gbuf = sb(es, "gbuf", (128, NFB - NPRE, 384), BF16)
    gpre = [sb(es, f"gpre{i}", (128, NPRE, 384), BF16) for i in range(2)]
    gbuf_b = [Buf() for _ in range(NFB - NPRE)]
    gpre_b = [[Buf() for _ in range(NPRE)] for _ in range(2)]

    def gslot(kpar, fb):
        if fb < NPRE:
            return gpre[kpar], fb, gpre_b[kpar][fb]
        return gbuf, fb - NPRE, gbuf_b[fb - NPRE]
    pending = []
    tgk = [0]
    halo = sb(es, "halo", (128, 8, 2), BF16)
    xt = [sb(es, f"fxt{i}", (128, 1024)) for i in range(3)]
    hs = xt
    junk = sb(es, "fjunk", (128, 1024), BF16)
    cv = [sb(es, f"cv{i}", (128, 384)) for i in range(3)]
    cg = [sb(es, f"cg{i}", (128, 384)) for i in range(3)]
    gg = [sb(es, f"gg{i}", (128, 384)) for i in range(3)]
    yt = [sb(es, f"yt{i}", (128, 1024)) for i in range(1)] * 2
    xo = xt
    st = sb(es, "fst", (128, 4))
    stf = sb(es, "fstf", (128, 2, 4))
    stf_b = [Buf() for _ in range(4)]
    std = sb(es, "fstd", (128, 2, 2))
    std_b = [Buf() for _ in range(2)]
    B = lambda n=1: [Buf() for _ in range(n)]
    h2T_b, g_b, junk_b, st_b = Buf(), Buf(), Buf(), Buf()
    xt_b, cv_b, cg_b, gg_b = B(3), B(3), B(3), B(3)
    yt_b = B(1) * 2
    hs_b = xt_b
    xo_b = xt_b
    def emit_down(s, tok0, hs0, t0, ln, kpar):
        for ti in range(ln // 128):
            tt = (hs0 + t0) // 128 + ti
            r0 = tok0 + tt * 128
            i = tt % 2
            b = next_ps(2)
            for hf in range(2):
                for fb in range(NFB):
                    gt_, gi_, gb_ = gslot(kpar, fb)
                    K.op(pe, lambda: nc.tensor.matmul(psum[:, b + hf, :], gt_[:, gi_, ti * 128:(ti + 1) * 128],
                                                      wd[:, fb, hf * 512:(hf + 1) * 512], start=(fb == 0), stop=(fb == NFB - 1)),
                         reads=[gb_, wd_b], writes=[ps_b[b + hf]], inc=(fb == NFB - 1))
            pv = psum[:, b:b + 2, :].rearrange("p a n -> p (a n)")
            K.op(act, lambda: nc.scalar.activation(out=junk[:], in_=pv, func=AF.Square, accum_out=std[:, 0, i:i + 1]),
                 reads=[ps_b[b], ps_b[b + 1]], writes=[junk_b, std_b[i]])
            K.op(act, lambda: nc.scalar.activation(out=std[:, 1, i:i + 1], in_=std[:, 0, i:i + 1], func=AF.Sqrt, bias=epsb[:, 0:1], scale=1.0 / D_MODEL),
                 reads=[std_b[i], cb], writes=[std_b[i]])
            K.op(dve, lambda: V.reciprocal(out=std[:, 1, i:i + 1], in_=std[:, 1, i:i + 1]), reads=[std_b[i]], writes=[std_b[i]])
            K.op(dve, lambda: V.tensor_tensor(out=yt[i][:], in0=pv, in1=gpost[:], op=ALU.mult),
                 reads=[ps_b[b], ps_b[b + 1], pb], writes=[yt_b[i]])
            K.dma(sp, xt[i][:], out[r0:r0 + 128, :], reads=[out_b[s][tt]], writes=[xt_b[i]])
            K.op(dve, lambda: V.scalar_tensor_tensor(out=xo[i][:], in0=yt[i][:], scalar=std[:, 1, i:i + 1], in1=xt[i][:],
                                                     op0=ALU.mult, op1=ALU.add),
                 reads=[yt_b[i], std_b[i], xt_b[i]], writes=[xo_b[i]])
            K.dma(sp, out[r0:r0 + 128, :], xo[i][:], reads=[xo_b[i]], writes=[out_b[s][tt]])

    it = [0]
    if l == 0:
        print("ffn sbuf bytes remaining", nc.sbuf_bytes_remaining)

    for s in range(nseq):
        tok0 = s * SEQ
        for half in range(2):
            hs0 = half * 1024
            t_lo = hs0 // 128 - 1
            if half == 1:
                K.op(dve, lambda: V.tensor_copy(out=h2T[:, :, 0:1], in_=halo[:, :, 0:1]), reads=[h2T_b], writes=[h2T_b])
            for tt in range(t_lo, t_lo + 10):
                if tt < 0 or tt >= NT or (half == 1 and tt == t_lo):
                    continue
                i = it[0] % 3
                it[0] += 1
                r0 = tok0 + tt * 128
                K.dma(sp, xt[i][:], out[r0:r0 + 128, :], reads=[out_b[s][tt]], writes=[xt_b[i]])
                js = (it[0] - 1) % 4
                K.op(act, lambda: nc.scalar.activation(out=junk[:], in_=xt[i][:], func=AF.Square, accum_out=stf[:, 0, js:js + 1]),
                     reads=[xt_b[i]], writes=[junk_b, stf_b[js]])
                K.op(act, lambda: nc.scalar.activation(out=stf[:, 1, js:js + 1], in_=stf[:, 0, js:js + 1], func=AF.Sqrt, bias=epsb[:, 0:1], scale=1.0 / D_MODEL),
                     reads=[stf_b[js], cb], writes=[stf_b[js]])
                K.op(dve, lambda: V.reciprocal(out=stf[:, 1, js:js + 1], in_=stf[:, 1, js:js + 1]), reads=[stf_b[js]], writes=[stf_b[js]])
                K.op(dve, lambda: V.tensor_scalar(out=hs[i][:], in0=xt[i][:], scalar1=stf[:, 1, js:js + 1], scalar2=None, op0=ALU.mult),
                     reads=[xt_b[i], stf_b[js]], writes=[hs_b[i]])
                b = next_ps(2)
                for c in range(8):
                    K.op(pe, lambda: nc.tensor.transpose(psum[:, b + c // 4, (c % 4) * 128:(c % 4 + 1) * 128],
                                                         hs[i][:, c * 128:(c + 1) * 128], ident[:]),
                         reads=[hs_b[i], cb], writes=[ps_b[b], ps_b[b + 1]], inc=(c == 7))
                j0 = tt * 128 - hs0 + 1
                lo, hi = max(j0, 0), min(j0 + 128, HW)
                for hh in range(2):
                    E = dve if hh == 0 else act
                    src = psum[:, b + hh, :].rearrange("p (c t) -> p c t", c=4)[:, :, lo - j0:hi - j0]
                    dst = h2T[:, hh * 4:(hh + 1) * 4, lo:hi]
                    if E is dve:
                        K.op(dve, lambda: V.tensor_copy(out=dst, in_=src), reads=[ps_b[b + hh]], writes=[h2T_b])
                    else:
                        K.op(act, lambda: nc.scalar.copy(out=dst, in_=src), reads=[ps_b[b + hh]], writes=[h2T_b])
            if half == 0:
                K.op(dve, lambda: V.tensor_copy(out=halo[:, :, 0:1], in_=h2T[:, :, 1024:1025]), reads=[h2T_b], writes=[h2T_b])
            for (t0, ln) in ((0, 384), (384, 384), (768, 256)):
                g_first = (hs0 + t0 == 0)
                g_last = (hs0 + t0 + ln == SEQ)
                lo = 1 if g_first else 0
                hi = 1 if g_last else 0
                w0c, w1c = t0 + lo, t0 + ln + 2 - hi
                nW = w1c - w0c
                ctr = 1 - lo
                kpar = tgk[0] % 2
                tgk[0] += 1
                for fb in range(NFB):
                    i = fb % 3
                    if fb == NPRE:
                        for fn in pending:
                            fn()
                        pending.clear()
                    for vg in range(2):
                        rb = vg * NFB + fb
                        b = next_ps()
                        for k in range(8):
                            K.op(pe, lambda: nc.tensor.matmul(psum[:, b, 0:nW], wu[:, k, rb * 128:(rb + 1) * 128],
                                                              h2T[:, k, w0c:w1c], start=(k == 0), stop=(k == 7)),
                                 reads=[wu_b, h2T_b], writes=[ps_b[b]], inc=(k == 7))
                        dst, dst_b = (cv[i], cv_b[i]) if vg == 0 else (cg[i], cg_b[i])
                        K.op(act, lambda: nc.scalar.activation(out=dst[:, 0:ln], in_=psum[:, b, ctr:ctr + ln], func=AF.Identity,
                                                               bias=cw[:, 3, rb:rb + 1], scale=cw[:, 1, rb:rb + 1]),
                             reads=[ps_b[b], pb], writes=[dst_b])
                        K.op(dve, lambda: V.scalar_tensor_tensor(out=dst[:, lo:ln], in0=psum[:, b, ctr + lo - 1:ctr + ln - 1],
                                                                 scalar=cw[:, 0, rb:rb + 1], in1=dst[:, lo:ln], op0=ALU.mult, op1=ALU.add),
                             reads=[ps_b[b], pb, dst_b], writes=[dst_b])
                        K.op(dve, lambda: V.scalar_tensor_tensor(out=dst[:, 0:ln - hi], in0=psum[:, b, ctr + 1:ctr + ln - hi + 1],
                                                                 scalar=cw[:, 2, rb:rb + 1], in1=dst[:, 0:ln - hi], op0=ALU.mult, op1=ALU.add),
                             reads=[ps_b[b], pb, dst_b], writes=[dst_b])
                    K.op(act, lambda: nc.scalar.activation(out=gg[i][:, 0:ln], in_=cg[i][:, 0:ln], func=AF.Gelu_apprx_tanh),
                         reads=[cg_b[i]], writes=[gg_b[i]])
                    gt_, gi_, gb_ = gslot(kpar, fb)
                    K.op(pool, lambda: nc.gpsimd.tensor_tensor(out=gt_[:, gi_, 0:ln], in0=gg[i][:, 0:ln], in1=cv[i][:, 0:ln], op=ALU.mult),
                         reads=[gg_b[i], cv_b[i]], writes=[gb_])
                pending.append(lambda s=s, tok0=tok0, hs0=hs0, t0=t0, ln=ln, kpar=kpar: emit_down(s, tok0, hs0, t0, ln, kpar))
    for fn in pending:
        fn()
    pending.clear()
_CACHE = {}


def _q_perm():
    cols = list(range(512))
    for c in range(4):
        for h in (c, 4 + c):
            cols.extend(range(512 + h * 64, 512 + (h + 1) * 64))
    cols.extend(range(1024, 1280))
    return np.asarray(cols)


def kernel(**inputs):
    n_cores = 8
    x = np.ascontiguousarray(np.asarray(inputs["x"], dtype=np.float32))
    nseq = x.shape[0] // n_cores
    if "nc" not in _CACHE:
        _CACHE["nc"] = build(nseq=nseq)[0]
        _CACHE["consts"] = _host_consts()
    nc = _CACHE["nc"]
    consts = _CACHE["consts"]
    shared = {}
    for k, v in inputs.items():
        if k == "x":
            continue
        a = np.ascontiguousarray(np.asarray(v, dtype=np.float32))
        if k == "w_in":
            a = np.ascontiguousarray(a[:, :, _q_perm()])
        shared[k] = a
    shared.update(consts)
    in_maps = []
    for i in range(n_cores):
        m = dict(shared)
        m["x"] = x[i * nseq:(i + 1) * nseq].reshape(nseq * SEQ, D_MODEL)
        in_maps.append(m)
    res = run_bass_kernel_spmd(nc, in_maps, core_ids=list(range(n_cores)))
    outs = [np.asarray(r["out"]).reshape(nseq, SEQ, D_MODEL) for r in res.results]
    return np.concatenate(outs, axis=0).astype(np.float32)
```

```python
import math
from contextlib import ExitStack
import numpy as np
import jax
import jax.numpy as jnp
import concourse.bass as bass
import concourse.mybir as mybir
from concourse.bass_utils import run_bass_kernel_spmd

F32 = mybir.dt.float32
BF16 = mybir.dt.bfloat16
I32 = mybir.dt.int32
AF = mybir.ActivationFunctionType
ALU = mybir.AluOpType

D_MODEL = 1024
SEQ = 2048
DEPTH = 4
SSM_W = 512
NG = 32
NST = 64
D_FF = 2816
NFB = 22
IN_W = 1280
EPS = 1e-6
JC = 4
LCH = 8 * JC
NSUB = SEQ // 8
NCH = NSUB // JC
NT = SEQ // 128
TWO_PI = 2.0 * math.pi

EC_TB = 0
EC_TC = EC_TB + 8
EC_SB = EC_TC + 8 * JC
EC_CC = EC_SB + 8 * JC
EC_L = EC_CC + 8 * JC
NEC = EC_L + 1
ANG_SHIFT = 64

MA_COLS = 2 * JC * 128
MB_COLS = 4 * JC * 128


class Buf:
    __slots__ = ("w", "r", "name")

    def __init__(self, name=""):
        self.w = None
        self.r = {}
        self.name = name


class Eng:
    def __init__(self, K, name, eng, is_pe=False):
        self.K = K
        self.name = name
        self.eng = eng
        self.is_pe = is_pe
        self.sems = []
        self.ep = -1
        self.cnt = 0
        self.seen = {}
        self.pending = []
        self._new_epoch()

    def _new_epoch(self):
        self.ep += 1
        self.cnt = 0
        self.sems.append(self.K.nc.alloc_semaphore(f"s_{self.name}_{self.ep}"))

class Kern:
    def __init__(self, nc):
        self.nc = nc
        self.pe = Eng(self, "pe", nc.tensor, True)
        self.act = Eng(self, "act", nc.scalar)
        self.dve = Eng(self, "dve", nc.vector)
        self.pool = Eng(self, "pool", nc.gpsimd)
        self.sp = Eng(self, "sp", nc.sync)
        self.engs = [self.pe, self.act, self.dve, self.pool, self.sp]
        self.slots = {}
        for e, n in ((self.sp, 20), (self.pool, 6), (self.act, 6)):
            self.slots[e.name] = [[nc.alloc_semaphore(f"d_{e.name}_{i}"), 0, ("dma", e.name, i)] for i in range(n)]
        self.slot_i = {k: 0 for k in self.slots}
        self.n_ins = 0

    def _wait(self, E, tick):
        sem, val, key = tick
        if key[0] == E.name and E.is_pe:
            return
        if E.seen.get(key, 0) >= val:
            return
        if key[0] != "dma":
            for (k2, v2) in E.seen.items():
                if k2[0] == key[0] and k2[0] != "dma" and k2[1] > key[1]:
                    return
        E.eng.wait_ge(sem, val)
        E.seen[key] = val
        self.n_ins += 1

    def _deps(self, E, reads, writes):
        need = []
        for b in reads:
            if b.w is not None:
                need.append(b.w)
        for b in writes:
            if b.w is not None:
                need.append(b.w)
            need.extend(b.r.values())
        for t in need:
            self._wait(E, t)

    def op(self, E, fn, reads=(), writes=(), inc=True):
        if E.cnt >= 28000 and not E.pending:
            E._new_epoch()
        self._deps(E, reads, writes)
        tick = (E.sems[E.ep], E.cnt + 1, (E.name, E.ep))
        ins = fn()
        self.n_ins += 1
        if inc:
            ins.then_inc(E.sems[E.ep], 1)
            E.cnt += 1
            E.pending = []
        else:
            E.pending.append(1)
        for b in reads:
            b.r[E.name] = tick
        for b in writes:
            b.w = tick
            b.r = {}
        return ins

    def dma(self, E, out, in_, reads=(), writes=(), **kw):
        slots = self.slots[E.name]
        i = self.slot_i[E.name]
        self.slot_i[E.name] = (i + 1) % len(slots)
        s = slots[i]
        if s[1] > 0:
            self._wait(E, (s[0], s[1], s[2]))
        self._deps(E, reads, writes)
        ins = E.eng.dma_start(out=out, in_=in_, **kw)
        s[1] += 16
        ins.then_inc(s[0], 16)
        self.n_ins += 1
        tick = (s[0], s[1], s[2])
        for b in reads:
            b.r[("dma", E.name, i)] = tick
        for b in writes:
            b.w = tick
            b.r = {}
        return ins

    def barrier(self):
        sp = self.sp
        for E in self.engs:
            if E is sp:
                continue
            if E.cnt > 0:
                self._wait(sp, (E.sems[E.ep], E.cnt, (E.name, E.ep)))
        for lst in self.slots.values():
            for s in lst:
                if s[1] > 0:
                    self._wait(sp, (s[0], s[1], s[2]))
        self.op(sp, lambda: sp.eng.nop())
        t = (sp.sems[sp.ep], sp.cnt, (sp.name, sp.ep))
        for E in self.engs:
            if E is not sp:
                self._wait(E, t)


def _t5_bucket(rel):
    n_buckets, max_distance = 32, 128
    half = n_buckets // 2
    max_exact = half // 2
    ret = jnp.where(rel > 0, half, 0)
    n = jnp.abs(rel)
    nf = jnp.maximum(n, 1).astype(jnp.float32)
    large = max_exact + (jnp.log(nf / max_exact) / math.log(max_distance / max_exact)
                         * (half - max_exact)).astype(jnp.int32)
    large = jnp.minimum(large, half - 1)
    return ret + jnp.where(n < max_exact, n, large)


def _host_consts():
    import ml_dtypes
    c = {}
    c["ident_f"] = np.eye(128, dtype=np.float32)
    ex = np.zeros((2, NEC), np.float32)
    for s in range(8):
        ex[0, EC_TB + s] = -s
        ex[1, EC_TB + s] = s
    for d in range(JC):
        for s in range(8):
            ex[0, EC_TC + d * 8 + s] = 8 * d + s
            ex[1, EC_TC + d * 8 + s] = 8 * d - s
            ex[0, EC_SB + d * 8 + s] = LCH - 1 - 8 * d - s
            ex[1, EC_SB + d * 8 + s] = 8 * d + s
            ex[0, EC_CC + d * 8 + s] = 8 * d + s + 1
            ex[1, EC_CC + d * 8 + s] = LCH - 8 * d - s
    ex[:, EC_L] = LCH
    c["extab"] = np.repeat(ex, 64, axis=0).astype(np.float32)
    sp = np.arange(128)[:, None] // 16
    s = np.arange(128)[None, :] // 16
    tm = np.stack([(s >= sp), (sp >= s)], axis=1).astype(np.float32)
    c["tmask"] = np.ascontiguousarray(tm)
    k = np.arange(128)[:, None, None]
    off = (np.arange(3) - 1)[None, :, None]
    q = np.arange(128)[None, None, :]
    rel = (k + 128 * off - q).astype(np.int32)
    with jax.default_device(jax.devices("cpu")[0]):
        bk = np.asarray(_t5_bucket(jnp.asarray(rel)))
    oh = (bk[:, None, :, :] == np.arange(32)[None, :, None, None]).astype(np.float32)
    c["onehot"] = oh.reshape(128, 32, 384).astype(ml_dtypes.bfloat16)
    c["vmask"] = (np.abs(rel) <= 128).astype(np.float32).reshape(128, 384)
    return c


def build(nseq=4, nlayers=DEPTH, debug=None):
    nc = bass.Bass("TRN2", target_bir_lowering=False)
    K = Kern(nc)
    pe, act, dve, pool, sp = K.pe, K.act, K.dve, K.pool, K.sp
    NTOK = nseq * SEQ

    def din(name, shape, dt=F32):
        return nc.dram_tensor(name, list(shape), dt, kind="ExternalInput").ap()

    x_in = din("x", (NTOK, D_MODEL))
    P = {}
    P["rel_bias"] = din("rel_bias", (32, 8))
    P["pre_mix_norm"] = din("pre_mix_norm", (DEPTH, D_MODEL))
    P["w_in"] = din("w_in", (DEPTH, D_MODEL, IN_W))
    P["lam_re"] = din("lam_re", (DEPTH, 2, NG, NST))
    P["lam_im"] = din("lam_im", (DEPTH, 2, NG, NST))
    P["log_step"] = din("log_step", (DEPTH, 2, NG))
    P["b_re"] = din("b_re", (DEPTH, 2, NG, NST, 16))
    P["b_im"] = din("b_im", (DEPTH, 2, NG, NST, 16))
    P["c_re"] = din("c_re", (DEPTH, 2, NG, 16, NST))
    P["c_im"] = din("c_im", (DEPTH, 2, NG, 16, NST))
    P["ssm_d"] = din("ssm_d", (DEPTH, SSM_W))
    P["w_glu"] = din("w_glu", (DEPTH, SSM_W, SSM_W))
    P["b_glu"] = din("b_glu", (DEPTH, SSM_W))
    P["attn_sink"] = din("attn_sink", (DEPTH, 8))
    P["ssm_out_norm"] = din("ssm_out_norm", (DEPTH, SSM_W))
    P["attn_out_norm"] = din("attn_out_norm", (DEPTH, SSM_W))
    P["w_out"] = din("w_out", (DEPTH, D_MODEL, D_MODEL))
    P["post_mix_norm"] = din("post_mix_norm", (DEPTH, D_MODEL))
    P["pre_ffn_norm"] = din("pre_ffn_norm", (DEPTH, D_MODEL))
    P["w_up"] = din("w_up", (DEPTH, D_MODEL, 2 * D_FF))
    P["conv_w"] = din("conv_w", (DEPTH, 3, 2 * D_FF))
    P["conv_b"] = din("conv_b", (DEPTH, 2 * D_FF))
    P["w_down"] = din("w_down", (DEPTH, D_FF, D_MODEL))
    P["post_ffn_norm"] = din("post_ffn_norm", (DEPTH, D_MODEL))
    c_ident = din("ident_f", (128, 128))
    c_extab = din("extab", (128, NEC))
    c_tmask = din("tmask", (128, 2, 128))
    c_onehot = din("onehot", (128, 32, 384), BF16)
    c_vmask = din("vmask", (128, 384))

    out = nc.dram_tensor("out", [NTOK, D_MODEL], F32, kind="ExternalOutput").ap()
    matsA = nc.dram_tensor("matsA", [NG, 128, MA_COLS], BF16).ap()
    matsB = nc.dram_tensor("matsB", [NG, 128, MB_COLS], BF16).ap()
    u_scr = nc.dram_tensor("u_scr", [NG * 16, SEQ], BF16).ap()
    z_scr = nc.dram_tensor("z_scr", [NG * 16, SEQ], BF16).ap()
    eb_scr = nc.dram_tensor("eb_scr", [128, 3 * 2 * 4 * 128], BF16).ap()
    matsA_b = [Buf() for _ in range(NG)]
    matsB_b = [Buf() for _ in range(NG)]
    u_scr_b, z_scr_b = Buf(), Buf()
    out_b = [[Buf() for _ in range(NT)] for _ in range(nseq)]
    dbg = {}
    if debug:
        for nm, (shp, dt_) in debug.items():
          if shp is not None:
            dbg[nm] = nc.dram_tensor("dbg_" + nm, list(shp), dt_, kind="ExternalOutput").ap()

    top = ExitStack()
    with top:
        uid = [0]

        def sb(es, name, shape, dt=F32):
            uid[0] += 1
            return es.enter_context(nc.sbuf_tensor(f"{name}_{uid[0]}", list(shape), dt))

        psum = top.enter_context(nc.psum_tensor("psum", [128, 8, 512], F32))
        ps_b = [Buf(f"ps{i}") for i in range(8)]
        ps_rr = [0]

        def next_ps(n=1):
            if n == 1:
                i = ps_rr[0] % 8
                ps_rr[0] += 1
                return i
            i = ps_rr[0] % 8
            if i % 2:
                i = (i + 1) % 8
            ps_rr[0] = i + 2
            return i

        ident = sb(top, "ident", (128, 128))
        identb = sb(top, "identb", (128, 128), BF16)
        ones_bf = sb(top, "ones_bf", (128, 1), BF16)
        epsb = sb(top, "epsb", (128, 1))
        aL = sb(top, "aL", (128, 2, 2, 32))
        aL_b = Buf("aL")
        cb = Buf("consts")
        ebB = Buf("eb")
        K.dma(sp, ident[:], c_ident, writes=[cb])
        K.op(dve, lambda: nc.vector.tensor_copy(out=identb[:], in_=ident[:]), reads=[cb], writes=[cb])
        K.op(dve, lambda: nc.vector.memset(ones_bf[:], 1.0), writes=[cb])
        K.op(dve, lambda: nc.vector.memset(epsb[:], EPS), writes=[cb])

        with ExitStack() as es:
            oh = sb(es, "oh", (128, 32, 384), BF16)
            eb = sb(es, "eb0", (128, 3, 2, 4, 128), BF16)
            rb = sb(es, "rb", (128, 256))
            vm = sb(es, "vm", (128, 384))
            acc = sb(es, "acc", (128, 8, 384))
            tb = Buf()
            K.dma(sp, oh[:], c_onehot, writes=[tb])
            K.dma(sp, vm[:], c_vmask, writes=[tb])
            K.dma(sp, rb[:], P["rel_bias"].rearrange("b h -> (b h)").partition_broadcast(128), writes=[tb])
            accb = [Buf() for _ in range(8)]
            for h in range(8):
                E = dve
                K.op(E, lambda h=h: nc.vector.tensor_scalar(out=acc[:, h, :], in0=oh[:, 0, :], scalar1=rb[:, h:h + 1],
                                                            scalar2=None, op0=ALU.mult), reads=[tb], writes=[accb[h]])
                for b in range(1, 32):
                    K.op(E, lambda h=h, b=b: nc.vector.scalar_tensor_tensor(
                        out=acc[:, h, :], in0=oh[:, b, :], scalar=rb[:, b * 8 + h:b * 8 + h + 1], in1=acc[:, h, :],
                        op0=ALU.mult, op1=ALU.add), reads=[tb, accb[h]], writes=[accb[h]])
                K.op(act, lambda h=h: nc.scalar.activation(out=acc[:, h, :], in_=acc[:, h, :], func=AF.Exp),
                     reads=[accb[h]], writes=[accb[h]])
                kv, c = h // 4, h % 4
                K.op(dve, lambda h=h, kv=kv, c=c: nc.vector.tensor_tensor(
                    out=eb[:, :, kv, c, :], in0=acc[:, h, :].rearrange("p (o q) -> p o q", o=3),
                    in1=vm[:].rearrange("p (o q) -> p o q", o=3), op=ALU.mult), reads=[accb[h], tb], writes=[ebB])
            K.dma(sp, eb_scr, eb[:].rearrange("p o k c q -> p (o k c q)"), reads=[ebB], writes=[ebB])
            K.barrier()

        for l in range(nlayers):
            x_src = x_in if l == 0 else out
            with ExitStack() as es:
                s5_prologue(nc, K, es, sb, psum, ps_b, next_ps, P, l, ident, c_extab, c_tmask,
                            aL, aL_b, matsA, matsB, matsA_b, matsB_b, dbg)
                K.barrier()
            with ExitStack() as es:
                mixer_phase(nc, K, es, sb, psum, ps_b, next_ps, P, l, nseq, x_src, out, out_b, ident, identb, ones_bf,
                            eb_scr, ebB, epsb, cb, aL, aL_b, matsA, matsB, matsA_b, matsB_b, u_scr, z_scr, u_scr_b, z_scr_b, dbg)
                K.barrier()
            with ExitStack() as es:
              if not (debug and "skip_ffn" in debug):
                ffn_phase(nc, K, es, sb, psum, ps_b, next_ps, P, l, nseq, out, out_b, ident, identb, epsb, cb, dbg)
                K.barrier()
        K.barrier()
    return nc, K


def make_staging(es, sb, tag, n=6, ch=2048):
    return ([sb(es, f"stg_{tag}{i}", (128, ch)) for i in range(n)], [Buf() for _ in range(n)], [0])


def load_cast_weight(nc, K, stage, dst, dst_b, src_rows, ncols, gain, gain_b, engines):
    stg, stg_b, itr = stage
    NS = len(stg)
    CH = 2048
    dq = [K.sp, K.act]
    it = itr[0]
    for r, src in enumerate(src_rows):
        for c0 in range(0, ncols, CH):
            cw = min(CH, ncols - c0)
            i = it % NS
            E = engines[it % len(engines)]
            Q = dq[it % 2]
            it += 1
            K.dma(Q, stg[i][:, 0:cw], src[:, c0:c0 + cw], writes=[stg_b[i]])
            if gain is None:
                if E is K.act:
                    K.op(E, lambda i=i, r=r, c0=c0, cw=cw: nc.scalar.copy(out=dst[:, r, c0:c0 + cw], in_=stg[i][:, 0:cw]),
                         reads=[stg_b[i]], writes=[dst_b])
                else:
                    K.op(E, lambda i=i, r=r, c0=c0, cw=cw, E=E: E.eng.tensor_copy(out=dst[:, r, c0:c0 + cw], in_=stg[i][:, 0:cw]),
                         reads=[stg_b[i]], writes=[dst_b])
            else:
                if E is K.act:
                    K.op(E, lambda i=i, r=r, c0=c0, cw=cw: nc.scalar.activation(
                        out=dst[:, r, c0:c0 + cw], in_=stg[i][:, 0:cw], func=AF.Copy, scale=gain[:, r:r + 1]),
                        reads=[stg_b[i], gain_b], writes=[dst_b])
                else:
                    K.op(E, lambda i=i, r=r, c0=c0, cw=cw, E=E: E.eng.tensor_scalar(
                        out=dst[:, r, c0:c0 + cw], in0=stg[i][:, 0:cw], scalar1=gain[:, r:r + 1], scalar2=None,
                        op0=ALU.mult), reads=[stg_b[i], gain_b], writes=[dst_b])
    itr[0] = it


def load_cols(nc, K, psum, ps_b, next_ps, ident, cb, ld, ld_b, dst, dst_b, src1d, n):
    K.dma(K.sp, ld[0:n, :], src1d.rearrange("(r p) -> r p", p=128), writes=[ld_b])
    b = next_ps()
    K.op(K.pe, lambda: nc.tensor.transpose(psum[:, b, 0:n], ld[0:n, :], ident[0:n, 0:n]), reads=[ld_b, cb], writes=[ps_b[b]])
    K.op(K.dve, lambda: nc.vector.tensor_copy(out=dst, in_=psum[:, b, 0:n]), reads=[ps_b[b]], writes=[dst_b])


def rstd_from_ssq(nc, K, ssq, rs, n, width, epsb, bufs_r, bufs_w):
    K.op(K.act, lambda: nc.scalar.activation(out=rs[:, 0:n], in_=ssq[:, 0:n], func=AF.Sqrt, bias=epsb[:, 0:1],
                                             scale=1.0 / width), reads=bufs_r, writes=bufs_w)
    K.op(K.dve, lambda: nc.vector.reciprocal(out=rs[:, 0:n], in_=rs[:, 0:n]), reads=bufs_w, writes=bufs_w)


def s5_prologue(nc, K, es, sb, psum, ps_b, next_ps, P, l, ident, c_extab, c_tmask,
                aL, aL_b, matsA, matsB, matsA_b, matsB_b, dbg):
    pe, act, dve, pool, sp = K.pe, K.act, K.dve, K.pool, K.sp
    V = nc.vector
    G = NG
    tb = Buf("s5tab")

    def t(name, shape, dt=F32):
        return sb(es, name, shape, dt)

    lamld = t("lamld", (32, 2, 128))
    lam = t("lam", (128, 2, 32))
    ls = t("ls", (128, 32))
    braw = t("braw", (128, 2, 32, 16))
    cld = t("cld", (128, 2, 4, 128))
    craw = t("craw", (128, 2, 32, 16))
    dvec = t("dvec", (128, 32))
    extab = t("extab_sb", (128, NEC))
    tmask = t("tmask_sb", (128, 2, 128))
    K.dma(sp, extab[:], c_extab, writes=[tb])
    K.dma(sp, tmask[:], c_tmask, writes=[tb])
    with nc.allow_non_contiguous_dma(reason="tiny param loads"):
        for ri, nm in enumerate(("lam_re", "lam_im")):
            K.dma(sp, lamld[:, ri, :].rearrange("g (d n) -> g d n", d=2), P[nm][l].rearrange("d g n -> g d n"), writes=[tb])
        for d in range(2):
            K.dma(sp, ls[d * 64:(d + 1) * 64, :], P["log_step"][l, d].partition_broadcast(64), writes=[tb])
            for ri, nm in enumerate(("b_re", "b_im")):
                K.dma(sp, braw[d * 64:(d + 1) * 64, ri, :, :], P[nm][l, d].rearrange("g n q -> n g q"), writes=[tb])
        for ri, nm in enumerate(("c_re", "c_im")):
            for d in range(2):
                K.dma(sp, cld[:, ri, :, d * 64:(d + 1) * 64],
                      P[nm][l, d].rearrange("(t gl) p n -> (gl p) t n", t=4), writes=[tb])
        for s in range(8):
            K.dma(sp, dvec[s * 16:(s + 1) * 16, :], P["ssm_d"][l].rearrange("(g q) -> q g", q=16), writes=[tb])
    for ri in range(2):
        b = next_ps()
        K.op(pe, lambda ri=ri, b=b: nc.tensor.transpose(psum[:, b, 0:32], lamld[:, ri, :], ident[0:32, 0:32]),
             reads=[tb], writes=[ps_b[b]])
        K.op(dve, lambda ri=ri, b=b: V.tensor_copy(out=lam[:, ri, :], in_=psum[:, b, 0:32]), reads=[ps_b[b]], writes=[tb])
        b = next_ps()
        for tt in range(4):
            K.op(pe, lambda ri=ri, b=b, tt=tt: nc.tensor.transpose(psum[:, b, tt * 128:(tt + 1) * 128], cld[:, ri, tt, :], ident[:]),
                 reads=[tb], writes=[ps_b[b]], inc=(tt == 3))
        K.op(dve, lambda ri=ri, b=b: V.tensor_copy(out=craw[:, ri, :, :].rearrange("p g q -> p (g q)"), in_=psum[:, b, :]),
             reads=[ps_b[b]], writes=[tb])

    dt_ = t("dt", (128, 32))
    ar = t("ar", (128, 32))
    ang = t("ang", (128, 32))
    lb = t("lb", (128, 2, 32))
    coef = t("coef", (128, 2, 32))
    tmp = t("tmpa", (128, 4, 32))

    def dv(fn, inc=True):
        K.op(dve, fn, reads=[tb], writes=[tb], inc=inc)

    def ac(fn):
        K.op(act, fn, reads=[tb], writes=[tb])

    ac(lambda: nc.scalar.activation(out=dt_[:], in_=ls[:], func=AF.Exp))
    dv(lambda: V.tensor_tensor(out=ar[:], in0=lam[:, 0, :], in1=dt_[:], op=ALU.mult))
    dv(lambda: V.tensor_tensor(out=ang[:], in0=lam[:, 1, :], in1=dt_[:], op=ALU.mult))

    PR = t("PR", (128, 32, NEC))
    PI = t("PI", (128, 32, NEC))
    A1 = t("A1", (128, 32, NEC))
    A2 = t("A2", (128, 32, NEC))
    A3i = t("A3i", (128, 32, NEC), I32)
    A4 = t("A4", (128, 32, NEC))
    ex_b = extab[:].unsqueeze(1).to_broadcast([128, 32, NEC])

    def bc_g(x):
        return x.unsqueeze(2).to_broadcast([128, 32, NEC])

    dv(lambda: V.tensor_tensor(out=A1[:], in0=ex_b, in1=bc_g(ar[:]), op=ALU.mult))
    ac(lambda: nc.scalar.activation(out=A4[:], in_=A1[:], func=AF.Exp))
    dv(lambda: V.tensor_tensor(out=A1[:], in0=ex_b, in1=bc_g(ang[:]), op=ALU.mult))

    def sin_of(dst, shift):
        dv(lambda: V.tensor_scalar(out=A2[:], in0=A1[:], scalar1=shift + ANG_SHIFT * TWO_PI, scalar2=1.0 / TWO_PI,
                                   op0=ALU.add, op1=ALU.mult))
        dv(lambda: V.tensor_copy(out=A3i[:], in_=A2[:]))
        dv(lambda: V.tensor_copy(out=dst[:], in_=A3i[:]))
        dv(lambda: V.tensor_tensor(out=A2[:], in0=A2[:], in1=dst[:], op=ALU.subtract))
        dv(lambda: V.tensor_scalar(out=dst[:], in0=A2[:], scalar1=0.5, scalar2=None, op0=ALU.is_gt))
        dv(lambda: V.tensor_tensor(out=A2[:], in0=A2[:], in1=dst[:], op=ALU.subtract))
        dv(lambda: V.tensor_scalar(out=dst[:], in0=A2[:], scalar1=-0.5, scalar2=None, op0=ALU.is_lt))
        dv(lambda: V.tensor_tensor(out=A2[:], in0=A2[:], in1=dst[:], op=ALU.add))
        dv(lambda: V.tensor_scalar(out=A2[:], in0=A2[:], scalar1=TWO_PI, scalar2=3.14159, op0=ALU.mult, op1=ALU.min))
        dv(lambda: V.tensor_scalar(out=A2[:], in0=A2[:], scalar1=-3.14159, scalar2=None, op0=ALU.max))
        ac(lambda: nc.scalar.activation(out=dst[:], in_=A2[:], func=AF.Sin))

    sin_of(PI, 0.0)
    sin_of(PR, math.pi / 2)
    dv(lambda: V.tensor_tensor(out=PR[:], in0=PR[:], in1=A4[:], op=ALU.mult))
    dv(lambda: V.tensor_tensor(out=PI[:], in0=PI[:], in1=A4[:], op=ALU.mult))

    ac(lambda: nc.scalar.activation(out=tmp[:, 0, :], in_=ar[:], func=AF.Exp))
    dv(lambda: V.tensor_copy(out=lb[0:64, 0, :], in_=PR[0:64, :, EC_TC + 1]))
    dv(lambda: V.tensor_copy(out=lb[0:64, 1, :], in_=PI[0:64, :, EC_TC + 1]))
    dv(lambda: V.tensor_copy(out=lb[64:128, 0, :], in_=PR[64:128, :, EC_TB + 1]))
    dv(lambda: V.tensor_copy(out=lb[64:128, 1, :], in_=PI[64:128, :, EC_TB + 1]))
    dv(lambda: V.tensor_tensor(out=tmp[:, 0, :], in0=lam[:, 0, :], in1=lam[:, 0, :], op=ALU.mult))
    dv(lambda: V.tensor_tensor(out=tmp[:, 1, :], in0=lam[:, 1, :], in1=lam[:, 1, :], op=ALU.mult))
    dv(lambda: V.tensor_tensor(out=tmp[:, 0, :], in0=tmp[:, 0, :], in1=tmp[:, 1, :], op=ALU.add))
    dv(lambda: V.reciprocal(out=tmp[:, 0, :], in_=tmp[:, 0, :]))
    dv(lambda: V.tensor_scalar(out=tmp[:, 1, :], in0=lb[:, 0, :], scalar1=-1.0, scalar2=None, op0=ALU.add))
    dv(lambda: V.tensor_tensor(out=tmp[:, 2, :], in0=tmp[:, 1, :], in1=lam[:, 0, :], op=ALU.mult))
    dv(lambda: V.tensor_tensor(out=tmp[:, 3, :], in0=lb[:, 1, :], in1=lam[:, 1, :], op=ALU.mult))
    dv(lambda: V.tensor_tensor(out=tmp[:, 2, :], in0=tmp[:, 2, :], in1=tmp[:, 3, :], op=ALU.add))
    dv(lambda: V.tensor_tensor(out=coef[:, 0, :], in0=tmp[:, 2, :], in1=tmp[:, 0, :], op=ALU.mult))
    dv(lambda: V.tensor_tensor(out=tmp[:, 2, :], in0=lb[:, 1, :], in1=lam[:, 0, :], op=ALU.mult))
    dv(lambda: V.tensor_tensor(out=tmp[:, 3, :], in0=tmp[:, 1, :], in1=lam[:, 1, :], op=ALU.mult))
    dv(lambda: V.tensor_tensor(out=tmp[:, 2, :], in0=tmp[:, 2, :], in1=tmp[:, 3, :], op=ALU.subtract))
    dv(lambda: V.tensor_tensor(out=coef[:, 1, :], in0=tmp[:, 2, :], in1=tmp[:, 0, :], op=ALU.mult))
    bb = t("bb", (128, 2, 32, 16))
    t16 = t("t16", (128, 32, 16))

    def bq(x):
        return x.unsqueeze(2).to_broadcast([128, 32, 16])

    dv(lambda: V.tensor_tensor(out=bb[:, 0], in0=braw[:, 0], in1=bq(coef[:, 0, :]), op=ALU.mult))
    dv(lambda: V.tensor_tensor(out=t16[:], in0=braw[:, 1], in1=bq(coef[:, 1, :]), op=ALU.mult))
    dv(lambda: V.tensor_tensor(out=bb[:, 0], in0=bb[:, 0], in1=t16[:], op=ALU.subtract))
    dv(lambda: V.tensor_tensor(out=bb[:, 1], in0=braw[:, 1], in1=bq(coef[:, 0, :]), op=ALU.mult))
    dv(lambda: V.tensor_tensor(out=t16[:], in0=braw[:, 0], in1=bq(coef[:, 1, :]), op=ALU.mult))
    dv(lambda: V.tensor_tensor(out=bb[:, 1], in0=bb[:, 1], in1=t16[:], op=ALU.add))

    if "s5tab" in dbg:
        K.dma(sp, dbg["s5tab"][:, 0:NEC], PR[:, 0, :], reads=[tb])
        K.dma(sp, dbg["s5tab"][:, NEC:2 * NEC], PI[:, 0, :], reads=[tb])
        K.dma(sp, dbg["s5tab"][:, 2 * NEC:2 * NEC + 32], bb[:, 0, 0:2, :].rearrange("p g q -> p (g q)"), reads=[tb])
        K.dma(sp, dbg["s5tab"][:, 2 * NEC + 32:2 * NEC + 64], bb[:, 1, 0:2, :].rearrange("p g q -> p (g q)"), reads=[tb])

    K.op(dve, lambda: V.tensor_copy(out=aL[:, 0, 0, :], in_=PR[:, :, EC_L]), reads=[tb], writes=[aL_b])
    K.op(dve, lambda: V.tensor_copy(out=aL[:, 0, 1, :], in_=PR[:, :, EC_L]), reads=[tb], writes=[aL_b])
    K.op(dve, lambda: V.tensor_scalar(out=aL[:, 1, 0, :], in0=PI[:, :, EC_L], scalar1=-1.0, scalar2=None, op0=ALU.mult), reads=[tb], writes=[aL_b])
    K.op(dve, lambda: V.tensor_copy(out=aL[:, 1, 1, :], in_=PI[:, :, EC_L]), reads=[tb], writes=[aL_b])

    NB = 8 + 8 * JC
    NC8 = 8 * JC
    GB = 2
    xb = [t(f"xb{i}", (128, 2, GB, NB, 16)) for i in range(2)]
    xc = [t(f"xc{i}", (128, 2, GB, NC8, 16)) for i in range(2)]
    xcc = [t(f"xcc{i}", (128, 2, GB, NC8, 16)) for i in range(2)]
    tq = [t(f"tq{i}", (128, GB, NB, 16)) for i in range(2)]
    mA = [t(f"mA{i}", (128, MA_COLS), BF16) for i in range(2)]
    mB = [t(f"mB{i}", (128, MB_COLS), BF16) for i in range(2)]
    dmat = [t(f"dmat{i}", (128, 128)) for i in range(2)]
    t0m = [t(f"t0m{i}", (128, 128)) for i in range(2)]
    xb_b = [Buf() for _ in range(2)]
    xc_b = [Buf() for _ in range(2)]
    xcc_b = [Buf() for _ in range(2)]
    tq_b = [Buf() for _ in range(2)]
    mA_b = [Buf() for _ in range(2)]
    mB_b = [Buf() for _ in range(2)]
    dm_b = [Buf() for _ in range(2)]

    def pwB(x, g0, c0, n):
        return x[:, g0:g0 + GB, c0:c0 + n].unsqueeze(3).to_broadcast([128, GB, n, 16])

    def bbB(ri, g0, n):
        return bb[:, ri, g0:g0 + GB, :].unsqueeze(2).to_broadcast([128, GB, n, 16])

    def ccB(ri, g0, n):
        return craw[:, ri, g0:g0 + GB, :].unsqueeze(2).to_broadcast([128, GB, n, 16])

    for gb in range(G // GB):
        g0 = gb * GB
        i = gb % 2
        E2 = pool if (gb % 4 == 3) else dve
        EN = E2.eng

        def tt_(out, in0, in1, op, reads, writes):
            K.op(E2, lambda: EN.tensor_tensor(out=out, in0=in0, in1=in1, op=op), reads=reads, writes=writes)

        for (dst0, c0, n) in ((0, EC_TB, 8), (8, EC_SB, NC8)):
            o_re = xb[i][:, 0, :, dst0:dst0 + n, :]
            o_im = xb[i][:, 1, :, dst0:dst0 + n, :]
            t_ = tq[i][:, :, dst0:dst0 + n, :]
            tt_(o_re, pwB(PR, g0, c0, n), bbB(0, g0, n), ALU.mult, [tb], [xb_b[i]])
            tt_(t_, pwB(PI, g0, c0, n), bbB(1, g0, n), ALU.mult, [tb], [tq_b[i]])
            tt_(o_re, o_re, t_, ALU.subtract, [tq_b[i], xb_b[i]], [xb_b[i]])
            tt_(o_im, pwB(PR, g0, c0, n), bbB(1, g0, n), ALU.mult, [tb], [xb_b[i]])
            tt_(t_, pwB(PI, g0, c0, n), bbB(0, g0, n), ALU.mult, [tb, xb_b[i]], [tq_b[i]])
            tt_(o_im, o_im, t_, ALU.add, [tq_b[i], xb_b[i]], [xb_b[i]])
        for (dstt, dstb, c0) in ((xc, xc_b, EC_TC), (xcc, xcc_b, EC_CC)):
            o_re = dstt[i][:, 0]
            o_im = dstt[i][:, 1]
            t_ = tq[i][:, :, 0:NC8, :]
            tt_(o_re, pwB(PR, g0, c0, NC8), ccB(0, g0, NC8), ALU.mult, [tb], [dstb[i]])
            tt_(t_, pwB(PI, g0, c0, NC8), ccB(1, g0, NC8), ALU.mult, [tb], [tq_b[i]])
            tt_(o_re, o_re, t_, ALU.subtract, [tq_b[i], dstb[i]], [dstb[i]])
            tt_(o_im, pwB(PI, g0, c0, NC8), ccB(0, g0, NC8), ALU.mult, [tb], [dstb[i]])
            tt_(t_, pwB(PR, g0, c0, NC8), ccB(1, g0, NC8), ALU.mult, [tb, dstb[i]], [tq_b[i]])
            tt_(o_im, o_im, t_, ALU.add, [tq_b[i], dstb[i]], [dstb[i]])
            K.op(E2, lambda: EN.tensor_scalar(out=o_im, in0=o_im, scalar1=-1.0, scalar2=None, op0=ALU.mult),
                 reads=[dstb[i]], writes=[dstb[i]])

        for gl in range(GB):
            g = g0 + gl
            j2 = g % 2
            for d in range(2):
                b = next_ps()
                rows = slice(d * 64, (d + 1) * 64)
                for ri in range(2):
                    K.op(pe, lambda: nc.tensor.matmul(
                        psum[:, b, 0:JC * 128],
                        xb[i][rows, ri, gl, 0:8, :].rearrange("p s q -> p (s q)"),
                        xc[i][rows, ri, gl, :, :].rearrange("p c q -> p (c q)"),
                        start=(ri == 0), stop=(ri == 1)),
                        reads=[xb_b[i], xc_b[i]], writes=[ps_b[b]], inc=(ri == 1))
                base = d * JC * 128
                if d == 0:
                    K.op(dve, lambda: V.tensor_scalar(out=dmat[j2][:], in0=ident[:], scalar1=dvec[:, g:g + 1],
                                                      scalar2=None, op0=ALU.mult), reads=[tb], writes=[dm_b[j2]])
                    K.op(dve, lambda: V.tensor_tensor(out=t0m[j2][:], in0=psum[:, b, 0:128], in1=tmask[:, 0, :], op=ALU.mult),
                         reads=[ps_b[b], tb, dm_b[j2]], writes=[dm_b[j2]])
                    K.op(dve, lambda: V.tensor_tensor(out=mB[j2][:, base:base + 128], in0=t0m[j2][:], in1=dmat[j2][:], op=ALU.add),
                         reads=[dm_b[j2]], writes=[mB_b[j2]])
                else:
                    K.op(dve, lambda: V.tensor_tensor(out=mB[j2][:, base:base + 128], in0=psum[:, b, 0:128],
                                                      in1=tmask[:, 1, :], op=ALU.mult),
                         reads=[ps_b[b], tb], writes=[mB_b[j2]])
                if JC > 1:
                    K.op(act, lambda: nc.scalar.copy(out=mB[j2][:, base + 128:base + JC * 128], in_=psum[:, b, 128:JC * 128]),
                         reads=[ps_b[b]], writes=[mB_b[j2]])
            K.op(act, lambda: nc.scalar.copy(out=mB[j2][:, 2 * JC * 128:3 * JC * 128],
                                             in_=xcc[i][:, 0, gl].rearrange("p c q -> p (c q)")),
                 reads=[xcc_b[i]], writes=[mB_b[j2]])
            K.op(act, lambda: nc.scalar.copy(out=mB[j2][:, 3 * JC * 128:4 * JC * 128],
                                             in_=xcc[i][:, 1, gl].rearrange("p c q -> p (c q)")),
                 reads=[xcc_b[i]], writes=[mB_b[j2]])
            K.dma(sp, matsB[g], mB[j2][:], reads=[mB_b[j2]], writes=[matsB_b[g]])
            for ri in range(2):
                b = next_ps()
                for j in range(JC):
                    K.op(pe, lambda: nc.tensor.transpose(
                        psum[:, b, j * 128:(j + 1) * 128],
                        xb[i][:, ri, gl, 8 + 8 * j:16 + 8 * j, :].rearrange("p s q -> p (s q)"), ident[:]),
                        reads=[xb_b[i]], writes=[ps_b[b]], inc=(j == JC - 1))
                K.op(act, lambda: nc.scalar.copy(out=mA[j2][:, ri * JC * 128:(ri + 1) * JC * 128], in_=psum[:, b, 0:JC * 128]),
                     reads=[ps_b[b]], writes=[mA_b[j2]])
            K.dma(sp, matsA[g], mA[j2][:], reads=[mA_b[j2]], writes=[matsA_b[g]])
    if "matsB0" in dbg:
        pass


def mixer_phase(nc, K, es, sb, psum, ps_b, next_ps, P, l, nseq, x_src, out, out_b, ident, identb, ones_bf,
                eb_scr, ebB, epsb, cb, aL, aL_b, matsA, matsB, matsA_b, matsB_b, u_scr, z_scr, u_scr_b, z_scr_b, dbg):
    pe, act, dve, pool, sp = K.pe, K.act, K.dve, K.pool, K.sp
    V = nc.vector
    G = NG
    wi = sb(es, "wi", (128, 8, IN_W), BF16)
    wg = sb(es, "wg", (128, 4, 512), BF16)
    wo = sb(es, "wo", (128, 8, 1024), BF16)
    gpm = sb(es, "gpm", (128, 8))
    gso = sb(es, "gso", (128, 8))
    bglu = sb(es, "bglu", (128, 4))
    esink = sb(es, "esink", (128, 8))
    gpost = sb(es, "gpost", (128, 1024))
    wi_b, wg_b, wo_b, pb = Buf(), Buf(), Buf(), Buf()
    eb = sb(es, "eb", (128, 3, 2, 4, 128), BF16)
    K.dma(sp, eb[:].rearrange("p o k c q -> p (o k c q)"), eb_scr, reads=[ebB], writes=[pb])
    ebB = pb
    with ExitStack() as es1:
        ld = sb(es1, "ldm", (8, 128))
        ld_b = Buf()
        LC = lambda dst, src, n: load_cols(nc, K, psum, ps_b, next_ps, ident, cb, ld, ld_b, dst, pb, src, n)
        LC(gpm[:], P["pre_mix_norm"][l], 8)
        LC(gso[:, 0:4], P["ssm_out_norm"][l], 4)
        LC(gso[:, 4:8], P["attn_out_norm"][l], 4)
        LC(bglu[:], P["b_glu"][l], 4)
        K.barrier()
    K.dma(sp, esink[:], P["attn_sink"][l].partition_broadcast(128), writes=[pb])
    K.dma(sp, gpost[:], P["post_mix_norm"][l].partition_broadcast(128), writes=[pb])
    K.op(act, lambda: nc.scalar.activation(out=esink[:], in_=esink[:], func=AF.Exp), reads=[pb], writes=[pb])
    with ExitStack() as es2:
        engs = [act, dve]
        stage = make_staging(es2, sb, "m")
        load_cast_weight(nc, K, stage, wi, wi_b, [P["w_in"][l, k * 128:(k + 1) * 128, :] for k in range(8)], IN_W,
                         gpm, pb, engs)
        load_cast_weight(nc, K, stage, wg, wg_b, [P["w_glu"][l, k * 128:(k + 1) * 128, :] for k in range(4)], 512,
                         None, pb, engs)
        load_cast_weight(nc, K, stage, wo, wo_b, [P["w_out"][l, k * 128:(k + 1) * 128, :] for k in range(8)], 1024,
                         gso, pb, engs)
        K.barrier()

    X1 = sb(es, "X1", (128, 4, SEQ), BF16)
    hTlo = sb(es, "hTlo", (128, 4, SEQ), BF16)
    hThi = sb(es, "hThi", (128, 4, SEQ), BF16)
    qT = sb(es, "qT", (128, 4, SEQ), BF16)
    UY = sb(es, "UY", (128, 4, SEQ), BF16)
    kT = sb(es, "kT", (128, SEQ), BF16)
    vaug = sb(es, "vaug", (128, NT, 2, 66), BF16)
    xt = [sb(es, f"xt{i}", (128, 1024)) for i in range(3)]
    hs = xt
    xo = xt
    junk = sb(es, "junk", (128, 1024), BF16)
    ssq = sb(es, "ssq", (128, 4, NT))
    rs = sb(es, "rs", (128, 4, NT))
    B = lambda n=1: [Buf() for _ in range(n)]
    X1_b, hTlo_b, hThi_b, qT_b, UY_b, kT_b, va_b, H_b, Ein_b = (Buf() for _ in range(9))
    Hf_b, Hb_b = Buf(), Buf()
    xt_b, mAb_b, mBb_b, et_b, pt_b, oa_b, den_b, gate_b, mt_b = B(3), B(2), B(2), B(3), B(6), B(2), B(2), B(2), B(2)
    hs_b = xt_b
    xo_b = xt_b
    junk_b, ssq_b, rs_b, st_b = Buf(), [Buf() for _ in range(4)], [Buf() for _ in range(4)], [Buf(), Buf()]
    m1s_b = [(Buf(), Buf()) for _ in range(4)]
    K.op(dve, lambda: V.memset(vaug[:, :, :, 64:66], 1.0), writes=[va_b])
    hT_b = [hTlo_b, hThi_b]
    hT = [hTlo, hThi]
    Uf = UY[:].rearrange("p c t -> p (c t)").rearrange("p (g s) -> p g s", g=NG)
    Zf = hTlo[:].rearrange("p c t -> p (c t)").rearrange("p (g s) -> p g s", g=NG)
    ysT, ysT_b = qT, qT_b
    yaT, yaT_b = hThi, hThi_b
    ysq, ysq_b = UY, UY_b

    for s in range(nseq):
        tok0 = s * SEQ
        def m1_a(tt):
            i = tt % 3
            r0 = tok0 + tt * 128
            rd = [out_b[s][tt]] if x_src is out else []
            K.dma(sp, xt[i][:], x_src[r0:r0 + 128, :], reads=rd, writes=[xt_b[i]])
            sq_b, r_b = m1s_b[tt % 4]
            K.op(act, lambda: nc.scalar.activation(out=junk[:], in_=xt[i][:], func=AF.Square, accum_out=ssq[:, 0, tt:tt + 1]),
                 reads=[xt_b[i]], writes=[junk_b, sq_b])
            K.op(act, lambda: nc.scalar.activation(out=rs[:, 0, tt:tt + 1], in_=ssq[:, 0, tt:tt + 1], func=AF.Sqrt,
                                                   bias=epsb[:, 0:1], scale=1.0 / D_MODEL), reads=[sq_b, cb], writes=[r_b])
            K.op(dve, lambda: V.reciprocal(out=rs[:, 0, tt:tt + 1], in_=rs[:, 0, tt:tt + 1]), reads=[r_b], writes=[r_b])
            K.op(dve, lambda: V.tensor_scalar(out=hs[i][:], in0=xt[i][:], scalar1=rs[:, 0, tt:tt + 1], scalar2=None, op0=ALU.mult),
                 reads=[xt_b[i], r_b], writes=[hs_b[i]])

        def m1_b(tt):
            i = tt % 3
            b = next_ps(2)
            for c in range(8):
                K.op(pe, lambda: nc.tensor.transpose(psum[:, b + c // 4, (c % 4) * 128:(c % 4 + 1) * 128],
                                                     hs[i][:, c * 128:(c + 1) * 128], ident[:]),
                     reads=[hs_b[i], cb], writes=[ps_b[b], ps_b[b + 1]], inc=(c == 7))
            K.op(dve, lambda: V.tensor_copy(out=hTlo[:, :, tt * 128:(tt + 1) * 128],
                                            in_=psum[:, b, :].rearrange("p (c t) -> p c t", c=4)),
                 reads=[ps_b[b]], writes=[hTlo_b])
            K.op(act, lambda: nc.scalar.copy(out=hThi[:, :, tt * 128:(tt + 1) * 128],
                                             in_=psum[:, b + 1, :].rearrange("p (c t) -> p c t", c=4)),
                 reads=[ps_b[b + 1]], writes=[hThi_b])

        m1_a(0)
        for tt in range(NT):
            if tt + 1 < NT:
                m1_a(tt + 1)
            m1_b(tt)
        for m in range(9):
            if m == 4:
                K.dma(sp, u_scr.rearrange("(c r) t -> r c t", c=4), X1[:], reads=[X1_b], writes=[u_scr_b])
                for s8 in range(8):
                    K.dma(sp, Uf[s8 * 16:(s8 + 1) * 16, :, :],
                          u_scr.rearrange("(g q) (s sub) -> s q g sub", q=16, s=8)[s8], reads=[u_scr_b], writes=[UY_b])
            for tg in range(4):
                b = next_ps()
                for k in range(8):
                    K.op(pe, lambda: nc.tensor.matmul(psum[:, b, :], wi[:, k, m * 128:(m + 1) * 128],
                                                      hT[k // 4][:, k % 4, tg * 512:(tg + 1) * 512],
                                                      start=(k == 0), stop=(k == 7)),
                         reads=[wi_b, hT_b[k // 4]], writes=[ps_b[b]], inc=(k == 7))
                if m < 4:
                    K.op(act, lambda: nc.scalar.copy(
                        out=X1[:, m, :].rearrange("p (s sub) -> p sub s", s=8)[:, tg * 64:(tg + 1) * 64, :],
                        in_=psum[:, b, :].rearrange("p (sub s) -> p sub s", s=8)), reads=[ps_b[b]], writes=[X1_b])
                elif m < 8:
                    K.op(dve, lambda: V.tensor_copy(out=qT[:, m - 4, tg * 512:(tg + 1) * 512], in_=psum[:, b, :]),
                         reads=[ps_b[b]], writes=[qT_b])
                else:
                    K.op(act, lambda: nc.scalar.copy(out=kT[:, tg * 512:(tg + 1) * 512], in_=psum[:, b, :]),
                         reads=[ps_b[b]], writes=[kT_b])
        for tt in range(NT):
            b = next_ps()
            for k in range(8):
                K.op(pe, lambda: nc.tensor.matmul(psum[:, b, 0:128], hT[k // 4][:, k % 4, tt * 128:(tt + 1) * 128],
                                                  wi[:, k, 1152:1280], start=(k == 0), stop=(k == 7)),
                     reads=[wi_b, hT_b[k // 4]], writes=[ps_b[b]], inc=(k == 7))
            K.op(dve, lambda: V.tensor_copy(out=vaug[:, tt, :, 0:64], in_=psum[:, b, 0:128].rearrange("p (k d) -> p k d", k=2)),
                 reads=[ps_b[b]], writes=[va_b])
        if s == 0 and "uT" in dbg:
            K.dma(sp, dbg["uT"], X1[:], reads=[X1_b])
            K.dma(sp, dbg["qT"], qT[:], reads=[qT_b])
            K.dma(sp, dbg["kT"], kT[:], reads=[kT_b])
            K.dma(sp, dbg["vaug"], vaug[:], reads=[va_b])

        e_s5 = ExitStack()
        H = sb(e_s5, "H", (128, 2, NG, NCH + 1))
        Ein = sb(e_s5, "Ein", (128, 2, NG, NCH), BF16)
        st1 = sb(e_s5, "st1", (128, 2, NG))
        st2 = sb(e_s5, "st2", (128, 2, NG))
        mAb = [sb(e_s5, f"mAb{i}", (128, MA_COLS), BF16) for i in range(2)]
        mBb = [sb(e_s5, f"mBb{i}", (128, MB_COLS), BF16) for i in range(2)]
        e_att = ExitStack()
        et = [sb(e_att, f"et{i}", (128, 512), BF16) for i in range(3)]
        pt = [sb(e_att, f"pt{i}", (128, 512), BF16) for i in range(6)]
        oa = [sb(e_att, f"oa{i}", (128, 512)) for i in range(2)]
        den = [sb(e_att, f"den{i}", (128, 4)) for i in range(2)]

        def att_block(qb):
            kbs = [kb for kb in (qb - 1, qb, qb + 1) if 0 <= kb < NT]
            for kv in range(2):
                rows = slice(kv * 64, (kv + 1) * 64)
                pts = []
                for kb in kbs:
                    ie = K_rr(K, "et", 3)
                    ip = K_rr(K, "pt", 6)
                    b = next_ps()
                    K.op(pe, lambda: nc.tensor.matmul(psum[:, b, :], kT[rows, kb * 128:(kb + 1) * 128],
                                                      qT[rows, :, qb * 128:(qb + 1) * 128], start=True, stop=True),
                         reads=[kT_b, qT_b], writes=[ps_b[b]])
                    K.op(act, lambda: nc.scalar.activation(out=et[ie][:], in_=psum[:, b, :], func=AF.Exp, scale=0.125),
                         reads=[ps_b[b]], writes=[et_b[ie]])
                    K.op(dve, lambda: V.tensor_tensor(out=pt[ip][:], in0=et[ie][:],
                                                      in1=eb[:, kb - qb + 1, kv].rearrange("p c q -> p (c q)"), op=ALU.mult),
                         reads=[et_b[ie], ebB], writes=[pt_b[ip]])
                    pts.append((ip, kb))
                b = next_ps()
                for c in range(4):
                    for n_, (ip, kb) in enumerate(pts):
                        K.op(pe, lambda: nc.tensor.matmul(psum[:, b, c * 65:(c + 1) * 65], pt[ip][:, c * 128:(c + 1) * 128],
                                                          vaug[:, kb, kv, 0:65], start=(n_ == 0), stop=(n_ == len(pts) - 1)),
                             reads=[pt_b[ip], va_b], writes=[ps_b[b]], inc=(c == 3 and n_ == len(pts) - 1))
                io = qb % 2
                ov = psum[:, b, 0:260].rearrange("p (c d) -> p c d", c=4)
                K.op(dve, lambda: V.tensor_tensor(out=den[kv][:], in0=ov[:, :, 64], in1=esink[:, kv * 4:(kv + 1) * 4], op=ALU.add),
                     reads=[ps_b[b], pb], writes=[den_b[kv]])
                K.op(dve, lambda: V.reciprocal(out=den[kv][:], in_=den[kv][:]), reads=[den_b[kv]], writes=[den_b[kv]])
                K.op(dve, lambda: V.tensor_tensor(out=oa[io][:, kv * 256:(kv + 1) * 256].rearrange("p (c d) -> p c d", c=4),
                                                  in0=ov[:, :, 0:64], in1=den[kv][:].unsqueeze(2).to_broadcast([128, 4, 64]),
                                                  op=ALU.mult), reads=[ps_b[b], den_b[kv]], writes=[oa_b[io]])
            K.op(act, lambda: nc.scalar.activation(out=junk[:, 0:512], in_=oa[io][:], func=AF.Square, accum_out=ssq[:, 2, qb:qb + 1]),
                 reads=[oa_b[io]], writes=[junk_b, ssq_b[2]])
            b = next_ps()
            for c in range(4):
                K.op(pe, lambda: nc.tensor.transpose(psum[:, b, c * 128:(c + 1) * 128], oa[io][:, c * 128:(c + 1) * 128], ident[:]),
                     reads=[oa_b[io], cb], writes=[ps_b[b]], inc=(c == 3))
            K.op(act, lambda: nc.scalar.copy(out=yaT[:, :, qb * 128:(qb + 1) * 128], in_=psum[:, b, :].rearrange("p (c t) -> p c t", c=4)),
                 reads=[ps_b[b]], writes=[yaT_b])


        def scan_step(c):
            K.op(dve, lambda: V.tensor_tensor(out=st1[:], in0=H[:, :, :, c], in1=aL[:, 0], op=ALU.mult), reads=[H_b, aL_b, Hf_b], writes=[Hf_b])
            K.op(dve, lambda: V.tensor_tensor(out=H[:, :, :, c + 1], in0=H[:, :, :, c + 1], in1=st1[:], op=ALU.add), reads=[H_b, aL_b, Hf_b], writes=[Hf_b])
            K.op(dve, lambda: V.tensor_tensor(out=st1[:], in0=H[:, ::-1, :, c], in1=aL[:, 1], op=ALU.mult), reads=[H_b, aL_b, Hf_b], writes=[Hf_b])
            K.op(dve, lambda: V.tensor_tensor(out=H[:, :, :, c + 1], in0=H[:, :, :, c + 1], in1=st1[:], op=ALU.add), reads=[H_b, aL_b, Hf_b], writes=[Hf_b])

        K.op(dve, lambda: V.memset(H[:, :, :, 0:1], 0.0), writes=[H_b, Hf_b, Hb_b])
        for g0 in range(0, G, 4):
            b = next_ps()
            for gl in range(4):
                g = g0 + gl
                im = g % 2
                K.dma(sp, mAb[im][:], matsA[g], reads=[matsA_b[g]], writes=[mAb_b[im]])
                for ri in range(2):
                    for j in range(JC):
                        K.op(pe, lambda: nc.tensor.matmul(
                            psum[:, b, (ri * 4 + gl) * NCH:(ri * 4 + gl + 1) * NCH],
                            mAb[im][:, (ri * JC + j) * 128:(ri * JC + j + 1) * 128],
                            Uf[:, g, :].rearrange("p (c j) -> p c j", j=JC)[:, :, j],
                            start=(j == 0), stop=(j == JC - 1)),
                            reads=[mAb_b[im], UY_b], writes=[ps_b[b]], inc=(ri == 1 and j == JC - 1))
            pv = psum[:, b, :].rearrange("p (r g c) -> p r g c", r=2, g=4)
            K.op(act, lambda: nc.scalar.copy(out=H[0:64, :, g0:g0 + 4, 1:NCH + 1], in_=pv[0:64]), reads=[ps_b[b]], writes=[H_b, Hf_b, Hb_b])
            K.op(act, lambda: nc.scalar.copy(out=H[64:128, :, g0:g0 + 4, 1:NCH + 1][:, :, :, ::-1], in_=pv[64:128]), reads=[ps_b[b]], writes=[H_b, Hf_b, Hb_b])
        spq = NCH // NT
        for qb in range(NT):
            att_block(qb)
            for c in range(qb * spq, (qb + 1) * spq):
                scan_step(c)
        K.op(dve, lambda: V.tensor_copy(out=Ein[0:64], in_=H[0:64, :, :, 0:NCH]), reads=[H_b, Hf_b], writes=[Ein_b])
        K.op(act, lambda: nc.scalar.copy(out=Ein[64:128], in_=H[64:128, :, :, 0:NCH][:, :, :, ::-1]), reads=[H_b, Hf_b], writes=[Ein_b])
        for g0 in range(0, G, 2):
            b = next_ps()
            K.op(dve, lambda: V.memset(psum[:, b, :], 0.0), writes=[ps_b[b]])
            for gl in range(2):
                g = g0 + gl
                im = g % 2
                K.dma(sp, mBb[im][:], matsB[g], reads=[matsB_b[g]], writes=[mBb_b[im]])
                Ug = Uf[:, g, :].rearrange("p (c j) -> p c j", j=JC)
                Yg = psum[:, b, gl * 256:(gl + 1) * 256].rearrange("p (c j) -> p c j", j=JC)
                mm = []
                for d in range(JC):
                    mm.append((Yg[:, :, d:JC], mBb[im][:, d * 128:(d + 1) * 128], Ug[:, :, 0:JC - d], [UY_b]))
                    mm.append((Yg[:, :, 0:JC - d], mBb[im][:, (JC + d) * 128:(JC + d + 1) * 128], Ug[:, :, d:JC], [UY_b]))
                for j in range(JC):
                    mm.append((Yg[:, :, j], mBb[im][:, (2 * JC + j) * 128:(2 * JC + j + 1) * 128], Ein[:, 0, g, :], [Ein_b]))
                    mm.append((Yg[:, :, j], mBb[im][:, (3 * JC + j) * 128:(3 * JC + j + 1) * 128], Ein[:, 1, g, :], [Ein_b]))
                for n_, (o_, l_, r_, rd_) in enumerate(mm):
                    K.op(pe, lambda: nc.tensor.matmul(o_, l_, r_, start=False, stop=(n_ == len(mm) - 1), skip_group_check=True),
                         reads=[mBb_b[im]] + rd_, writes=[ps_b[b]], inc=(n_ == len(mm) - 1))
            K.op(act, lambda: nc.scalar.activation(out=Zf[:, g0:g0 + 2, :], in_=psum[:, b, :].rearrange("p (g s) -> p g s", g=2),
                                                   func=AF.Gelu_apprx_tanh), reads=[ps_b[b]], writes=[hTlo_b])
        for s8 in range(8):
            K.dma(sp, z_scr.rearrange("(g q) (s sub) -> s q g sub", q=16, s=8)[s8], Zf[s8 * 16:(s8 + 1) * 16, :, :],
                  reads=[hTlo_b], writes=[z_scr_b])
        K.dma(sp, X1[:], z_scr.rearrange("(c r) t -> r c t", c=4), reads=[z_scr_b], writes=[X1_b])
        K.barrier()
        e_att.close()
        gate = [sb(e_s5, f"gate{i}", (128, 512)) for i in range(2)]
        for m in range(4):
            for tg in range(4):
                b = next_ps()
                ig = (m * 4 + tg) % 2
                for k in range(4):
                    K.op(pe, lambda: nc.tensor.matmul(psum[:, b, :], wg[:, k, m * 128:(m + 1) * 128], X1[:, k, tg * 512:(tg + 1) * 512],
                                                      start=(k == 0), stop=(k == 3)),
                         reads=[wg_b, X1_b], writes=[ps_b[b]], inc=(k == 3))
                K.op(act, lambda: nc.scalar.activation(out=gate[ig][:], in_=psum[:, b, :], func=AF.Sigmoid, bias=bglu[:, m:m + 1]),
                     reads=[ps_b[b], pb], writes=[gate_b[ig]])
                nat = lambda tns: tns[:, m, :].rearrange("p (sub s) -> p s sub", s=8)[:, 2 * tg:2 * tg + 2, :]
                K.op(dve, lambda: V.tensor_tensor(out=nat(ysT), in0=X1[:, m, tg * 512:(tg + 1) * 512].rearrange("p (s sub) -> p s sub", s=2),
                                                  in1=gate[ig][:].rearrange("p (s sub) -> p s sub", s=2), op=ALU.mult),
                     reads=[X1_b, gate_b[ig]], writes=[ysT_b])
        K.op(act, lambda: nc.scalar.activation(out=ysq[:], in_=ysT[:], func=AF.Square), reads=[ysT_b], writes=[ysq_b])

        K.barrier()
        e_s5.close()
        if s == 0 and "ysT" in dbg:
            K.dma(sp, dbg["ysT"], ysT[:], reads=[ysT_b])
            K.dma(sp, dbg["yaT"], yaT[:], reads=[yaT_b])
            K.dma(sp, dbg["zT"], X1[:], reads=[X1_b])
        e_m4 = ExitStack()
        mt = [sb(e_m4, f"mt{i}", (128, 1024)) for i in range(2)]
        b = next_ps()
        for tt in range(NT):
            for k in range(4):
                K.op(pe, lambda: nc.tensor.matmul(psum[:, b, tt:tt + 1], ysq[:, k, tt * 128:(tt + 1) * 128], ones_bf[:, 0:1],
                                                  start=(k == 0), stop=(k == 3)),
                     reads=[ysq_b, cb], writes=[ps_b[b]], inc=(k == 3 and tt == NT - 1))
        K.op(act, lambda: nc.scalar.activation(out=rs[:, 1, :], in_=psum[:, b, 0:NT], func=AF.Sqrt, bias=epsb[:, 0:1], scale=1.0 / SSM_W),
             reads=[ps_b[b], cb], writes=[rs_b[1]])
        K.op(dve, lambda: V.reciprocal(out=rs[:, 1, :], in_=rs[:, 1, :]), reads=[rs_b[1]], writes=[rs_b[1]])
        K.op(act, lambda: nc.scalar.activation(out=rs[:, 2, :], in_=ssq[:, 2, :], func=AF.Sqrt, bias=epsb[:, 0:1], scale=1.0 / SSM_W),
             reads=[ssq_b[2], cb], writes=[rs_b[2]])
        K.op(dve, lambda: V.reciprocal(out=rs[:, 2, :], in_=rs[:, 2, :]), reads=[rs_b[2]], writes=[rs_b[2]])
        for tt in range(NT):
            i = tt % 2
            r0 = tok0 + tt * 128
            ba = next_ps(2)
            for hf in range(2):
                for k in range(4):
                    K.op(pe, lambda: nc.tensor.matmul(psum[:, ba + hf, :], ysT[:, k, tt * 128:(tt + 1) * 128], wo[:, k, hf * 512:(hf + 1) * 512],
                                                      start=(k == 0), stop=(k == 3)),
                         reads=[ysT_b, wo_b], writes=[ps_b[ba + hf]], inc=(k == 3))
            bb_ = next_ps(2)
            for hf in range(2):
                for k in range(4):
                    K.op(pe, lambda: nc.tensor.matmul(psum[:, bb_ + hf, :], yaT[:, k, tt * 128:(tt + 1) * 128], wo[:, 4 + k, hf * 512:(hf + 1) * 512],
                                                      start=(k == 0), stop=(k == 3)),
                         reads=[yaT_b, wo_b], writes=[ps_b[bb_ + hf]], inc=(k == 3))
            K.op(act, lambda: nc.scalar.activation(out=mt[i][:], in_=psum[:, ba:ba + 2, :].rearrange("p a n -> p (a n)"), func=AF.Copy,
                                                   scale=rs[:, 1, tt:tt + 1]), reads=[ps_b[ba], ps_b[ba + 1], rs_b[1]], writes=[mt_b[i]])
            K.op(dve, lambda: V.scalar_tensor_tensor(out=mt[i][:], in0=psum[:, bb_:bb_ + 2, :].rearrange("p a n -> p (a n)"),
                                                     scalar=rs[:, 2, tt:tt + 1], in1=mt[i][:], op0=ALU.mult, op1=ALU.add),
                 reads=[ps_b[bb_], ps_b[bb_ + 1], rs_b[2], mt_b[i]], writes=[mt_b[i]])
            K.op(act, lambda: nc.scalar.activation(out=junk[:], in_=mt[i][:], func=AF.Square, accum_out=ssq[:, 3, tt:tt + 1]),
                 reads=[mt_b[i]], writes=[junk_b, ssq_b[3]])
            K.op(act, lambda: nc.scalar.activation(out=rs[:, 3, tt:tt + 1], in_=ssq[:, 3, tt:tt + 1], func=AF.Sqrt,
                                                   bias=epsb[:, 0:1], scale=1.0 / D_MODEL), reads=[ssq_b[3], cb], writes=[rs_b[3]])
            K.op(dve, lambda: V.reciprocal(out=rs[:, 3, tt:tt + 1], in_=rs[:, 3, tt:tt + 1]), reads=[rs_b[3]], writes=[rs_b[3]])
            K.op(dve, lambda: V.tensor_tensor(out=mt[i][:], in0=mt[i][:], in1=gpost[:], op=ALU.mult),
                 reads=[mt_b[i], pb], writes=[mt_b[i]])
            rd = [out_b[s][tt]] if x_src is out else []
            K.dma(sp, xt[i][:], x_src[r0:r0 + 128, :], reads=rd, writes=[xt_b[i]])
            K.op(dve, lambda: V.scalar_tensor_tensor(out=xo[i][:], in0=mt[i][:], scalar=rs[:, 3, tt:tt + 1], in1=xt[i][:],
                                                     op0=ALU.mult, op1=ALU.add),
                 reads=[mt_b[i], rs_b[3], xt_b[i]], writes=[xo_b[i]])
            K.dma(sp, out[r0:r0 + 128, :], xo[i][:], reads=[xo_b[i]], writes=[out_b[s][tt]])
        K.barrier()
        e_m4.close()


_RR = {}


def K_rr(K, key, n):
    v = _RR.get(key, 0)
    _RR[key] = (v + 1) % n
    return v


def ffn_phase(nc, K, es, sb, psum, ps_b, next_ps, P, l, nseq, out, out_b, ident, identb, epsb, cb, dbg):
    pe, act, dve, pool, sp = K.pe, K.act, K.dve, K.pool, K.sp
    V = nc.vector
    wu = sb(es, "wu", (128, 8, 2 * D_FF), BF16)
    wd = sb(es, "wd", (128, NFB, 1024), BF16)
    gpf = sb(es, "gpf", (128, 8))
    cw = sb(es, "cw", (128, 4, 2 * NFB))
    gpost = sb(es, "gpost2", (128, 1024))
    wu_b, wd_b, pb = Buf(), Buf(), Buf()
    with ExitStack() as es1:
        ld = sb(es1, "ldf", (2 * NFB, 128))
        ld_b = Buf()
        LC = lambda dst, src, n: load_cols(nc, K, psum, ps_b, next_ps, ident, cb, ld, ld_b, dst, pb, src, n)
        LC(gpf[:], P["pre_ffn_norm"][l], 8)
        for j in range(3):
            LC(cw[:, j, :], P["conv_w"][l, j], 2 * NFB)
        LC(cw[:, 3, :], P["conv_b"][l], 2 * NFB)
        K.barrier()
    K.dma(sp, gpost[:], P["post_ffn_norm"][l].partition_broadcast(128), writes=[pb])
    with ExitStack() as es2:
        engs = [act, dve]
        stage = make_staging(es2, sb, "f")
        load_cast_weight(nc, K, stage, wu, wu_b, [P["w_up"][l, k * 128:(k + 1) * 128, :] for k in range(8)], 2 * D_FF,
                         gpf, pb, engs)
        load_cast_weight(nc, K, stage, wd, wd_b, [P["w_down"][l, k * 128:(k + 1) * 128, :] for k in range(NFB)], 1024,
                         None, pb, engs)
        K.barrier()

    HW = 1026
    h2T = sb(es, "h2T", (128, 8, HW), BF16)
    NPRE = 4
    gbuf = sb(es, "gbuf", (128, NFB - NPRE, 384), BF16)
    gpre = [sb(es, f"gpre{i}", (128, NPRE, 384), BF16) for i in range(2)]
    gbuf_b = [Buf() for _ in range(NFB - NPRE)]
    gpre_b = [[Buf() for _ in range(NPRE)] for _ in range(2)]

    def gslot(kpar, fb):
        if fb < NPRE:
            return gpre[kpar], fb, gpre_b[kpar][fb]
        return gbuf, fb - NPRE, gbuf_b[fb - NPRE]
    pending = []
    tgk = [0]
    halo = sb(es, "halo", (128, 8, 2), BF16)
    xt = [sb(es, f"fxt{i}", (128, 1024)) for i in range(3)]
    hs = xt
    junk = sb(es, "fjunk", (128, 1024), BF16)
    cv = [sb(es, f"cv{i}", (128, 384)) for i in range(3)]
    cg = [sb(es, f"cg{i}", (128, 384)) for i in range(3)]
    gg = [sb(es, f"gg{i}", (128, 384)) for i in range(3)]
    yt = [sb(es, f"yt{i}", (128, 1024)) for i in range(1)] * 2
    xo = xt
    st = sb(es, "fst", (128, 4))
    stf = sb(es, "fstf", (128, 2, 4))
    stf_b = [Buf() for _ in range(4)]
    std = sb(es, "fstd", (128, 2, 2))
    std_b = [Buf() for _ in range(2)]
    B = lambda n=1: [Buf() for _ in range(n)]
    h2T_b, g_b, junk_b, st_b = Buf(), Buf(), Buf(), Buf()
    xt_b, cv_b, cg_b, gg_b = B(3), B(3), B(3), B(3)
    yt_b = B(1) * 2
    hs_b = xt_b
    xo_b = xt_b
    def emit_down(s, tok0, hs0, t0, ln, kpar):
        for ti in range(ln // 128):
            tt = (hs0 + t0) // 128 + ti
            r0 = tok0 + tt * 128
            i = tt % 2
            b = next_ps(2)
            for hf in range(2):
                for fb in range(NFB):
                    gt_, gi_, gb_ = gslot(kpar, fb)
                    K.op(pe, lambda: nc.tensor.matmul(psum[:, b + hf, :], gt_[:, gi_, ti * 128:(ti + 1) * 128],
                                                      wd[:, fb, hf * 512:(hf + 1) * 512], start=(fb == 0), stop=(fb == NFB - 1)),
                         reads=[gb_, wd_b], writes=[ps_b[b + hf]], inc=(fb == NFB - 1))
            pv = psum[:, b:b + 2, :].rearrange("p a n -> p (a n)")
            K.op(act, lambda: nc.scalar.activation(out=junk[:], in_=pv, func=AF.Square, accum_out=std[:, 0, i:i + 1]),
                 reads=[ps_b[b], ps_b[b + 1]], writes=[junk_b, std_b[i]])
            K.op(act, lambda: nc.scalar.activation(out=std[:, 1, i:i + 1], in_=std[:, 0, i:i + 1], func=AF.Sqrt, bias=epsb[:, 0:1], scale=1.0 / D_MODEL),
                 reads=[std_b[i], cb], writes=[std_b[i]])
            K.op(dve, lambda: V.reciprocal(out=std[:, 1, i:i + 1], in_=std[:, 1, i:i + 1]), reads=[std_b[i]], writes=[std_b[i]])
            K.op(dve, lambda: V.tensor_tensor(out=yt[i][:], in0=pv, in1=gpost[:], op=ALU.mult),
                 reads=[ps_b[b], ps_b[b + 1], pb], writes=[yt_b[i]])
            K.dma(sp, xt[i][:], out[r0:r0 + 128, :], reads=[out_b[s][tt]], writes=[xt_b[i]])
            K.op(dve, lambda: V.scalar_tensor_tensor(out=xo[i][:], in0=yt[i][:], scalar=std[:, 1, i:i + 1], in1=xt[i][:],
                                                     op0=ALU.mult, op1=ALU.add),
                 reads=[yt_b[i], std_b[i], xt_b[i]], writes=[xo_b[i]])
            K.dma(sp, out[r0:r0 + 128, :], xo[i][:], reads=[xo_b[i]], writes=[out_b[s][tt]])

    it = [0]
    if l == 0:
        print("ffn sbuf bytes remaining", nc.sbuf_bytes_remaining)

    for s in range(nseq):
        tok0 = s * SEQ
        for half in range(2):
            hs0 = half * 1024
            t_lo = hs0 // 128 - 1
            if half == 1:
                K.op(dve, lambda: V.tensor_copy(out=h2T[:, :, 0:1], in_=halo[:, :, 0:1]), reads=[h2T_b], writes=[h2T_b])
            tiles = [tt for tt in range(t_lo, t_lo + 10) if not (tt < 0 or tt >= NT or (half == 1 and tt == t_lo))]

            def fill_a(tt):
                i = it[0] % 3
                js = it[0] % 4
                it[0] += 1
                r0 = tok0 + tt * 128
                K.dma(sp, xt[i][:], out[r0:r0 + 128, :], reads=[out_b[s][tt]], writes=[xt_b[i]])
                K.op(act, lambda: nc.scalar.activation(out=junk[:], in_=xt[i][:], func=AF.Square, accum_out=stf[:, 0, js:js + 1]),
                     reads=[xt_b[i]], writes=[junk_b, stf_b[js]])
                K.op(act, lambda: nc.scalar.activation(out=stf[:, 1, js:js + 1], in_=stf[:, 0, js:js + 1], func=AF.Sqrt, bias=epsb[:, 0:1], scale=1.0 / D_MODEL),
                     reads=[stf_b[js], cb], writes=[stf_b[js]])
                K.op(dve, lambda: V.reciprocal(out=stf[:, 1, js:js + 1], in_=stf[:, 1, js:js + 1]), reads=[stf_b[js]], writes=[stf_b[js]])
                K.op(dve, lambda: V.tensor_scalar(out=hs[i][:], in0=xt[i][:], scalar1=stf[:, 1, js:js + 1], scalar2=None, op0=ALU.mult),
                     reads=[xt_b[i], stf_b[js]], writes=[hs_b[i]])
                return i

            def fill_b(tt, i):
                b = next_ps(2)
                for c in range(8):
                    K.op(pe, lambda: nc.tensor.transpose(psum[:, b + c // 4, (c % 4) * 128:(c % 4 + 1) * 128],
                                                         hs[i][:, c * 128:(c + 1) * 128], ident[:]),
                         reads=[hs_b[i], cb], writes=[ps_b[b], ps_b[b + 1]], inc=(c == 7))
                j0 = tt * 128 - hs0 + 1
                lo, hi = max(j0, 0), min(j0 + 128, HW)
                for hh in range(2):
                    src = psum[:, b + hh, :].rearrange("p (c t) -> p c t", c=4)[:, :, lo - j0:hi - j0]
                    dst = h2T[:, hh * 4:(hh + 1) * 4, lo:hi]
                    if hh == 0:
                        K.op(dve, lambda: V.tensor_copy(out=dst, in_=src), reads=[ps_b[b + hh]], writes=[h2T_b])
                    else:
                        K.op(act, lambda: nc.scalar.copy(out=dst, in_=src), reads=[ps_b[b + hh]], writes=[h2T_b])

            ia = fill_a(tiles[0])
            for n_, tt in enumerate(tiles):
                ia_next = fill_a(tiles[n_ + 1]) if n_ + 1 < len(tiles) else None
                fill_b(tt, ia)
                ia = ia_next
            if half == 0:
                K.op(dve, lambda: V.tensor_copy(out=halo[:, :, 0:1], in_=h2T[:, :, 1024:1025]), reads=[h2T_b], writes=[h2T_b])
            for (t0, ln) in ((0, 384), (384, 384), (768, 256)):
                g_first = (hs0 + t0 == 0)
                g_last = (hs0 + t0 + ln == SEQ)
                lo = 1 if g_first else 0
                hi = 1 if g_last else 0
                w0c, w1c = t0 + lo, t0 + ln + 2 - hi
                nW = w1c - w0c
                ctr = 1 - lo
                kpar = tgk[0] % 2
                tgk[0] += 1
                for fb in range(NFB):
                    i = fb % 3
                    if fb == NPRE:
                        for fn in pending:
                            fn()
                        pending.clear()
                    for vg in range(2):
                        rb = vg * NFB + fb
                        b = next_ps()
                        for k in range(8):
                            K.op(pe, lambda: nc.tensor.matmul(psum[:, b, 0:nW], wu[:, k, rb * 128:(rb + 1) * 128],
                                                              h2T[:, k, w0c:w1c], start=(k == 0), stop=(k == 7)),
                                 reads=[wu_b, h2T_b], writes=[ps_b[b]], inc=(k == 7))
                        dst, dst_b = (cv[i], cv_b[i]) if vg == 0 else (cg[i], cg_b[i])
                        K.op(act, lambda: nc.scalar.activation(out=dst[:, 0:ln], in_=psum[:, b, ctr:ctr + ln], func=AF.Identity,
                                                               bias=cw[:, 3, rb:rb + 1], scale=cw[:, 1, rb:rb + 1]),
                             reads=[ps_b[b], pb], writes=[dst_b])
                        K.op(dve, lambda: V.scalar_tensor_tensor(out=dst[:, lo:ln], in0=psum[:, b, ctr + lo - 1:ctr + ln - 1],
                                                                 scalar=cw[:, 0, rb:rb + 1], in1=dst[:, lo:ln], op0=ALU.mult, op1=ALU.add),
                             reads=[ps_b[b], pb, dst_b], writes=[dst_b])
                        K.op(dve, lambda: V.scalar_tensor_tensor(out=dst[:, 0:ln - hi], in0=psum[:, b, ctr + 1:ctr + ln - hi + 1],
                                                                 scalar=cw[:, 2, rb:rb + 1], in1=dst[:, 0:ln - hi], op0=ALU.mult, op1=ALU.add),
                             reads=[ps_b[b], pb, dst_b], writes=[dst_b])
                    K.op(act, lambda: nc.scalar.activation(out=gg[i][:, 0:ln], in_=cg[i][:, 0:ln], func=AF.Gelu_apprx_tanh),
                         reads=[cg_b[i]], writes=[gg_b[i]])
                    gt_, gi_, gb_ = gslot(kpar, fb)
                    K.op(pool, lambda: nc.gpsimd.tensor_tensor(out=gt_[:, gi_, 0:ln], in0=gg[i][:, 0:ln], in1=cv[i][:, 0:ln], op=ALU.mult),
                         reads=[gg_b[i], cv_b[i]], writes=[gb_])
                pending.append(lambda s=s, tok0=tok0, hs0=hs0, t0=t0, ln=ln, kpar=kpar: emit_down(s, tok0, hs0, t0, ln, kpar))
    for fn in pending:
        fn()
    pending.clear()
_CACHE = {}


def _q_perm():
    cols = list(range(512))
    for c in range(4):
        for h in (c, 4 + c):
            cols.extend(range(512 + h * 64, 512 + (h + 1) * 64))
    cols.extend(range(1024, 1280))
    return np.asarray(cols)


def kernel(**inputs):
    n_cores = 8
    x = np.ascontiguousarray(np.asarray(inputs["x"], dtype=np.float32))
    nseq = x.shape[0] // n_cores
    if "nc" not in _CACHE:
        _CACHE["nc"] = build(nseq=nseq)[0]
        _CACHE["consts"] = _host_consts()
    nc = _CACHE["nc"]
    consts = _CACHE["consts"]
    shared = {}
    for k, v in inputs.items():
        if k == "x":
            continue
        a = np.ascontiguousarray(np.asarray(v, dtype=np.float32))
        if k == "w_in":
            a = np.ascontiguousarray(a[:, :, _q_perm()])
        shared[k] = a
    shared.update(consts)
    in_maps = []
    for i in range(n_cores):
        m = dict(shared)
        m["x"] = x[i * nseq:(i + 1) * nseq].reshape(nseq * SEQ, D_MODEL)
        in_maps.append(m)
    res = run_bass_kernel_spmd(nc, in_maps, core_ids=list(range(n_cores)))
    outs = [np.asarray(r["out"]).reshape(nseq, SEQ, D_MODEL) for r in res.results]
    return np.concatenate(outs, axis=0).astype(np.float32)
```

```python
import math
from contextlib import ExitStack
import numpy as np
import jax
import jax.numpy as jnp
import concourse.bass as bass
import concourse.mybir as mybir
from concourse.bass_utils import run_bass_kernel_spmd

F32 = mybir.dt.float32
BF16 = mybir.dt.bfloat16
I32 = mybir.dt.int32
AF = mybir.ActivationFunctionType
ALU = mybir.AluOpType

D_MODEL = 1024
SEQ = 2048
DEPTH = 4
SSM_W = 512
NG = 32
NST = 64
D_FF = 2816
NFB = 22
IN_W = 1280
EPS = 1e-6
JC = 4
LCH = 8 * JC
NSUB = SEQ // 8
NCH = NSUB // JC
NT = SEQ // 128
TWO_PI = 2.0 * math.pi

EC_TB = 0
EC_TC = EC_TB + 8
EC_SB = EC_TC + 8 * JC
EC_CC = EC_SB + 8 * JC
EC_L = EC_CC + 8 * JC
NEC = EC_L + 1
ANG_SHIFT = 64

MA_COLS = 2 * JC * 128
MB_COLS = 4 * JC * 128


class Buf:
    __slots__ = ("w", "r", "name")

    def __init__(self, name=""):
        self.w = None
        self.r = {}
        self.name = name


class Eng:
    def __init__(self, K, name, eng, is_pe=False):
        self.K = K
        self.name = name
        self.eng = eng
        self.is_pe = is_pe
        self.sems = []
        self.ep = -1
        self.cnt = 0
        self.seen = {}
        self.pending = []
        self._new_epoch()

    def _new_epoch(self):
        self.ep += 1
        self.cnt = 0
        self.sems.append(self.K.nc.alloc_semaphore(f"s_{self.name}_{self.ep}"))

class Kern:
    def __init__(self, nc):
        self.nc = nc
        self.pe = Eng(self, "pe", nc.tensor, True)
        self.act = Eng(self, "act", nc.scalar)
        self.dve = Eng(self, "dve", nc.vector)
        self.pool = Eng(self, "pool", nc.gpsimd)
        self.sp = Eng(self, "sp", nc.sync)
        self.engs = [self.pe, self.act, self.dve, self.pool, self.sp]
        self.slots = {}
        for e, n in ((self.sp, 20), (self.pool, 6), (self.act, 6)):
            self.slots[e.name] = [[nc.alloc_semaphore(f"d_{e.name}_{i}"), 0, ("dma", e.name, i)] for i in range(n)]
        self.slot_i = {k: 0 for k in self.slots}
        self.n_ins = 0

    def _wait(self, E, tick):
        sem, val, key = tick
        if key[0] == E.name and E.is_pe:
            return
        if E.seen.get(key, 0) >= val:
            return
        if key[0] != "dma":
            for (k2, v2) in E.seen.items():
                if k2[0] == key[0] and k2[0] != "dma" and k2[1] > key[1]:
                    return
        E.eng.wait_ge(sem, val)
        E.seen[key] = val
        self.n_ins += 1

    def _deps(self, E, reads, writes):
        need = []
        for b in reads:
            if b.w is not None:
                need.append(b.w)
        for b in writes:
            if b.w is not None:
                need.append(b.w)
            need.extend(b.r.values())
        for t in need:
            self._wait(E, t)

    def op(self, E, fn, reads=(), writes=(), inc=True):
        if E.cnt >= 28000 and not E.pending:
            E._new_epoch()
        self._deps(E, reads, writes)
        tick = (E.sems[E.ep], E.cnt + 1, (E.name, E.ep))
        ins = fn()
        self.n_ins += 1
        if inc:
            ins.then_inc(E.sems[E.ep], 1)
            E.cnt += 1
            E.pending = []
        else:
            E.pending.append(1)
        for b in reads:
            b.r[E.name] = tick
        for b in writes:
            b.w = tick
            b.r = {}
        return ins

    def dma(self, E, out, in_, reads=(), writes=(), **kw):
        slots = self.slots[E.name]
        i = self.slot_i[E.name]
        self.slot_i[E.name] = (i + 1) % len(slots)
        s = slots[i]
        if s[1] > 0:
            self._wait(E, (s[0], s[1], s[2]))
        self._deps(E, reads, writes)
        ins = E.eng.dma_start(out=out, in_=in_, **kw)
        s[1] += 16
        ins.then_inc(s[0], 16)
        self.n_ins += 1
        tick = (s[0], s[1], s[2])
        for b in reads:
            b.r[("dma", E.name, i)] = tick
        for b in writes:
            b.w = tick
            b.r = {}
        return ins

    def barrier(self):
        sp = self.sp
        for E in self.engs:
            if E is sp:
                continue
            if E.cnt > 0:
                self._wait(sp, (E.sems[E.ep], E.cnt, (E.name, E.ep)))
        for lst in self.slots.values():
            for s in lst:
                if s[1] > 0:
                    self._wait(sp, (s[0], s[1], s[2]))
        self.op(sp, lambda: sp.eng.nop())
        t = (sp.sems[sp.ep], sp.cnt, (sp.name, sp.ep))
        for E in self.engs:
            if E is not sp:
                self._wait(E, t)


def _t5_bucket(rel):
    n_buckets, max_distance = 32, 128
    half = n_buckets // 2
    max_exact = half // 2
    ret = jnp.where(rel > 0, half, 0)
    n = jnp.abs(rel)
    nf = jnp.maximum(n, 1).astype(jnp.float32)
    large = max_exact + (jnp.log(nf / max_exact) / math.log(max_distance / max_exact)
                         * (half - max_exact)).astype(jnp.int32)
    large = jnp.minimum(large, half - 1)
    return ret + jnp.where(n < max_exact, n, large)


def _host_consts():
    import ml_dtypes
    c = {}
    c["ident_f"] = np.eye(128, dtype=np.float32)
    ex = np.zeros((2, NEC), np.float32)
    for s in range(8):
        ex[0, EC_TB + s] = -s
        ex[1, EC_TB + s] = s
    for d in range(JC):
        for s in range(8):
            ex[0, EC_TC + d * 8 + s] = 8 * d + s
            ex[1, EC_TC + d * 8 + s] = 8 * d - s
            ex[0, EC_SB + d * 8 + s] = LCH - 1 - 8 * d - s
            ex[1, EC_SB + d * 8 + s] = 8 * d + s
            ex[0, EC_CC + d * 8 + s] = 8 * d + s + 1
            ex[1, EC_CC + d * 8 + s] = LCH - 8 * d - s
    ex[:, EC_L] = LCH
    c["extab"] = np.repeat(ex, 64, axis=0).astype(np.float32)
    sp = np.arange(128)[:, None] // 16
    s = np.arange(128)[None, :] // 16
    tm = np.stack([(s >= sp), (sp >= s)], axis=1).astype(np.float32)
    c["tmask"] = np.ascontiguousarray(tm)
    k = np.arange(128)[:, None, None]
    off = (np.arange(3) - 1)[None, :, None]
    q = np.arange(128)[None, None, :]
    rel = (k + 128 * off - q).astype(np.int32)
    with jax.default_device(jax.devices("cpu")[0]):
        bk = np.asarray(_t5_bucket(jnp.asarray(rel)))
    oh = (bk[:, None, :, :] == np.arange(32)[None, :, None, None]).astype(np.float32)
    c["onehot"] = oh.reshape(128, 32, 384).astype(ml_dtypes.bfloat16)
    c["vmask"] = (np.abs(rel) <= 128).astype(np.float32).reshape(128, 384)
    return c


def build(nseq=4, nlayers=DEPTH, debug=None):
    nc = bass.Bass("TRN2", target_bir_lowering=False)
    K = Kern(nc)
    pe, act, dve, pool, sp = K.pe, K.act, K.dve, K.pool, K.sp
    NTOK = nseq * SEQ

    def din(name, shape, dt=F32):
        return nc.dram_tensor(name, list(shape), dt, kind="ExternalInput").ap()

    x_in = din("x", (NTOK, D_MODEL))
    P = {}
    P["rel_bias"] = din("rel_bias", (32, 8))
    P["pre_mix_norm"] = din("pre_mix_norm", (DEPTH, D_MODEL))
    P["w_in"] = din("w_in", (DEPTH, D_MODEL, IN_W))
    P["lam_re"] = din("lam_re", (DEPTH, 2, NG, NST))
    P["lam_im"] = din("lam_im", (DEPTH, 2, NG, NST))
    P["log_step"] = din("log_step", (DEPTH, 2, NG))
    P["b_re"] = din("b_re", (DEPTH, 2, NG, NST, 16))
    P["b_im"] = din("b_im", (DEPTH, 2, NG, NST, 16))
    P["c_re"] = din("c_re", (DEPTH, 2, NG, 16, NST))
    P["c_im"] = din("c_im", (DEPTH, 2, NG, 16, NST))
    P["ssm_d"] = din("ssm_d", (DEPTH, SSM_W))
    P["w_glu"] = din("w_glu", (DEPTH, SSM_W, SSM_W))
    P["b_glu"] = din("b_glu", (DEPTH, SSM_W))
    P["attn_sink"] = din("attn_sink", (DEPTH, 8))
    P["ssm_out_norm"] = din("ssm_out_norm", (DEPTH, SSM_W))
    P["attn_out_norm"] = din("attn_out_norm", (DEPTH, SSM_W))
    P["w_out"] = din("w_out", (DEPTH, D_MODEL, D_MODEL))
    P["post_mix_norm"] = din("post_mix_norm", (DEPTH, D_MODEL))
    P["pre_ffn_norm"] = din("pre_ffn_norm", (DEPTH, D_MODEL))
    P["w_up"] = din("w_up", (DEPTH, D_MODEL, 2 * D_FF))
    P["conv_w"] = din("conv_w", (DEPTH, 3, 2 * D_FF))
    P["conv_b"] = din("conv_b", (DEPTH, 2 * D_FF))
    P["w_down"] = din("w_down", (DEPTH, D_FF, D_MODEL))
    P["post_ffn_norm"] = din("post_ffn_norm", (DEPTH, D_MODEL))
    c_ident = din("ident_f", (128, 128))
    c_extab = din("extab", (128, NEC))
    c_tmask = din("tmask", (128, 2, 128))
    c_onehot = din("onehot", (128, 32, 384), BF16)
    c_vmask = din("vmask", (128, 384))

    out = nc.dram_tensor("out", [NTOK, D_MODEL], F32, kind="ExternalOutput").ap()
    matsA = nc.dram_tensor("matsA", [NG, 128, MA_COLS], BF16).ap()
    matsB = nc.dram_tensor("matsB", [NG, 128, MB_COLS], BF16).ap()
    u_scr = nc.dram_tensor("u_scr", [NG * 16, SEQ], BF16).ap()
    z_scr = nc.dram_tensor("z_scr", [NG * 16, SEQ], BF16).ap()
    eb_scr = nc.dram_tensor("eb_scr", [128, 3 * 2 * 4 * 128], BF16).ap()
    matsA_b = [Buf() for _ in range(NG)]
    matsB_b = [Buf() for _ in range(NG)]
    u_scr_b, z_scr_b = Buf(), Buf()
    out_b = [[Buf() for _ in range(NT)] for _ in range(nseq)]
    dbg = {}
    if debug:
        for nm, (shp, dt_) in debug.items():
          if shp is not None:
            dbg[nm] = nc.dram_tensor("dbg_" + nm, list(shp), dt_, kind="ExternalOutput").ap()

    top = ExitStack()
    with top:
        uid = [0]

        def sb(es, name, shape, dt=F32):
            uid[0] += 1
            return es.enter_context(nc.sbuf_tensor(f"{name}_{uid[0]}", list(shape), dt))

        psum = top.enter_context(nc.psum_tensor("psum", [128, 8, 512], F32))
        ps_b = [Buf(f"ps{i}") for i in range(8)]
        ps_rr = [0]

        def next_ps(n=1):
            if n == 1:
                i = ps_rr[0] % 8
                ps_rr[0] += 1
                return i
            i = ps_rr[0] % 8
            if i % 2:
                i = (i + 1) % 8
            ps_rr[0] = i + 2
            return i

        ident = sb(top, "ident", (128, 128))
        identb = sb(top, "identb", (128, 128), BF16)
        ones_bf = sb(top, "ones_bf", (128, 1), BF16)
        epsb = sb(top, "epsb", (128, 1))
        aL = sb(top, "aL", (128, 2, 2, 32))
        aL_b = Buf("aL")
        cb = Buf("consts")
        ebB = Buf("eb")
        K.dma(sp, ident[:], c_ident, writes=[cb])
        K.op(dve, lambda: nc.vector.tensor_copy(out=identb[:], in_=ident[:]), reads=[cb], writes=[cb])
        K.op(dve, lambda: nc.vector.memset(ones_bf[:], 1.0), writes=[cb])
        K.op(dve, lambda: nc.vector.memset(epsb[:], EPS), writes=[cb])

        with ExitStack() as es:
            oh = sb(es, "oh", (128, 32, 384), BF16)
            eb = sb(es, "eb0", (128, 3, 2, 4, 128), BF16)
            rb = sb(es, "rb", (128, 256))
            vm = sb(es, "vm", (128, 384))
            acc = sb(es, "acc", (128, 8, 384))
            tb = Buf()
            K.dma(sp, oh[:], c_onehot, writes=[tb])
            K.dma(sp, vm[:], c_vmask, writes=[tb])
            K.dma(sp, rb[:], P["rel_bias"].rearrange("b h -> (b h)").partition_broadcast(128), writes=[tb])
            accb = [Buf() for _ in range(8)]
            for h in range(8):
                E = dve
                K.op(E, lambda h=h: nc.vector.tensor_scalar(out=acc[:, h, :], in0=oh[:, 0, :], scalar1=rb[:, h:h + 1],
                                                            scalar2=None, op0=ALU.mult), reads=[tb], writes=[accb[h]])
                for b in range(1, 32):
                    K.op(E, lambda h=h, b=b: nc.vector.scalar_tensor_tensor(
                        out=acc[:, h, :], in0=oh[:, b, :], scalar=rb[:, b * 8 + h:b * 8 + h + 1], in1=acc[:, h, :],
                        op0=ALU.mult, op1=ALU.add), reads=[tb, accb[h]], writes=[accb[h]])
                K.op(act, lambda h=h: nc.scalar.activation(out=acc[:, h, :], in_=acc[:, h, :], func=AF.Exp),
                     reads=[accb[h]], writes=[accb[h]])
                kv, c = h // 4, h % 4
                K.op(dve, lambda h=h, kv=kv, c=c: nc.vector.tensor_tensor(
                    out=eb[:, :, kv, c, :], in0=acc[:, h, :].rearrange("p (o q) -> p o q", o=3),
                    in1=vm[:].rearrange("p (o q) -> p o q", o=3), op=ALU.mult), reads=[accb[h], tb], writes=[ebB])
            K.dma(sp, eb_scr, eb[:].rearrange("p o k c q -> p (o k c q)"), reads=[ebB], writes=[ebB])
            K.barrier()

        for l in range(nlayers):
            x_src = x_in if l == 0 else out
            with ExitStack() as es:
                s5_prologue(nc, K, es, sb, psum, ps_b, next_ps, P, l, ident, c_extab, c_tmask,
                            aL, aL_b, matsA, matsB, matsA_b, matsB_b, dbg)
                K.barrier()
            with ExitStack() as es:
                mixer_phase(nc, K, es, sb, psum, ps_b, next_ps, P, l, nseq, x_src, out, out_b, ident, identb, ones_bf,
                            eb_scr, ebB, epsb, cb, aL, aL_b, matsA, matsB, matsA_b, matsB_b, u_scr, z_scr, u_scr_b, z_scr_b, dbg)
                K.barrier()
            with ExitStack() as es:
              if not (debug and "skip_ffn" in debug):
                ffn_phase(nc, K, es, sb, psum, ps_b, next_ps, P, l, nseq, out, out_b, ident, identb, epsb, cb, dbg)
                K.barrier()
        K.barrier()
    return nc, K


def make_staging(es, sb, tag, n=6, ch=2048):
    return ([sb(es, f"stg_{tag}{i}", (128, ch)) for i in range(n)], [Buf() for _ in range(n)], [0])


def load_cast_weight(nc, K, stage, dst, dst_b, src_rows, ncols, gain, gain_b, engines):
    stg, stg_b, itr = stage
    NS = len(stg)
    CH = 2048
    dq = [K.sp, K.act]
    it = itr[0]
    for r, src in enumerate(src_rows):
        for c0 in range(0, ncols, CH):
            cw = min(CH, ncols - c0)
            i = it % NS
            E = engines[it % len(engines)]
            Q = dq[it % 2]
            it += 1
            K.dma(Q, stg[i][:, 0:cw], src[:, c0:c0 + cw], writes=[stg_b[i]])
            if gain is None:
                if E is K.act:
                    K.op(E, lambda i=i, r=r, c0=c0, cw=cw: nc.scalar.copy(out=dst[:, r, c0:c0 + cw], in_=stg[i][:, 0:cw]),
                         reads=[stg_b[i]], writes=[dst_b])
                else:
                    K.op(E, lambda i=i, r=r, c0=c0, cw=cw, E=E: E.eng.tensor_copy(out=dst[:, r, c0:c0 + cw], in_=stg[i][:, 0:cw]),
                         reads=[stg_b[i]], writes=[dst_b])
            else:
                if E is K.act:
                    K.op(E, lambda i=i, r=r, c0=c0, cw=cw: nc.scalar.activation(
                        out=dst[:, r, c0:c0 + cw], in_=stg[i][:, 0:cw], func=AF.Copy, scale=gain[:, r:r + 1]),
                        reads=[stg_b[i], gain_b], writes=[dst_b])
                else:
                    K.op(E, lambda i=i, r=r, c0=c0, cw=cw, E=E: E.eng.tensor_scalar(
                        out=dst[:, r, c0:c0 + cw], in0=stg[i][:, 0:cw], scalar1=gain[:, r:r + 1], scalar2=None,
                        op0=ALU.mult), reads=[stg_b[i], gain_b], writes=[dst_b])
    itr[0] = it


def load_cols(nc, K, psum, ps_b, next_ps, ident, cb, ld, ld_b, dst, dst_b, src1d, n):
    K.dma(K.sp, ld[0:n, :], src1d.rearrange("(r p) -> r p", p=128), writes=[ld_b])
    b = next_ps()
    K.op(K.pe, lambda: nc.tensor.transpose(psum[:, b, 0:n], ld[0:n, :], ident[0:n, 0:n]), reads=[ld_b, cb], writes=[ps_b[b]])
    K.op(K.dve, lambda: nc.vector.tensor_copy(out=dst, in_=psum[:, b, 0:n]), reads=[ps_b[b]], writes=[dst_b])


def rstd_from_ssq(nc, K, ssq, rs, n, width, epsb, bufs_r, bufs_w):
    K.op(K.act, lambda: nc.scalar.activation(out=rs[:, 0:n], in_=ssq[:, 0:n], func=AF.Sqrt, bias=epsb[:, 0:1],
                                             scale=1.0 / width), reads=bufs_r, writes=bufs_w)
    K.op(K.dve, lambda: nc.vector.reciprocal(out=rs[:, 0:n], in_=rs[:, 0:n]), reads=bufs_w, writes=bufs_w)


def s5_prologue(nc, K, es, sb, psum, ps_b, next_ps, P, l, ident, c_extab, c_tmask,
                aL, aL_b, matsA, matsB, matsA_b, matsB_b, dbg):
    pe, act, dve, pool, sp = K.pe, K.act, K.dve, K.pool, K.sp
    V = nc.vector
    G = NG
    tb = Buf("s5tab")

    def t(name, shape, dt=F32):
        return sb(es, name, shape, dt)

    lamld = t("lamld", (32, 2, 128))
    lam = t("lam", (128, 2, 32))
    ls = t("ls", (128, 32))
    braw = t("braw", (128, 2, 32, 16))
    cld = t("cld", (128, 2, 4, 128))
    craw = t("craw", (128, 2, 32, 16))
    dvec = t("dvec", (128, 32))
    extab = t("extab_sb", (128, NEC))
    tmask = t("tmask_sb", (128, 2, 128))
    K.dma(sp, extab[:], c_extab, writes=[tb])
    K.dma(sp, tmask[:], c_tmask, writes=[tb])
    with nc.allow_non_contiguous_dma(reason="tiny param loads"):
        for ri, nm in enumerate(("lam_re", "lam_im")):
            K.dma(sp, lamld[:, ri, :].rearrange("g (d n) -> g d n", d=2), P[nm][l].rearrange("d g n -> g d n"), writes=[tb])
        for d in range(2):
            K.dma(sp, ls[d * 64:(d + 1) * 64, :], P["log_step"][l, d].partition_broadcast(64), writes=[tb])
            for ri, nm in enumerate(("b_re", "b_im")):
                K.dma(sp, braw[d * 64:(d + 1) * 64, ri, :, :], P[nm][l, d].rearrange("g n q -> n g q"), writes=[tb])
        for ri, nm in enumerate(("c_re", "c_im")):
            for d in range(2):
                K.dma(sp, cld[:, ri, :, d * 64:(d + 1) * 64],
                      P[nm][l, d].rearrange("(t gl) p n -> (gl p) t n", t=4), writes=[tb])
        for s in range(8):
            K.dma(sp, dvec[s * 16:(s + 1) * 16, :], P["ssm_d"][l].rearrange("(g q) -> q g", q=16), writes=[tb])
    for ri in range(2):
        b = next_ps()
        K.op(pe, lambda ri=ri, b=b: nc.tensor.transpose(psum[:, b, 0:32], lamld[:, ri, :], ident[0:32, 0:32]),
             reads=[tb], writes=[ps_b[b]])
        K.op(dve, lambda ri=ri, b=b: V.tensor_copy(out=lam[:, ri, :], in_=psum[:, b, 0:32]), reads=[ps_b[b]], writes=[tb])
        b = next_ps()
        for tt in range(4):
            K.op(pe, lambda ri=ri, b=b, tt=tt: nc.tensor.transpose(psum[:, b, tt * 128:(tt + 1) * 128], cld[:, ri, tt, :], ident[:]),
                 reads=[tb], writes=[ps_b[b]], inc=(tt == 3))
        K.op(dve, lambda ri=ri, b=b: V.tensor_copy(out=craw[:, ri, :, :].rearrange("p g q -> p (g q)"), in_=psum[:, b, :]),
             reads=[ps_b[b]], writes=[tb])

    dt_ = t("dt", (128, 32))
    ar = t("ar", (128, 32))
    ang = t("ang", (128, 32))
    lb = t("lb", (128, 2, 32))
    coef = t("coef", (128, 2, 32))
    tmp = t("tmpa", (128, 4, 32))

    def dv(fn, inc=True):
        K.op(dve, fn, reads=[tb], writes=[tb], inc=inc)

    def ac(fn):
        K.op(act, fn, reads=[tb], writes=[tb])

    ac(lambda: nc.scalar.activation(out=dt_[:], in_=ls[:], func=AF.Exp))
    dv(lambda: V.tensor_tensor(out=ar[:], in0=lam[:, 0, :], in1=dt_[:], op=ALU.mult))
    dv(lambda: V.tensor_tensor(out=ang[:], in0=lam[:, 1, :], in1=dt_[:], op=ALU.mult))

    PR = t("PR", (128, 32, NEC))
    PI = t("PI", (128, 32, NEC))
    A1 = t("A1", (128, 32, NEC))
    A2 = t("A2", (128, 32, NEC))
    A3i = t("A3i", (128, 32, NEC), I32)
    A4 = t("A4", (128, 32, NEC))
    ex_b = extab[:].unsqueeze(1).to_broadcast([128, 32, NEC])

    def bc_g(x):
        return x.unsqueeze(2).to_broadcast([128, 32, NEC])

    dv(lambda: V.tensor_tensor(out=A1[:], in0=ex_b, in1=bc_g(ar[:]), op=ALU.mult))
    ac(lambda: nc.scalar.activation(out=A4[:], in_=A1[:], func=AF.Exp))
    dv(lambda: V.tensor_tensor(out=A1[:], in0=ex_b, in1=bc_g(ang[:]), op=ALU.mult))

    def sin_of(dst, shift):
        dv(lambda: V.tensor_scalar(out=A2[:], in0=A1[:], scalar1=shift + ANG_SHIFT * TWO_PI, scalar2=1.0 / TWO_PI,
                                   op0=ALU.add, op1=ALU.mult))
        dv(lambda: V.tensor_copy(out=A3i[:], in_=A2[:]))
        dv(lambda: V.tensor_copy(out=dst[:], in_=A3i[:]))
        dv(lambda: V.tensor_tensor(out=A2[:], in0=A2[:], in1=dst[:], op=ALU.subtract))
        dv(lambda: V.tensor_scalar(out=dst[:], in0=A2[:], scalar1=0.5, scalar2=None, op0=ALU.is_gt))
        dv(lambda: V.tensor_tensor(out=A2[:], in0=A2[:], in1=dst[:], op=ALU.subtract))
        dv(lambda: V.tensor_scalar(out=dst[:], in0=A2[:], scalar1=-0.5, scalar2=None, op0=ALU.is_lt))
        dv(lambda: V.tensor_tensor(out=A2[:], in0=A2[:], in1=dst[:], op=ALU.add))
        dv(lambda: V.tensor_scalar(out=A2[:], in0=A2[:], scalar1=TWO_PI, scalar2=3.14159, op0=ALU.mult, op1=ALU.min))
        dv(lambda: V.tensor_scalar(out=A2[:], in0=A2[:], scalar1=-3.14159, scalar2=None, op0=ALU.max))
        ac(lambda: nc.scalar.activation(out=dst[:], in_=A2[:], func=AF.Sin))

    sin_of(PI, 0.0)
    sin_of(PR, math.pi / 2)
    dv(lambda: V.tensor_tensor(out=PR[:], in0=PR[:], in1=A4[:], op=ALU.mult))
    dv(lambda: V.tensor_tensor(out=PI[:], in0=PI[:], in1=A4[:], op=ALU.mult))

    ac(lambda: nc.scalar.activation(out=tmp[:, 0, :], in_=ar[:], func=AF.Exp))
    dv(lambda: V.tensor_copy(out=lb[0:64, 0, :], in_=PR[0:64, :, EC_TC + 1]))
    dv(lambda: V.tensor_copy(out=lb[0:64, 1, :], in_=PI[0:64, :, EC_TC + 1]))
    dv(lambda: V.tensor_copy(out=lb[64:128, 0, :], in_=PR[64:128, :, EC_TB + 1]))
    dv(lambda: V.tensor_copy(out=lb[64:128, 1, :], in_=PI[64:128, :, EC_TB + 1]))
    dv(lambda: V.tensor_tensor(out=tmp[:, 0, :], in0=lam[:, 0, :], in1=lam[:, 0, :], op=ALU.mult))
    dv(lambda: V.tensor_tensor(out=tmp[:, 1, :], in0=lam[:, 1, :], in1=lam[:, 1, :], op=ALU.mult))
    dv(lambda: V.tensor_tensor(out=tmp[:, 0, :], in0=tmp[:, 0, :], in1=tmp[:, 1, :], op=ALU.add))
    dv(lambda: V.reciprocal(out=tmp[:, 0, :], in_=tmp[:, 0, :]))
    dv(lambda: V.tensor_scalar(out=tmp[:, 1, :], in0=lb[:, 0, :], scalar1=-1.0, scalar2=None, op0=ALU.add))
    dv(lambda: V.tensor_tensor(out=tmp[:, 2, :], in0=tmp[:, 1, :], in1=lam[:, 0, :], op=ALU.mult))
    dv(lambda: V.tensor_tensor(out=tmp[:, 3, :], in0=lb[:, 1, :], in1=lam[:, 1, :], op=ALU.mult))
    dv(lambda: V.tensor_tensor(out=tmp[:, 2, :], in0=tmp[:, 2, :], in1=tmp[:, 3, :], op=ALU.add))
    dv(lambda: V.tensor_tensor(out=coef[:, 0, :], in0=tmp[:, 2, :], in1=tmp[:, 0, :], op=ALU.mult))
    dv(lambda: V.tensor_tensor(out=tmp[:, 2, :], in0=lb[:, 1, :], in1=lam[:, 0, :], op=ALU.mult))
    dv(lambda: V.tensor_tensor(out=tmp[:, 3, :], in0=tmp[:, 1, :], in1=lam[:, 1, :], op=ALU.mult))
    dv(lambda: V.tensor_tensor(out=tmp[:, 2, :], in0=tmp[:, 2, :], in1=tmp[:, 3, :], op=ALU.subtract))
    dv(lambda: V.tensor_tensor(out=coef[:, 1, :], in0=tmp[:, 2, :], in1=tmp[:, 0, :], op=ALU.mult))
    bb = t("bb", (128, 2, 32, 16))
    t16 = t("t16", (128, 32, 16))

    def bq(x):
        return x.unsqueeze(2).to_broadcast([128, 32, 16])

    dv(lambda: V.tensor_tensor(out=bb[:, 0], in0=braw[:, 0], in1=bq(coef[:, 0, :]), op=ALU.mult))
    dv(lambda: V.tensor_tensor(out=t16[:], in0=braw[:, 1], in1=bq(coef[:, 1, :]), op=ALU.mult))
    dv(lambda: V.tensor_tensor(out=bb[:, 0], in0=bb[:, 0], in1=t16[:], op=ALU.subtract))
    dv(lambda: V.tensor_tensor(out=bb[:, 1], in0=braw[:, 1], in1=bq(coef[:, 0, :]), op=ALU.mult))
    dv(lambda: V.tensor_tensor(out=t16[:], in0=braw[:, 0], in1=bq(coef[:, 1, :]), op=ALU.mult))
    dv(lambda: V.tensor_tensor(out=bb[:, 1], in0=bb[:, 1], in1=t16[:], op=ALU.add))

    if "s5tab" in dbg:
        K.dma(sp, dbg["s5tab"][:, 0:NEC], PR[:, 0, :], reads=[tb])
        K.dma(sp, dbg["s5tab"][:, NEC:2 * NEC], PI[:, 0, :], reads=[tb])
        K.dma(sp, dbg["s5tab"][:, 2 * NEC:2 * NEC + 32], bb[:, 0, 0:2, :].rearrange("p g q -> p (g q)"), reads=[tb])
        K.dma(sp, dbg["s5tab"][:, 2 * NEC + 32:2 * NEC + 64], bb[:, 1, 0:2, :].rearrange("p g q -> p (g q)"), reads=[tb])

    K.op(dve, lambda: V.tensor_copy(out=aL[:, 0, 0, :], in_=PR[:, :, EC_L]), reads=[tb], writes=[aL_b])
    K.op(dve, lambda: V.tensor_copy(out=aL[:, 0, 1, :], in_=PR[:, :, EC_L]), reads=[tb], writes=[aL_b])
    K.op(dve, lambda: V.tensor_scalar(out=aL[:, 1, 0, :], in0=PI[:, :, EC_L], scalar1=-1.0, scalar2=None, op0=ALU.mult), reads=[tb], writes=[aL_b])
    K.op(dve, lambda: V.tensor_copy(out=aL[:, 1, 1, :], in_=PI[:, :, EC_L]), reads=[tb], writes=[aL_b])

    NB = 8 + 8 * JC
    NC8 = 8 * JC
    GB = 2
    xb = [t(f"xb{i}", (128, 2, GB, NB, 16)) for i in range(2)]
    xc = [t(f"xc{i}", (128, 2, GB, NC8, 16)) for i in range(2)]
    xcc = [t(f"xcc{i}", (128, 2, GB, NC8, 16)) for i in range(2)]
    tq = [t(f"tq{i}", (128, GB, NB, 16)) for i in range(2)]
    mA = [t(f"mA{i}", (128, MA_COLS), BF16) for i in range(2)]
    mB = [t(f"mB{i}", (128, MB_COLS), BF16) for i in range(2)]
    dmat = [t(f"dmat{i}", (128, 128)) for i in range(2)]
    t0m = [t(f"t0m{i}", (128, 128)) for i in range(2)]
    xb_b = [Buf() for _ in range(2)]
    xc_b = [Buf() for _ in range(2)]
    xcc_b = [Buf() for _ in range(2)]
    tq_b = [Buf() for _ in range(2)]
    mA_b = [Buf() for _ in range(2)]
    mB_b = [Buf() for _ in range(2)]
    dm_b = [Buf() for _ in range(2)]

    def pwB(x, g0, c0, n):
        return x[:, g0:g0 + GB, c0:c0 + n].unsqueeze(3).to_broadcast([128, GB, n, 16])

    def bbB(ri, g0, n):
        return bb[:, ri, g0:g0 + GB, :].unsqueeze(2).to_broadcast([128, GB, n, 16])

    def ccB(ri, g0, n):
        return craw[:, ri, g0:g0 + GB, :].unsqueeze(2).to_broadcast([128, GB, n, 16])

    for gb in range(G // GB):
        g0 = gb * GB
        i = gb % 2
        E2 = pool if (gb % 4 == 3) else dve
        EN = E2.eng

        def tt_(out, in0, in1, op, reads, writes):
            K.op(E2, lambda: EN.tensor_tensor(out=out, in0=in0, in1=in1, op=op), reads=reads, writes=writes)

        for (dst0, c0, n) in ((0, EC_TB, 8), (8, EC_SB, NC8)):
            o_re = xb[i][:, 0, :, dst0:dst0 + n, :]
            o_im = xb[i][:, 1, :, dst0:dst0 + n, :]
            t_ = tq[i][:, :, dst0:dst0 + n, :]
            tt_(o_re, pwB(PR, g0, c0, n), bbB(0, g0, n), ALU.mult, [tb], [xb_b[i]])
            tt_(t_, pwB(PI, g0, c0, n), bbB(1, g0, n), ALU.mult, [tb], [tq_b[i]])
            tt_(o_re, o_re, t_, ALU.subtract, [tq_b[i], xb_b[i]], [xb_b[i]])
            tt_(o_im, pwB(PR, g0, c0, n), bbB(1, g0, n), ALU.mult, [tb], [xb_b[i]])
            tt_(t_, pwB(PI, g0, c0, n), bbB(0, g0, n), ALU.mult, [tb, xb_b[i]], [tq_b[i]])
            tt_(o_im, o_im, t_, ALU.add, [tq_b[i], xb_b[i]], [xb_b[i]])
        for (dstt, dstb, c0) in ((xc, xc_b, EC_TC), (xcc, xcc_b, EC_CC)):
            o_re = dstt[i][:, 0]
            o_im = dstt[i][:, 1]
            t_ = tq[i][:, :, 0:NC8, :]
            tt_(o_re, pwB(PR, g0, c0, NC8), ccB(0, g0, NC8), ALU.mult, [tb], [dstb[i]])
            tt_(t_, pwB(PI, g0, c0, NC8), ccB(1, g0, NC8), ALU.mult, [tb], [tq_b[i]])
            tt_(o_re, o_re, t_, ALU.subtract, [tq_b[i], dstb[i]], [dstb[i]])
            tt_(o_im, pwB(PI, g0, c0, NC8), ccB(0, g0, NC8), ALU.mult, [tb], [dstb[i]])
            tt_(t_, pwB(PR, g0, c0, NC8), ccB(1, g0, NC8), ALU.mult, [tb, dstb[i]], [tq_b[i]])
            tt_(o_im, o_im, t_, ALU.add, [tq_b[i], dstb[i]], [dstb[i]])
            K.op(E2, lambda: EN.tensor_scalar(out=o_im, in0=o_im, scalar1=-1.0, scalar2=None, op0=ALU.mult),
                 reads=[dstb[i]], writes=[dstb[i]])

        for gl in range(GB):
            g = g0 + gl
            j2 = g % 2
            for d in range(2):
                b = next_ps()
                rows = slice(d * 64, (d + 1) * 64)
                for ri in range(2):
                    K.op(pe, lambda: nc.tensor.matmul(
                        psum[:, b, 0:JC * 128],
                        xb[i][rows, ri, gl, 0:8, :].rearrange("p s q -> p (s q)"),
                        xc[i][rows, ri, gl, :, :].rearrange("p c q -> p (c q)"),
                        start=(ri == 0), stop=(ri == 1)),
                        reads=[xb_b[i], xc_b[i]], writes=[ps_b[b]], inc=(ri == 1))
                base = d * JC * 128
                if d == 0:
                    K.op(dve, lambda: V.tensor_scalar(out=dmat[j2][:], in0=ident[:], scalar1=dvec[:, g:g + 1],
                                                      scalar2=None, op0=ALU.mult), reads=[tb], writes=[dm_b[j2]])
                    K.op(dve, lambda: V.tensor_tensor(out=t0m[j2][:], in0=psum[:, b, 0:128], in1=tmask[:, 0, :], op=ALU.mult),
                         reads=[ps_b[b], tb, dm_b[j2]], writes=[dm_b[j2]])
                    K.op(dve, lambda: V.tensor_tensor(out=mB[j2][:, base:base + 128], in0=t0m[j2][:], in1=dmat[j2][:], op=ALU.add),
                         reads=[dm_b[j2]], writes=[mB_b[j2]])
                else:
                    K.op(dve, lambda: V.tensor_tensor(out=mB[j2][:, base:base + 128], in0=psum[:, b, 0:128],
                                                      in1=tmask[:, 1, :], op=ALU.mult),
                         reads=[ps_b[b], tb], writes=[mB_b[j2]])
                if JC > 1:
                    K.op(act, lambda: nc.scalar.copy(out=mB[j2][:, base + 128:base + JC * 128], in_=psum[:, b, 128:JC * 128]),
                         reads=[ps_b[b]], writes=[mB_b[j2]])
            K.op(act, lambda: nc.scalar.copy(out=mB[j2][:, 2 * JC * 128:3 * JC * 128],
                                             in_=xcc[i][:, 0, gl].rearrange("p c q -> p (c q)")),
                 reads=[xcc_b[i]], writes=[mB_b[j2]])
            K.op(act, lambda: nc.scalar.copy(out=mB[j2][:, 3 * JC * 128:4 * JC * 128],
                                             in_=xcc[i][:, 1, gl].rearrange("p c q -> p (c q)")),
                 reads=[xcc_b[i]], writes=[mB_b[j2]])
            K.dma(sp, matsB[g], mB[j2][:], reads=[mB_b[j2]], writes=[matsB_b[g]])
            for ri in range(2):
                b = next_ps()
                for j in range(JC):
                    K.op(pe, lambda: nc.tensor.transpose(
                        psum[:, b, j * 128:(j + 1) * 128],
                        xb[i][:, ri, gl, 8 + 8 * j:16 + 8 * j, :].rearrange("p s q -> p (s q)"), ident[:]),
                        reads=[xb_b[i]], writes=[ps_b[b]], inc=(j == JC - 1))
                K.op(act, lambda: nc.scalar.copy(out=mA[j2][:, ri * JC * 128:(ri + 1) * JC * 128], in_=psum[:, b, 0:JC * 128]),
                     reads=[ps_b[b]], writes=[mA_b[j2]])
            K.dma(sp, matsA[g], mA[j2][:], reads=[mA_b[j2]], writes=[matsA_b[g]])
    if "matsB0" in dbg:
        pass


def mixer_phase(nc, K, es, sb, psum, ps_b, next_ps, P, l, nseq, x_src, out, out_b, ident, identb, ones_bf,
                eb_scr, ebB, epsb, cb, aL, aL_b, matsA, matsB, matsA_b, matsB_b, u_scr, z_scr, u_scr_b, z_scr_b, dbg):
    pe, act, dve, pool, sp = K.pe, K.act, K.dve, K.pool, K.sp
    V = nc.vector
    G = NG
    wi = sb(es, "wi", (128, 8, IN_W), BF16)
    wg = sb(es, "wg", (128, 4, 512), BF16)
    wo = sb(es, "wo", (128, 8, 1024), BF16)
    gpm = sb(es, "gpm", (128, 8))
    gso = sb(es, "gso", (128, 8))
    bglu = sb(es, "bglu", (128, 4))
    esink = sb(es, "esink", (128, 8))
    gpost = sb(es, "gpost", (128, 1024))
    wi_b, wg_b, wo_b, pb = Buf(), Buf(), Buf(), Buf()
    eb = sb(es, "eb", (128, 3, 2, 4, 128), BF16)
    K.dma(sp, eb[:].rearrange("p o k c q -> p (o k c q)"), eb_scr, reads=[ebB], writes=[pb])
    ebB = pb
    with ExitStack() as es1:
        ld = sb(es1, "ldm", (8, 128))
        ld_b = Buf()
        LC = lambda dst, src, n: load_cols(nc, K, psum, ps_b, next_ps, ident, cb, ld, ld_b, dst, pb, src, n)
        LC(gpm[:], P["pre_mix_norm"][l], 8)
        LC(gso[:, 0:4], P["ssm_out_norm"][l], 4)
        LC(gso[:, 4:8], P["attn_out_norm"][l], 4)
        LC(bglu[:], P["b_glu"][l], 4)
        K.barrier()
    K.dma(sp, esink[:], P["attn_sink"][l].partition_broadcast(128), writes=[pb])
    K.dma(sp, gpost[:], P["post_mix_norm"][l].partition_broadcast(128), writes=[pb])
    K.op(act, lambda: nc.scalar.activation(out=esink[:], in_=esink[:], func=AF.Exp), reads=[pb], writes=[pb])
    with ExitStack() as es2:
        engs = [act, dve]
        stage = make_staging(es2, sb, "m")
        load_cast_weight(nc, K, stage, wi, wi_b, [P["w_in"][l, k * 128:(k + 1) * 128, :] for k in range(8)], IN_W,
                         gpm, pb, engs)
        load_cast_weight(nc, K, stage, wg, wg_b, [P["w_glu"][l, k * 128:(k + 1) * 128, :] for k in range(4)], 512,
                         None, pb, engs)
        load_cast_weight(nc, K, stage, wo, wo_b, [P["w_out"][l, k * 128:(k + 1) * 128, :] for k in range(8)], 1024,
                         gso, pb, engs)
        K.barrier()

    X1 = sb(es, "X1", (128, 4, SEQ), BF16)
    hTlo = sb(es, "hTlo", (128, 4, SEQ), BF16)
    hThi = sb(es, "hThi", (128, 4, SEQ), BF16)
    qT = sb(es, "qT", (128, 4, SEQ), BF16)
    UY = sb(es, "UY", (128, 4, SEQ), BF16)
    kT = sb(es, "kT", (128, SEQ), BF16)
    vaug = sb(es, "vaug", (128, NT, 2, 66), BF16)
    xt = [sb(es, f"xt{i}", (128, 1024)) for i in range(3)]
    hs = xt
    xo = xt
    junk = sb(es, "junk", (128, 1024), BF16)
    ssq = sb(es, "ssq", (128, 4, NT))
    rs = sb(es, "rs", (128, 4, NT))
    B = lambda n=1: [Buf() for _ in range(n)]
    X1_b, hTlo_b, hThi_b, qT_b, UY_b, kT_b, va_b, H_b, Ein_b = (Buf() for _ in range(9))
    Hf_b, Hb_b = Buf(), Buf()
    xt_b, mAb_b, mBb_b, et_b, pt_b, oa_b, den_b, gate_b, mt_b = B(3), B(2), B(2), B(3), B(6), B(2), B(2), B(2), B(2)
    hs_b = xt_b
    xo_b = xt_b
    junk_b, ssq_b, rs_b, st_b = Buf(), [Buf() for _ in range(4)], [Buf() for _ in range(4)], [Buf(), Buf()]
    m1s_b = [(Buf(), Buf()) for _ in range(4)]
    K.op(dve, lambda: V.memset(vaug[:, :, :, 64:66], 1.0), writes=[va_b])
    hT_b = [hTlo_b, hThi_b]
    hT = [hTlo, hThi]
    Uf = UY[:].rearrange("p c t -> p (c t)").rearrange("p (g s) -> p g s", g=NG)
    Zf = hTlo[:].rearrange("p c t -> p (c t)").rearrange("p (g s) -> p g s", g=NG)
    ysT, ysT_b = qT, qT_b
    yaT, yaT_b = hThi, hThi_b
    ysq, ysq_b = UY, UY_b

    for s in range(nseq):
        tok0 = s * SEQ
        def m1_a(tt):
            i = tt % 3
            r0 = tok0 + tt * 128
            rd = [out_b[s][tt]] if x_src is out else []
            K.dma(sp, xt[i][:], x_src[r0:r0 + 128, :], reads=rd, writes=[xt_b[i]])
            sq_b, r_b = m1s_b[tt % 4]
            K.op(act, lambda: nc.scalar.activation(out=junk[:], in_=xt[i][:], func=AF.Square, accum_out=ssq[:, 0, tt:tt + 1]),
                 reads=[xt_b[i]], writes=[junk_b, sq_b])
            K.op(act, lambda: nc.scalar.activation(out=rs[:, 0, tt:tt + 1], in_=ssq[:, 0, tt:tt + 1], func=AF.Sqrt,
                                                   bias=epsb[:, 0:1], scale=1.0 / D_MODEL), reads=[sq_b, cb], writes=[r_b])
            K.op(dve, lambda: V.reciprocal(out=rs[:, 0, tt:tt + 1], in_=rs[:, 0, tt:tt + 1]), reads=[r_b], writes=[r_b])
            K.op(dve, lambda: V.tensor_scalar(out=hs[i][:], in0=xt[i][:], scalar1=rs[:, 0, tt:tt + 1], scalar2=None, op0=ALU.mult),
                 reads=[xt_b[i], r_b], writes=[hs_b[i]])

        def m1_b(tt):
            i = tt % 3
            b = next_ps(2)
            for c in range(8):
                K.op(pe, lambda: nc.tensor.transpose(psum[:, b + c // 4, (c % 4) * 128:(c % 4 + 1) * 128],
                                                     hs[i][:, c * 128:(c + 1) * 128], ident[:]),
                     reads=[hs_b[i], cb], writes=[ps_b[b], ps_b[b + 1]], inc=(c == 7))
            K.op(dve, lambda: V.tensor_copy(out=hTlo[:, :, tt * 128:(tt + 1) * 128],
                                            in_=psum[:, b, :].rearrange("p (c t) -> p c t", c=4)),
                 reads=[ps_b[b]], writes=[hTlo_b])
            K.op(act, lambda: nc.scalar.copy(out=hThi[:, :, tt * 128:(tt + 1) * 128],
                                             in_=psum[:, b + 1, :].rearrange("p (c t) -> p c t", c=4)),
                 reads=[ps_b[b + 1]], writes=[hThi_b])

        m1_a(0)
        for tt in range(NT):
            if tt + 1 < NT:
                m1_a(tt + 1)
            m1_b(tt)
        for m in range(9):
            if m == 4:
                K.dma(sp, u_scr.rearrange("(c r) t -> r c t", c=4), X1[:], reads=[X1_b], writes=[u_scr_b])
                for s8 in range(8):
                    K.dma(sp, Uf[s8 * 16:(s8 + 1) * 16, :, :],
                          u_scr.rearrange("(g q) (s sub) -> s q g sub", q=16, s=8)[s8], reads=[u_scr_b], writes=[UY_b])
            for tg in range(4):
                b = next_ps()
                for k in range(8):
                    K.op(pe, lambda: nc.tensor.matmul(psum[:, b, :], wi[:, k, m * 128:(m + 1) * 128],
                                                      hT[k // 4][:, k % 4, tg * 512:(tg + 1) * 512],
                                                      start=(k == 0), stop=(k == 7)),
                         reads=[wi_b, hT_b[k // 4]], writes=[ps_b[b]], inc=(k == 7))
                if m < 4:
                    K.op(act, lambda: nc.scalar.copy(
                        out=X1[:, m, :].rearrange("p (s sub) -> p sub s", s=8)[:, tg * 64:(tg + 1) * 64, :],
                        in_=psum[:, b, :].rearrange("p (sub s) -> p sub s", s=8)), reads=[ps_b[b]], writes=[X1_b])
                elif m < 8:
                    K.op(dve, lambda: V.tensor_copy(out=qT[:, m - 4, tg * 512:(tg + 1) * 512], in_=psum[:, b, :]),
                         reads=[ps_b[b]], writes=[qT_b])
                else:
                    K.op(act, lambda: nc.scalar.copy(out=kT[:, tg * 512:(tg + 1) * 512], in_=psum[:, b, :]),
                         reads=[ps_b[b]], writes=[kT_b])
        for tt in range(NT):
            b = next_ps()
            for k in range(8):
                K.op(pe, lambda: nc.tensor.matmul(psum[:, b, 0:128], hT[k // 4][:, k % 4, tt * 128:(tt + 1) * 128],
                                                  wi[:, k, 1152:1280], start=(k == 0), stop=(k == 7)),
                     reads=[wi_b, hT_b[k // 4]], writes=[ps_b[b]], inc=(k == 7))
            K.op(dve, lambda: V.tensor_copy(out=vaug[:, tt, :, 0:64], in_=psum[:, b, 0:128].rearrange("p (k d) -> p k d", k=2)),
                 reads=[ps_b[b]], writes=[va_b])
        if s == 0 and "uT" in dbg:
            K.dma(sp, dbg["uT"], X1[:], reads=[X1_b])
            K.dma(sp, dbg["qT"], qT[:], reads=[qT_b])
            K.dma(sp, dbg["kT"], kT[:], reads=[kT_b])
            K.dma(sp, dbg["vaug"], vaug[:], reads=[va_b])

        e_s5 = ExitStack()
        H = sb(e_s5, "H", (128, 2, NG, NCH + 1))
        Ein = sb(e_s5, "Ein", (128, 2, NG, NCH), BF16)
        st1 = sb(e_s5, "st1", (128, 2, NG))
        st2 = sb(e_s5, "st2", (128, 2, NG))
        mAb = [sb(e_s5, f"mAb{i}", (128, MA_COLS), BF16) for i in range(2)]
        mBb = [sb(e_s5, f"mBb{i}", (128, MB_COLS), BF16) for i in range(2)]
        e_att = ExitStack()
        et = [sb(e_att, f"et{i}", (128, 512), BF16) for i in range(3)]
        pt = [sb(e_att, f"pt{i}", (128, 512), BF16) for i in range(6)]
        oa = [sb(e_att, f"oa{i}", (128, 512)) for i in range(2)]
        den = [sb(e_att, f"den{i}", (128, 4)) for i in range(2)]

        if l == 0 and s == 0:
            print("mixer sbuf bytes remaining", nc.sbuf_bytes_remaining)

        def att_block(qb):
            kbs = [kb for kb in (qb - 1, qb, qb + 1) if 0 <= kb < NT]
            for kv in range(2):
                rows = slice(kv * 64, (kv + 1) * 64)
                pts = []
                for kb in kbs:
                    ie = K_rr(K, "et", 3)
                    ip = K_rr(K, "pt", 6)
                    b = next_ps()
                    K.op(pe, lambda: nc.tensor.matmul(psum[:, b, :], kT[rows, kb * 128:(kb + 1) * 128],
                                                      qT[rows, :, qb * 128:(qb + 1) * 128], start=True, stop=True),
                         reads=[kT_b, qT_b], writes=[ps_b[b]])
                    K.op(act, lambda: nc.scalar.activation(out=et[ie][:], in_=psum[:, b, :], func=AF.Exp, scale=0.125),
                         reads=[ps_b[b]], writes=[et_b[ie]])
                    K.op(dve, lambda: V.tensor_tensor(out=pt[ip][:], in0=et[ie][:],
                                                      in1=eb[:, kb - qb + 1, kv].rearrange("p c q -> p (c q)"), op=ALU.mult),
                         reads=[et_b[ie], ebB], writes=[pt_b[ip]])
                    pts.append((ip, kb))
                b = next_ps()
                for c in range(4):
                    for n_, (ip, kb) in enumerate(pts):
                        K.op(pe, lambda: nc.tensor.matmul(psum[:, b, c * 65:(c + 1) * 65], pt[ip][:, c * 128:(c + 1) * 128],
                                                          vaug[:, kb, kv, 0:65], start=(n_ == 0), stop=(n_ == len(pts) - 1)),
                             reads=[pt_b[ip], va_b], writes=[ps_b[b]], inc=(c == 3 and n_ == len(pts) - 1))
                io = qb % 2
                ov = psum[:, b, 0:260].rearrange("p (c d) -> p c d", c=4)
                K.op(dve, lambda: V.tensor_tensor(out=den[kv][:], in0=ov[:, :, 64], in1=esink[:, kv * 4:(kv + 1) * 4], op=ALU.add),
                     reads=[ps_b[b], pb], writes=[den_b[kv]])
                K.op(dve, lambda: V.reciprocal(out=den[kv][:], in_=den[kv][:]), reads=[den_b[kv]], writes=[den_b[kv]])
                K.op(dve, lambda: V.tensor_tensor(out=oa[io][:, kv * 256:(kv + 1) * 256].rearrange("p (c d) -> p c d", c=4),
                                                  in0=ov[:, :, 0:64], in1=den[kv][:].unsqueeze(2).to_broadcast([128, 4, 64]),
                                                  op=ALU.mult), reads=[ps_b[b], den_b[kv]], writes=[oa_b[io]])
            K.op(act, lambda: nc.scalar.activation(out=junk[:, 0:512], in_=oa[io][:], func=AF.Square, accum_out=ssq[:, 2, qb:qb + 1]),
                 reads=[oa_b[io]], writes=[junk_b, ssq_b[2]])
            b = next_ps()
            for c in range(4):
                K.op(pe, lambda: nc.tensor.transpose(psum[:, b, c * 128:(c + 1) * 128], oa[io][:, c * 128:(c + 1) * 128], ident[:]),
                     reads=[oa_b[io], cb], writes=[ps_b[b]], inc=(c == 3))
            K.op(act, lambda: nc.scalar.copy(out=yaT[:, :, qb * 128:(qb + 1) * 128], in_=psum[:, b, :].rearrange("p (c t) -> p c t", c=4)),
                 reads=[ps_b[b]], writes=[yaT_b])


        def scan_step(c):
            K.op(dve, lambda: V.tensor_tensor(out=st1[:], in0=H[:, :, :, c], in1=aL[:, 0], op=ALU.mult), reads=[H_b, aL_b, Hf_b], writes=[Hf_b])
            K.op(dve, lambda: V.tensor_tensor(out=H[:, :, :, c + 1], in0=H[:, :, :, c + 1], in1=st1[:], op=ALU.add), reads=[H_b, aL_b, Hf_b], writes=[Hf_b])
            K.op(dve, lambda: V.tensor_tensor(out=st1[:], in0=H[:, ::-1, :, c], in1=aL[:, 1], op=ALU.mult), reads=[H_b, aL_b, Hf_b], writes=[Hf_b])
            K.op(dve, lambda: V.tensor_tensor(out=H[:, :, :, c + 1], in0=H[:, :, :, c + 1], in1=st1[:], op=ALU.add), reads=[H_b, aL_b, Hf_b], writes=[Hf_b])

        K.op(dve, lambda: V.memset(H[:, :, :, 0:1], 0.0), writes=[H_b, Hf_b, Hb_b])

        def mA_load(g):
            K.dma(sp, mAb[g % 2][:], matsA[g], reads=[matsA_b[g]], writes=[mAb_b[g % 2]])

        def passA_batch(g0):
            b = next_ps()
            for gl in range(4):
                g = g0 + gl
                im = g % 2
                for ri in range(2):
                    for j in range(JC):
                        K.op(pe, lambda: nc.tensor.matmul(
                            psum[:, b, (ri * 4 + gl) * NCH:(ri * 4 + gl + 1) * NCH],
                            mAb[im][:, (ri * JC + j) * 128:(ri * JC + j + 1) * 128],
                            Uf[:, g, :].rearrange("p (c j) -> p c j", j=JC)[:, :, j],
                            start=(j == 0), stop=(j == JC - 1)),
                            reads=[mAb_b[im], UY_b], writes=[ps_b[b]], inc=(ri == 1 and j == JC - 1))
                if g + 2 < G:
                    mA_load(g + 2)
            pv = psum[:, b, :].rearrange("p (r g c) -> p r g c", r=2, g=4)
            K.op(act, lambda: nc.scalar.copy(out=H[0:64, :, g0:g0 + 4, 1:NCH + 1], in_=pv[0:64]), reads=[ps_b[b]], writes=[H_b, Hf_b, Hb_b])
            K.op(act, lambda: nc.scalar.copy(out=H[64:128, :, g0:g0 + 4, 1:NCH + 1][:, :, :, ::-1], in_=pv[64:128]), reads=[ps_b[b]], writes=[H_b, Hf_b, Hb_b])

        assert NCH % (NT // 2) == 0 and G // 4 == NT // 2
        mA_load(0)
        mA_load(1)
        for qb in range(NT // 2):
            att_block(qb)
            passA_batch(4 * qb)
        spq = NCH // (NT // 2)
        for qb in range(NT // 2, NT):
            att_block(qb)
            for c in range((qb - NT // 2) * spq, (qb - NT // 2 + 1) * spq):
                scan_step(c)
        K.op(dve, lambda: V.tensor_copy(out=Ein[0:64], in_=H[0:64, :, :, 0:NCH]), reads=[H_b, Hf_b], writes=[Ein_b])
        K.op(act, lambda: nc.scalar.copy(out=Ein[64:128], in_=H[64:128, :, :, 0:NCH][:, :, :, ::-1]), reads=[H_b, Hf_b], writes=[Ein_b])
        def mB_load(g):
            K.dma(sp, mBb[g % 2][:], matsB[g], reads=[matsB_b[g]], writes=[mBb_b[g % 2]])
        mB_load(0)
        mB_load(1)
        for g0 in range(0, G, 2):
            b = next_ps()
            K.op(dve, lambda: V.memset(psum[:, b, :], 0.0), writes=[ps_b[b]])
            for gl in range(2):
                g = g0 + gl
                im = g % 2
                Ug = Uf[:, g, :].rearrange("p (c j) -> p c j", j=JC)
                Yg = psum[:, b, gl * 256:(gl + 1) * 256].rearrange("p (c j) -> p c j", j=JC)
                mm = []
                for d in range(JC):
                    mm.append((Yg[:, :, d:JC], mBb[im][:, d * 128:(d + 1) * 128], Ug[:, :, 0:JC - d], [UY_b]))
                    mm.append((Yg[:, :, 0:JC - d], mBb[im][:, (JC + d) * 128:(JC + d + 1) * 128], Ug[:, :, d:JC], [UY_b]))
                for j in range(JC):
                    mm.append((Yg[:, :, j], mBb[im][:, (2 * JC + j) * 128:(2 * JC + j + 1) * 128], Ein[:, 0, g, :], [Ein_b]))
                    mm.append((Yg[:, :, j], mBb[im][:, (3 * JC + j) * 128:(3 * JC + j + 1) * 128], Ein[:, 1, g, :], [Ein_b]))
                for n_, (o_, l_, r_, rd_) in enumerate(mm):
                    K.op(pe, lambda: nc.tensor.matmul(o_, l_, r_, start=False, stop=(n_ == len(mm) - 1), skip_group_check=True),
                         reads=[mBb_b[im]] + rd_, writes=[ps_b[b]], inc=(n_ == len(mm) - 1))
                if g + 2 < G:
                    mB_load(g + 2)
            K.op(act, lambda: nc.scalar.activation(out=Zf[:, g0:g0 + 2, :], in_=psum[:, b, :].rearrange("p (g s) -> p g s", g=2),
                                                   func=AF.Gelu_apprx_tanh), reads=[ps_b[b]], writes=[hTlo_b])
        for s8 in range(8):
            K.dma(sp, z_scr.rearrange("(g q) (s sub) -> s q g sub", q=16, s=8)[s8], Zf[s8 * 16:(s8 + 1) * 16, :, :],
                  reads=[hTlo_b], writes=[z_scr_b])
        K.dma(sp, X1[:], z_scr.rearrange("(c r) t -> r c t", c=4), reads=[z_scr_b], writes=[X1_b])
        K.barrier()
        e_att.close()
        gate = [sb(e_s5, f"gate{i}", (128, 512)) for i in range(2)]
        for m in range(4):
            for tg in range(4):
                b = next_ps()
                ig = (m * 4 + tg) % 2
                for k in range(4):
                    K.op(pe, lambda: nc.tensor.matmul(psum[:, b, :], wg[:, k, m * 128:(m + 1) * 128], X1[:, k, tg * 512:(tg + 1) * 512],
                                                      start=(k == 0), stop=(k == 3)),
                         reads=[wg_b, X1_b], writes=[ps_b[b]], inc=(k == 3))
                K.op(act, lambda: nc.scalar.activation(out=gate[ig][:], in_=psum[:, b, :], func=AF.Sigmoid, bias=bglu[:, m:m + 1]),
                     reads=[ps_b[b], pb], writes=[gate_b[ig]])
                nat = lambda tns: tns[:, m, :].rearrange("p (sub s) -> p s sub", s=8)[:, 2 * tg:2 * tg + 2, :]
                K.op(dve, lambda: V.tensor_tensor(out=nat(ysT), in0=X1[:, m, tg * 512:(tg + 1) * 512].rearrange("p (s sub) -> p s sub", s=2),
                                                  in1=gate[ig][:].rearrange("p (s sub) -> p s sub", s=2), op=ALU.mult),
                     reads=[X1_b, gate_b[ig]], writes=[ysT_b])
        K.op(act, lambda: nc.scalar.activation(out=ysq[:], in_=ysT[:], func=AF.Square), reads=[ysT_b], writes=[ysq_b])

        K.barrier()
        e_s5.close()
        if s == 0 and "ysT" in dbg:
            K.dma(sp, dbg["ysT"], ysT[:], reads=[ysT_b])
            K.dma(sp, dbg["yaT"], yaT[:], reads=[yaT_b])
            K.dma(sp, dbg["zT"], X1[:], reads=[X1_b])
        e_m4 = ExitStack()
        mt = [sb(e_m4, f"mt{i}", (128, 1024)) for i in range(2)]
        b = next_ps()
        for tt in range(NT):
            for k in range(4):
                K.op(pe, lambda: nc.tensor.matmul(psum[:, b, tt:tt + 1], ysq[:, k, tt * 128:(tt + 1) * 128], ones_bf[:, 0:1],
                                                  start=(k == 0), stop=(k == 3)),
                     reads=[ysq_b, cb], writes=[ps_b[b]], inc=(k == 3 and tt == NT - 1))
        K.op(act, lambda: nc.scalar.activation(out=rs[:, 1, :], in_=psum[:, b, 0:NT], func=AF.Sqrt, bias=epsb[:, 0:1], scale=1.0 / SSM_W),
             reads=[ps_b[b], cb], writes=[rs_b[1]])
        K.op(dve, lambda: V.reciprocal(out=rs[:, 1, :], in_=rs[:, 1, :]), reads=[rs_b[1]], writes=[rs_b[1]])
        K.op(act, lambda: nc.scalar.activation(out=rs[:, 2, :], in_=ssq[:, 2, :], func=AF.Sqrt, bias=epsb[:, 0:1], scale=1.0 / SSM_W),
             reads=[ssq_b[2], cb], writes=[rs_b[2]])
        K.op(dve, lambda: V.reciprocal(out=rs[:, 2, :], in_=rs[:, 2, :]), reads=[rs_b[2]], writes=[rs_b[2]])
        for tt in range(NT):
            i = tt % 2
            r0 = tok0 + tt * 128
            ba = next_ps(2)
            for hf in range(2):
                for k in range(4):
                    K.op(pe, lambda: nc.tensor.matmul(psum[:, ba + hf, :], ysT[:, k, tt * 128:(tt + 1) * 128], wo[:, k, hf * 512:(hf + 1) * 512],
                                                      start=(k == 0), stop=(k == 3)),
                         reads=[ysT_b, wo_b], writes=[ps_b[ba + hf]], inc=(k == 3))
            bb_ = next_ps(2)
            for hf in range(2):
                for k in range(4):
                    K.op(pe, lambda: nc.tensor.matmul(psum[:, bb_ + hf, :], yaT[:, k, tt * 128:(tt + 1) * 128], wo[:, 4 + k, hf * 512:(hf + 1) * 512],
                                                      start=(k == 0), stop=(k == 3)),
                         reads=[yaT_b, wo_b], writes=[ps_b[bb_ + hf]], inc=(k == 3))
            K.op(act, lambda: nc.scalar.activation(out=mt[i][:], in_=psum[:, ba:ba + 2, :].rearrange("p a n -> p (a n)"), func=AF.Copy,
                                                   scale=rs[:, 1, tt:tt + 1]), reads=[ps_b[ba], ps_b[ba + 1], rs_b[1]], writes=[mt_b[i]])
            K.op(dve, lambda: V.scalar_tensor_tensor(out=mt[i][:], in0=psum[:, bb_:bb_ + 2, :].rearrange("p a n -> p (a n)"),
                                                     scalar=rs[:, 2, tt:tt + 1], in1=mt[i][:], op0=ALU.mult, op1=ALU.add),
                 reads=[ps_b[bb_], ps_b[bb_ + 1], rs_b[2], mt_b[i]], writes=[mt_b[i]])
            K.op(act, lambda: nc.scalar.activation(out=junk[:], in_=mt[i][:], func=AF.Square, accum_out=ssq[:, 3, tt:tt + 1]),
                 reads=[mt_b[i]], writes=[junk_b, ssq_b[3]])
            K.op(act, lambda: nc.scalar.activation(out=rs[:, 3, tt:tt + 1], in_=ssq[:, 3, tt:tt + 1], func=AF.Sqrt,
                                                   bias=epsb[:, 0:1], scale=1.0 / D_MODEL), reads=[ssq_b[3], cb], writes=[rs_b[3]])
            K.op(dve, lambda: V.reciprocal(out=rs[:, 3, tt:tt + 1], in_=rs[:, 3, tt:tt + 1]), reads=[rs_b[3]], writes=[rs_b[3]])
            K.op(dve, lambda: V.tensor_tensor(out=mt[i][:], in0=mt[i][:], in1=gpost[:], op=ALU.mult),
                 reads=[mt_b[i], pb], writes=[mt_b[i]])
            rd = [out_b[s][tt]] if x_src is out else []
            K.dma(sp, xt[i][:], x_src[r0:r0 + 128, :], reads=rd, writes=[xt_b[i]])
            K.op(dve, lambda: V.scalar_tensor_tensor(out=xo[i][:], in0=mt[i][:], scalar=rs[:, 3, tt:tt + 1], in1=xt[i][:],
                                                     op0=ALU.mult, op1=ALU.add),
                 reads=[mt_b[i], rs_b[3], xt_b[i]], writes=[xo_b[i]])
            K.dma(sp, out[r0:r0 + 128, :], xo[i][:], reads=[xo_b[i]], writes=[out_b[s][tt]])
        K.barrier()
        e_m4.close()


_RR = {}


def K_rr(K, key, n):
    v = _RR.get(key, 0)
    _RR[key] = (v + 1) % n
    return v


def ffn_phase(nc, K, es, sb, psum, ps_b, next_ps, P, l, nseq, out, out_b, ident, identb, epsb, cb, dbg):
    pe, act, dve, pool, sp = K.pe, K.act, K.dve, K.pool, K.sp
    V = nc.vector
    wu = sb(es, "wu", (128, 8, 2 * D_FF), BF16)
    wd = sb(es, "wd", (128, NFB, 1024), BF16)
    gpf = sb(es, "gpf", (128, 8))
    cw = sb(es, "cw", (128, 4, 2 * NFB))
    gpost = sb(es, "gpost2", (128, 1024))
    wu_b, wd_b, pb = Buf(), Buf(), Buf()
    with ExitStack() as es1:
        ld = sb(es1, "ldf", (2 * NFB, 128))
        ld_b = Buf()
        LC = lambda dst, src, n: load_cols(nc, K, psum, ps_b, next_ps, ident, cb, ld, ld_b, dst, pb, src, n)
        LC(gpf[:], P["pre_ffn_norm"][l], 8)
        for j in range(3):
            LC(cw[:, j, :], P["conv_w"][l, j], 2 * NFB)
        LC(cw[:, 3, :], P["conv_b"][l], 2 * NFB)
        K.barrier()
    K.dma(sp, gpost[:], P["post_ffn_norm"][l].partition_broadcast(128), writes=[pb])
    with ExitStack() as es2:
        engs = [act, dve]
        stage = make_staging(es2, sb, "f")
        load_cast_weight(nc, K, stage, wu, wu_b, [P["w_up"][l, k * 128:(k + 1) * 128, :] for k in range(8)], 2 * D_FF,
                         gpf, pb, engs)
        load_cast_weight(nc, K, stage, wd, wd_b, [P["w_down"][l, k * 128:(k + 1) * 128, :] for k in range(NFB)], 1024,
                         None, pb, engs)
        K.barrier()

    HW = 1026
    h2T = sb(es, "h2T", (128, 8, HW), BF16)
    NPRE = 4
    gbuf = sb(es, "gbuf", (128, NFB - NPRE, 384), BF16)
    gpre = [sb(es, f"gpre{i}", (128, NPRE, 384), BF16) for i in range(2)]
    gbuf_b = [Buf() for _ in range(NFB - NPRE)]
    gpre_b = [[Buf() for _ in range(NPRE)] for _ in range(2)]

    def gslot(kpar, fb):
        if fb < NPRE:
            return gpre[kpar], fb, gpre_b[kpar][fb]
        return gbuf, fb - NPRE, gbuf_b[fb - NPRE]
    pending = []
    tgk = [0]
    halo = sb(es, "halo", (128, 8, 2), BF16)
    xt = [sb(es, f"fxt{i}", (128, 1024)) for i in range(3)]
    hs = xt
    junk = sb(es, "fjunk", (128, 1024), BF16)
    cv = [sb(es, f"cv{i}", (128, 384)) for i in range(3)]
    cg = [sb(es, f"cg{i}", (128, 384)) for i in range(3)]
    gg = [sb(es, f"gg{i}", (128, 384)) for i in range(3)]
    yt = [sb(es, f"yt{i}", (128, 1024)) for i in range(1)] * 2
    xo = xt
    st = sb(es, "fst", (128, 4))
    stf = sb(es, "fstf", (128, 2, 4))
    stf_b = [Buf() for _ in range(4)]
    std = sb(es, "fstd", (128, 2, 2))
    std_b = [Buf() for _ in range(2)]
    B = lambda n=1: [Buf() for _ in range(n)]
    h2T_b, g_b, junk_b, st_b = Buf(), Buf(), Buf(), Buf()
    xt_b, cv_b, cg_b, gg_b = B(3), B(3), B(3), B(3)
    yt_b = B(1) * 2
    hs_b = xt_b
    xo_b = xt_b
    def emit_down(s, tok0, hs0, t0, ln, kpar):
        for ti in range(ln // 128):
            tt = (hs0 + t0) // 128 + ti
            r0 = tok0 + tt * 128
            i = tt % 2
            b = next_ps(2)
            for hf in range(2):
                for fb in range(NFB):
                    gt_, gi_, gb_ = gslot(kpar, fb)
                    K.op(pe, lambda: nc.tensor.matmul(psum[:, b + hf, :], gt_[:, gi_, ti * 128:(ti + 1) * 128],
                                                      wd[:, fb, hf * 512:(hf + 1) * 512], start=(fb == 0), stop=(fb == NFB - 1)),
                         reads=[gb_, wd_b], writes=[ps_b[b + hf]], inc=(fb == NFB - 1))
            pv = psum[:, b:b + 2, :].rearrange("p a n -> p (a n)")
            K.op(act, lambda: nc.scalar.activation(out=junk[:], in_=pv, func=AF.Square, accum_out=std[:, 0, i:i + 1]),
                 reads=[ps_b[b], ps_b[b + 1]], writes=[junk_b, std_b[i]])
            K.op(act, lambda: nc.scalar.activation(out=std[:, 1, i:i + 1], in_=std[:, 0, i:i + 1], func=AF.Sqrt, bias=epsb[:, 0:1], scale=1.0 / D_MODEL),
                 reads=[std_b[i], cb], writes=[std_b[i]])
            K.op(dve, lambda: V.reciprocal(out=std[:, 1, i:i + 1], in_=std[:, 1, i:i + 1]), reads=[std_b[i]], writes=[std_b[i]])
            K.op(dve, lambda: V.tensor_tensor(out=yt[i][:], in0=pv, in1=gpost[:], op=ALU.mult),
                 reads=[ps_b[b], ps_b[b + 1], pb], writes=[yt_b[i]])
            K.dma(sp, xt[i][:], out[r0:r0 + 128, :], reads=[out_b[s][tt]], writes=[xt_b[i]])
            K.op(dve, lambda: V.scalar_tensor_tensor(out=xo[i][:], in0=yt[i][:], scalar=std[:, 1, i:i + 1], in1=xt[i][:],
                                                     op0=ALU.mult, op1=ALU.add),
                 reads=[yt_b[i], std_b[i], xt_b[i]], writes=[xo_b[i]])
            K.dma(sp, out[r0:r0 + 128, :], xo[i][:], reads=[xo_b[i]], writes=[out_b[s][tt]])

    it = [0]
    if l == 0:
        print("ffn sbuf bytes remaining", nc.sbuf_bytes_remaining)

    for s in range(nseq):
        tok0 = s * SEQ
        for half in range(2):
            hs0 = half * 1024
            t_lo = hs0 // 128 - 1
            if half == 1:
                K.op(dve, lambda: V.tensor_copy(out=h2T[:, :, 0:1], in_=halo[:, :, 0:1]), reads=[h2T_b], writes=[h2T_b])
            tiles = [tt for tt in range(t_lo, t_lo + 10) if not (tt < 0 or tt >= NT or (half == 1 and tt == t_lo))]

            def fill_a(tt):
                i = it[0] % 3
                js = it[0] % 4
                it[0] += 1
                r0 = tok0 + tt * 128
                K.dma(sp, xt[i][:], out[r0:r0 + 128, :], reads=[out_b[s][tt]], writes=[xt_b[i]])
                K.op(act, lambda: nc.scalar.activation(out=junk[:], in_=xt[i][:], func=AF.Square, accum_out=stf[:, 0, js:js + 1]),
                     reads=[xt_b[i]], writes=[junk_b, stf_b[js]])
                K.op(act, lambda: nc.scalar.activation(out=stf[:, 1, js:js + 1], in_=stf[:, 0, js:js + 1], func=AF.Sqrt, bias=epsb[:, 0:1], scale=1.0 / D_MODEL),
                     reads=[stf_b[js], cb], writes=[stf_b[js]])
                K.op(dve, lambda: V.reciprocal(out=stf[:, 1, js:js + 1], in_=stf[:, 1, js:js + 1]), reads=[stf_b[js]], writes=[stf_b[js]])
                K.op(dve, lambda: V.tensor_scalar(out=hs[i][:], in0=xt[i][:], scalar1=stf[:, 1, js:js + 1], scalar2=None, op0=ALU.mult),
                     reads=[xt_b[i], stf_b[js]], writes=[hs_b[i]])
                return i

            def fill_b(tt, i):
                b = next_ps(2)
                for c in range(8):
                    K.op(pe, lambda: nc.tensor.transpose(psum[:, b + c // 4, (c % 4) * 128:(c % 4 + 1) * 128],
                                                         hs[i][:, c * 128:(c + 1) * 128], ident[:]),
                         reads=[hs_b[i], cb], writes=[ps_b[b], ps_b[b + 1]], inc=(c == 7))
                j0 = tt * 128 - hs0 + 1
                lo, hi = max(j0, 0), min(j0 + 128, HW)
                for hh in range(2):
                    src = psum[:, b + hh, :].rearrange("p (c t) -> p c t", c=4)[:, :, lo - j0:hi - j0]
                    dst = h2T[:, hh * 4:(hh + 1) * 4, lo:hi]
                    if hh == 0:
                        K.op(dve, lambda: V.tensor_copy(out=dst, in_=src), reads=[ps_b[b + hh]], writes=[h2T_b])
                    else:
                        K.op(act, lambda: nc.scalar.copy(out=dst, in_=src), reads=[ps_b[b + hh]], writes=[h2T_b])

            ia = fill_a(tiles[0])
            for n_, tt in enumerate(tiles):
                ia_next = fill_a(tiles[n_ + 1]) if n_ + 1 < len(tiles) else None
                fill_b(tt, ia)
                ia = ia_next
            if half == 0:
                K.op(dve, lambda: V.tensor_copy(out=halo[:, :, 0:1], in_=h2T[:, :, 1024:1025]), reads=[h2T_b], writes=[h2T_b])
            for (t0, ln) in ((0, 384), (384, 384), (768, 256)):
                g_first = (hs0 + t0 == 0)
                g_last = (hs0 + t0 + ln == SEQ)
                lo = 1 if g_first else 0
                hi = 1 if g_last else 0
                w0c, w1c = t0 + lo, t0 + ln + 2 - hi
                nW = w1c - w0c
                ctr = 1 - lo
                kpar = tgk[0] % 2
                tgk[0] += 1
                for fb in range(NFB):
                    i = fb % 3
                    if fb == NPRE:
                        for fn in pending:
                            fn()
                        pending.clear()
                    for vg in range(2):
                        rb = vg * NFB + fb
                        b = next_ps()
                        for k in range(8):
                            K.op(pe, lambda: nc.tensor.matmul(psum[:, b, 0:nW], wu[:, k, rb * 128:(rb + 1) * 128],
                                                              h2T[:, k, w0c:w1c], start=(k == 0), stop=(k == 7)),
                                 reads=[wu_b, h2T_b], writes=[ps_b[b]], inc=(k == 7))
                        dst, dst_b = (cv[i], cv_b[i]) if vg == 0 else (cg[i], cg_b[i])
                        K.op(act, lambda: nc.scalar.activation(out=dst[:, 0:ln], in_=psum[:, b, ctr:ctr + ln], func=AF.Identity,
                                                               bias=cw[:, 3, rb:rb + 1], scale=cw[:, 1, rb:rb + 1]),
                             reads=[ps_b[b], pb], writes=[dst_b])
                        K.op(dve, lambda: V.scalar_tensor_tensor(out=dst[:, lo:ln], in0=psum[:, b, ctr + lo - 1:ctr + ln - 1],
                                                                 scalar=cw[:, 0, rb:rb + 1], in1=dst[:, lo:ln], op0=ALU.mult, op1=ALU.add),
                             reads=[ps_b[b], pb, dst_b], writes=[dst_b])
                        K.op(dve, lambda: V.scalar_tensor_tensor(out=dst[:, 0:ln - hi], in0=psum[:, b, ctr + 1:ctr + ln - hi + 1],
                                                                 scalar=cw[:, 2, rb:rb + 1], in1=dst[:, 0:ln - hi], op0=ALU.mult, op1=ALU.add),
                             reads=[ps_b[b], pb, dst_b], writes=[dst_b])
                    K.op(act, lambda: nc.scalar.activation(out=gg[i][:, 0:ln], in_=cg[i][:, 0:ln], func=AF.Gelu_apprx_tanh),
                         reads=[cg_b[i]], writes=[gg_b[i]])
                    gt_, gi_, gb_ = gslot(kpar, fb)
                    K.op(pool, lambda: nc.gpsimd.tensor_tensor(out=gt_[:, gi_, 0:ln], in0=gg[i][:, 0:ln], in1=cv[i][:, 0:ln], op=ALU.mult),
                         reads=[gg_b[i], cv_b[i]], writes=[gb_])
                pending.append(lambda s=s, tok0=tok0, hs0=hs0, t0=t0, ln=ln, kpar=kpar: emit_down(s, tok0, hs0, t0, ln, kpar))
    for fn in pending:
        fn()
    pending.clear()
_CACHE = {}


def _q_perm():
    cols = list(range(512))
    for c in range(4):
        for h in (c, 4 + c):
            cols.extend(range(512 + h * 64, 512 + (h + 1) * 64))
    cols.extend(range(1024, 1280))
    return np.asarray(cols)


def kernel(**inputs):
    n_cores = 8
    x = np.ascontiguousarray(np.asarray(inputs["x"], dtype=np.float32))
    nseq = x.shape[0] // n_cores
    if "nc" not in _CACHE:
        _CACHE["nc"] = build(nseq=nseq)[0]
        _CACHE["consts"] = _host_consts()
    nc = _CACHE["nc"]
    consts = _CACHE["consts"]
    shared = {}
    for k, v in inputs.items():
        if k == "x":
            continue
        a = np.ascontiguousarray(np.asarray(v, dtype=np.float32))
        if k == "w_in":
            a = np.ascontiguousarray(a[:, :, _q_perm()])
        shared[k] = a
    shared.update(consts)
    in_maps = []
    for i in range(n_cores):
        m = dict(shared)
        m["x"] = x[i * nseq:(i + 1) * nseq].reshape(nseq * SEQ, D_MODEL)
        in_maps.append(m)
    res = run_bass_kernel_spmd(nc, in_maps, core_ids=list(range(n_cores)))
    outs = [np.asarray(r["out"]).reshape(nseq, SEQ, D_MODEL) for r in res.results]
    return np.concatenate(outs, axis=0).astype(np.float32)
```

```python
import math
from contextlib import ExitStack
import numpy as np
import jax
import jax.numpy as jnp
import concourse.bass as bass
import concourse.mybir as mybir
from concourse.bass_utils import run_bass_kernel_spmd

F32 = mybir.dt.float32
BF16 = mybir.dt.bfloat16
I32 = mybir.dt.int32
AF = mybir.ActivationFunctionType
ALU = mybir.AluOpType

D_MODEL = 1024
SEQ = 2048
DEPTH = 4
SSM_W = 512
NG = 32
NST = 64
D_FF = 2816
NFB = 22
IN_W = 1280
EPS = 1e-6
JC = 4
LCH = 8 * JC
NSUB = SEQ // 8
NCH = NSUB // JC
NT = SEQ // 128
TWO_PI = 2.0 * math.pi

EC_TB = 0
EC_TC = EC_TB + 8
EC_SB = EC_TC + 8 * JC
EC_CC = EC_SB + 8 * JC
EC_L = EC_CC + 8 * JC
NEC = EC_L + 1
ANG_SHIFT = 64

MA_COLS = 2 * JC * 128
MB_COLS = 4 * JC * 128


class Buf:
    __slots__ = ("w", "r", "name")

    def __init__(self, name=""):
        self.w = None
        self.r = {}
        self.name = name


class Eng:
    def __init__(self, K, name, eng, is_pe=False):
        self.K = K
        self.name = name
        self.eng = eng
        self.is_pe = is_pe
        self.sems = []
        self.ep = -1
        self.cnt = 0
        self.seen = {}
        self.pending = []
        self._new_epoch()

    def _new_epoch(self):
        self.ep += 1
        self.cnt = 0
        self.sems.append(self.K.nc.alloc_semaphore(f"s_{self.name}_{self.ep}"))

class Kern:
    def __init__(self, nc):
        self.nc = nc
        self.pe = Eng(self, "pe", nc.tensor, True)
        self.act = Eng(self, "act", nc.scalar)
        self.dve = Eng(self, "dve", nc.vector)
        self.pool = Eng(self, "pool", nc.gpsimd)
        self.sp = Eng(self, "sp", nc.sync)
        self.engs = [self.pe, self.act, self.dve, self.pool, self.sp]
        self.slots = {}
        for e, n in ((self.sp, 20), (self.pool, 6), (self.act, 6)):
            self.slots[e.name] = [[nc.alloc_semaphore(f"d_{e.name}_{i}"), 0, ("dma", e.name, i)] for i in range(n)]
        self.slot_i = {k: 0 for k in self.slots}
        self.n_ins = 0

    def _wait(self, E, tick):
        sem, val, key = tick
        if key[0] == E.name and E.is_pe:
            return
        if E.seen.get(key, 0) >= val:
            return
        if key[0] != "dma":
            for (k2, v2) in E.seen.items():
                if k2[0] == key[0] and k2[0] != "dma" and k2[1] > key[1]:
                    return
        E.eng.wait_ge(sem, val)
        E.seen[key] = val
        self.n_ins += 1

    def _deps(self, E, reads, writes):
        need = []
        for b in reads:
            if b.w is not None:
                need.append(b.w)
        for b in writes:
            if b.w is not None:
                need.append(b.w)
            need.extend(b.r.values())
        for t in need:
            self._wait(E, t)

    def op(self, E, fn, reads=(), writes=(), inc=True):
        if E.cnt >= 28000 and not E.pending:
            E._new_epoch()
        self._deps(E, reads, writes)
        tick = (E.sems[E.ep], E.cnt + 1, (E.name, E.ep))
        ins = fn()
        self.n_ins += 1
        if inc:
            ins.then_inc(E.sems[E.ep], 1)
            E.cnt += 1
            E.pending = []
        else:
            E.pending.append(1)
        for b in reads:
            b.r[E.name] = tick
        for b in writes:
            b.w = tick
            b.r = {}
        return ins

    def dma(self, E, out, in_, reads=(), writes=(), **kw):
        slots = self.slots[E.name]
        i = self.slot_i[E.name]
        self.slot_i[E.name] = (i + 1) % len(slots)
        s = slots[i]
        if s[1] > 0:
            self._wait(E, (s[0], s[1], s[2]))
        self._deps(E, reads, writes)
        ins = E.eng.dma_start(out=out, in_=in_, **kw)
        s[1] += 16
        ins.then_inc(s[0], 16)
        self.n_ins += 1
        tick = (s[0], s[1], s[2])
        for b in reads:
            b.r[("dma", E.name, i)] = tick
        for b in writes:
            b.w = tick
            b.r = {}
        return ins

    def barrier(self):
        sp = self.sp
        for E in self.engs:
            if E is sp:
                continue
            if E.cnt > 0:
                self._wait(sp, (E.sems[E.ep], E.cnt, (E.name, E.ep)))
        for lst in self.slots.values():
            for s in lst:
                if s[1] > 0:
                    self._wait(sp, (s[0], s[1], s[2]))
        self.op(sp, lambda: sp.eng.nop())
        t = (sp.sems[sp.ep], sp.cnt, (sp.name, sp.ep))
        for E in self.engs:
            if E is not sp:
                self._wait(E, t)


def _t5_bucket(rel):
    n_buckets, max_distance = 32, 128
    half = n_buckets // 2
    max_exact = half // 2
    ret = jnp.where(rel > 0, half, 0)
    n = jnp.abs(rel)
    nf = jnp.maximum(n, 1).astype(jnp.float32)
    large = max_exact + (jnp.log(nf / max_exact) / math.log(max_distance / max_exact)
                         * (half - max_exact)).astype(jnp.int32)
    large = jnp.minimum(large, half - 1)
    return ret + jnp.where(n < max_exact, n, large)


def _host_consts():
    import ml_dtypes
    c = {}
    c["ident_f"] = np.eye(128, dtype=np.float32)
    ex = np.zeros((2, NEC), np.float32)
    for s in range(8):
        ex[0, EC_TB + s] = -s
        ex[1, EC_TB + s] = s
    for d in range(JC):
        for s in range(8):
            ex[0, EC_TC + d * 8 + s] = 8 * d + s
            ex[1, EC_TC + d * 8 + s] = 8 * d - s
            ex[0, EC_SB + d * 8 + s] = LCH - 1 - 8 * d - s
            ex[1, EC_SB + d * 8 + s] = 8 * d + s
            ex[0, EC_CC + d * 8 + s] = 8 * d + s + 1
            ex[1, EC_CC + d * 8 + s] = LCH - 8 * d - s
    ex[:, EC_L] = LCH
    c["extab"] = np.repeat(ex, 64, axis=0).astype(np.float32)
    sp = np.arange(128)[:, None] // 16
    s = np.arange(128)[None, :] // 16
    tm = np.stack([(s >= sp), (sp >= s)], axis=1).astype(np.float32)
    c["tmask"] = np.ascontiguousarray(tm)
    k = np.arange(128)[:, None, None]
    off = (np.arange(3) - 1)[None, :, None]
    q = np.arange(128)[None, None, :]
    rel = (k + 128 * off - q).astype(np.int32)
    with jax.default_device(jax.devices("cpu")[0]):
        bk = np.asarray(_t5_bucket(jnp.asarray(rel)))
    oh = (bk[:, None, :, :] == np.arange(32)[None, :, None, None]).astype(np.float32)
    c["onehot"] = oh.reshape(128, 32, 384).astype(ml_dtypes.bfloat16)
    c["vmask"] = (np.abs(rel) <= 128).astype(np.float32).reshape(128, 384)
    return c


def build(nseq=4, nlayers=DEPTH, debug=None):
    nc = bass.Bass("TRN2", target_bir_lowering=False)
    K = Kern(nc)
    pe, act, dve, pool, sp = K.pe, K.act, K.dve, K.pool, K.sp
    NTOK = nseq * SEQ

    def din(name, shape, dt=F32):
        return nc.dram_tensor(name, list(shape), dt, kind="ExternalInput").ap()

    x_in = din("x", (NTOK, D_MODEL))
    P = {}
    P["rel_bias"] = din("rel_bias", (32, 8))
    P["pre_mix_norm"] = din("pre_mix_norm", (DEPTH, D_MODEL))
    P["w_in"] = din("w_in", (DEPTH, D_MODEL, IN_W))
    P["lam_re"] = din("lam_re", (DEPTH, 2, NG, NST))
    P["lam_im"] = din("lam_im", (DEPTH, 2, NG, NST))
    P["log_step"] = din("log_step", (DEPTH, 2, NG))
    P["b_re"] = din("b_re", (DEPTH, 2, NG, NST, 16))
    P["b_im"] = din("b_im", (DEPTH, 2, NG, NST, 16))
    P["c_re"] = din("c_re", (DEPTH, 2, NG, 16, NST))
    P["c_im"] = din("c_im", (DEPTH, 2, NG, 16, NST))
    P["ssm_d"] = din("ssm_d", (DEPTH, SSM_W))
    P["w_glu"] = din("w_glu", (DEPTH, SSM_W, SSM_W))
    P["b_glu"] = din("b_glu", (DEPTH, SSM_W))
    P["attn_sink"] = din("attn_sink", (DEPTH, 8))
    P["ssm_out_norm"] = din("ssm_out_norm", (DEPTH, SSM_W))
    P["attn_out_norm"] = din("attn_out_norm", (DEPTH, SSM_W))
    P["w_out"] = din("w_out", (DEPTH, D_MODEL, D_MODEL))
    P["post_mix_norm"] = din("post_mix_norm", (DEPTH, D_MODEL))
    P["pre_ffn_norm"] = din("pre_ffn_norm", (DEPTH, D_MODEL))
    P["w_up"] = din("w_up", (DEPTH, D_MODEL, 2 * D_FF))
    P["conv_w"] = din("conv_w", (DEPTH, 3, 2 * D_FF))
    P["conv_b"] = din("conv_b", (DEPTH, 2 * D_FF))
    P["w_down"] = din("w_down", (DEPTH, D_FF, D_MODEL))
    P["post_ffn_norm"] = din("post_ffn_norm", (DEPTH, D_MODEL))
    c_ident = din("ident_f", (128, 128))
    c_extab = din("extab", (128, NEC))
    c_tmask = din("tmask", (128, 2, 128))
    c_onehot = din("onehot", (128, 32, 384), BF16)
    c_vmask = din("vmask", (128, 384))

    out = nc.dram_tensor("out", [NTOK, D_MODEL], F32, kind="ExternalOutput").ap()
    matsA = nc.dram_tensor("matsA", [NG, 128, MA_COLS], BF16).ap()
    matsB = nc.dram_tensor("matsB", [NG, 128, MB_COLS], BF16).ap()
    u_scr = nc.dram_tensor("u_scr", [NG * 16, SEQ], BF16).ap()
    z_scr = nc.dram_tensor("z_scr", [NG * 16, SEQ], BF16).ap()
    eb_scr = nc.dram_tensor("eb_scr", [128, 3 * 2 * 4 * 128], BF16).ap()
    matsA_b = [Buf() for _ in range(NG)]
    matsB_b = [Buf() for _ in range(NG)]
    u_scr_b, z_scr_b = Buf(), Buf()
    out_b = [[Buf() for _ in range(NT)] for _ in range(nseq)]
    dbg = {}
    if debug:
        for nm, (shp, dt_) in debug.items():
          if shp is not None:
            dbg[nm] = nc.dram_tensor("dbg_" + nm, list(shp), dt_, kind="ExternalOutput").ap()

    top = ExitStack()
    with top:
        uid = [0]

        def sb(es, name, shape, dt=F32):
            uid[0] += 1
            return es.enter_context(nc.sbuf_tensor(f"{name}_{uid[0]}", list(shape), dt))

        psum = top.enter_context(nc.psum_tensor("psum", [128, 8, 512], F32))
        ps_b = [Buf(f"ps{i}") for i in range(8)]
        ps_rr = [0]

        def next_ps(n=1):
            if n == 1:
                i = ps_rr[0] % 8
                ps_rr[0] += 1
                return i
            i = ps_rr[0] % 8
            if i % 2:
                i = (i + 1) % 8
            ps_rr[0] = i + 2
            return i

        ident = sb(top, "ident", (128, 128))
        identb = sb(top, "identb", (128, 128), BF16)
        ones_bf = sb(top, "ones_bf", (128, 1), BF16)
        epsb = sb(top, "epsb", (128, 1))
        aL = sb(top, "aL", (128, 2, 2, 32))
        aL_b = Buf("aL")
        cb = Buf("consts")
        ebB = Buf("eb")
        K.dma(sp, ident[:], c_ident, writes=[cb])
        K.op(dve, lambda: nc.vector.tensor_copy(out=identb[:], in_=ident[:]), reads=[cb], writes=[cb])
        K.op(dve, lambda: nc.vector.memset(ones_bf[:], 1.0), writes=[cb])
        K.op(dve, lambda: nc.vector.memset(epsb[:], EPS), writes=[cb])

        with ExitStack() as es:
            oh = sb(es, "oh", (128, 32, 384), BF16)
            eb = sb(es, "eb0", (128, 3, 2, 4, 128), BF16)
            rb = sb(es, "rb", (128, 256))
            vm = sb(es, "vm", (128, 384))
            acc = sb(es, "acc", (128, 8, 384))
            tb = Buf()
            K.dma(sp, oh[:], c_onehot, writes=[tb])
            K.dma(sp, vm[:], c_vmask, writes=[tb])
            K.dma(sp, rb[:], P["rel_bias"].rearrange("b h -> (b h)").partition_broadcast(128), writes=[tb])
            accb = [Buf() for _ in range(8)]
            for h in range(8):
                E = dve
                K.op(E, lambda h=h: nc.vector.tensor_scalar(out=acc[:, h, :], in0=oh[:, 0, :], scalar1=rb[:, h:h + 1],
                                                            scalar2=None, op0=ALU.mult), reads=[tb], writes=[accb[h]])
                for b in range(1, 32):
                    K.op(E, lambda h=h, b=b: nc.vector.scalar_tensor_tensor(
                        out=acc[:, h, :], in0=oh[:, b, :], scalar=rb[:, b * 8 + h:b * 8 + h + 1], in1=acc[:, h, :],
                        op0=ALU.mult, op1=ALU.add), reads=[tb, accb[h]], writes=[accb[h]])
                K.op(act, lambda h=h: nc.scalar.activation(out=acc[:, h, :], in_=acc[:, h, :], func=AF.Exp),
                     reads=[accb[h]], writes=[accb[h]])
                kv, c = h // 4, h % 4
                K.op(dve, lambda h=h, kv=kv, c=c: nc.vector.tensor_tensor(
                    out=eb[:, :, kv, c, :], in0=acc[:, h, :].rearrange("p (o q) -> p o q", o=3),
                    in1=vm[:].rearrange("p (o q) -> p o q", o=3), op=ALU.mult), reads=[accb[h], tb], writes=[ebB])
            K.dma(sp, eb_scr, eb[:].rearrange("p o k c q -> p (o k c q)"), reads=[ebB], writes=[ebB])
            K.barrier()

        for l in range(nlayers):
            x_src = x_in if l == 0 else out
            with ExitStack() as es:
                s5_prologue(nc, K, es, sb, psum, ps_b, next_ps, P, l, ident, c_extab, c_tmask,
                            aL, aL_b, matsA, matsB, matsA_b, matsB_b, dbg)
                K.barrier()
            with ExitStack() as es:
                mixer_phase(nc, K, es, sb, psum, ps_b, next_ps, P, l, nseq, x_src, out, out_b, ident, identb, ones_bf,
                            eb_scr, ebB, epsb, cb, aL, aL_b, matsA, matsB, matsA_b, matsB_b, u_scr, z_scr, u_scr_b, z_scr_b, dbg)
                K.barrier()
            with ExitStack() as es:
              if not (debug and "skip_ffn" in debug):
                ffn_phase(nc, K, es, sb, psum, ps_b, next_ps, P, l, nseq, out, out_b, ident, identb, epsb, cb, dbg)
                K.barrier()
        K.barrier()
    return nc, K


def make_staging(es, sb, tag, n=6, ch=2048):
    return ([sb(es, f"stg_{tag}{i}", (128, ch)) for i in range(n)], [Buf() for _ in range(n)], [0])


def load_cast_weight(nc, K, stage, dst, dst_b, src_rows, ncols, gain, gain_b, engines):
    stg, stg_b, itr = stage
    NS = len(stg)
    CH = 2048
    dq = [K.sp, K.act]
    it = itr[0]
    for r, src in enumerate(src_rows):
        for c0 in range(0, ncols, CH):
            cw = min(CH, ncols - c0)
            i = it % NS
            E = engines[it % len(engines)]
            Q = dq[it % 2]
            it += 1
            K.dma(Q, stg[i][:, 0:cw], src[:, c0:c0 + cw], writes=[stg_b[i]])
            if gain is None:
                if E is K.act:
                    K.op(E, lambda i=i, r=r, c0=c0, cw=cw: nc.scalar.copy(out=dst[:, r, c0:c0 + cw], in_=stg[i][:, 0:cw]),
                         reads=[stg_b[i]], writes=[dst_b])
                else:
                    K.op(E, lambda i=i, r=r, c0=c0, cw=cw, E=E: E.eng.tensor_copy(out=dst[:, r, c0:c0 + cw], in_=stg[i][:, 0:cw]),
                         reads=[stg_b[i]], writes=[dst_b])
            else:
                if E is K.act:
                    K.op(E, lambda i=i, r=r, c0=c0, cw=cw: nc.scalar.activation(
                        out=dst[:, r, c0:c0 + cw], in_=stg[i][:, 0:cw], func=AF.Copy, scale=gain[:, r:r + 1]),
                        reads=[stg_b[i], gain_b], writes=[dst_b])
                else:
                    K.op(E, lambda i=i, r=r, c0=c0, cw=cw, E=E: E.eng.tensor_scalar(
                        out=dst[:, r, c0:c0 + cw], in0=stg[i][:, 0:cw], scalar1=gain[:, r:r + 1], scalar2=None,
                        op0=ALU.mult), reads=[stg_b[i], gain_b], writes=[dst_b])
    itr[0] = it


def load_cols(nc, K, psum, ps_b, next_ps, ident, cb, ld, ld_b, dst, dst_b, src1d, n):
    K.dma(K.sp, ld[0:n, :], src1d.rearrange("(r p) -> r p", p=128), writes=[ld_b])
    b = next_ps()
    K.op(K.pe, lambda: nc.tensor.transpose(psum[:, b, 0:n], ld[0:n, :], ident[0:n, 0:n]), reads=[ld_b, cb], writes=[ps_b[b]])
    K.op(K.dve, lambda: nc.vector.tensor_copy(out=dst, in_=psum[:, b, 0:n]), reads=[ps_b[b]], writes=[dst_b])


def rstd_from_ssq(nc, K, ssq, rs, n, width, epsb, bufs_r, bufs_w):
    K.op(K.act, lambda: nc.scalar.activation(out=rs[:, 0:n], in_=ssq[:, 0:n], func=AF.Sqrt, bias=epsb[:, 0:1],
                                             scale=1.0 / width), reads=bufs_r, writes=bufs_w)
    K.op(K.dve, lambda: nc.vector.reciprocal(out=rs[:, 0:n], in_=rs[:, 0:n]), reads=bufs_w, writes=bufs_w)


def s5_prologue(nc, K, es, sb, psum, ps_b, next_ps, P, l, ident, c_extab, c_tmask,
                aL, aL_b, matsA, matsB, matsA_b, matsB_b, dbg):
    pe, act, dve, pool, sp = K.pe, K.act, K.dve, K.pool, K.sp
    V = nc.vector
    G = NG
    tb = Buf("s5tab")

    def t(name, shape, dt=F32):
        return sb(es, name, shape, dt)

    lamld = t("lamld", (32, 2, 128))
    lam = t("lam", (128, 2, 32))
    ls = t("ls", (128, 32))
    braw = t("braw", (128, 2, 32, 16))
    cld = t("cld", (128, 2, 4, 128))
    craw = t("craw", (128, 2, 32, 16))
    dvec = t("dvec", (128, 32))
    extab = t("extab_sb", (128, NEC))
    tmask = t("tmask_sb", (128, 2, 128))
    K.dma(sp, extab[:], c_extab, writes=[tb])
    K.dma(sp, tmask[:], c_tmask, writes=[tb])
    with nc.allow_non_contiguous_dma(reason="tiny param loads"):
        for ri, nm in enumerate(("lam_re", "lam_im")):
            K.dma(sp, lamld[:, ri, :].rearrange("g (d n) -> g d n", d=2), P[nm][l].rearrange("d g n -> g d n"), writes=[tb])
        for d in range(2):
            K.dma(sp, ls[d * 64:(d + 1) * 64, :], P["log_step"][l, d].partition_broadcast(64), writes=[tb])
            for ri, nm in enumerate(("b_re", "b_im")):
                K.dma(sp, braw[d * 64:(d + 1) * 64, ri, :, :], P[nm][l, d].rearrange("g n q -> n g q"), writes=[tb])
        for ri, nm in enumerate(("c_re", "c_im")):
            for d in range(2):
                K.dma(sp, cld[:, ri, :, d * 64:(d + 1) * 64],
                      P[nm][l, d].rearrange("(t gl) p n -> (gl p) t n", t=4), writes=[tb])
        for s in range(8):
            K.dma(sp, dvec[s * 16:(s + 1) * 16, :], P["ssm_d"][l].rearrange("(g q) -> q g", q=16), writes=[tb])
    for ri in range(2):
        b = next_ps()
        K.op(pe, lambda ri=ri, b=b: nc.tensor.transpose(psum[:, b, 0:32], lamld[:, ri, :], ident[0:32, 0:32]),
             reads=[tb], writes=[ps_b[b]])
        K.op(dve, lambda ri=ri, b=b: V.tensor_copy(out=lam[:, ri, :], in_=psum[:, b, 0:32]), reads=[ps_b[b]], writes=[tb])
        b = next_ps()
        for tt in range(4):
            K.op(pe, lambda ri=ri, b=b, tt=tt: nc.tensor.transpose(psum[:, b, tt * 128:(tt + 1) * 128], cld[:, ri, tt, :], ident[:]),
                 reads=[tb], writes=[ps_b[b]], inc=(tt == 3))
        K.op(dve, lambda ri=ri, b=b: V.tensor_copy(out=craw[:, ri, :, :].rearrange("p g q -> p (g q)"), in_=psum[:, b, :]),
             reads=[ps_b[b]], writes=[tb])

    dt_ = t("dt", (128, 32))
    ar = t("ar", (128, 32))
    ang = t("ang", (128, 32))
    lb = t("lb", (128, 2, 32))
    coef = t("coef", (128, 2, 32))
    tmp = t("tmpa", (128, 4, 32))

    def dv(fn, inc=True):
        K.op(dve, fn, reads=[tb], writes=[tb], inc=inc)

    def ac(fn):
        K.op(act, fn, reads=[tb], writes=[tb])

    ac(lambda: nc.scalar.activation(out=dt_[:], in_=ls[:], func=AF.Exp))
    dv(lambda: V.tensor_tensor(out=ar[:], in0=lam[:, 0, :], in1=dt_[:], op=ALU.mult))
    dv(lambda: V.tensor_tensor(out=ang[:], in0=lam[:, 1, :], in1=dt_[:], op=ALU.mult))

    PR = t("PR", (128, 32, NEC))
    PI = t("PI", (128, 32, NEC))
    A1 = t("A1", (128, 32, NEC))
    A2 = t("A2", (128, 32, NEC))
    A3i = t("A3i", (128, 32, NEC), I32)
    A4 = t("A4", (128, 32, NEC))
    ex_b = extab[:].unsqueeze(1).to_broadcast([128, 32, NEC])

    def bc_g(x):
        return x.unsqueeze(2).to_broadcast([128, 32, NEC])

    dv(lambda: V.tensor_tensor(out=A1[:], in0=ex_b, in1=bc_g(ar[:]), op=ALU.mult))
    ac(lambda: nc.scalar.activation(out=A4[:], in_=A1[:], func=AF.Exp))
    dv(lambda: V.tensor_tensor(out=A1[:], in0=ex_b, in1=bc_g(ang[:]), op=ALU.mult))

    def sin_of(dst, shift):
        dv(lambda: V.tensor_scalar(out=A2[:], in0=A1[:], scalar1=shift + ANG_SHIFT * TWO_PI, scalar2=1.0 / TWO_PI,
                                   op0=ALU.add, op1=ALU.mult))
        dv(lambda: V.tensor_copy(out=A3i[:], in_=A2[:]))
        dv(lambda: V.tensor_copy(out=dst[:], in_=A3i[:]))
        dv(lambda: V.tensor_tensor(out=A2[:], in0=A2[:], in1=dst[:], op=ALU.subtract))
        dv(lambda: V.tensor_scalar(out=dst[:], in0=A2[:], scalar1=0.5, scalar2=None, op0=ALU.is_gt))
        dv(lambda: V.tensor_tensor(out=A2[:], in0=A2[:], in1=dst[:], op=ALU.subtract))
        dv(lambda: V.tensor_scalar(out=dst[:], in0=A2[:], scalar1=-0.5, scalar2=None, op0=ALU.is_lt))
        dv(lambda: V.tensor_tensor(out=A2[:], in0=A2[:], in1=dst[:], op=ALU.add))
        dv(lambda: V.tensor_scalar(out=A2[:], in0=A2[:], scalar1=TWO_PI, scalar2=3.14159, op0=ALU.mult, op1=ALU.min))
        dv(lambda: V.tensor_scalar(out=A2[:], in0=A2[:], scalar1=-3.14159, scalar2=None, op0=ALU.max))
        ac(lambda: nc.scalar.activation(out=dst[:], in_=A2[:], func=AF.Sin))

    sin_of(PI, 0.0)
    sin_of(PR, math.pi / 2)
    dv(lambda: V.tensor_tensor(out=PR[:], in0=PR[:], in1=A4[:], op=ALU.mult))
    dv(lambda: V.tensor_tensor(out=PI[:], in0=PI[:], in1=A4[:], op=ALU.mult))

    ac(lambda: nc.scalar.activation(out=tmp[:, 0, :], in_=ar[:], func=AF.Exp))
    dv(lambda: V.tensor_copy(out=lb[0:64, 0, :], in_=PR[0:64, :, EC_TC + 1]))
    dv(lambda: V.tensor_copy(out=lb[0:64, 1, :], in_=PI[0:64, :, EC_TC + 1]))
    dv(lambda: V.tensor_copy(out=lb[64:128, 0, :], in_=PR[64:128, :, EC_TB + 1]))
    dv(lambda: V.tensor_copy(out=lb[64:128, 1, :], in_=PI[64:128, :, EC_TB + 1]))
    dv(lambda: V.tensor_tensor(out=tmp[:, 0, :], in0=lam[:, 0, :], in1=lam[:, 0, :], op=ALU.mult))
    dv(lambda: V.tensor_tensor(out=tmp[:, 1, :], in0=lam[:, 1, :], in1=lam[:, 1, :], op=ALU.mult))
    dv(lambda: V.tensor_tensor(out=tmp[:, 0, :], in0=tmp[:, 0, :], in1=tmp[:, 1, :], op=ALU.add))
    dv(lambda: V.reciprocal(out=tmp[:, 0, :], in_=tmp[:, 0, :]))
    dv(lambda: V.tensor_scalar(out=tmp[:, 1, :], in0=lb[:, 0, :], scalar1=-1.0, scalar2=None, op0=ALU.add))
    dv(lambda: V.tensor_tensor(out=tmp[:, 2, :], in0=tmp[:, 1, :], in1=lam[:, 0, :], op=ALU.mult))
    dv(lambda: V.tensor_tensor(out=tmp[:, 3, :], in0=lb[:, 1, :], in1=lam[:, 1, :], op=ALU.mult))
    dv(lambda: V.tensor_tensor(out=tmp[:, 2, :], in0=tmp[:, 2, :], in1=tmp[:, 3, :], op=ALU.add))
    dv(lambda: V.tensor_tensor(out=coef[:, 0, :], in0=tmp[:, 2, :], in1=tmp[:, 0, :], op=ALU.mult))
    dv(lambda: V.tensor_tensor(out=tmp[:, 2, :], in0=lb[:, 1, :], in1=lam[:, 0, :], op=ALU.mult))
    dv(lambda: V.tensor_tensor(out=tmp[:, 3, :], in0=tmp[:, 1, :], in1=lam[:, 1, :], op=ALU.mult))
    dv(lambda: V.tensor_tensor(out=tmp[:, 2, :], in0=tmp[:, 2, :], in1=tmp[:, 3, :], op=ALU.subtract))
    dv(lambda: V.tensor_tensor(out=coef[:, 1, :], in0=tmp[:, 2, :], in1=tmp[:, 0, :], op=ALU.mult))
    bb = t("bb", (128, 2, 32, 16))
    t16 = t("t16", (128, 32, 16))

    def bq(x):
        return x.unsqueeze(2).to_broadcast([128, 32, 16])

    dv(lambda: V.tensor_tensor(out=bb[:, 0], in0=braw[:, 0], in1=bq(coef[:, 0, :]), op=ALU.mult))
    dv(lambda: V.tensor_tensor(out=t16[:], in0=braw[:, 1], in1=bq(coef[:, 1, :]), op=ALU.mult))
    dv(lambda: V.tensor_tensor(out=bb[:, 0], in0=bb[:, 0], in1=t16[:], op=ALU.subtract))
    dv(lambda: V.tensor_tensor(out=bb[:, 1], in0=braw[:, 1], in1=bq(coef[:, 0, :]), op=ALU.mult))
    dv(lambda: V.tensor_tensor(out=t16[:], in0=braw[:, 0], in1=bq(coef[:, 1, :]), op=ALU.mult))
    dv(lambda: V.tensor_tensor(out=bb[:, 1], in0=bb[:, 1], in1=t16[:], op=ALU.add))

    if "s5tab" in dbg:
        K.dma(sp, dbg["s5tab"][:, 0:NEC], PR[:, 0, :], reads=[tb])
        K.dma(sp, dbg["s5tab"][:, NEC:2 * NEC], PI[:, 0, :], reads=[tb])
        K.dma(sp, dbg["s5tab"][:, 2 * NEC:2 * NEC + 32], bb[:, 0, 0:2, :].rearrange("p g q -> p (g q)"), reads=[tb])
        K.dma(sp, dbg["s5tab"][:, 2 * NEC + 32:2 * NEC + 64], bb[:, 1, 0:2, :].rearrange("p g q -> p (g q)"), reads=[tb])

    K.op(dve, lambda: V.tensor_copy(out=aL[:, 0, 0, :], in_=PR[:, :, EC_L]), reads=[tb], writes=[aL_b])
    K.op(dve, lambda: V.tensor_copy(out=aL[:, 0, 1, :], in_=PR[:, :, EC_L]), reads=[tb], writes=[aL_b])
    K.op(dve, lambda: V.tensor_scalar(out=aL[:, 1, 0, :], in0=PI[:, :, EC_L], scalar1=-1.0, scalar2=None, op0=ALU.mult), reads=[tb], writes=[aL_b])
    K.op(dve, lambda: V.tensor_copy(out=aL[:, 1, 1, :], in_=PI[:, :, EC_L]), reads=[tb], writes=[aL_b])

    NB = 8 + 8 * JC
    NC8 = 8 * JC
    GB = 2
    xb = [t(f"xb{i}", (128, 2, GB, NB, 16)) for i in range(2)]
    xc = [t(f"xc{i}", (128, 2, GB, NC8, 16)) for i in range(2)]
    xcc = [t(f"xcc{i}", (128, 2, GB, NC8, 16)) for i in range(2)]
    tq = [t(f"tq{i}", (128, GB, NB, 16)) for i in range(2)]
    mA = [t(f"mA{i}", (128, MA_COLS), BF16) for i in range(2)]
    mB = [t(f"mB{i}", (128, MB_COLS), BF16) for i in range(2)]
    dmat = [t(f"dmat{i}", (128, 128)) for i in range(2)]
    t0m = [t(f"t0m{i}", (128, 128)) for i in range(2)]
    xb_b = [Buf() for _ in range(2)]
    xc_b = [Buf() for _ in range(2)]
    xcc_b = [Buf() for _ in range(2)]
    tq_b = [Buf() for _ in range(2)]
    mA_b = [Buf() for _ in range(2)]
    mB_b = [Buf() for _ in range(2)]
    dm_b = [Buf() for _ in range(2)]

    def pwB(x, g0, c0, n):
        return x[:, g0:g0 + GB, c0:c0 + n].unsqueeze(3).to_broadcast([128, GB, n, 16])

    def bbB(ri, g0, n):
        return bb[:, ri, g0:g0 + GB, :].unsqueeze(2).to_broadcast([128, GB, n, 16])

    def ccB(ri, g0, n):
        return craw[:, ri, g0:g0 + GB, :].unsqueeze(2).to_broadcast([128, GB, n, 16])

    for gb in range(G // GB):
        g0 = gb * GB
        i = gb % 2
        E2 = pool if (gb % 4 == 3) else dve
        EN = E2.eng

        def tt_(out, in0, in1, op, reads, writes):
            K.op(E2, lambda: EN.tensor_tensor(out=out, in0=in0, in1=in1, op=op), reads=reads, writes=writes)

        for (dst0, c0, n) in ((0, EC_TB, 8), (8, EC_SB, NC8)):
            o_re = xb[i][:, 0, :, dst0:dst0 + n, :]
            o_im = xb[i][:, 1, :, dst0:dst0 + n, :]
            t_ = tq[i][:, :, dst0:dst0 + n, :]
            tt_(o_re, pwB(PR, g0, c0, n), bbB(0, g0, n), ALU.mult, [tb], [xb_b[i]])
            tt_(t_, pwB(PI, g0, c0, n), bbB(1, g0, n), ALU.mult, [tb], [tq_b[i]])
            tt_(o_re, o_re, t_, ALU.subtract, [tq_b[i], xb_b[i]], [xb_b[i]])
            tt_(o_im, pwB(PR, g0, c0, n), bbB(1, g0, n), ALU.mult, [tb], [xb_b[i]])
            tt_(t_, pwB(PI, g0, c0, n), bbB(0, g0, n), ALU.mult, [tb, xb_b[i]], [tq_b[i]])
            tt_(o_im, o_im, t_, ALU.add, [tq_b[i], xb_b[i]], [xb_b[i]])
        for (dstt, dstb, c0) in ((xc, xc_b, EC_TC), (xcc, xcc_b, EC_CC)):
            o_re = dstt[i][:, 0]
            o_im = dstt[i][:, 1]
            t_ = tq[i][:, :, 0:NC8, :]
            tt_(o_re, pwB(PR, g0, c0, NC8), ccB(0, g0, NC8), ALU.mult, [tb], [dstb[i]])
            tt_(t_, pwB(PI, g0, c0, NC8), ccB(1, g0, NC8), ALU.mult, [tb], [tq_b[i]])
            tt_(o_re, o_re, t_, ALU.subtract, [tq_b[i], dstb[i]], [dstb[i]])
            tt_(o_im, pwB(PI, g0, c0, NC8), ccB(0, g0, NC8), ALU.mult, [tb], [dstb[i]])
            tt_(t_, pwB(PR, g0, c0, NC8), ccB(1, g0, NC8), ALU.mult, [tb, dstb[i]], [tq_b[i]])
            tt_(o_im, o_im, t_, ALU.add, [tq_b[i], dstb[i]], [dstb[i]])
            K.op(E2, lambda: EN.tensor_scalar(out=o_im, in0=o_im, scalar1=-1.0, scalar2=None, op0=ALU.mult),
                 reads=[dstb[i]], writes=[dstb[i]])

        for gl in range(GB):
            g = g0 + gl
            j2 = g % 2
            for d in range(2):
                b = next_ps()
                rows = slice(d * 64, (d + 1) * 64)
                for ri in range(2):
                    K.op(pe, lambda: nc.tensor.matmul(
                        psum[:, b, 0:JC * 128],
                        xb[i][rows, ri, gl, 0:8, :].rearrange("p s q -> p (s q)"),
                        xc[i][rows, ri, gl, :, :].rearrange("p c q -> p (c q)"),
                        start=(ri == 0), stop=(ri == 1)),
                        reads=[xb_b[i], xc_b[i]], writes=[ps_b[b]], inc=(ri == 1))
                base = d * JC * 128
                if d == 0:
                    K.op(dve, lambda: V.tensor_scalar(out=dmat[j2][:], in0=ident[:], scalar1=dvec[:, g:g + 1],
                                                      scalar2=None, op0=ALU.mult), reads=[tb], writes=[dm_b[j2]])
                    K.op(dve, lambda: V.tensor_tensor(out=t0m[j2][:], in0=psum[:, b, 0:128], in1=tmask[:, 0, :], op=ALU.mult),
                         reads=[ps_b[b], tb, dm_b[j2]], writes=[dm_b[j2]])
                    K.op(dve, lambda: V.tensor_tensor(out=mB[j2][:, base:base + 128], in0=t0m[j2][:], in1=dmat[j2][:], op=ALU.add),
                         reads=[dm_b[j2]], writes=[mB_b[j2]])
                else:
                    K.op(dve, lambda: V.tensor_tensor(out=mB[j2][:, base:base + 128], in0=psum[:, b, 0:128],
                                                      in1=tmask[:, 1, :], op=ALU.mult),
                         reads=[ps_b[b], tb], writes=[mB_b[j2]])
                if JC > 1:
                    K.op(act, lambda: nc.scalar.copy(out=mB[j2][:, base + 128:base + JC * 128], in_=psum[:, b, 128:JC * 128]),
                         reads=[ps_b[b]], writes=[mB_b[j2]])
            K.op(act, lambda: nc.scalar.copy(out=mB[j2][:, 2 * JC * 128:3 * JC * 128],
                                             in_=xcc[i][:, 0, gl].rearrange("p c q -> p (c q)")),
                 reads=[xcc_b[i]], writes=[mB_b[j2]])
            K.op(act, lambda: nc.scalar.copy(out=mB[j2][:, 3 * JC * 128:4 * JC * 128],
                                             in_=xcc[i][:, 1, gl].rearrange("p c q -> p (c q)")),
                 reads=[xcc_b[i]], writes=[mB_b[j2]])
            K.dma(sp, matsB[g], mB[j2][:], reads=[mB_b[j2]], writes=[matsB_b[g]])
            for ri in range(2):
                b = next_ps()
                for j in range(JC):
                    K.op(pe, lambda: nc.tensor.transpose(
                        psum[:, b, j * 128:(j + 1) * 128],
                        xb[i][:, ri, gl, 8 + 8 * j:16 + 8 * j, :].rearrange("p s q -> p (s q)"), ident[:]),
                        reads=[xb_b[i]], writes=[ps_b[b]], inc=(j == JC - 1))
                K.op(act, lambda: nc.scalar.copy(out=mA[j2][:, ri * JC * 128:(ri + 1) * JC * 128], in_=psum[:, b, 0:JC * 128]),
                     reads=[ps_b[b]], writes=[mA_b[j2]])
            K.dma(sp, matsA[g], mA[j2][:], reads=[mA_b[j2]], writes=[matsA_b[g]])
    if "matsB0" in dbg:
        pass


def mixer_phase(nc, K, es, sb, psum, ps_b, next_ps, P, l, nseq, x_src, out, out_b, ident, identb, ones_bf,
                eb_scr, ebB, epsb, cb, aL, aL_b, matsA, matsB, matsA_b, matsB_b, u_scr, z_scr, u_scr_b, z_scr_b, dbg):
    pe, act, dve, pool, sp = K.pe, K.act, K.dve, K.pool, K.sp
    V = nc.vector
    G = NG
    wi = sb(es, "wi", (128, 8, IN_W), BF16)
    wg = sb(es, "wg", (128, 4, 512), BF16)
    wo = sb(es, "wo", (128, 8, 1024), BF16)
    gpm = sb(es, "gpm", (128, 8))
    gso = sb(es, "gso", (128, 8))
    bglu = sb(es, "bglu", (128, 4))
    esink = sb(es, "esink", (128, 8))
    gpost = sb(es, "gpost", (128, 1024))
    wi_b, wg_b, wo_b, pb = Buf(), Buf(), Buf(), Buf()
    eb = sb(es, "eb", (128, 3, 2, 4, 128), BF16)
    K.dma(sp, eb[:].rearrange("p o k c q -> p (o k c q)"), eb_scr, reads=[ebB], writes=[pb])
    ebB = pb
    with ExitStack() as es1:
        ld = sb(es1, "ldm", (8, 128))
        ld_b = Buf()
        LC = lambda dst, src, n: load_cols(nc, K, psum, ps_b, next_ps, ident, cb, ld, ld_b, dst, pb, src, n)
        LC(gpm[:], P["pre_mix_norm"][l], 8)
        LC(gso[:, 0:4], P["ssm_out_norm"][l], 4)
        LC(gso[:, 4:8], P["attn_out_norm"][l], 4)
        LC(bglu[:], P["b_glu"][l], 4)
        K.barrier()
    K.dma(sp, esink[:], P["attn_sink"][l].partition_broadcast(128), writes=[pb])
    K.dma(sp, gpost[:], P["post_mix_norm"][l].partition_broadcast(128), writes=[pb])
    K.op(act, lambda: nc.scalar.activation(out=esink[:], in_=esink[:], func=AF.Exp), reads=[pb], writes=[pb])
    with ExitStack() as es2:
        engs = [act, dve]
        stage = make_staging(es2, sb, "m")
        load_cast_weight(nc, K, stage, wi, wi_b, [P["w_in"][l, k * 128:(k + 1) * 128, :] for k in range(8)], IN_W,
                         gpm, pb, engs)
        load_cast_weight(nc, K, stage, wg, wg_b, [P["w_glu"][l, k * 128:(k + 1) * 128, :] for k in range(4)], 512,
                         None, pb, engs)
        load_cast_weight(nc, K, stage, wo, wo_b, [P["w_out"][l, k * 128:(k + 1) * 128, :] for k in range(8)], 1024,
                         gso, pb, engs)
        K.barrier()

    X1 = sb(es, "X1", (128, 4, SEQ), BF16)
    hTlo = sb(es, "hTlo", (128, 4, SEQ), BF16)
    hThi = sb(es, "hThi", (128, 4, SEQ), BF16)
    qT = sb(es, "qT", (128, 4, SEQ), BF16)
    UY = sb(es, "UY", (128, 4, SEQ), BF16)
    kT = sb(es, "kT", (128, SEQ), BF16)
    vaug = sb(es, "vaug", (128, NT, 2, 66), BF16)
    xt = [sb(es, f"xt{i}", (128, 1024)) for i in range(3)]
    hs = xt
    xo = xt
    junk = sb(es, "junk", (128, 1024), BF16)
    ssq = sb(es, "ssq", (128, 4, NT))
    rs = sb(es, "rs", (128, 4, NT))
    B = lambda n=1: [Buf() for _ in range(n)]
    X1_b, hTlo_b, hThi_b, qT_b, UY_b, kT_b, va_b, H_b, Ein_b = (Buf() for _ in range(9))
    Hf_b, Hb_b = Buf(), Buf()
    xt_b, mAb_b, mBb_b, et_b, pt_b, oa_b, den_b, gate_b, mt_b = B(3), B(2), B(2), B(3), B(6), B(2), B(2), B(2), B(2)
    hs_b = xt_b
    xo_b = xt_b
    junk_b, ssq_b, rs_b, st_b = Buf(), [Buf() for _ in range(4)], [Buf() for _ in range(4)], [Buf(), Buf()]
    m1s_b = [(Buf(), Buf()) for _ in range(4)]
    K.op(dve, lambda: V.memset(vaug[:, :, :, 64:66], 1.0), writes=[va_b])
    hT_b = [hTlo_b, hThi_b]
    hT = [hTlo, hThi]
    Uf = UY[:].rearrange("p c t -> p (c t)").rearrange("p (g s) -> p g s", g=NG)
    Zf = hTlo[:].rearrange("p c t -> p (c t)").rearrange("p (g s) -> p g s", g=NG)
    ysT, ysT_b = qT, qT_b
    yaT, yaT_b = hThi, hThi_b
    ysq, ysq_b = UY, UY_b

    for s in range(nseq):
        tok0 = s * SEQ
        def m1_a(tt):
            i = tt % 3
            r0 = tok0 + tt * 128
            rd = [out_b[s][tt]] if x_src is out else []
            K.dma(sp, xt[i][:], x_src[r0:r0 + 128, :], reads=rd, writes=[xt_b[i]])
            sq_b, r_b = m1s_b[tt % 4]
            K.op(act, lambda: nc.scalar.activation(out=junk[:], in_=xt[i][:], func=AF.Square, accum_out=ssq[:, 0, tt:tt + 1]),
                 reads=[xt_b[i]], writes=[junk_b, sq_b])
            K.op(act, lambda: nc.scalar.activation(out=rs[:, 0, tt:tt + 1], in_=ssq[:, 0, tt:tt + 1], func=AF.Sqrt,
                                                   bias=epsb[:, 0:1], scale=1.0 / D_MODEL), reads=[sq_b, cb], writes=[r_b])
            K.op(dve, lambda: V.reciprocal(out=rs[:, 0, tt:tt + 1], in_=rs[:, 0, tt:tt + 1]), reads=[r_b], writes=[r_b])
            K.op(dve, lambda: V.tensor_scalar(out=hs[i][:], in0=xt[i][:], scalar1=rs[:, 0, tt:tt + 1], scalar2=None, op0=ALU.mult),
                 reads=[xt_b[i], r_b], writes=[hs_b[i]])

        def m1_b(tt):
            i = tt % 3
            b = next_ps(2)
            for c in range(8):
                K.op(pe, lambda: nc.tensor.transpose(psum[:, b + c // 4, (c % 4) * 128:(c % 4 + 1) * 128],
                                                     hs[i][:, c * 128:(c + 1) * 128], ident[:]),
                     reads=[hs_b[i], cb], writes=[ps_b[b], ps_b[b + 1]], inc=(c == 7))
            K.op(dve, lambda: V.tensor_copy(out=hTlo[:, :, tt * 128:(tt + 1) * 128],
                                            in_=psum[:, b, :].rearrange("p (c t) -> p c t", c=4)),
                 reads=[ps_b[b]], writes=[hTlo_b])
            K.op(act, lambda: nc.scalar.copy(out=hThi[:, :, tt * 128:(tt + 1) * 128],
                                             in_=psum[:, b + 1, :].rearrange("p (c t) -> p c t", c=4)),
                 reads=[ps_b[b + 1]], writes=[hThi_b])

        m1_a(0)
        for tt in range(NT):
            if tt + 1 < NT:
                m1_a(tt + 1)
            m1_b(tt)
        for m in range(9):
            if m == 4:
                K.dma(sp, u_scr.rearrange("(c r) t -> r c t", c=4), X1[:], reads=[X1_b], writes=[u_scr_b])
                for s8 in range(8):
                    K.dma(sp, Uf[s8 * 16:(s8 + 1) * 16, :, :],
                          u_scr.rearrange("(g q) (s sub) -> s q g sub", q=16, s=8)[s8], reads=[u_scr_b], writes=[UY_b])
            for tg in range(4):
                b = next_ps()
                for k in range(8):
                    K.op(pe, lambda: nc.tensor.matmul(psum[:, b, :], wi[:, k, m * 128:(m + 1) * 128],
                                                      hT[k // 4][:, k % 4, tg * 512:(tg + 1) * 512],
                                                      start=(k == 0), stop=(k == 7)),
                         reads=[wi_b, hT_b[k // 4]], writes=[ps_b[b]], inc=(k == 7))
                if m < 4:
                    K.op(act, lambda: nc.scalar.copy(
                        out=X1[:, m, :].rearrange("p (s sub) -> p sub s", s=8)[:, tg * 64:(tg + 1) * 64, :],
                        in_=psum[:, b, :].rearrange("p (sub s) -> p sub s", s=8)), reads=[ps_b[b]], writes=[X1_b])
                elif m < 8:
                    K.op(dve, lambda: V.tensor_copy(out=qT[:, m - 4, tg * 512:(tg + 1) * 512], in_=psum[:, b, :]),
                         reads=[ps_b[b]], writes=[qT_b])
                else:
                    K.op(act, lambda: nc.scalar.copy(out=kT[:, tg * 512:(tg + 1) * 512], in_=psum[:, b, :]),
                         reads=[ps_b[b]], writes=[kT_b])
        for tt in range(NT):
            b = next_ps()
            for k in range(8):
                K.op(pe, lambda: nc.tensor.matmul(psum[:, b, 0:128], hT[k // 4][:, k % 4, tt * 128:(tt + 1) * 128],
                                                  wi[:, k, 1152:1280], start=(k == 0), stop=(k == 7)),
                     reads=[wi_b, hT_b[k // 4]], writes=[ps_b[b]], inc=(k == 7))
            K.op(dve, lambda: V.tensor_copy(out=vaug[:, tt, :, 0:64], in_=psum[:, b, 0:128].rearrange("p (k d) -> p k d", k=2)),
                 reads=[ps_b[b]], writes=[va_b])
        if s == 0 and "uT" in dbg:
            K.dma(sp, dbg["uT"], X1[:], reads=[X1_b])
            K.dma(sp, dbg["qT"], qT[:], reads=[qT_b])
            K.dma(sp, dbg["kT"], kT[:], reads=[kT_b])
            K.dma(sp, dbg["vaug"], vaug[:], reads=[va_b])

        e_s5 = ExitStack()
        H = sb(e_s5, "H", (128, 2, NG, NCH + 1))
        Ein = sb(e_s5, "Ein", (128, 2, NG, NCH), BF16)
        st1 = sb(e_s5, "st1", (128, 2, NG))
        st2 = sb(e_s5, "st2", (128, 2, NG))
        mAb = [sb(e_s5, f"mAb{i}", (128, MA_COLS), BF16) for i in range(2)]
        mBb = [sb(e_s5, f"mBb{i}", (128, MB_COLS), BF16) for i in range(2)]
        e_att = ExitStack()
        et = [sb(e_att, f"et{i}", (128, 512), BF16) for i in range(3)]
        pt = [sb(e_att, f"pt{i}", (128, 512), BF16) for i in range(6)]
        oa = [sb(e_att, f"oa{i}", (128, 512)) for i in range(2)]
        den = [sb(e_att, f"den{i}", (128, 4)) for i in range(2)]

        if l == 0 and s == 0:
            print("mixer sbuf bytes remaining", nc.sbuf_bytes_remaining)

        def att_S(qb, kv):
            kbs = [kb for kb in (qb - 1, qb, qb + 1) if 0 <= kb < NT]
            rows = slice(kv * 64, (kv + 1) * 64)
            pts = []
            for kb in kbs:
                ie = K_rr(K, "et", 3)
                ip = K_rr(K, "pt", 6)
                b = next_ps()
                K.op(pe, lambda: nc.tensor.matmul(psum[:, b, :], kT[rows, kb * 128:(kb + 1) * 128],
                                                  qT[rows, :, qb * 128:(qb + 1) * 128], start=True, stop=True),
                     reads=[kT_b, qT_b], writes=[ps_b[b]])
                K.op(act, lambda: nc.scalar.activation(out=et[ie][:], in_=psum[:, b, :], func=AF.Exp, scale=0.125),
                     reads=[ps_b[b]], writes=[et_b[ie]])
                K.op(dve, lambda: V.tensor_tensor(out=pt[ip][:], in0=et[ie][:],
                                                  in1=eb[:, kb - qb + 1, kv].rearrange("p c q -> p (c q)"), op=ALU.mult),
                     reads=[et_b[ie], ebB], writes=[pt_b[ip]])
                pts.append((ip, kb))
            return pts

        def att_P(qb, kv, pts):
            b = next_ps()
            for c in range(4):
                for n_, (ip, kb) in enumerate(pts):
                    K.op(pe, lambda: nc.tensor.matmul(psum[:, b, c * 65:(c + 1) * 65], pt[ip][:, c * 128:(c + 1) * 128],
                                                      vaug[:, kb, kv, 0:65], start=(n_ == 0), stop=(n_ == len(pts) - 1)),
                         reads=[pt_b[ip], va_b], writes=[ps_b[b]], inc=(c == 3 and n_ == len(pts) - 1))
            io = qb % 2
            ov = psum[:, b, 0:260].rearrange("p (c d) -> p c d", c=4)
            K.op(dve, lambda: V.tensor_tensor(out=den[kv][:], in0=ov[:, :, 64], in1=esink[:, kv * 4:(kv + 1) * 4], op=ALU.add),
                 reads=[ps_b[b], pb], writes=[den_b[kv]])
            K.op(dve, lambda: V.reciprocal(out=den[kv][:], in_=den[kv][:]), reads=[den_b[kv]], writes=[den_b[kv]])
            K.op(dve, lambda: V.tensor_tensor(out=oa[io][:, kv * 256:(kv + 1) * 256].rearrange("p (c d) -> p c d", c=4),
                                              in0=ov[:, :, 0:64], in1=den[kv][:].unsqueeze(2).to_broadcast([128, 4, 64]),
                                              op=ALU.mult), reads=[ps_b[b], den_b[kv]], writes=[oa_b[io]])

        def att_F(qb):
            io = qb % 2
            K.op(act, lambda: nc.scalar.activation(out=junk[:, 0:512], in_=oa[io][:], func=AF.Square, accum_out=ssq[:, 2, qb:qb + 1]),
                 reads=[oa_b[io]], writes=[junk_b, ssq_b[2]])
            b = next_ps()
            for c in range(4):
                K.op(pe, lambda: nc.tensor.transpose(psum[:, b, c * 128:(c + 1) * 128], oa[io][:, c * 128:(c + 1) * 128], ident[:]),
                     reads=[oa_b[io], cb], writes=[ps_b[b]], inc=(c == 3))
            K.op(act, lambda: nc.scalar.copy(out=yaT[:, :, qb * 128:(qb + 1) * 128], in_=psum[:, b, :].rearrange("p (c t) -> p c t", c=4)),
                 reads=[ps_b[b]], writes=[yaT_b])

        def scan_step(c):
            hs_ = (slice(0, NG // 2), slice(NG // 2, NG))
            hb_ = (Hf_b, Hb_b)
            for (in_sw, ai) in ((False, 0), (True, 1)):
                for h2 in range(2):
                    gs = hs_[h2]
                    src = H[:, ::-1, gs, c] if in_sw else H[:, :, gs, c]
                    K.op(dve, lambda: V.tensor_tensor(out=st1[:, :, gs], in0=src, in1=aL[:, ai, :, gs], op=ALU.mult),
                         reads=[H_b, aL_b, hb_[h2]], writes=[hb_[h2]])
                for h2 in range(2):
                    gs = hs_[h2]
                    K.op(dve, lambda: V.tensor_tensor(out=H[:, :, gs, c + 1], in0=H[:, :, gs, c + 1], in1=st1[:, :, gs], op=ALU.add),
                         reads=[H_b, aL_b, hb_[h2]], writes=[hb_[h2]])

        K.op(dve, lambda: V.memset(H[:, :, :, 0:1], 0.0), writes=[H_b, Hf_b, Hb_b])

        def mA_load(g):
            K.dma(sp, mAb[g % 2][:], matsA[g], reads=[matsA_b[g]], writes=[mAb_b[g % 2]])

        def passA_batch(g0):
            b = next_ps()
            for gl in range(4):
                g = g0 + gl
                im = g % 2
                for ri in range(2):
                    for j in range(JC):
                        K.op(pe, lambda: nc.tensor.matmul(
                            psum[:, b, (ri * 4 + gl) * NCH:(ri * 4 + gl + 1) * NCH],
                            mAb[im][:, (ri * JC + j) * 128:(ri * JC + j + 1) * 128],
                            Uf[:, g, :].rearrange("p (c j) -> p c j", j=JC)[:, :, j],
                            start=(j == 0), stop=(j == JC - 1)),
                            reads=[mAb_b[im], UY_b], writes=[ps_b[b]], inc=(ri == 1 and j == JC - 1))
                if g + 2 < G:
                    mA_load(g + 2)
            pv = psum[:, b, :].rearrange("p (r g c) -> p r g c", r=2, g=4)
            K.op(act, lambda: nc.scalar.copy(out=H[0:64, :, g0:g0 + 4, 1:NCH + 1], in_=pv[0:64]), reads=[ps_b[b]], writes=[H_b, Hf_b, Hb_b])
            K.op(act, lambda: nc.scalar.copy(out=H[64:128, :, g0:g0 + 4, 1:NCH + 1][:, :, :, ::-1], in_=pv[64:128]), reads=[ps_b[b]], writes=[H_b, Hf_b, Hb_b])

        assert NCH % (NT // 2) == 0 and G // 4 == NT // 2
        mA_load(0)
        mA_load(1)
        tasks = [(qb, kv) for qb in range(NT) for kv in range(2)]
        spq = NCH // (NT // 2)
        cur = att_S(*tasks[0])
        fin = None
        for ti, (qb, kv) in enumerate(tasks):
            nxt = att_S(*tasks[ti + 1]) if ti + 1 < len(tasks) else None
            att_P(qb, kv, cur)
            cur = nxt
            if fin is not None:
                att_F(fin)
                fin = None
            if kv == 1:
                fin = qb
                if qb < NT // 2:
                    passA_batch(4 * qb)
                else:
                    for c in range((qb - NT // 2) * spq, (qb - NT // 2 + 1) * spq):
                        scan_step(c)
        att_F(fin)
        K.op(dve, lambda: V.tensor_copy(out=Ein[0:64], in_=H[0:64, :, :, 0:NCH]), reads=[H_b, Hf_b, Hb_b], writes=[Ein_b])
        K.op(act, lambda: nc.scalar.copy(out=Ein[64:128], in_=H[64:128, :, :, 0:NCH][:, :, :, ::-1]), reads=[H_b, Hf_b, Hb_b], writes=[Ein_b])
        def mB_load(g):
            K.dma(sp, mBb[g % 2][:], matsB[g], reads=[matsB_b[g]], writes=[mBb_b[g % 2]])
        mB_load(0)
        mB_load(1)
        for g0 in range(0, G, 2):
            b = next_ps()
            K.op(dve, lambda: V.memset(psum[:, b, :], 0.0), writes=[ps_b[b]])
            for gl in range(2):
                g = g0 + gl
                im = g % 2
                Ug = Uf[:, g, :].rearrange("p (c j) -> p c j", j=JC)
                Yg = psum[:, b, gl * 256:(gl + 1) * 256].rearrange("p (c j) -> p c j", j=JC)
                mm = []
                for d in range(JC):
                    mm.append((Yg[:, :, d:JC], mBb[im][:, d * 128:(d + 1) * 128], Ug[:, :, 0:JC - d], [UY_b]))
                    mm.append((Yg[:, :, 0:JC - d], mBb[im][:, (JC + d) * 128:(JC + d + 1) * 128], Ug[:, :, d:JC], [UY_b]))
                for j in range(JC):
                    mm.append((Yg[:, :, j], mBb[im][:, (2 * JC + j) * 128:(2 * JC + j + 1) * 128], Ein[:, 0, g, :], [Ein_b]))
                    mm.append((Yg[:, :, j], mBb[im][:, (3 * JC + j) * 128:(3 * JC + j + 1) * 128], Ein[:, 1, g, :], [Ein_b]))
                for n_, (o_, l_, r_, rd_) in enumerate(mm):
                    K.op(pe, lambda: nc.tensor.matmul(o_, l_, r_, start=False, stop=(n_ == len(mm) - 1), skip_group_check=True),
                         reads=[mBb_b[im]] + rd_, writes=[ps_b[b]], inc=(n_ == len(mm) - 1))
                if g + 2 < G:
                    mB_load(g + 2)
            K.op(act, lambda: nc.scalar.activation(out=Zf[:, g0:g0 + 2, :], in_=psum[:, b, :].rearrange("p (g s) -> p g s", g=2),
                                                   func=AF.Gelu_apprx_tanh), reads=[ps_b[b]], writes=[hTlo_b])
        for s8 in range(8):
            K.dma(sp, z_scr.rearrange("(g q) (s sub) -> s q g sub", q=16, s=8)[s8], Zf[s8 * 16:(s8 + 1) * 16, :, :],
                  reads=[hTlo_b], writes=[z_scr_b])
        K.dma(sp, X1[:], z_scr.rearrange("(c r) t -> r c t", c=4), reads=[z_scr_b], writes=[X1_b])
        K.barrier()
        e_att.close()
        gate = [sb(e_s5, f"gate{i}", (128, 512)) for i in range(2)]
        for m in range(4):
            for tg in range(4):
                b = next_ps()
                ig = (m * 4 + tg) % 2
                for k in range(4):
                    K.op(pe, lambda: nc.tensor.matmul(psum[:, b, :], wg[:, k, m * 128:(m + 1) * 128], X1[:, k, tg * 512:(tg + 1) * 512],
                                                      start=(k == 0), stop=(k == 3)),
                         reads=[wg_b, X1_b], writes=[ps_b[b]], inc=(k == 3))
                K.op(act, lambda: nc.scalar.activation(out=gate[ig][:], in_=psum[:, b, :], func=AF.Sigmoid, bias=bglu[:, m:m + 1]),
                     reads=[ps_b[b], pb], writes=[gate_b[ig]])
                nat = lambda tns: tns[:, m, :].rearrange("p (sub s) -> p s sub", s=8)[:, 2 * tg:2 * tg + 2, :]
                K.op(dve, lambda: V.tensor_tensor(out=nat(ysT), in0=X1[:, m, tg * 512:(tg + 1) * 512].rearrange("p (s sub) -> p s sub", s=2),
                                                  in1=gate[ig][:].rearrange("p (s sub) -> p s sub", s=2), op=ALU.mult),
                     reads=[X1_b, gate_b[ig]], writes=[ysT_b])
        K.op(act, lambda: nc.scalar.activation(out=ysq[:], in_=ysT[:], func=AF.Square), reads=[ysT_b], writes=[ysq_b])

        K.barrier()
        e_s5.close()
        if s == 0 and "ysT" in dbg:
            K.dma(sp, dbg["ysT"], ysT[:], reads=[ysT_b])
            K.dma(sp, dbg["yaT"], yaT[:], reads=[yaT_b])
            K.dma(sp, dbg["zT"], X1[:], reads=[X1_b])
        e_m4 = ExitStack()
        mt = [sb(e_m4, f"mt{i}", (128, 1024)) for i in range(2)]
        b = next_ps()
        for tt in range(NT):
            for k in range(4):
                K.op(pe, lambda: nc.tensor.matmul(psum[:, b, tt:tt + 1], ysq[:, k, tt * 128:(tt + 1) * 128], ones_bf[:, 0:1],
                                                  start=(k == 0), stop=(k == 3)),
                     reads=[ysq_b, cb], writes=[ps_b[b]], inc=(k == 3 and tt == NT - 1))
        K.op(act, lambda: nc.scalar.activation(out=rs[:, 1, :], in_=psum[:, b, 0:NT], func=AF.Sqrt, bias=epsb[:, 0:1], scale=1.0 / SSM_W),
             reads=[ps_b[b], cb], writes=[rs_b[1]])
        K.op(dve, lambda: V.reciprocal(out=rs[:, 1, :], in_=rs[:, 1, :]), reads=[rs_b[1]], writes=[rs_b[1]])
        K.op(act, lambda: nc.scalar.activation(out=rs[:, 2, :], in_=ssq[:, 2, :], func=AF.Sqrt, bias=epsb[:, 0:1], scale=1.0 / SSM_W),
             reads=[ssq_b[2], cb], writes=[rs_b[2]])
        K.op(dve, lambda: V.reciprocal(out=rs[:, 2, :], in_=rs[:, 2, :]), reads=[rs_b[2]], writes=[rs_b[2]])
        for tt in range(NT):
            i = tt % 2
            r0 = tok0 + tt * 128
            ba = next_ps(2)
            for hf in range(2):
                for k in range(4):
                    K.op(pe, lambda: nc.tensor.matmul(psum[:, ba + hf, :], ysT[:, k, tt * 128:(tt + 1) * 128], wo[:, k, hf * 512:(hf + 1) * 512],
                                                      start=(k == 0), stop=(k == 3)),
                         reads=[ysT_b, wo_b], writes=[ps_b[ba + hf]], inc=(k == 3))
            bb_ = next_ps(2)
            for hf in range(2):
                for k in range(4):
                    K.op(pe, lambda: nc.tensor.matmul(psum[:, bb_ + hf, :], yaT[:, k, tt * 128:(tt + 1) * 128], wo[:, 4 + k, hf * 512:(hf + 1) * 512],
                                                      start=(k == 0), stop=(k == 3)),
                         reads=[yaT_b, wo_b], writes=[ps_b[bb_ + hf]], inc=(k == 3))
            K.op(act, lambda: nc.scalar.activation(out=mt[i][:], in_=psum[:, ba:ba + 2, :].rearrange("p a n -> p (a n)"), func=AF.Copy,
                                                   scale=rs[:, 1, tt:tt + 1]), reads=[ps_b[ba], ps_b[ba + 1], rs_b[1]], writes=[mt_b[i]])
            K.op(dve, lambda: V.scalar_tensor_tensor(out=mt[i][:], in0=psum[:, bb_:bb_ + 2, :].rearrange("p a n -> p (a n)"),
                                                     scalar=rs[:, 2, tt:tt + 1], in1=mt[i][:], op0=ALU.mult, op1=ALU.add),
                 reads=[ps_b[bb_], ps_b[bb_ + 1], rs_b[2], mt_b[i]], writes=[mt_b[i]])
            K.op(act, lambda: nc.scalar.activation(out=junk[:], in_=mt[i][:], func=AF.Square, accum_out=ssq[:, 3, tt:tt + 1]),
                 reads=[mt_b[i]], writes=[junk_b, ssq_b[3]])
            K.op(act, lambda: nc.scalar.activation(out=rs[:, 3, tt:tt + 1], in_=ssq[:, 3, tt:tt + 1], func=AF.Sqrt,
                                                   bias=epsb[:, 0:1], scale=1.0 / D_MODEL), reads=[ssq_b[3], cb], writes=[rs_b[3]])
            K.op(dve, lambda: V.reciprocal(out=rs[:, 3, tt:tt + 1], in_=rs[:, 3, tt:tt + 1]), reads=[rs_b[3]], writes=[rs_b[3]])
            K.op(dve, lambda: V.tensor_tensor(out=mt[i][:], in0=mt[i][:], in1=gpost[:], op=ALU.mult),
                 reads=[mt_b[i], pb], writes=[mt_b[i]])
            rd = [out_b[s][tt]] if x_src is out else []
            K.dma(sp, xt[i][:], x_src[r0:r0 + 128, :], reads=rd, writes=[xt_b[i]])
            K.op(dve, lambda: V.scalar_tensor_tensor(out=xo[i][:], in0=mt[i][:], scalar=rs[:, 3, tt:tt + 1], in1=xt[i][:],
                                                     op0=ALU.mult, op1=ALU.add),
                 reads=[mt_b[i], rs_b[3], xt_b[i]], writes=[xo_b[i]])
            K.dma(sp, out[r0:r0 + 128, :], xo[i][:], reads=[xo_b[i]], writes=[out_b[s][tt]])
        K.barrier()
        e_m4.close()


_RR = {}


def K_rr(K, key, n):
    v = _RR.get(key, 0)
    _RR[key] = (v + 1) % n
    return v


def ffn_phase(nc, K, es, sb, psum, ps_b, next_ps, P, l, nseq, out, out_b, ident, identb, epsb, cb, dbg):
    pe, act, dve, pool, sp = K.pe, K.act, K.dve, K.pool, K.sp
    V = nc.vector
    wu = sb(es, "wu", (128, 8, 2 * D_FF), BF16)
    wd = sb(es, "wd", (128, NFB, 1024), BF16)
    gpf = sb(es, "gpf", (128, 8))
    cw = sb(es, "cw", (128, 4, 2 * NFB))
    gpost = sb(es, "gpost2", (128, 1024))
    wu_b, wd_b, pb = Buf(), Buf(), Buf()
    with ExitStack() as es1:
        ld = sb(es1, "ldf", (2 * NFB, 128))
        ld_b = Buf()
        LC = lambda dst, src, n: load_cols(nc, K, psum, ps_b, next_ps, ident, cb, ld, ld_b, dst, pb, src, n)
        LC(gpf[:], P["pre_ffn_norm"][l], 8)
        for j in range(3):
            LC(cw[:, j, :], P["conv_w"][l, j], 2 * NFB)
        LC(cw[:, 3, :], P["conv_b"][l], 2 * NFB)
        K.barrier()
    K.dma(sp, gpost[:], P["post_ffn_norm"][l].partition_broadcast(128), writes=[pb])
    with ExitStack() as es2:
        engs = [act, dve]
        stage = make_staging(es2, sb, "f")
        load_cast_weight(nc, K, stage, wu, wu_b, [P["w_up"][l, k * 128:(k + 1) * 128, :] for k in range(8)], 2 * D_FF,
                         gpf, pb, engs)
        load_cast_weight(nc, K, stage, wd, wd_b, [P["w_down"][l, k * 128:(k + 1) * 128, :] for k in range(NFB)], 1024,
                         None, pb, engs)
        K.barrier()

    HW = 1026
    h2T = sb(es, "h2T", (128, 8, HW), BF16)
    NPRE = 4
    gbuf = sb(es, "gbuf", (128, NFB - NPRE, 384), BF16)
    gpre = [sb(es, f"gpre{i}", (128, NPRE, 384), BF16) for i in range(2)]
    gbuf_b = [Buf() for _ in range(NFB - NPRE)]
    gpre_b = [[Buf() for _ in range(NPRE)] for _ in range(2)]

    def gslot(kpar, fb):
        if fb < NPRE:
            return gpre[kpar], fb, gpre_b[kpar][fb]
        return gbuf, fb - NPRE, gbuf_b[fb - NPRE]
    pending = []
    tgk = [0]
    halo = sb(es, "halo", (128, 8, 2), BF16)
    xt = [sb(es, f"fxt{i}", (128, 1024)) for i in range(3)]
    hs = xt
    junk = sb(es, "fjunk", (128, 1024), BF16)
    cv = [sb(es, f"cv{i}", (128, 384)) for i in range(3)]
    cg = [sb(es, f"cg{i}", (128, 384)) for i in range(3)]
    gg = [sb(es, f"gg{i}", (128, 384)) for i in range(3)]
    yt = [sb(es, f"yt{i}", (128, 1024)) for i in range(1)] * 2
    xo = xt
    st = sb(es, "fst", (128, 4))
    stf = sb(es, "fstf", (128, 2, 4))
    stf_b = [Buf() for _ in range(4)]
    std = sb(es, "fstd", (128, 2, 2))
    std_b = [Buf() for _ in range(2)]
    B = lambda n=1: [Buf() for _ in range(n)]
    h2T_b, g_b, junk_b, st_b = Buf(), Buf(), Buf(), Buf()
    xt_b, cv_b, cg_b, gg_b = B(3), B(3), B(3), B(3)
    yt_b = B(1) * 2
    hs_b = xt_b
    xo_b = xt_b
    def emit_down(s, tok0, hs0, t0, ln, kpar):
        for ti in range(ln // 128):
            tt = (hs0 + t0) // 128 + ti
            r0 = tok0 + tt * 128
            i = tt % 2
            b = next_ps(2)
            for hf in range(2):
                for fb in range(NFB):
                    gt_, gi_, gb_ = gslot(kpar, fb)
                    K.op(pe, lambda: nc.tensor.matmul(psum[:, b + hf, :], gt_[:, gi_, ti * 128:(ti + 1) * 128],
                                                      wd[:, fb, hf * 512:(hf + 1) * 512], start=(fb == 0), stop=(fb == NFB - 1)),
                         reads=[gb_, wd_b], writes=[ps_b[b + hf]], inc=(fb == NFB - 1))
            pv = psum[:, b:b + 2, :].rearrange("p a n -> p (a n)")
            K.op(act, lambda: nc.scalar.activation(out=junk[:], in_=pv, func=AF.Square, accum_out=std[:, 0, i:i + 1]),
                 reads=[ps_b[b], ps_b[b + 1]], writes=[junk_b, std_b[i]])
            K.op(act, lambda: nc.scalar.activation(out=std[:, 1, i:i + 1], in_=std[:, 0, i:i + 1], func=AF.Sqrt, bias=epsb[:, 0:1], scale=1.0 / D_MODEL),
                 reads=[std_b[i], cb], writes=[std_b[i]])
            K.op(dve, lambda: V.reciprocal(out=std[:, 1, i:i + 1], in_=std[:, 1, i:i + 1]), reads=[std_b[i]], writes=[std_b[i]])
            K.op(dve, lambda: V.tensor_tensor(out=yt[i][:], in0=pv, in1=gpost[:], op=ALU.mult),
                 reads=[ps_b[b], ps_b[b + 1], pb], writes=[yt_b[i]])
            K.dma(sp, xt[i][:], out[r0:r0 + 128, :], reads=[out_b[s][tt]], writes=[xt_b[i]])
            K.op(dve, lambda: V.scalar_tensor_tensor(out=xo[i][:], in0=yt[i][:], scalar=std[:, 1, i:i + 1], in1=xt[i][:],
                                                     op0=ALU.mult, op1=ALU.add),
                 reads=[yt_b[i], std_b[i], xt_b[i]], writes=[xo_b[i]])
            K.dma(sp, out[r0:r0 + 128, :], xo[i][:], reads=[xo_b[i]], writes=[out_b[s][tt]])

    it = [0]
    if l == 0:
        print("ffn sbuf bytes remaining", nc.sbuf_bytes_remaining)

    for s in range(nseq):
        tok0 = s * SEQ
        for half in range(2):
            hs0 = half * 1024
            t_lo = hs0 // 128 - 1
            if half == 1:
                K.op(dve, lambda: V.tensor_copy(out=h2T[:, :, 0:1], in_=halo[:, :, 0:1]), reads=[h2T_b], writes=[h2T_b])
            tiles = [tt for tt in range(t_lo, t_lo + 10) if not (tt < 0 or tt >= NT or (half == 1 and tt == t_lo))]

            def fill_a(tt):
                i = it[0] % 3
                js = it[0] % 4
                it[0] += 1
                r0 = tok0 + tt * 128
                K.dma(sp, xt[i][:], out[r0:r0 + 128, :], reads=[out_b[s][tt]], writes=[xt_b[i]])
                K.op(act, lambda: nc.scalar.activation(out=junk[:], in_=xt[i][:], func=AF.Square, accum_out=stf[:, 0, js:js + 1]),
                     reads=[xt_b[i]], writes=[junk_b, stf_b[js]])
                K.op(act, lambda: nc.scalar.activation(out=stf[:, 1, js:js + 1], in_=stf[:, 0, js:js + 1], func=AF.Sqrt, bias=epsb[:, 0:1], scale=1.0 / D_MODEL),
                     reads=[stf_b[js], cb], writes=[stf_b[js]])
                K.op(dve, lambda: V.reciprocal(out=stf[:, 1, js:js + 1], in_=stf[:, 1, js:js + 1]), reads=[stf_b[js]], writes=[stf_b[js]])
                K.op(dve, lambda: V.tensor_scalar(out=hs[i][:], in0=xt[i][:], scalar1=stf[:, 1, js:js + 1], scalar2=None, op0=ALU.mult),
                     reads=[xt_b[i], stf_b[js]], writes=[hs_b[i]])
                return i

            def fill_b(tt, i):
                b = next_ps(2)
                for c in range(8):
                    K.op(pe, lambda: nc.tensor.transpose(psum[:, b + c // 4, (c % 4) * 128:(c % 4 + 1) * 128],
                                                         hs[i][:, c * 128:(c + 1) * 128], ident[:]),
                         reads=[hs_b[i], cb], writes=[ps_b[b], ps_b[b + 1]], inc=(c == 7))
                j0 = tt * 128 - hs0 + 1
                lo, hi = max(j0, 0), min(j0 + 128, HW)
                for hh in range(2):
                    src = psum[:, b + hh, :].rearrange("p (c t) -> p c t", c=4)[:, :, lo - j0:hi - j0]
                    dst = h2T[:, hh * 4:(hh + 1) * 4, lo:hi]
                    if hh == 0:
                        K.op(dve, lambda: V.tensor_copy(out=dst, in_=src), reads=[ps_b[b + hh]], writes=[h2T_b])
                    else:
                        K.op(act, lambda: nc.scalar.copy(out=dst, in_=src), reads=[ps_b[b + hh]], writes=[h2T_b])

            ia = fill_a(tiles[0])
            for n_, tt in enumerate(tiles):
                ia_next = fill_a(tiles[n_ + 1]) if n_ + 1 < len(tiles) else None
                fill_b(tt, ia)
                ia = ia_next
            if half == 0:
                K.op(dve, lambda: V.tensor_copy(out=halo[:, :, 0:1], in_=h2T[:, :, 1024:1025]), reads=[h2T_b], writes=[h2T_b])
            for (t0, ln) in ((0, 384), (384, 384), (768, 256)):
                g_first = (hs0 + t0 == 0)
                g_last = (hs0 + t0 + ln == SEQ)
                lo = 1 if g_first else 0
                hi = 1 if g_last else 0
                w0c, w1c = t0 + lo, t0 + ln + 2 - hi
                nW = w1c - w0c
                ctr = 1 - lo
                kpar = tgk[0] % 2
                tgk[0] += 1
                for fb in range(NFB):
                    i = fb % 3
                    if fb == NPRE:
                        for fn in pending:
                            fn()
                        pending.clear()
                    for vg in range(2):
                        rb = vg * NFB + fb
                        b = next_ps()
                        for k in range(8):
                            K.op(pe, lambda: nc.tensor.matmul(psum[:, b, 0:nW], wu[:, k, rb * 128:(rb + 1) * 128],
                                                              h2T[:, k, w0c:w1c], start=(k == 0), stop=(k == 7)),
                                 reads=[wu_b, h2T_b], writes=[ps_b[b]], inc=(k == 7))
                        dst, dst_b = (cv[i], cv_b[i]) if vg == 0 else (cg[i], cg_b[i])
                        K.op(act, lambda: nc.scalar.activation(out=dst[:, 0:ln], in_=psum[:, b, ctr:ctr + ln], func=AF.Identity,
                                                               bias=cw[:, 3, rb:rb + 1], scale=cw[:, 1, rb:rb + 1]),
                             reads=[ps_b[b], pb], writes=[dst_b])
                        K.op(dve, lambda: V.scalar_tensor_tensor(out=dst[:, lo:ln], in0=psum[:, b, ctr + lo - 1:ctr + ln - 1],
                                                                 scalar=cw[:, 0, rb:rb + 1], in1=dst[:, lo:ln], op0=ALU.mult, op1=ALU.add),
                             reads=[ps_b[b], pb, dst_b], writes=[dst_b])
                        K.op(dve, lambda: V.scalar_tensor_tensor(out=dst[:, 0:ln - hi], in0=psum[:, b, ctr + 1:ctr + ln - hi + 1],
                                                                 scalar=cw[:, 2, rb:rb + 1], in1=dst[:, 0:ln - hi], op0=ALU.mult, op1=ALU.add),
                             reads=[ps_b[b], pb, dst_b], writes=[dst_b])
                    K.op(act, lambda: nc.scalar.activation(out=gg[i][:, 0:ln], in_=cg[i][:, 0:ln], func=AF.Gelu_apprx_tanh),
                         reads=[cg_b[i]], writes=[gg_b[i]])
                    gt_, gi_, gb_ = gslot(kpar, fb)
                    K.op(pool, lambda: nc.gpsimd.tensor_tensor(out=gt_[:, gi_, 0:ln], in0=gg[i][:, 0:ln], in1=cv[i][:, 0:ln], op=ALU.mult),
                         reads=[gg_b[i], cv_b[i]], writes=[gb_])
                pending.append(lambda s=s, tok0=tok0, hs0=hs0, t0=t0, ln=ln, kpar=kpar: emit_down(s, tok0, hs0, t0, ln, kpar))
    for fn in pending:
        fn()
    pending.clear()
_CACHE = {}


def _q_perm():
    cols = list(range(512))
    for c in range(4):
        for h in (c, 4 + c):
            cols.extend(range(512 + h * 64, 512 + (h + 1) * 64))
    cols.extend(range(1024, 1280))
    return np.asarray(cols)


def kernel(**inputs):
    n_cores = 8
    x = np.ascontiguousarray(np.asarray(inputs["x"], dtype=np.float32))
    nseq = x.shape[0] // n_cores
    if "nc" not in _CACHE:
        _CACHE["nc"] = build(nseq=nseq)[0]
        _CACHE["consts"] = _host_consts()
    nc = _CACHE["nc"]
    consts = _CACHE["consts"]
    shared = {}
    for k, v in inputs.items():
        if k == "x":
            continue
        a = np.ascontiguousarray(np.asarray(v, dtype=np.float32))
        if k == "w_in":
            a = np.ascontiguousarray(a[:, :, _q_perm()])
        shared[k] = a
    shared.update(consts)
    in_maps = []
    for i in range(n_cores):
        m = dict(shared)
        m["x"] = x[i * nseq:(i + 1) * nseq].reshape(nseq * SEQ, D_MODEL)
        in_maps.append(m)
    res = run_bass_kernel_spmd(nc, in_maps, core_ids=list(range(n_cores)))
    outs = [np.asarray(r["out"]).reshape(nseq, SEQ, D_MODEL) for r in res.results]
    return np.concatenate(outs, axis=0).astype(np.float32)
```

```python
import math
from contextlib import ExitStack
import numpy as np
import jax
import jax.numpy as jnp
import concourse.bass as bass
import concourse.mybir as mybir
from concourse.bass_utils import run_bass_kernel_spmd

F32 = mybir.dt.float32
BF16 = mybir.dt.bfloat16
I32 = mybir.dt.int32
AF = mybir.ActivationFunctionType
ALU = mybir.AluOpType

D_MODEL = 1024
SEQ = 2048
DEPTH = 4
SSM_W = 512
NG = 32
NST = 64
D_FF = 2816
NFB = 22
IN_W = 1280
EPS = 1e-6
JC = 4
LCH = 8 * JC
NSUB = SEQ // 8
NCH = NSUB // JC
NT = SEQ // 128
TWO_PI = 2.0 * math.pi

EC_TB = 0
EC_TC = EC_TB + 8
EC_SB = EC_TC + 8 * JC
EC_CC = EC_SB + 8 * JC
EC_L = EC_CC + 8 * JC
NEC = EC_L + 1
ANG_SHIFT = 64

MA_COLS = 2 * JC * 128
MB_COLS = 4 * JC * 128


class Buf:
    __slots__ = ("w", "r", "name")

    def __init__(self, name=""):
        self.w = None
        self.r = {}
        self.name = name


class Eng:
    def __init__(self, K, name, eng, is_pe=False):
        self.K = K
        self.name = name
        self.eng = eng
        self.is_pe = is_pe
        self.sems = []
        self.ep = -1
        self.cnt = 0
        self.seen = {}
        self.pending = []
        self._new_epoch()

    def _new_epoch(self):
        self.ep += 1
        self.cnt = 0
        self.sems.append(self.K.nc.alloc_semaphore(f"s_{self.name}_{self.ep}"))

class Kern:
    def __init__(self, nc):
        self.nc = nc
        self.pe = Eng(self, "pe", nc.tensor, True)
        self.act = Eng(self, "act", nc.scalar)
        self.dve = Eng(self, "dve", nc.vector)
        self.pool = Eng(self, "pool", nc.gpsimd)
        self.sp = Eng(self, "sp", nc.sync)
        self.engs = [self.pe, self.act, self.dve, self.pool, self.sp]
        self.slots = {}
        for e, n in ((self.sp, 20), (self.pool, 6), (self.act, 6)):
            self.slots[e.name] = [[nc.alloc_semaphore(f"d_{e.name}_{i}"), 0, ("dma", e.name, i)] for i in range(n)]
        self.slot_i = {k: 0 for k in self.slots}
        self.n_ins = 0

    def _wait(self, E, tick):
        sem, val, key = tick
        if key[0] == E.name and E.is_pe:
            return
        if E.seen.get(key, 0) >= val:
            return
        if key[0] != "dma":
            for (k2, v2) in E.seen.items():
                if k2[0] == key[0] and k2[0] != "dma" and k2[1] > key[1]:
                    return
        E.eng.wait_ge(sem, val)
        E.seen[key] = val
        self.n_ins += 1

    def _deps(self, E, reads, writes):
        need = []
        for b in reads:
            if b.w is not None:
                need.append(b.w)
        for b in writes:
            if b.w is not None:
                need.append(b.w)
            need.extend(b.r.values())
        for t in need:
            self._wait(E, t)

    def op(self, E, fn, reads=(), writes=(), inc=True):
        if E.cnt >= 28000 and not E.pending:
            E._new_epoch()
        self._deps(E, reads, writes)
        tick = (E.sems[E.ep], E.cnt + 1, (E.name, E.ep))
        ins = fn()
        self.n_ins += 1
        if inc:
            ins.then_inc(E.sems[E.ep], 1)
            E.cnt += 1
            E.pending = []
        else:
            E.pending.append(1)
        for b in reads:
            b.r[E.name] = tick
        for b in writes:
            b.w = tick
            b.r = {}
        return ins

    def dma(self, E, out, in_, reads=(), writes=(), **kw):
        slots = self.slots[E.name]
        i = self.slot_i[E.name]
        self.slot_i[E.name] = (i + 1) % len(slots)
        s = slots[i]
        if s[1] > 0:
            self._wait(E, (s[0], s[1], s[2]))
        self._deps(E, reads, writes)
        ins = E.eng.dma_start(out=out, in_=in_, **kw)
        s[1] += 16
        ins.then_inc(s[0], 16)
        self.n_ins += 1
        tick = (s[0], s[1], s[2])
        for b in reads:
            b.r[("dma", E.name, i)] = tick
        for b in writes:
            b.w = tick
            b.r = {}
        return ins

    def barrier(self):
        sp = self.sp
        for E in self.engs:
            if E is sp:
                continue
            if E.cnt > 0:
                self._wait(sp, (E.sems[E.ep], E.cnt, (E.name, E.ep)))
        for lst in self.slots.values():
            for s in lst:
                if s[1] > 0:
                    self._wait(sp, (s[0], s[1], s[2]))
        self.op(sp, lambda: sp.eng.nop())
        t = (sp.sems[sp.ep], sp.cnt, (sp.name, sp.ep))
        for E in self.engs:
            if E is not sp:
                self._wait(E, t)


def _t5_bucket(rel):
    n_buckets, max_distance = 32, 128
    half = n_buckets // 2
    max_exact = half // 2
    ret = jnp.where(rel > 0, half, 0)
    n = jnp.abs(rel)
    nf = jnp.maximum(n, 1).astype(jnp.float32)
    large = max_exact + (jnp.log(nf / max_exact) / math.log(max_distance / max_exact)
                         * (half - max_exact)).astype(jnp.int32)
    large = jnp.minimum(large, half - 1)
    return ret + jnp.where(n < max_exact, n, large)


def _host_consts():
    import ml_dtypes
    c = {}
    c["ident_f"] = np.eye(128, dtype=np.float32)
    ex = np.zeros((2, NEC), np.float32)
    for s in range(8):
        ex[0, EC_TB + s] = -s
        ex[1, EC_TB + s] = s
    for d in range(JC):
        for s in range(8):
            ex[0, EC_TC + d * 8 + s] = 8 * d + s
            ex[1, EC_TC + d * 8 + s] = 8 * d - s
            ex[0, EC_SB + d * 8 + s] = LCH - 1 - 8 * d - s
            ex[1, EC_SB + d * 8 + s] = 8 * d + s
            ex[0, EC_CC + d * 8 + s] = 8 * d + s + 1
            ex[1, EC_CC + d * 8 + s] = LCH - 8 * d - s
    ex[:, EC_L] = LCH
    c["extab"] = np.repeat(ex, 64, axis=0).astype(np.float32)
    sp = np.arange(128)[:, None] // 16
    s = np.arange(128)[None, :] // 16
    tm = np.stack([(s >= sp), (sp >= s)], axis=1).astype(np.float32)
    c["tmask"] = np.ascontiguousarray(tm)
    k = np.arange(128)[:, None, None]
    off = (np.arange(3) - 1)[None, :, None]
    q = np.arange(128)[None, None, :]
    rel = (k + 128 * off - q).astype(np.int32)
    with jax.default_device(jax.devices("cpu")[0]):
        bk = np.asarray(_t5_bucket(jnp.asarray(rel)))
    oh = (bk[:, None, :, :] == np.arange(32)[None, :, None, None]).astype(np.float32)
    c["onehot"] = oh.reshape(128, 32, 384).astype(ml_dtypes.bfloat16)
    c["vmask"] = (np.abs(rel) <= 128).astype(np.float32).reshape(128, 384)
    return c


def build(nseq=4, nlayers=DEPTH, debug=None):
    nc = bass.Bass("TRN2", target_bir_lowering=False)
    K = Kern(nc)
    pe, act, dve, pool, sp = K.pe, K.act, K.dve, K.pool, K.sp
    NTOK = nseq * SEQ

    def din(name, shape, dt=F32):
        return nc.dram_tensor(name, list(shape), dt, kind="ExternalInput").ap()

    x_in = din("x", (NTOK, D_MODEL))
    P = {}
    P["rel_bias"] = din("rel_bias", (32, 8))
    P["pre_mix_norm"] = din("pre_mix_norm", (DEPTH, D_MODEL))
    P["w_in"] = din("w_in", (DEPTH, D_MODEL, IN_W))
    P["lam_re"] = din("lam_re", (DEPTH, 2, NG, NST))
    P["lam_im"] = din("lam_im", (DEPTH, 2, NG, NST))
    P["log_step"] = din("log_step", (DEPTH, 2, NG))
    P["b_re"] = din("b_re", (DEPTH, 2, NG, NST, 16))
    P["b_im"] = din("b_im", (DEPTH, 2, NG, NST, 16))
    P["c_re"] = din("c_re", (DEPTH, 2, NG, 16, NST))
    P["c_im"] = din("c_im", (DEPTH, 2, NG, 16, NST))
    P["ssm_d"] = din("ssm_d", (DEPTH, SSM_W))
    P["w_glu"] = din("w_glu", (DEPTH, SSM_W, SSM_W))
    P["b_glu"] = din("b_glu", (DEPTH, SSM_W))
    P["attn_sink"] = din("attn_sink", (DEPTH, 8))
    P["ssm_out_norm"] = din("ssm_out_norm", (DEPTH, SSM_W))
    P["attn_out_norm"] = din("attn_out_norm", (DEPTH, SSM_W))
    P["w_out"] = din("w_out", (DEPTH, D_MODEL, D_MODEL))
    P["post_mix_norm"] = din("post_mix_norm", (DEPTH, D_MODEL))
    P["pre_ffn_norm"] = din("pre_ffn_norm", (DEPTH, D_MODEL))
    P["w_up"] = din("w_up", (DEPTH, D_MODEL, 2 * D_FF))
    P["conv_w"] = din("conv_w", (DEPTH, 3, 2 * D_FF))
    P["conv_b"] = din("conv_b", (DEPTH, 2 * D_FF))
    P["w_down"] = din("w_down", (DEPTH, D_FF, D_MODEL))
    P["post_ffn_norm"] = din("post_ffn_norm", (DEPTH, D_MODEL))
    c_ident = din("ident_f", (128, 128))
    c_extab = din("extab", (128, NEC))
    c_tmask = din("tmask", (128, 2, 128))
    c_onehot = din("onehot", (128, 32, 384), BF16)
    c_vmask = din("vmask", (128, 384))

    out = nc.dram_tensor("out", [NTOK, D_MODEL], F32, kind="ExternalOutput").ap()
    matsA = nc.dram_tensor("matsA", [NG, 128, MA_COLS], BF16).ap()
    matsB = nc.dram_tensor("matsB", [NG, 128, MB_COLS], BF16).ap()
    u_scr = nc.dram_tensor("u_scr", [NG * 16, SEQ], BF16).ap()
    z_scr = nc.dram_tensor("z_scr", [NG * 16, SEQ], BF16).ap()
    eb_scr = nc.dram_tensor("eb_scr", [128, 3 * 2 * 4 * 128], BF16).ap()
    matsA_b = [Buf() for _ in range(NG)]
    matsB_b = [Buf() for _ in range(NG)]
    u_scr_b, z_scr_b = Buf(), Buf()
    out_b = [[Buf() for _ in range(NT)] for _ in range(nseq)]
    dbg = {}
    if debug:
        for nm, (shp, dt_) in debug.items():
          if shp is not None:
            dbg[nm] = nc.dram_tensor("dbg_" + nm, list(shp), dt_, kind="ExternalOutput").ap()

    top = ExitStack()
    with top:
        uid = [0]

        def sb(es, name, shape, dt=F32):
            uid[0] += 1
            return es.enter_context(nc.sbuf_tensor(f"{name}_{uid[0]}", list(shape), dt))

        psum = top.enter_context(nc.psum_tensor("psum", [128, 8, 512], F32))
        ps_b = [Buf(f"ps{i}") for i in range(8)]
        ps_rr = [0]

        def next_ps(n=1):
            if n == 1:
                i = ps_rr[0] % 8
                ps_rr[0] += 1
                return i
            i = ps_rr[0] % 8
            if i % 2:
                i = (i + 1) % 8
            ps_rr[0] = i + 2
            return i

        ident = sb(top, "ident", (128, 128))
        identb = sb(top, "identb", (128, 128), BF16)
        ones_bf = sb(top, "ones_bf", (128, 1), BF16)
        epsb = sb(top, "epsb", (128, 1))
        aL = sb(top, "aL", (128, 2, 2, 32))
        aL_b = Buf("aL")
        cb = Buf("consts")
        ebB = Buf("eb")
        K.dma(sp, ident[:], c_ident, writes=[cb])
        K.op(dve, lambda: nc.vector.tensor_copy(out=identb[:], in_=ident[:]), reads=[cb], writes=[cb])
        K.op(dve, lambda: nc.vector.memset(ones_bf[:], 1.0), writes=[cb])
        K.op(dve, lambda: nc.vector.memset(epsb[:], EPS), writes=[cb])

        with ExitStack() as es:
            oh = sb(es, "oh", (128, 32, 384), BF16)
            eb = sb(es, "eb0", (128, 3, 2, 4, 128), BF16)
            rb = sb(es, "rb", (128, 256))
            vm = sb(es, "vm", (128, 384))
            acc = sb(es, "acc", (128, 8, 384))
            tb = Buf()
            K.dma(sp, oh[:], c_onehot, writes=[tb])
            K.dma(sp, vm[:], c_vmask, writes=[tb])
            K.dma(sp, rb[:], P["rel_bias"].rearrange("b h -> (b h)").partition_broadcast(128), writes=[tb])
            accb = [Buf() for _ in range(8)]
            for h in range(8):
                E = dve
                K.op(E, lambda h=h: nc.vector.tensor_scalar(out=acc[:, h, :], in0=oh[:, 0, :], scalar1=rb[:, h:h + 1],
                                                            scalar2=None, op0=ALU.mult), reads=[tb], writes=[accb[h]])
                for b in range(1, 32):
                    K.op(E, lambda h=h, b=b: nc.vector.scalar_tensor_tensor(
                        out=acc[:, h, :], in0=oh[:, b, :], scalar=rb[:, b * 8 + h:b * 8 + h + 1], in1=acc[:, h, :],
                        op0=ALU.mult, op1=ALU.add), reads=[tb, accb[h]], writes=[accb[h]])
                K.op(act, lambda h=h: nc.scalar.activation(out=acc[:, h, :], in_=acc[:, h, :], func=AF.Exp),
                     reads=[accb[h]], writes=[accb[h]])
                kv, c = h // 4, h % 4
                K.op(dve, lambda h=h, kv=kv, c=c: nc.vector.tensor_tensor(
                    out=eb[:, :, kv, c, :], in0=acc[:, h, :].rearrange("p (o q) -> p o q", o=3),
                    in1=vm[:].rearrange("p (o q) -> p o q", o=3), op=ALU.mult), reads=[accb[h], tb], writes=[ebB])
            K.dma(sp, eb_scr, eb[:].rearrange("p o k c q -> p (o k c q)"), reads=[ebB], writes=[ebB])
            K.barrier()

        for l in range(nlayers):
            x_src = x_in if l == 0 else out
            with ExitStack() as es:
                s5_prologue(nc, K, es, sb, psum, ps_b, next_ps, P, l, ident, c_extab, c_tmask,
                            aL, aL_b, matsA, matsB, matsA_b, matsB_b, dbg)
                K.barrier()
            with ExitStack() as es:
                mixer_phase(nc, K, es, sb, psum, ps_b, next_ps, P, l, nseq, x_src, out, out_b, ident, identb, ones_bf,
                            eb_scr, ebB, epsb, cb, aL, aL_b, matsA, matsB, matsA_b, matsB_b, u_scr, z_scr, u_scr_b, z_scr_b, dbg)
                K.barrier()
            with ExitStack() as es:
              if not (debug and "skip_ffn" in debug):
                ffn_phase(nc, K, es, sb, psum, ps_b, next_ps, P, l, nseq, out, out_b, ident, identb, epsb, cb, dbg)
                K.barrier()
        K.barrier()
    return nc, K


def make_staging(es, sb, tag, n=6, ch=2048):
    return ([sb(es, f"stg_{tag}{i}", (128, ch)) for i in range(n)], [Buf() for _ in range(n)], [0])


def load_cast_weight(nc, K, stage, dst, dst_b, src_rows, ncols, gain, gain_b, engines):
    stg, stg_b, itr = stage
    NS = len(stg)
    CH = 2048
    dq = [K.sp, K.act]
    it = itr[0]
    for r, src in enumerate(src_rows):
        for c0 in range(0, ncols, CH):
            cw = min(CH, ncols - c0)
            i = it % NS
            E = engines[it % len(engines)]
            Q = dq[it % 2]
            it += 1
            K.dma(Q, stg[i][:, 0:cw], src[:, c0:c0 + cw], writes=[stg_b[i]])
            if gain is None:
                if E is K.act:
                    K.op(E, lambda i=i, r=r, c0=c0, cw=cw: nc.scalar.copy(out=dst[:, r, c0:c0 + cw], in_=stg[i][:, 0:cw]),
                         reads=[stg_b[i]], writes=[dst_b])
                else:
                    K.op(E, lambda i=i, r=r, c0=c0, cw=cw, E=E: E.eng.tensor_copy(out=dst[:, r, c0:c0 + cw], in_=stg[i][:, 0:cw]),
                         reads=[stg_b[i]], writes=[dst_b])
            else:
                if E is K.act:
                    K.op(E, lambda i=i, r=r, c0=c0, cw=cw: nc.scalar.activation(
                        out=dst[:, r, c0:c0 + cw], in_=stg[i][:, 0:cw], func=AF.Copy, scale=gain[:, r:r + 1]),
                        reads=[stg_b[i], gain_b], writes=[dst_b])
                else:
                    K.op(E, lambda i=i, r=r, c0=c0, cw=cw, E=E: E.eng.tensor_scalar(
                        out=dst[:, r, c0:c0 + cw], in0=stg[i][:, 0:cw], scalar1=gain[:, r:r + 1], scalar2=None,
                        op0=ALU.mult), reads=[stg_b[i], gain_b], writes=[dst_b])
    itr[0] = it


def load_cols(nc, K, psum, ps_b, next_ps, ident, cb, ld, ld_b, dst, dst_b, src1d, n):
    K.dma(K.sp, ld[0:n, :], src1d.rearrange("(r p) -> r p", p=128), writes=[ld_b])
    b = next_ps()
    K.op(K.pe, lambda: nc.tensor.transpose(psum[:, b, 0:n], ld[0:n, :], ident[0:n, 0:n]), reads=[ld_b, cb], writes=[ps_b[b]])
    K.op(K.dve, lambda: nc.vector.tensor_copy(out=dst, in_=psum[:, b, 0:n]), reads=[ps_b[b]], writes=[dst_b])


def rstd_from_ssq(nc, K, ssq, rs, n, width, epsb, bufs_r, bufs_w):
    K.op(K.act, lambda: nc.scalar.activation(out=rs[:, 0:n], in_=ssq[:, 0:n], func=AF.Sqrt, bias=epsb[:, 0:1],
                                             scale=1.0 / width), reads=bufs_r, writes=bufs_w)
    K.op(K.dve, lambda: nc.vector.reciprocal(out=rs[:, 0:n], in_=rs[:, 0:n]), reads=bufs_w, writes=bufs_w)


def s5_prologue(nc, K, es, sb, psum, ps_b, next_ps, P, l, ident, c_extab, c_tmask,
                aL, aL_b, matsA, matsB, matsA_b, matsB_b, dbg):
    pe, act, dve, pool, sp = K.pe, K.act, K.dve, K.pool, K.sp
    V = nc.vector
    G = NG
    tb = Buf("s5tab")

    def t(name, shape, dt=F32):
        return sb(es, name, shape, dt)

    lamld = t("lamld", (32, 2, 128))
    lam = t("lam", (128, 2, 32))
    ls = t("ls", (128, 32))
    braw = t("braw", (128, 2, 32, 16))
    cld = t("cld", (128, 2, 4, 128))
    craw = t("craw", (128, 2, 32, 16))
    dvec = t("dvec", (128, 32))
    extab = t("extab_sb", (128, NEC))
    tmask = t("tmask_sb", (128, 2, 128))
    K.dma(sp, extab[:], c_extab, writes=[tb])
    K.dma(sp, tmask[:], c_tmask, writes=[tb])
    with nc.allow_non_contiguous_dma(reason="tiny param loads"):
        for ri, nm in enumerate(("lam_re", "lam_im")):
            K.dma(sp, lamld[:, ri, :].rearrange("g (d n) -> g d n", d=2), P[nm][l].rearrange("d g n -> g d n"), writes=[tb])
        for d in range(2):
            K.dma(sp, ls[d * 64:(d + 1) * 64, :], P["log_step"][l, d].partition_broadcast(64), writes=[tb])
            for ri, nm in enumerate(("b_re", "b_im")):
                K.dma(sp, braw[d * 64:(d + 1) * 64, ri, :, :], P[nm][l, d].rearrange("g n q -> n g q"), writes=[tb])
        for ri, nm in enumerate(("c_re", "c_im")):
            for d in range(2):
                K.dma(sp, cld[:, ri, :, d * 64:(d + 1) * 64],
                      P[nm][l, d].rearrange("(t gl) p n -> (gl p) t n", t=4), writes=[tb])
        for s in range(8):
            K.dma(sp, dvec[s * 16:(s + 1) * 16, :], P["ssm_d"][l].rearrange("(g q) -> q g", q=16), writes=[tb])
    for ri in range(2):
        b = next_ps()
        K.op(pe, lambda ri=ri, b=b: nc.tensor.transpose(psum[:, b, 0:32], lamld[:, ri, :], ident[0:32, 0:32]),
             reads=[tb], writes=[ps_b[b]])
        K.op(dve, lambda ri=ri, b=b: V.tensor_copy(out=lam[:, ri, :], in_=psum[:, b, 0:32]), reads=[ps_b[b]], writes=[tb])
        b = next_ps()
        for tt in range(4):
            K.op(pe, lambda ri=ri, b=b, tt=tt: nc.tensor.transpose(psum[:, b, tt * 128:(tt + 1) * 128], cld[:, ri, tt, :], ident[:]),
                 reads=[tb], writes=[ps_b[b]], inc=(tt == 3))
        K.op(dve, lambda ri=ri, b=b: V.tensor_copy(out=craw[:, ri, :, :].rearrange("p g q -> p (g q)"), in_=psum[:, b, :]),
             reads=[ps_b[b]], writes=[tb])

    dt_ = t("dt", (128, 32))
    ar = t("ar", (128, 32))
    ang = t("ang", (128, 32))
    lb = t("lb", (128, 2, 32))
    coef = t("coef", (128, 2, 32))
    tmp = t("tmpa", (128, 4, 32))

    def dv(fn, inc=True):
        K.op(dve, fn, reads=[tb], writes=[tb], inc=inc)

    def ac(fn):
        K.op(act, fn, reads=[tb], writes=[tb])

    ac(lambda: nc.scalar.activation(out=dt_[:], in_=ls[:], func=AF.Exp))
    dv(lambda: V.tensor_tensor(out=ar[:], in0=lam[:, 0, :], in1=dt_[:], op=ALU.mult))
    dv(lambda: V.tensor_tensor(out=ang[:], in0=lam[:, 1, :], in1=dt_[:], op=ALU.mult))

    PR = t("PR", (128, 32, NEC))
    PI = t("PI", (128, 32, NEC))
    A1 = t("A1", (128, 32, NEC))
    A2 = t("A2", (128, 32, NEC))
    A3i = t("A3i", (128, 32, NEC), I32)
    A4 = t("A4", (128, 32, NEC))
    ex_b = extab[:].unsqueeze(1).to_broadcast([128, 32, NEC])

    def bc_g(x):
        return x.unsqueeze(2).to_broadcast([128, 32, NEC])

    dv(lambda: V.tensor_tensor(out=A1[:], in0=ex_b, in1=bc_g(ar[:]), op=ALU.mult))
    ac(lambda: nc.scalar.activation(out=A4[:], in_=A1[:], func=AF.Exp))
    dv(lambda: V.tensor_tensor(out=A1[:], in0=ex_b, in1=bc_g(ang[:]), op=ALU.mult))

    def sin_of(dst, shift):
        dv(lambda: V.tensor_scalar(out=A2[:], in0=A1[:], scalar1=shift + ANG_SHIFT * TWO_PI, scalar2=1.0 / TWO_PI,
                                   op0=ALU.add, op1=ALU.mult))
        dv(lambda: V.tensor_copy(out=A3i[:], in_=A2[:]))
        dv(lambda: V.tensor_copy(out=dst[:], in_=A3i[:]))
        dv(lambda: V.tensor_tensor(out=A2[:], in0=A2[:], in1=dst[:], op=ALU.subtract))
        dv(lambda: V.tensor_scalar(out=dst[:], in0=A2[:], scalar1=0.5, scalar2=None, op0=ALU.is_gt))
        dv(lambda: V.tensor_tensor(out=A2[:], in0=A2[:], in1=dst[:], op=ALU.subtract))
        dv(lambda: V.tensor_scalar(out=dst[:], in0=A2[:], scalar1=-0.5, scalar2=None, op0=ALU.is_lt))
        dv(lambda: V.tensor_tensor(out=A2[:], in0=A2[:], in1=dst[:], op=ALU.add))
        dv(lambda: V.tensor_scalar(out=A2[:], in0=A2[:], scalar1=TWO_PI, scalar2=3.14159, op0=ALU.mult, op1=ALU.min))
        dv(lambda: V.tensor_scalar(out=A2[:], in0=A2[:], scalar1=-3.14159, scalar2=None, op0=ALU.max))
        ac(lambda: nc.scalar.activation(out=dst[:], in_=A2[:], func=AF.Sin))

    sin_of(PI, 0.0)
    sin_of(PR, math.pi / 2)
    dv(lambda: V.tensor_tensor(out=PR[:], in0=PR[:], in1=A4[:], op=ALU.mult))
    dv(lambda: V.tensor_tensor(out=PI[:], in0=PI[:], in1=A4[:], op=ALU.mult))

    ac(lambda: nc.scalar.activation(out=tmp[:, 0, :], in_=ar[:], func=AF.Exp))
    dv(lambda: V.tensor_copy(out=lb[0:64, 0, :], in_=PR[0:64, :, EC_TC + 1]))
    dv(lambda: V.tensor_copy(out=lb[0:64, 1, :], in_=PI[0:64, :, EC_TC + 1]))
    dv(lambda: V.tensor_copy(out=lb[64:128, 0, :], in_=PR[64:128, :, EC_TB + 1]))
    dv(lambda: V.tensor_copy(out=lb[64:128, 1, :], in_=PI[64:128, :, EC_TB + 1]))
    dv(lambda: V.tensor_tensor(out=tmp[:, 0, :], in0=lam[:, 0, :], in1=lam[:, 0, :], op=ALU.mult))
    dv(lambda: V.tensor_tensor(out=tmp[:, 1, :], in0=lam[:, 1, :], in1=lam[:, 1, :], op=ALU.mult))
    dv(lambda: V.tensor_tensor(out=tmp[:, 0, :], in0=tmp[:, 0, :], in1=tmp[:, 1, :], op=ALU.add))
    dv(lambda: V.reciprocal(out=tmp[:, 0, :], in_=tmp[:, 0, :]))
    dv(lambda: V.tensor_scalar(out=tmp[:, 1, :], in0=lb[:, 0, :], scalar1=-1.0, scalar2=None, op0=ALU.add))
    dv(lambda: V.tensor_tensor(out=tmp[:, 2, :], in0=tmp[:, 1, :], in1=lam[:, 0, :], op=ALU.mult))
    dv(lambda: V.tensor_tensor(out=tmp[:, 3, :], in0=lb[:, 1, :], in1=lam[:, 1, :], op=ALU.mult))
    dv(lambda: V.tensor_tensor(out=tmp[:, 2, :], in0=tmp[:, 2, :], in1=tmp[:, 3, :], op=ALU.add))
    dv(lambda: V.tensor_tensor(out=coef[:, 0, :], in0=tmp[:, 2, :], in1=tmp[:, 0, :], op=ALU.mult))
    dv(lambda: V.tensor_tensor(out=tmp[:, 2, :], in0=lb[:, 1, :], in1=lam[:, 0, :], op=ALU.mult))
    dv(lambda: V.tensor_tensor(out=tmp[:, 3, :], in0=tmp[:, 1, :], in1=lam[:, 1, :], op=ALU.mult))
    dv(lambda: V.tensor_tensor(out=tmp[:, 2, :], in0=tmp[:, 2, :], in1=tmp[:, 3, :], op=ALU.subtract))
    dv(lambda: V.tensor_tensor(out=coef[:, 1, :], in0=tmp[:, 2, :], in1=tmp[:, 0, :], op=ALU.mult))
    bb = t("bb", (128, 2, 32, 16))
    t16 = t("t16", (128, 32, 16))

    def bq(x):
        return x.unsqueeze(2).to_broadcast([128, 32, 16])

    dv(lambda: V.tensor_tensor(out=bb[:, 0], in0=braw[:, 0], in1=bq(coef[:, 0, :]), op=ALU.mult))
    dv(lambda: V.tensor_tensor(out=t16[:], in0=braw[:, 1], in1=bq(coef[:, 1, :]), op=ALU.mult))
    dv(lambda: V.tensor_tensor(out=bb[:, 0], in0=bb[:, 0], in1=t16[:], op=ALU.subtract))
    dv(lambda: V.tensor_tensor(out=bb[:, 1], in0=braw[:, 1], in1=bq(coef[:, 0, :]), op=ALU.mult))
    dv(lambda: V.tensor_tensor(out=t16[:], in0=braw[:, 0], in1=bq(coef[:, 1, :]), op=ALU.mult))
    dv(lambda: V.tensor_tensor(out=bb[:, 1], in0=bb[:, 1], in1=t16[:], op=ALU.add))

    if "s5tab" in dbg:
        K.dma(sp, dbg["s5tab"][:, 0:NEC], PR[:, 0, :], reads=[tb])
        K.dma(sp, dbg["s5tab"][:, NEC:2 * NEC], PI[:, 0, :], reads=[tb])
        K.dma(sp, dbg["s5tab"][:, 2 * NEC:2 * NEC + 32], bb[:, 0, 0:2, :].rearrange("p g q -> p (g q)"), reads=[tb])
        K.dma(sp, dbg["s5tab"][:, 2 * NEC + 32:2 * NEC + 64], bb[:, 1, 0:2, :].rearrange("p g q -> p (g q)"), reads=[tb])

    K.op(dve, lambda: V.tensor_copy(out=aL[:, 0, 0, :], in_=PR[:, :, EC_L]), reads=[tb], writes=[aL_b])
    K.op(dve, lambda: V.tensor_copy(out=aL[:, 0, 1, :], in_=PR[:, :, EC_L]), reads=[tb], writes=[aL_b])
    K.op(dve, lambda: V.tensor_scalar(out=aL[:, 1, 0, :], in0=PI[:, :, EC_L], scalar1=-1.0, scalar2=None, op0=ALU.mult), reads=[tb], writes=[aL_b])
    K.op(dve, lambda: V.tensor_copy(out=aL[:, 1, 1, :], in_=PI[:, :, EC_L]), reads=[tb], writes=[aL_b])

    NB = 8 + 8 * JC
    NC8 = 8 * JC
    GB = 2
    xb = [t(f"xb{i}", (128, 2, GB, NB, 16)) for i in range(2)]
    xc = [t(f"xc{i}", (128, 2, GB, NC8, 16)) for i in range(2)]
    xcc = [t(f"xcc{i}", (128, 2, GB, NC8, 16)) for i in range(2)]
    tq = [t(f"tq{i}", (128, GB, NB, 16)) for i in range(2)]
    mA = [t(f"mA{i}", (128, MA_COLS), BF16) for i in range(2)]
    mB = [t(f"mB{i}", (128, MB_COLS), BF16) for i in range(2)]
    dmat = [t(f"dmat{i}", (128, 128)) for i in range(2)]
    t0m = [t(f"t0m{i}", (128, 128)) for i in range(2)]
    xb_b = [Buf() for _ in range(2)]
    xc_b = [Buf() for _ in range(2)]
    xcc_b = [Buf() for _ in range(2)]
    tq_b = [Buf() for _ in range(2)]
    mA_b = [Buf() for _ in range(2)]
    mB_b = [Buf() for _ in range(2)]
    dm_b = [Buf() for _ in range(2)]

    def pwB(x, g0, c0, n):
        return x[:, g0:g0 + GB, c0:c0 + n].unsqueeze(3).to_broadcast([128, GB, n, 16])

    def bbB(ri, g0, n):
        return bb[:, ri, g0:g0 + GB, :].unsqueeze(2).to_broadcast([128, GB, n, 16])

    def ccB(ri, g0, n):
        return craw[:, ri, g0:g0 + GB, :].unsqueeze(2).to_broadcast([128, GB, n, 16])

    ncraw = t("ncraw", (128, 2, 32, 16))
    dv(lambda: V.tensor_scalar(out=ncraw[:], in0=craw[:], scalar1=-1.0, scalar2=None, op0=ALU.mult))

    def nccB(ri, g0, n):
        return ncraw[:, ri, g0:g0 + GB, :].unsqueeze(2).to_broadcast([128, GB, n, 16])

    for gb in range(G // GB):
        g0 = gb * GB
        i = gb % 2
        E2 = pool if (gb % 3 == 2) else dve
        EN = E2.eng

        def tt_(out, in0, in1, op, reads, writes):
            K.op(E2, lambda: EN.tensor_tensor(out=out, in0=in0, in1=in1, op=op), reads=reads, writes=writes)

        for (dst0, c0, n) in ((0, EC_TB, 8), (8, EC_SB, NC8)):
            o_re = xb[i][:, 0, :, dst0:dst0 + n, :]
            o_im = xb[i][:, 1, :, dst0:dst0 + n, :]
            t_ = tq[i][:, :, dst0:dst0 + n, :]
            tt_(o_re, pwB(PR, g0, c0, n), bbB(0, g0, n), ALU.mult, [tb], [xb_b[i]])
            tt_(t_, pwB(PI, g0, c0, n), bbB(1, g0, n), ALU.mult, [tb], [tq_b[i]])
            tt_(o_re, o_re, t_, ALU.subtract, [tq_b[i], xb_b[i]], [xb_b[i]])
            tt_(o_im, pwB(PR, g0, c0, n), bbB(1, g0, n), ALU.mult, [tb], [xb_b[i]])
            tt_(t_, pwB(PI, g0, c0, n), bbB(0, g0, n), ALU.mult, [tb, xb_b[i]], [tq_b[i]])
            tt_(o_im, o_im, t_, ALU.add, [tq_b[i], xb_b[i]], [xb_b[i]])
        for (dstt, dstb, c0) in ((xc, xc_b, EC_TC), (xcc, xcc_b, EC_CC)):
            o_re = dstt[i][:, 0]
            o_im = dstt[i][:, 1]
            t_ = tq[i][:, :, 0:NC8, :]
            tt_(o_re, pwB(PR, g0, c0, NC8), ccB(0, g0, NC8), ALU.mult, [tb], [dstb[i]])
            tt_(t_, pwB(PI, g0, c0, NC8), ccB(1, g0, NC8), ALU.mult, [tb], [tq_b[i]])
            tt_(o_re, o_re, t_, ALU.subtract, [tq_b[i], dstb[i]], [dstb[i]])
            tt_(o_im, pwB(PI, g0, c0, NC8), nccB(0, g0, NC8), ALU.mult, [tb], [dstb[i]])
            tt_(t_, pwB(PR, g0, c0, NC8), nccB(1, g0, NC8), ALU.mult, [tb, dstb[i]], [tq_b[i]])
            tt_(o_im, o_im, t_, ALU.add, [tq_b[i], dstb[i]], [dstb[i]])

        for gl in range(GB):
            g = g0 + gl
            j2 = g % 2
            for d in range(2):
                b = next_ps()
                rows = slice(d * 64, (d + 1) * 64)
                for ri in range(2):
                    K.op(pe, lambda: nc.tensor.matmul(
                        psum[:, b, 0:JC * 128],
                        xb[i][rows, ri, gl, 0:8, :].rearrange("p s q -> p (s q)"),
                        xc[i][rows, ri, gl, :, :].rearrange("p c q -> p (c q)"),
                        start=(ri == 0), stop=(ri == 1)),
                        reads=[xb_b[i], xc_b[i]], writes=[ps_b[b]], inc=(ri == 1))
                base = d * JC * 128
                if d == 0:
                    K.op(dve, lambda: V.tensor_scalar(out=dmat[j2][:], in0=ident[:], scalar1=dvec[:, g:g + 1],
                                                      scalar2=None, op0=ALU.mult), reads=[tb], writes=[dm_b[j2]])
                    K.op(dve, lambda: V.tensor_tensor(out=t0m[j2][:], in0=psum[:, b, 0:128], in1=tmask[:, 0, :], op=ALU.mult),
                         reads=[ps_b[b], tb, dm_b[j2]], writes=[dm_b[j2]])
                    K.op(dve, lambda: V.tensor_tensor(out=mB[j2][:, base:base + 128], in0=t0m[j2][:], in1=dmat[j2][:], op=ALU.add),
                         reads=[dm_b[j2]], writes=[mB_b[j2]])
                else:
                    K.op(dve, lambda: V.tensor_tensor(out=mB[j2][:, base:base + 128], in0=psum[:, b, 0:128],
                                                      in1=tmask[:, 1, :], op=ALU.mult),
                         reads=[ps_b[b], tb], writes=[mB_b[j2]])
                if JC > 1:
                    K.op(act, lambda: nc.scalar.copy(out=mB[j2][:, base + 128:base + JC * 128], in_=psum[:, b, 128:JC * 128]),
                         reads=[ps_b[b]], writes=[mB_b[j2]])
            K.op(act, lambda: nc.scalar.copy(out=mB[j2][:, 2 * JC * 128:3 * JC * 128],
                                             in_=xcc[i][:, 0, gl].rearrange("p c q -> p (c q)")),
                 reads=[xcc_b[i]], writes=[mB_b[j2]])
            K.op(act, lambda: nc.scalar.copy(out=mB[j2][:, 3 * JC * 128:4 * JC * 128],
                                             in_=xcc[i][:, 1, gl].rearrange("p c q -> p (c q)")),
                 reads=[xcc_b[i]], writes=[mB_b[j2]])
            K.dma(sp, matsB[g], mB[j2][:], reads=[mB_b[j2]], writes=[matsB_b[g]])
            for ri in range(2):
                b = next_ps()
                for j in range(JC):
                    K.op(pe, lambda: nc.tensor.transpose(
                        psum[:, b, j * 128:(j + 1) * 128],
                        xb[i][:, ri, gl, 8 + 8 * j:16 + 8 * j, :].rearrange("p s q -> p (s q)"), ident[:]),
                        reads=[xb_b[i]], writes=[ps_b[b]], inc=(j == JC - 1))
                K.op(act, lambda: nc.scalar.copy(out=mA[j2][:, ri * JC * 128:(ri + 1) * JC * 128], in_=psum[:, b, 0:JC * 128]),
                     reads=[ps_b[b]], writes=[mA_b[j2]])
            K.dma(sp, matsA[g], mA[j2][:], reads=[mA_b[j2]], writes=[matsA_b[g]])
    if "matsB0" in dbg:
        pass


def mixer_phase(nc, K, es, sb, psum, ps_b, next_ps, P, l, nseq, x_src, out, out_b, ident, identb, ones_bf,
                eb_scr, ebB, epsb, cb, aL, aL_b, matsA, matsB, matsA_b, matsB_b, u_scr, z_scr, u_scr_b, z_scr_b, dbg):
    pe, act, dve, pool, sp = K.pe, K.act, K.dve, K.pool, K.sp
    V = nc.vector
    G = NG
    wi = sb(es, "wi", (128, 8, IN_W), BF16)
    wg = sb(es, "wg", (128, 4, 512), BF16)
    wo = sb(es, "wo", (128, 8, 1024), BF16)
    gpm = sb(es, "gpm", (128, 8))
    gso = sb(es, "gso", (128, 8))
    bglu = sb(es, "bglu", (128, 4))
    esink = sb(es, "esink", (128, 8))
    gpost = sb(es, "gpost", (128, 1024))
    wi_b, wg_b, wo_b, pb = Buf(), Buf(), Buf(), Buf()
    eb = sb(es, "eb", (128, 3, 2, 4, 128), BF16)
    K.dma(sp, eb[:].rearrange("p o k c q -> p (o k c q)"), eb_scr, reads=[ebB], writes=[pb])
    ebB = pb
    with ExitStack() as es1:
        ld = sb(es1, "ldm", (8, 128))
        ld_b = Buf()
        LC = lambda dst, src, n: load_cols(nc, K, psum, ps_b, next_ps, ident, cb, ld, ld_b, dst, pb, src, n)
        LC(gpm[:], P["pre_mix_norm"][l], 8)
        LC(gso[:, 0:4], P["ssm_out_norm"][l], 4)
        LC(gso[:, 4:8], P["attn_out_norm"][l], 4)
        LC(bglu[:], P["b_glu"][l], 4)
        K.barrier()
    K.dma(sp, esink[:], P["attn_sink"][l].partition_broadcast(128), writes=[pb])
    K.dma(sp, gpost[:], P["post_mix_norm"][l].partition_broadcast(128), writes=[pb])
    K.op(act, lambda: nc.scalar.activation(out=esink[:], in_=esink[:], func=AF.Exp), reads=[pb], writes=[pb])
    with ExitStack() as es2:
        engs = [act, dve]
        stage = make_staging(es2, sb, "m")
        load_cast_weight(nc, K, stage, wi, wi_b, [P["w_in"][l, k * 128:(k + 1) * 128, :] for k in range(8)], IN_W,
                         gpm, pb, engs)
        load_cast_weight(nc, K, stage, wg, wg_b, [P["w_glu"][l, k * 128:(k + 1) * 128, :] for k in range(4)], 512,
                         None, pb, engs)
        load_cast_weight(nc, K, stage, wo, wo_b, [P["w_out"][l, k * 128:(k + 1) * 128, :] for k in range(8)], 1024,
                         gso, pb, engs)
        K.barrier()

    X1 = sb(es, "X1", (128, 4, SEQ), BF16)
    hTlo = sb(es, "hTlo", (128, 4, SEQ), BF16)
    hThi = sb(es, "hThi", (128, 4, SEQ), BF16)
    qT = sb(es, "qT", (128, 4, SEQ), BF16)
    UY = sb(es, "UY", (128, 4, SEQ), BF16)
    kT = sb(es, "kT", (128, SEQ), BF16)
    vaug = sb(es, "vaug", (128, NT, 2, 66), BF16)
    xt = [sb(es, f"xt{i}", (128, 1024)) for i in range(3)]
    hs = xt
    xo = xt
    junk = sb(es, "junk", (128, 1024), BF16)
    ssq = sb(es, "ssq", (128, 4, NT))
    rs = sb(es, "rs", (128, 4, NT))
    B = lambda n=1: [Buf() for _ in range(n)]
    X1_b, hTlo_b, hThi_b, qT_b, UY_b, kT_b, va_b, H_b, Ein_b = (Buf() for _ in range(9))
    Hf_b, Hb_b = Buf(), Buf()
    xt_b, mAb_b, mBb_b, et_b, pt_b, oa_b, den_b, gate_b, mt_b = B(3), B(2), B(2), B(3), B(6), B(2), B(2), B(2), B(2)
    hs_b = xt_b
    xo_b = xt_b
    junk_b, ssq_b, rs_b, st_b = Buf(), [Buf() for _ in range(4)], [Buf() for _ in range(4)], [Buf(), Buf()]
    m1s_b = [(Buf(), Buf()) for _ in range(4)]
    K.op(dve, lambda: V.memset(vaug[:, :, :, 64:66], 1.0), writes=[va_b])
    hT_b = [hTlo_b, hThi_b]
    hT = [hTlo, hThi]
    Uf = UY[:].rearrange("p c t -> p (c t)").rearrange("p (g s) -> p g s", g=NG)
    Zf = hTlo[:].rearrange("p c t -> p (c t)").rearrange("p (g s) -> p g s", g=NG)
    ysT, ysT_b = qT, qT_b
    yaT, yaT_b = hThi, hThi_b
    ysq, ysq_b = UY, UY_b

    for s in range(nseq):
        tok0 = s * SEQ
        def m1_a(tt):
            i = tt % 3
            r0 = tok0 + tt * 128
            rd = [out_b[s][tt]] if x_src is out else []
            K.dma(sp, xt[i][:], x_src[r0:r0 + 128, :], reads=rd, writes=[xt_b[i]])
            sq_b, r_b = m1s_b[tt % 4]
            K.op(act, lambda: nc.scalar.activation(out=junk[:], in_=xt[i][:], func=AF.Square, accum_out=ssq[:, 0, tt:tt + 1]),
                 reads=[xt_b[i]], writes=[junk_b, sq_b])
            K.op(act, lambda: nc.scalar.activation(out=rs[:, 0, tt:tt + 1], in_=ssq[:, 0, tt:tt + 1], func=AF.Sqrt,
                                                   bias=epsb[:, 0:1], scale=1.0 / D_MODEL), reads=[sq_b, cb], writes=[r_b])
            K.op(dve, lambda: V.reciprocal(out=rs[:, 0, tt:tt + 1], in_=rs[:, 0, tt:tt + 1]), reads=[r_b], writes=[r_b])
            K.op(dve, lambda: V.tensor_scalar(out=hs[i][:], in0=xt[i][:], scalar1=rs[:, 0, tt:tt + 1], scalar2=None, op0=ALU.mult),
                 reads=[xt_b[i], r_b], writes=[hs_b[i]])

        def m1_b(tt):
            i = tt % 3
            b = next_ps(2)
            for c in range(8):
                K.op(pe, lambda: nc.tensor.transpose(psum[:, b + c // 4, (c % 4) * 128:(c % 4 + 1) * 128],
                                                     hs[i][:, c * 128:(c + 1) * 128], ident[:]),
                     reads=[hs_b[i], cb], writes=[ps_b[b], ps_b[b + 1]], inc=(c == 7))
            K.op(dve, lambda: V.tensor_copy(out=hTlo[:, :, tt * 128:(tt + 1) * 128],
                                            in_=psum[:, b, :].rearrange("p (c t) -> p c t", c=4)),
                 reads=[ps_b[b]], writes=[hTlo_b])
            K.op(act, lambda: nc.scalar.copy(out=hThi[:, :, tt * 128:(tt + 1) * 128],
                                             in_=psum[:, b + 1, :].rearrange("p (c t) -> p c t", c=4)),
                 reads=[ps_b[b + 1]], writes=[hThi_b])

        m1_a(0)
        for tt in range(NT):
            if tt + 1 < NT:
                m1_a(tt + 1)
            m1_b(tt)
        for m in range(9):
            if m == 4:
                K.dma(sp, u_scr.rearrange("(c r) t -> r c t", c=4), X1[:], reads=[X1_b], writes=[u_scr_b])
                for s8 in range(8):
                    K.dma(sp, Uf[s8 * 16:(s8 + 1) * 16, :, :],
                          u_scr.rearrange("(g q) (s sub) -> s q g sub", q=16, s=8)[s8], reads=[u_scr_b], writes=[UY_b])
            for tg in range(4):
                b = next_ps()
                for k in range(8):
                    K.op(pe, lambda: nc.tensor.matmul(psum[:, b, :], wi[:, k, m * 128:(m + 1) * 128],
                                                      hT[k // 4][:, k % 4, tg * 512:(tg + 1) * 512],
                                                      start=(k == 0), stop=(k == 7)),
                         reads=[wi_b, hT_b[k // 4]], writes=[ps_b[b]], inc=(k == 7))
                if m < 4:
                    K.op(act, lambda: nc.scalar.copy(
                        out=X1[:, m, :].rearrange("p (s sub) -> p sub s", s=8)[:, tg * 64:(tg + 1) * 64, :],
                        in_=psum[:, b, :].rearrange("p (sub s) -> p sub s", s=8)), reads=[ps_b[b]], writes=[X1_b])
                elif m < 8:
                    K.op(dve, lambda: V.tensor_copy(out=qT[:, m - 4, tg * 512:(tg + 1) * 512], in_=psum[:, b, :]),
                         reads=[ps_b[b]], writes=[qT_b])
                else:
                    K.op(act, lambda: nc.scalar.copy(out=kT[:, tg * 512:(tg + 1) * 512], in_=psum[:, b, :]),
                         reads=[ps_b[b]], writes=[kT_b])
        for tt in range(NT):
            b = next_ps()
            for k in range(8):
                K.op(pe, lambda: nc.tensor.matmul(psum[:, b, 0:128], hT[k // 4][:, k % 4, tt * 128:(tt + 1) * 128],
                                                  wi[:, k, 1152:1280], start=(k == 0), stop=(k == 7)),
                     reads=[wi_b, hT_b[k // 4]], writes=[ps_b[b]], inc=(k == 7))
            K.op(dve, lambda: V.tensor_copy(out=vaug[:, tt, :, 0:64], in_=psum[:, b, 0:128].rearrange("p (k d) -> p k d", k=2)),
                 reads=[ps_b[b]], writes=[va_b])
        if s == 0 and "uT" in dbg:
            K.dma(sp, dbg["uT"], X1[:], reads=[X1_b])
            K.dma(sp, dbg["qT"], qT[:], reads=[qT_b])
            K.dma(sp, dbg["kT"], kT[:], reads=[kT_b])
            K.dma(sp, dbg["vaug"], vaug[:], reads=[va_b])

        e_s5 = ExitStack()
        H = sb(e_s5, "H", (128, 2, NG, NCH + 1))
        Ein = sb(e_s5, "Ein", (128, 2, NG, NCH), BF16)
        st1 = sb(e_s5, "st1", (128, 2, NG))
        st2 = sb(e_s5, "st2", (128, 2, NG))
        mAb = [sb(e_s5, f"mAb{i}", (128, MA_COLS), BF16) for i in range(2)]
        mBb = [sb(e_s5, f"mBb{i}", (128, MB_COLS), BF16) for i in range(2)]
        e_att = ExitStack()
        et = [sb(e_att, f"et{i}", (128, 512), BF16) for i in range(3)]
        pt = [sb(e_att, f"pt{i}", (128, 512), BF16) for i in range(6)]
        oa = [sb(e_att, f"oa{i}", (128, 512)) for i in range(2)]
        den = [sb(e_att, f"den{i}", (128, 4)) for i in range(2)]

        if l == 0 and s == 0:
            print("mixer sbuf bytes remaining", nc.sbuf_bytes_remaining)

        def att_S(qb, kv):
            kbs = [kb for kb in (qb - 1, qb, qb + 1) if 0 <= kb < NT]
            rows = slice(kv * 64, (kv + 1) * 64)
            pts = []
            for kb in kbs:
                ie = K_rr(K, "et", 3)
                ip = K_rr(K, "pt", 6)
                b = next_ps()
                K.op(pe, lambda: nc.tensor.matmul(psum[:, b, :], kT[rows, kb * 128:(kb + 1) * 128],
                                                  qT[rows, :, qb * 128:(qb + 1) * 128], start=True, stop=True),
                     reads=[kT_b, qT_b], writes=[ps_b[b]])
                K.op(act, lambda: nc.scalar.activation(out=et[ie][:], in_=psum[:, b, :], func=AF.Exp, scale=0.125),
                     reads=[ps_b[b]], writes=[et_b[ie]])
                EM = pool if qb >= NT // 2 else dve
                K.op(EM, lambda: EM.eng.tensor_tensor(out=pt[ip][:], in0=et[ie][:],
                                                      in1=eb[:, kb - qb + 1, kv].rearrange("p c q -> p (c q)"), op=ALU.mult),
                     reads=[et_b[ie], ebB], writes=[pt_b[ip]])
                pts.append((ip, kb))
            return pts

        def att_P(qb, kv, pts):
            b = next_ps()
            for c in range(4):
                for n_, (ip, kb) in enumerate(pts):
                    K.op(pe, lambda: nc.tensor.matmul(psum[:, b, c * 65:(c + 1) * 65], pt[ip][:, c * 128:(c + 1) * 128],
                                                      vaug[:, kb, kv, 0:65], start=(n_ == 0), stop=(n_ == len(pts) - 1)),
                         reads=[pt_b[ip], va_b], writes=[ps_b[b]], inc=(c == 3 and n_ == len(pts) - 1))
            io = qb % 2
            ov = psum[:, b, 0:260].rearrange("p (c d) -> p c d", c=4)
            K.op(dve, lambda: V.tensor_tensor(out=den[kv][:], in0=ov[:, :, 64], in1=esink[:, kv * 4:(kv + 1) * 4], op=ALU.add),
                 reads=[ps_b[b], pb], writes=[den_b[kv]])
            K.op(dve, lambda: V.reciprocal(out=den[kv][:], in_=den[kv][:]), reads=[den_b[kv]], writes=[den_b[kv]])
            K.op(dve, lambda: V.tensor_tensor(out=oa[io][:, kv * 256:(kv + 1) * 256].rearrange("p (c d) -> p c d", c=4),
                                              in0=ov[:, :, 0:64], in1=den[kv][:].unsqueeze(2).to_broadcast([128, 4, 64]),
                                              op=ALU.mult), reads=[ps_b[b], den_b[kv]], writes=[oa_b[io]])

        def att_F(qb):
            io = qb % 2
            K.op(act, lambda: nc.scalar.activation(out=junk[:, 0:512], in_=oa[io][:], func=AF.Square, accum_out=ssq[:, 2, qb:qb + 1]),
                 reads=[oa_b[io]], writes=[junk_b, ssq_b[2]])
            b = next_ps()
            for c in range(4):
                K.op(pe, lambda: nc.tensor.transpose(psum[:, b, c * 128:(c + 1) * 128], oa[io][:, c * 128:(c + 1) * 128], ident[:]),
                     reads=[oa_b[io], cb], writes=[ps_b[b]], inc=(c == 3))
            K.op(act, lambda: nc.scalar.copy(out=yaT[:, :, qb * 128:(qb + 1) * 128], in_=psum[:, b, :].rearrange("p (c t) -> p c t", c=4)),
                 reads=[ps_b[b]], writes=[yaT_b])

        def scan_step(c):
            hs_ = (slice(0, NG // 2), slice(NG // 2, NG))
            hb_ = (Hf_b, Hb_b)
            for (in_sw, ai) in ((False, 0), (True, 1)):
                for h2 in range(2):
                    gs = hs_[h2]
                    src = H[:, ::-1, gs, c] if in_sw else H[:, :, gs, c]
                    K.op(dve, lambda: V.tensor_tensor(out=st1[:, :, gs], in0=src, in1=aL[:, ai, :, gs], op=ALU.mult),
                         reads=[H_b, aL_b, hb_[h2]], writes=[hb_[h2]])
                for h2 in range(2):
                    gs = hs_[h2]
                    K.op(dve, lambda: V.tensor_tensor(out=H[:, :, gs, c + 1], in0=H[:, :, gs, c + 1], in1=st1[:, :, gs], op=ALU.add),
                         reads=[H_b, aL_b, hb_[h2]], writes=[hb_[h2]])

        K.op(dve, lambda: V.memset(H[:, :, :, 0:1], 0.0), writes=[H_b, Hf_b, Hb_b])

        def mA_load(g):
            K.dma(sp, mAb[g % 2][:], matsA[g], reads=[matsA_b[g]], writes=[mAb_b[g % 2]])

        def passA_batch(g0):
            b = next_ps()
            for gl in range(4):
                g = g0 + gl
                im = g % 2
                for ri in range(2):
                    for j in range(JC):
                        K.op(pe, lambda: nc.tensor.matmul(
                            psum[:, b, (ri * 4 + gl) * NCH:(ri * 4 + gl + 1) * NCH],
                            mAb[im][:, (ri * JC + j) * 128:(ri * JC + j + 1) * 128],
                            Uf[:, g, :].rearrange("p (c j) -> p c j", j=JC)[:, :, j],
                            start=(j == 0), stop=(j == JC - 1)),
                            reads=[mAb_b[im], UY_b], writes=[ps_b[b]], inc=(ri == 1 and j == JC - 1))
                if g + 2 < G:
                    mA_load(g + 2)
            pv = psum[:, b, :].rearrange("p (r g c) -> p r g c", r=2, g=4)
            K.op(act, lambda: nc.scalar.copy(out=H[0:64, :, g0:g0 + 4, 1:NCH + 1], in_=pv[0:64]), reads=[ps_b[b]], writes=[H_b, Hf_b, Hb_b])
            K.op(act, lambda: nc.scalar.copy(out=H[64:128, :, g0:g0 + 4, 1:NCH + 1][:, :, :, ::-1], in_=pv[64:128]), reads=[ps_b[b]], writes=[H_b, Hf_b, Hb_b])

        assert NCH % (NT // 2) == 0 and G // 4 == NT // 2
        mA_load(0)
        mA_load(1)
        tasks = [(qb, kv) for qb in range(NT) for kv in range(2)]
        spq = NCH // (NT // 2)
        cur = att_S(*tasks[0])
        fin = None
        for ti, (qb, kv) in enumerate(tasks):
            nxt = att_S(*tasks[ti + 1]) if ti + 1 < len(tasks) else None
            att_P(qb, kv, cur)
            cur = nxt
            if fin is not None:
                att_F(fin)
                fin = None
            if kv == 1:
                fin = qb
                if qb < NT // 2:
                    passA_batch(4 * qb)
                else:
                    for c in range((qb - NT // 2) * spq, (qb - NT // 2 + 1) * spq):
                        scan_step(c)
        att_F(fin)
        K.op(dve, lambda: V.tensor_copy(out=Ein[0:64], in_=H[0:64, :, :, 0:NCH]), reads=[H_b, Hf_b, Hb_b], writes=[Ein_b])
        K.op(act, lambda: nc.scalar.copy(out=Ein[64:128], in_=H[64:128, :, :, 0:NCH][:, :, :, ::-1]), reads=[H_b, Hf_b, Hb_b], writes=[Ein_b])
        def mB_load(g):
            K.dma(sp, mBb[g % 2][:], matsB[g], reads=[matsB_b[g]], writes=[mBb_b[g % 2]])
        mB_load(0)
        mB_load(1)
        for g0 in range(0, G, 2):
            b = next_ps()
            K.op(dve, lambda: V.memset(psum[:, b, :], 0.0), writes=[ps_b[b]])
            for gl in range(2):
                g = g0 + gl
                im = g % 2
                Ug = Uf[:, g, :].rearrange("p (c j) -> p c j", j=JC)
                Yg = psum[:, b, gl * 256:(gl + 1) * 256].rearrange("p (c j) -> p c j", j=JC)
                mm = []
                for d in range(JC):
                    mm.append((Yg[:, :, d:JC], mBb[im][:, d * 128:(d + 1) * 128], Ug[:, :, 0:JC - d], [UY_b]))
                    mm.append((Yg[:, :, 0:JC - d], mBb[im][:, (JC + d) * 128:(JC + d + 1) * 128], Ug[:, :, d:JC], [UY_b]))
                for j in range(JC):
                    mm.append((Yg[:, :, j], mBb[im][:, (2 * JC + j) * 128:(2 * JC + j + 1) * 128], Ein[:, 0, g, :], [Ein_b]))
                    mm.append((Yg[:, :, j], mBb[im][:, (3 * JC + j) * 128:(3 * JC + j + 1) * 128], Ein[:, 1, g, :], [Ein_b]))
                for n_, (o_, l_, r_, rd_) in enumerate(mm):
                    K.op(pe, lambda: nc.tensor.matmul(o_, l_, r_, start=False, stop=(n_ == len(mm) - 1), skip_group_check=True),
                         reads=[mBb_b[im]] + rd_, writes=[ps_b[b]], inc=(n_ == len(mm) - 1))
                if g + 2 < G:
                    mB_load(g + 2)
            K.op(act, lambda: nc.scalar.activation(out=Zf[:, g0:g0 + 2, :], in_=psum[:, b, :].rearrange("p (g s) -> p g s", g=2),
                                                   func=AF.Gelu_apprx_tanh), reads=[ps_b[b]], writes=[hTlo_b])
        for s8 in range(8):
            K.dma(sp, z_scr.rearrange("(g q) (s sub) -> s q g sub", q=16, s=8)[s8], Zf[s8 * 16:(s8 + 1) * 16, :, :],
                  reads=[hTlo_b], writes=[z_scr_b])
        K.dma(sp, X1[:], z_scr.rearrange("(c r) t -> r c t", c=4), reads=[z_scr_b], writes=[X1_b])
        K.barrier()
        e_att.close()
        gate = [sb(e_s5, f"gate{i}", (128, 512)) for i in range(2)]
        for m in range(4):
            for tg in range(4):
                b = next_ps()
                ig = (m * 4 + tg) % 2
                for k in range(4):
                    K.op(pe, lambda: nc.tensor.matmul(psum[:, b, :], wg[:, k, m * 128:(m + 1) * 128], X1[:, k, tg * 512:(tg + 1) * 512],
                                                      start=(k == 0), stop=(k == 3)),
                         reads=[wg_b, X1_b], writes=[ps_b[b]], inc=(k == 3))
                K.op(act, lambda: nc.scalar.activation(out=gate[ig][:], in_=psum[:, b, :], func=AF.Sigmoid, bias=bglu[:, m:m + 1]),
                     reads=[ps_b[b], pb], writes=[gate_b[ig]])
                nat = lambda tns: tns[:, m, :].rearrange("p (sub s) -> p s sub", s=8)[:, 2 * tg:2 * tg + 2, :]
                K.op(dve, lambda: V.tensor_tensor(out=nat(ysT), in0=X1[:, m, tg * 512:(tg + 1) * 512].rearrange("p (s sub) -> p s sub", s=2),
                                                  in1=gate[ig][:].rearrange("p (s sub) -> p s sub", s=2), op=ALU.mult),
                     reads=[X1_b, gate_b[ig]], writes=[ysT_b])
        K.op(act, lambda: nc.scalar.activation(out=ysq[:], in_=ysT[:], func=AF.Square), reads=[ysT_b], writes=[ysq_b])

        K.barrier()
        e_s5.close()
        if s == 0 and "ysT" in dbg:
            K.dma(sp, dbg["ysT"], ysT[:], reads=[ysT_b])
            K.dma(sp, dbg["yaT"], yaT[:], reads=[yaT_b])
            K.dma(sp, dbg["zT"], X1[:], reads=[X1_b])
        e_m4 = ExitStack()
        mt = [sb(e_m4, f"mt{i}", (128, 1024)) for i in range(2)]
        b = next_ps()
        for tt in range(NT):
            for k in range(4):
                K.op(pe, lambda: nc.tensor.matmul(psum[:, b, tt:tt + 1], ysq[:, k, tt * 128:(tt + 1) * 128], ones_bf[:, 0:1],
                                                  start=(k == 0), stop=(k == 3)),
                     reads=[ysq_b, cb], writes=[ps_b[b]], inc=(k == 3 and tt == NT - 1))
        K.op(act, lambda: nc.scalar.activation(out=rs[:, 1, :], in_=psum[:, b, 0:NT], func=AF.Sqrt, bias=epsb[:, 0:1], scale=1.0 / SSM_W),
             reads=[ps_b[b], cb], writes=[rs_b[1]])
        K.op(dve, lambda: V.reciprocal(out=rs[:, 1, :], in_=rs[:, 1, :]), reads=[rs_b[1]], writes=[rs_b[1]])
        K.op(act, lambda: nc.scalar.activation(out=rs[:, 2, :], in_=ssq[:, 2, :], func=AF.Sqrt, bias=epsb[:, 0:1], scale=1.0 / SSM_W),
             reads=[ssq_b[2], cb], writes=[rs_b[2]])
        K.op(dve, lambda: V.reciprocal(out=rs[:, 2, :], in_=rs[:, 2, :]), reads=[rs_b[2]], writes=[rs_b[2]])
        for tt in range(NT):
            i = tt % 2
            r0 = tok0 + tt * 128
            ba = next_ps(2)
            for hf in range(2):
                for k in range(4):
                    K.op(pe, lambda: nc.tensor.matmul(psum[:, ba + hf, :], ysT[:, k, tt * 128:(tt + 1) * 128], wo[:, k, hf * 512:(hf + 1) * 512],
                                                      start=(k == 0), stop=(k == 3)),
                         reads=[ysT_b, wo_b], writes=[ps_b[ba + hf]], inc=(k == 3))
            bb_ = next_ps(2)
            for hf in range(2):
                for k in range(4):
                    K.op(pe, lambda: nc.tensor.matmul(psum[:, bb_ + hf, :], yaT[:, k, tt * 128:(tt + 1) * 128], wo[:, 4 + k, hf * 512:(hf + 1) * 512],
                                                      start=(k == 0), stop=(k == 3)),
                         reads=[yaT_b, wo_b], writes=[ps_b[bb_ + hf]], inc=(k == 3))
            K.op(act, lambda: nc.scalar.activation(out=mt[i][:], in_=psum[:, ba:ba + 2, :].rearrange("p a n -> p (a n)"), func=AF.Copy,
                                                   scale=rs[:, 1, tt:tt + 1]), reads=[ps_b[ba], ps_b[ba + 1], rs_b[1]], writes=[mt_b[i]])
            K.op(dve, lambda: V.scalar_tensor_tensor(out=mt[i][:], in0=psum[:, bb_:bb_ + 2, :].rearrange("p a n -> p (a n)"),
                                                     scalar=rs[:, 2, tt:tt + 1], in1=mt[i][:], op0=ALU.mult, op1=ALU.add),
                 reads=[ps_b[bb_], ps_b[bb_ + 1], rs_b[2], mt_b[i]], writes=[mt_b[i]])
            K.op(act, lambda: nc.scalar.activation(out=junk[:], in_=mt[i][:], func=AF.Square, accum_out=ssq[:, 3, tt:tt + 1]),
                 reads=[mt_b[i]], writes=[junk_b, ssq_b[3]])
            K.op(act, lambda: nc.scalar.activation(out=rs[:, 3, tt:tt + 1], in_=ssq[:, 3, tt:tt + 1], func=AF.Sqrt,
                                                   bias=epsb[:, 0:1], scale=1.0 / D_MODEL), reads=[ssq_b[3], cb], writes=[rs_b[3]])
            K.op(dve, lambda: V.reciprocal(out=rs[:, 3, tt:tt + 1], in_=rs[:, 3, tt:tt + 1]), reads=[rs_b[3]], writes=[rs_b[3]])
            K.op(dve, lambda: V.tensor_tensor(out=mt[i][:], in0=mt[i][:], in1=gpost[:], op=ALU.mult),
                 reads=[mt_b[i], pb], writes=[mt_b[i]])
            rd = [out_b[s][tt]] if x_src is out else []
            K.dma(sp, xt[i][:], x_src[r0:r0 + 128, :], reads=rd, writes=[xt_b[i]])
            K.op(dve, lambda: V.scalar_tensor_tensor(out=xo[i][:], in0=mt[i][:], scalar=rs[:, 3, tt:tt + 1], in1=xt[i][:],
                                                     op0=ALU.mult, op1=ALU.add),
                 reads=[mt_b[i], rs_b[3], xt_b[i]], writes=[xo_b[i]])
            K.dma(sp, out[r0:r0 + 128, :], xo[i][:], reads=[xo_b[i]], writes=[out_b[s][tt]])
        K.barrier()
        e_m4.close()


_RR = {}


def K_rr(K, key, n):
    v = _RR.get(key, 0)
    _RR[key] = (v + 1) % n
    return v


def ffn_phase(nc, K, es, sb, psum, ps_b, next_ps, P, l, nseq, out, out_b, ident, identb, epsb, cb, dbg):
    pe, act, dve, pool, sp = K.pe, K.act, K.dve, K.pool, K.sp
    V = nc.vector
    wu = sb(es, "wu", (128, 8, 2 * D_FF), BF16)
    wd = sb(es, "wd", (128, NFB, 1024), BF16)
    gpf = sb(es, "gpf", (128, 8))
    cw = sb(es, "cw", (128, 4, 2 * NFB))
    gpost = sb(es, "gpost2", (128, 1024))
    wu_b, wd_b, pb = Buf(), Buf(), Buf()
    with ExitStack() as es1:
        ld = sb(es1, "ldf", (2 * NFB, 128))
        ld_b = Buf()
        LC = lambda dst, src, n: load_cols(nc, K, psum, ps_b, next_ps, ident, cb, ld, ld_b, dst, pb, src, n)
        LC(gpf[:], P["pre_ffn_norm"][l], 8)
        for j in range(3):
            LC(cw[:, j, :], P["conv_w"][l, j], 2 * NFB)
        LC(cw[:, 3, :], P["conv_b"][l], 2 * NFB)
        K.barrier()
    K.dma(sp, gpost[:], P["post_ffn_norm"][l].partition_broadcast(128), writes=[pb])
    with ExitStack() as es2:
        engs = [act, dve]
        stage = make_staging(es2, sb, "f")
        load_cast_weight(nc, K, stage, wu, wu_b, [P["w_up"][l, k * 128:(k + 1) * 128, :] for k in range(8)], 2 * D_FF,
                         gpf, pb, engs)
        load_cast_weight(nc, K, stage, wd, wd_b, [P["w_down"][l, k * 128:(k + 1) * 128, :] for k in range(NFB)], 1024,
                         None, pb, engs)
        K.barrier()

    HW = 1026
    h2T = sb(es, "h2T", (128, 8, HW), BF16)
    NPRE = 4
    gbuf = sb(es, "gbuf", (128, NFB - NPRE, 384), BF16)
    gpre = [sb(es, f"gpre{i}", (128, NPRE, 384), BF16) for i in range(2)]
    gbuf_b = [Buf() for _ in range(NFB - NPRE)]
    gpre_b = [[Buf() for _ in range(NPRE)] for _ in range(2)]

    def gslot(kpar, fb):
        if fb < NPRE:
            return gpre[kpar], fb, gpre_b[kpar][fb]
        return gbuf, fb - NPRE, gbuf_b[fb - NPRE]
    pending = []
    tgk = [0]
    halo = sb(es, "halo", (128, 8, 2), BF16)
    xt = [sb(es, f"fxt{i}", (128, 1024)) for i in range(3)]
    hs = xt
    junk = sb(es, "fjunk", (128, 1024), BF16)
    cv = [sb(es, f"cv{i}", (128, 384)) for i in range(3)]
    cg = [sb(es, f"cg{i}", (128, 384)) for i in range(3)]
    gg = [sb(es, f"gg{i}", (128, 384)) for i in range(3)]
    yt = [sb(es, f"yt{i}", (128, 1024)) for i in range(1)] * 2
    xo = xt
    st = sb(es, "fst", (128, 4))
    stf = sb(es, "fstf", (128, 2, 4))
    stf_b = [Buf() for _ in range(4)]
    std = sb(es, "fstd", (128, 2, 2))
    std_b = [Buf() for _ in range(2)]
    B = lambda n=1: [Buf() for _ in range(n)]
    h2T_b, g_b, junk_b, st_b = Buf(), Buf(), Buf(), Buf()
    xt_b, cv_b, cg_b, gg_b = B(3), B(3), B(3), B(3)
    yt_b = B(1) * 2
    hs_b = xt_b
    xo_b = xt_b
    def emit_down(s, tok0, hs0, t0, ln, kpar):
        for ti in range(ln // 128):
            tt = (hs0 + t0) // 128 + ti
            r0 = tok0 + tt * 128
            i = tt % 2
            b = next_ps(2)
            for hf in range(2):
                for fb in range(NFB):
                    gt_, gi_, gb_ = gslot(kpar, fb)
                    K.op(pe, lambda: nc.tensor.matmul(psum[:, b + hf, :], gt_[:, gi_, ti * 128:(ti + 1) * 128],
                                                      wd[:, fb, hf * 512:(hf + 1) * 512], start=(fb == 0), stop=(fb == NFB - 1)),
                         reads=[gb_, wd_b], writes=[ps_b[b + hf]], inc=(fb == NFB - 1))
            pv = psum[:, b:b + 2, :].rearrange("p a n -> p (a n)")
            K.op(act, lambda: nc.scalar.activation(out=junk[:], in_=pv, func=AF.Square, accum_out=std[:, 0, i:i + 1]),
                 reads=[ps_b[b], ps_b[b + 1]], writes=[junk_b, std_b[i]])
            K.op(act, lambda: nc.scalar.activation(out=std[:, 1, i:i + 1], in_=std[:, 0, i:i + 1], func=AF.Sqrt, bias=epsb[:, 0:1], scale=1.0 / D_MODEL),
                 reads=[std_b[i], cb], writes=[std_b[i]])
            K.op(dve, lambda: V.reciprocal(out=std[:, 1, i:i + 1], in_=std[:, 1, i:i + 1]), reads=[std_b[i]], writes=[std_b[i]])
            K.op(dve, lambda: V.tensor_tensor(out=yt[i][:], in0=pv, in1=gpost[:], op=ALU.mult),
                 reads=[ps_b[b], ps_b[b + 1], pb], writes=[yt_b[i]])
            K.dma(sp, xt[i][:], out[r0:r0 + 128, :], reads=[out_b[s][tt]], writes=[xt_b[i]])
            K.op(dve, lambda: V.scalar_tensor_tensor(out=xo[i][:], in0=yt[i][:], scalar=std[:, 1, i:i + 1], in1=xt[i][:],
                                                     op0=ALU.mult, op1=ALU.add),
                 reads=[yt_b[i], std_b[i], xt_b[i]], writes=[xo_b[i]])
            K.dma(sp, out[r0:r0 + 128, :], xo[i][:], reads=[xo_b[i]], writes=[out_b[s][tt]])

    it = [0]
    if l == 0:
        print("ffn sbuf bytes remaining", nc.sbuf_bytes_remaining)

    for s in range(nseq):
        tok0 = s * SEQ
        for half in range(2):
            hs0 = half * 1024
            t_lo = hs0 // 128 - 1
            if half == 1:
                K.op(dve, lambda: V.tensor_copy(out=h2T[:, :, 0:1], in_=halo[:, :, 0:1]), reads=[h2T_b], writes=[h2T_b])
            tiles = [tt for tt in range(t_lo, t_lo + 10) if not (tt < 0 or tt >= NT or (half == 1 and tt == t_lo))]

            def fill_a(tt):
                i = it[0] % 3
                js = it[0] % 4
                it[0] += 1
                r0 = tok0 + tt * 128
                K.dma(sp, xt[i][:], out[r0:r0 + 128, :], reads=[out_b[s][tt]], writes=[xt_b[i]])
                K.op(act, lambda: nc.scalar.activation(out=junk[:], in_=xt[i][:], func=AF.Square, accum_out=stf[:, 0, js:js + 1]),
                     reads=[xt_b[i]], writes=[junk_b, stf_b[js]])
                K.op(act, lambda: nc.scalar.activation(out=stf[:, 1, js:js + 1], in_=stf[:, 0, js:js + 1], func=AF.Sqrt, bias=epsb[:, 0:1], scale=1.0 / D_MODEL),
                     reads=[stf_b[js], cb], writes=[stf_b[js]])
                K.op(dve, lambda: V.reciprocal(out=stf[:, 1, js:js + 1], in_=stf[:, 1, js:js + 1]), reads=[stf_b[js]], writes=[stf_b[js]])
                K.op(dve, lambda: V.tensor_scalar(out=hs[i][:], in0=xt[i][:], scalar1=stf[:, 1, js:js + 1], scalar2=None, op0=ALU.mult),
                     reads=[xt_b[i], stf_b[js]], writes=[hs_b[i]])
                return i

            def fill_b(tt, i):
                b = next_ps(2)
                for c in range(8):
                    K.op(pe, lambda: nc.tensor.transpose(psum[:, b + c // 4, (c % 4) * 128:(c % 4 + 1) * 128],
                                                         hs[i][:, c * 128:(c + 1) * 128], ident[:]),
                         reads=[hs_b[i], cb], writes=[ps_b[b], ps_b[b + 1]], inc=(c == 7))
                j0 = tt * 128 - hs0 + 1
                lo, hi = max(j0, 0), min(j0 + 128, HW)
                for hh in range(2):
                    src = psum[:, b + hh, :].rearrange("p (c t) -> p c t", c=4)[:, :, lo - j0:hi - j0]
                    dst = h2T[:, hh * 4:(hh + 1) * 4, lo:hi]
                    if hh == 0:
                        K.op(dve, lambda: V.tensor_copy(out=dst, in_=src), reads=[ps_b[b + hh]], writes=[h2T_b])
                    else:
                        K.op(act, lambda: nc.scalar.copy(out=dst, in_=src), reads=[ps_b[b + hh]], writes=[h2T_b])

            ia = fill_a(tiles[0])
            for n_, tt in enumerate(tiles):
                ia_next = fill_a(tiles[n_ + 1]) if n_ + 1 < len(tiles) else None
                fill_b(tt, ia)
                ia = ia_next
            if half == 0:
                K.op(dve, lambda: V.tensor_copy(out=halo[:, :, 0:1], in_=h2T[:, :, 1024:1025]), reads=[h2T_b], writes=[h2T_b])
            for (t0, ln) in ((0, 384), (384, 384), (768, 256)):
                g_first = (hs0 + t0 == 0)
                g_last = (hs0 + t0 + ln == SEQ)
                lo = 1 if g_first else 0
                hi = 1 if g_last else 0
                w0c, w1c = t0 + lo, t0 + ln + 2 - hi
                nW = w1c - w0c
                ctr = 1 - lo
                kpar = tgk[0] % 2
                tgk[0] += 1
                for fb in range(NFB):
                    i = fb % 3
                    if fb == NPRE:
                        for fn in pending:
                            fn()
                        pending.clear()
                    for vg in range(2):
                        rb = vg * NFB + fb
                        b = next_ps()
                        for k in range(8):
                            K.op(pe, lambda: nc.tensor.matmul(psum[:, b, 0:nW], wu[:, k, rb * 128:(rb + 1) * 128],
                                                              h2T[:, k, w0c:w1c], start=(k == 0), stop=(k == 7)),
                                 reads=[wu_b, h2T_b], writes=[ps_b[b]], inc=(k == 7))
                        dst, dst_b = (cv[i], cv_b[i]) if vg == 0 else (cg[i], cg_b[i])
                        K.op(act, lambda: nc.scalar.activation(out=dst[:, 0:ln], in_=psum[:, b, ctr:ctr + ln], func=AF.Identity,
                                                               bias=cw[:, 3, rb:rb + 1], scale=cw[:, 1, rb:rb + 1]),
                             reads=[ps_b[b], pb], writes=[dst_b])
                        K.op(dve, lambda: V.scalar_tensor_tensor(out=dst[:, lo:ln], in0=psum[:, b, ctr + lo - 1:ctr + ln - 1],
                                                                 scalar=cw[:, 0, rb:rb + 1], in1=dst[:, lo:ln], op0=ALU.mult, op1=ALU.add),
                             reads=[ps_b[b], pb, dst_b], writes=[dst_b])
                        K.op(dve, lambda: V.scalar_tensor_tensor(out=dst[:, 0:ln - hi], in0=psum[:, b, ctr + 1:ctr + ln - hi + 1],
                                                                 scalar=cw[:, 2, rb:rb + 1], in1=dst[:, 0:ln - hi], op0=ALU.mult, op1=ALU.add),
                             reads=[ps_b[b], pb, dst_b], writes=[dst_b])
                    K.op(act, lambda: nc.scalar.activation(out=gg[i][:, 0:ln], in_=cg[i][:, 0:ln], func=AF.Gelu_apprx_tanh),
                         reads=[cg_b[i]], writes=[gg_b[i]])
                    gt_, gi_, gb_ = gslot(kpar, fb)
                    K.op(pool, lambda: nc.gpsimd.tensor_tensor(out=gt_[:, gi_, 0:ln], in0=gg[i][:, 0:ln], in1=cv[i][:, 0:ln], op=ALU.mult),
                         reads=[gg_b[i], cv_b[i]], writes=[gb_])
                pending.append(lambda s=s, tok0=tok0, hs0=hs0, t0=t0, ln=ln, kpar=kpar: emit_down(s, tok0, hs0, t0, ln, kpar))
    for fn in pending:
        fn()
    pending.clear()
_CACHE = {}


def _q_perm():
    cols = list(range(512))
    for c in range(4):
        for h in (c, 4 + c):
            cols.extend(range(512 + h * 64, 512 + (h + 1) * 64))
    cols.extend(range(1024, 1280))
    return np.asarray(cols)


def kernel(**inputs):
    n_cores = 8
    x = np.ascontiguousarray(np.asarray(inputs["x"], dtype=np.float32))
    nseq = x.shape[0] // n_cores
    if "nc" not in _CACHE:
        _CACHE["nc"] = build(nseq=nseq)[0]
        _CACHE["consts"] = _host_consts()
    nc = _CACHE["nc"]
    consts = _CACHE["consts"]
    shared = {}
    for k, v in inputs.items():
        if k == "x":
            continue
        a = np.ascontiguousarray(np.asarray(v, dtype=np.float32))
        if k == "w_in":
            a = np.ascontiguousarray(a[:, :, _q_perm()])
        shared[k] = a
    shared.update(consts)
    in_maps = []
    for i in range(n_cores):
        m = dict(shared)
        m["x"] = x[i * nseq:(i + 1) * nseq].reshape(nseq * SEQ, D_MODEL)
        in_maps.append(m)
    res = run_bass_kernel_spmd(nc, in_maps, core_ids=list(range(n_cores)))
    outs = [np.asarray(r["out"]).reshape(nseq, SEQ, D_MODEL) for r in res.results]
    return np.concatenate(outs, axis=0).astype(np.float32)
```

```python
import math
from contextlib import ExitStack
import numpy as np
import jax
import jax.numpy as jnp
import concourse.bass as bass
import concourse.mybir as mybir
from concourse.bass_utils import run_bass_kernel_spmd

F32 = mybir.dt.float32
BF16 = mybir.dt.bfloat16
I32 = mybir.dt.int32
AF = mybir.ActivationFunctionType
ALU = mybir.AluOpType

D_MODEL = 1024
SEQ = 2048
DEPTH = 4
SSM_W = 512
NG = 32
NST = 64
D_FF = 2816
NFB = 22
IN_W = 1280
EPS = 1e-6
JC = 4
LCH = 8 * JC
NSUB = SEQ // 8
NCH = NSUB // JC
NT = SEQ // 128
TWO_PI = 2.0 * math.pi

EC_TB = 0
EC_TC = EC_TB + 8
EC_SB = EC_TC + 8 * JC
EC_CC = EC_SB + 8 * JC
EC_L = EC_CC + 8 * JC
NEC = EC_L + 1
ANG_SHIFT = 64

MA_COLS = 2 * JC * 128
MB_COLS = 4 * JC * 128


class Buf:
    __slots__ = ("w", "r", "name")

    def __init__(self, name=""):
        self.w = None
        self.r = {}
        self.name = name


class Eng:
    def __init__(self, K, name, eng, is_pe=False):
        self.K = K
        self.name = name
        self.eng = eng
        self.is_pe = is_pe
        self.sems = []
        self.ep = -1
        self.cnt = 0
        self.seen = {}
        self.pending = []
        self._new_epoch()

    def _new_epoch(self):
        self.ep += 1
        self.cnt = 0
        self.sems.append(self.K.nc.alloc_semaphore(f"s_{self.name}_{self.ep}"))

class Kern:
    def __init__(self, nc):
        self.nc = nc
        self.pe = Eng(self, "pe", nc.tensor, True)
        self.act = Eng(self, "act", nc.scalar)
        self.dve = Eng(self, "dve", nc.vector)
        self.pool = Eng(self, "pool", nc.gpsimd)
        self.sp = Eng(self, "sp", nc.sync)
        self.engs = [self.pe, self.act, self.dve, self.pool, self.sp]
        self.slots = {}
        for e, n in ((self.sp, 20), (self.pool, 6), (self.act, 6)):
            self.slots[e.name] = [[nc.alloc_semaphore(f"d_{e.name}_{i}"), 0, ("dma", e.name, i)] for i in range(n)]
        self.slot_i = {k: 0 for k in self.slots}
        self.n_ins = 0

    def _wait(self, E, tick):
        sem, val, key = tick
        if key[0] == E.name and E.is_pe:
            return
        if E.seen.get(key, 0) >= val:
            return
        if key[0] != "dma":
            for (k2, v2) in E.seen.items():
                if k2[0] == key[0] and k2[0] != "dma" and k2[1] > key[1]:
                    return
        E.eng.wait_ge(sem, val)
        E.seen[key] = val
        self.n_ins += 1

    def _deps(self, E, reads, writes):
        need = []
        for b in reads:
            if b.w is not None:
                need.append(b.w)
        for b in writes:
            if b.w is not None:
                need.append(b.w)
            need.extend(b.r.values())
        for t in need:
            self._wait(E, t)

    def op(self, E, fn, reads=(), writes=(), inc=True):
        if E.cnt >= 28000 and not E.pending:
            E._new_epoch()
        self._deps(E, reads, writes)
        tick = (E.sems[E.ep], E.cnt + 1, (E.name, E.ep))
        ins = fn()
        self.n_ins += 1
        if inc:
            ins.then_inc(E.sems[E.ep], 1)
            E.cnt += 1
            E.pending = []
        else:
            E.pending.append(1)
        for b in reads:
            b.r[E.name] = tick
        for b in writes:
            b.w = tick
            b.r = {}
        return ins

    def dma(self, E, out, in_, reads=(), writes=(), **kw):
        slots = self.slots[E.name]
        i = self.slot_i[E.name]
        self.slot_i[E.name] = (i + 1) % len(slots)
        s = slots[i]
        if s[1] > 0:
            self._wait(E, (s[0], s[1], s[2]))
        self._deps(E, reads, writes)
        ins = E.eng.dma_start(out=out, in_=in_, **kw)
        s[1] += 16
        ins.then_inc(s[0], 16)
        self.n_ins += 1
        tick = (s[0], s[1], s[2])
        for b in reads:
            b.r[("dma", E.name, i)] = tick
        for b in writes:
            b.w = tick
            b.r = {}
        return ins

    def barrier(self):
        sp = self.sp
        for E in self.engs:
            if E is sp:
                continue
            if E.cnt > 0:
                self._wait(sp, (E.sems[E.ep], E.cnt, (E.name, E.ep)))
        for lst in self.slots.values():
            for s in lst:
                if s[1] > 0:
                    self._wait(sp, (s[0], s[1], s[2]))
        self.op(sp, lambda: sp.eng.nop())
        t = (sp.sems[sp.ep], sp.cnt, (sp.name, sp.ep))
        for E in self.engs:
            if E is not sp:
                self._wait(E, t)


def _t5_bucket(rel):
    n_buckets, max_distance = 32, 128
    half = n_buckets // 2
    max_exact = half // 2
    ret = jnp.where(rel > 0, half, 0)
    n = jnp.abs(rel)
    nf = jnp.maximum(n, 1).astype(jnp.float32)
    large = max_exact + (jnp.log(nf / max_exact) / math.log(max_distance / max_exact)
                         * (half - max_exact)).astype(jnp.int32)
    large = jnp.minimum(large, half - 1)
    return ret + jnp.where(n < max_exact, n, large)


def _host_consts():
    import ml_dtypes
    c = {}
    c["ident_f"] = np.eye(128, dtype=np.float32)
    ex = np.zeros((2, NEC), np.float32)
    for s in range(8):
        ex[0, EC_TB + s] = -s
        ex[1, EC_TB + s] = s
    for d in range(JC):
        for s in range(8):
            ex[0, EC_TC + d * 8 + s] = 8 * d + s
            ex[1, EC_TC + d * 8 + s] = 8 * d - s
            ex[0, EC_SB + d * 8 + s] = LCH - 1 - 8 * d - s
            ex[1, EC_SB + d * 8 + s] = 8 * d + s
            ex[0, EC_CC + d * 8 + s] = 8 * d + s + 1
            ex[1, EC_CC + d * 8 + s] = LCH - 8 * d - s
    ex[:, EC_L] = LCH
    c["extab"] = np.repeat(ex, 64, axis=0).astype(np.float32)
    sp = np.arange(128)[:, None] // 16
    s = np.arange(128)[None, :] // 16
    tm = np.stack([(s >= sp), (sp >= s)], axis=1).astype(np.float32)
    c["tmask"] = np.ascontiguousarray(tm)
    k = np.arange(128)[:, None, None]
    off = (np.arange(3) - 1)[None, :, None]
    q = np.arange(128)[None, None, :]
    rel = (k + 128 * off - q).astype(np.int32)
    with jax.default_device(jax.devices("cpu")[0]):
        bk = np.asarray(_t5_bucket(jnp.asarray(rel)))
    oh = (bk[:, None, :, :] == np.arange(32)[None, :, None, None]).astype(np.float32)
    c["onehot"] = oh.reshape(128, 32, 384).astype(ml_dtypes.bfloat16)
    c["vmask"] = (np.abs(rel) <= 128).astype(np.float32).reshape(128, 384)
    return c


def build(nseq=4, nlayers=DEPTH, debug=None):
    nc = bass.Bass("TRN2", target_bir_lowering=False)
    K = Kern(nc)
    pe, act, dve, pool, sp = K.pe, K.act, K.dve, K.pool, K.sp
    NTOK = nseq * SEQ

    def din(name, shape, dt=F32):
        return nc.dram_tensor(name, list(shape), dt, kind="ExternalInput").ap()

    x_in = din("x", (NTOK, D_MODEL))
    P = {}
    P["rel_bias"] = din("rel_bias", (32, 8))
    P["pre_mix_norm"] = din("pre_mix_norm", (DEPTH, D_MODEL))
    P["w_in"] = din("w_in", (DEPTH, D_MODEL, IN_W))
    P["lam_re"] = din("lam_re", (DEPTH, 2, NG, NST))
    P["lam_im"] = din("lam_im", (DEPTH, 2, NG, NST))
    P["log_step"] = din("log_step", (DEPTH, 2, NG))
    P["b_re"] = din("b_re", (DEPTH, 2, NG, NST, 16))
    P["b_im"] = din("b_im", (DEPTH, 2, NG, NST, 16))
    P["c_re"] = din("c_re", (DEPTH, 2, NG, 16, NST))
    P["c_im"] = din("c_im", (DEPTH, 2, NG, 16, NST))
    P["ssm_d"] = din("ssm_d", (DEPTH, SSM_W))
    P["w_glu"] = din("w_glu", (DEPTH, SSM_W, SSM_W))
    P["b_glu"] = din("b_glu", (DEPTH, SSM_W))
    P["attn_sink"] = din("attn_sink", (DEPTH, 8))
    P["ssm_out_norm"] = din("ssm_out_norm", (DEPTH, SSM_W))
    P["attn_out_norm"] = din("attn_out_norm", (DEPTH, SSM_W))
    P["w_out"] = din("w_out", (DEPTH, D_MODEL, D_MODEL))
    P["post_mix_norm"] = din("post_mix_norm", (DEPTH, D_MODEL))
    P["pre_ffn_norm"] = din("pre_ffn_norm", (DEPTH, D_MODEL))
    P["w_up"] = din("w_up", (DEPTH, D_MODEL, 2 * D_FF))
    P["conv_w"] = din("conv_w", (DEPTH, 3, 2 * D_FF))
    P["conv_b"] = din("conv_b", (DEPTH, 2 * D_FF))
    P["w_down"] = din("w_down", (DEPTH, D_FF, D_MODEL))
    P["post_ffn_norm"] = din("post_ffn_norm", (DEPTH, D_MODEL))
    c_ident = din("ident_f", (128, 128))
    c_extab = din("extab", (128, NEC))
    c_tmask = din("tmask", (128, 2, 128))
    c_onehot = din("onehot", (128, 32, 384), BF16)
    c_vmask = din("vmask", (128, 384))

    out = nc.dram_tensor("out", [NTOK, D_MODEL], F32, kind="ExternalOutput").ap()
    matsA = nc.dram_tensor("matsA", [NG, 128, MA_COLS], BF16).ap()
    matsB = nc.dram_tensor("matsB", [NG, 128, MB_COLS], BF16).ap()
    u_scr = nc.dram_tensor("u_scr", [NG * 16, SEQ], BF16).ap()
    z_scr = nc.dram_tensor("z_scr", [NG * 16, SEQ], BF16).ap()
    eb_scr = nc.dram_tensor("eb_scr", [128, 3 * 2 * 4 * 128], BF16).ap()
    matsA_b = [Buf() for _ in range(NG)]
    matsB_b = [Buf() for _ in range(NG)]
    u_scr_b, z_scr_b = Buf(), Buf()
    out_b = [[Buf() for _ in range(NT)] for _ in range(nseq)]
    dbg = {}
    if debug:
        for nm, (shp, dt_) in debug.items():
          if shp is not None:
            dbg[nm] = nc.dram_tensor("dbg_" + nm, list(shp), dt_, kind="ExternalOutput").ap()

    top = ExitStack()
    with top:
        uid = [0]

        def sb(es, name, shape, dt=F32):
            uid[0] += 1
            return es.enter_context(nc.sbuf_tensor(f"{name}_{uid[0]}", list(shape), dt))

        psum = top.enter_context(nc.psum_tensor("psum", [128, 8, 512], F32))
        ps_b = [Buf(f"ps{i}") for i in range(8)]
        ps_rr = [0]

        def next_ps(n=1):
            if n == 1:
                i = ps_rr[0] % 8
                ps_rr[0] += 1
                return i
            i = ps_rr[0] % 8
            if i % 2:
                i = (i + 1) % 8
            ps_rr[0] = i + 2
            return i

        ident = sb(top, "ident", (128, 128))
        identb = sb(top, "identb", (128, 128), BF16)
        ones_bf = sb(top, "ones_bf", (128, 1), BF16)
        epsb = sb(top, "epsb", (128, 1))
        aL = sb(top, "aL", (128, 2, 2, 32))
        aL_b = Buf("aL")
        cb = Buf("consts")
        ebB = Buf("eb")
        K.dma(sp, ident[:], c_ident, writes=[cb])
        K.op(dve, lambda: nc.vector.tensor_copy(out=identb[:], in_=ident[:]), reads=[cb], writes=[cb])
        K.op(dve, lambda: nc.vector.memset(ones_bf[:], 1.0), writes=[cb])
        K.op(dve, lambda: nc.vector.memset(epsb[:], EPS), writes=[cb])

        with ExitStack() as es:
            oh = sb(es, "oh", (128, 32, 384), BF16)
            eb = sb(es, "eb0", (128, 3, 2, 4, 128), BF16)
            rb = sb(es, "rb", (128, 256))
            vm = sb(es, "vm", (128, 384))
            acc = sb(es, "acc", (128, 8, 384))
            tb = Buf()
            K.dma(sp, oh[:], c_onehot, writes=[tb])
            K.dma(sp, vm[:], c_vmask, writes=[tb])
            K.dma(sp, rb[:], P["rel_bias"].rearrange("b h -> (b h)").partition_broadcast(128), writes=[tb])
            accb = [Buf() for _ in range(8)]
            for h in range(8):
                E = dve
                K.op(E, lambda h=h: nc.vector.tensor_scalar(out=acc[:, h, :], in0=oh[:, 0, :], scalar1=rb[:, h:h + 1],
                                                            scalar2=None, op0=ALU.mult), reads=[tb], writes=[accb[h]])
                for b in range(1, 32):
                    K.op(E, lambda h=h, b=b: nc.vector.scalar_tensor_tensor(
                        out=acc[:, h, :], in0=oh[:, b, :], scalar=rb[:, b * 8 + h:b * 8 + h + 1], in1=acc[:, h, :],
                        op0=ALU.mult, op1=ALU.add), reads=[tb, accb[h]], writes=[accb[h]])
                K.op(act, lambda h=h: nc.scalar.activation(out=acc[:, h, :], in_=acc[:, h, :], func=AF.Exp),
                     reads=[accb[h]], writes=[accb[h]])
                kv, c = h // 4, h % 4
                K.op(dve, lambda h=h, kv=kv, c=c: nc.vector.tensor_tensor(
                    out=eb[:, :, kv, c, :], in0=acc[:, h, :].rearrange("p (o q) -> p o q", o=3),
                    in1=vm[:].rearrange("p (o q) -> p o q", o=3), op=ALU.mult), reads=[accb[h], tb], writes=[ebB])
            K.dma(sp, eb_scr, eb[:].rearrange("p o k c q -> p (o k c q)"), reads=[ebB], writes=[ebB])
            K.barrier()

        for l in range(nlayers):
            x_src = x_in if l == 0 else out
            with ExitStack() as es:
                s5_prologue(nc, K, es, sb, psum, ps_b, next_ps, P, l, ident, c_extab, c_tmask,
                            aL, aL_b, matsA, matsB, matsA_b, matsB_b, dbg)
                K.barrier()
            with ExitStack() as es:
                mixer_phase(nc, K, es, sb, psum, ps_b, next_ps, P, l, nseq, x_src, out, out_b, ident, identb, ones_bf,
                            eb_scr, ebB, epsb, cb, aL, aL_b, matsA, matsB, matsA_b, matsB_b, u_scr, z_scr, u_scr_b, z_scr_b, dbg)
                K.barrier()
            with ExitStack() as es:
              if not (debug and "skip_ffn" in debug):
                ffn_phase(nc, K, es, sb, psum, ps_b, next_ps, P, l, nseq, out, out_b, ident, identb, epsb, cb, dbg)
                K.barrier()
        K.barrier()
    return nc, K


def make_staging(es, sb, tag, n=6, ch=2048):
    return ([sb(es, f"stg_{tag}{i}", (128, ch)) for i in range(n)], [Buf() for _ in range(n)], [0])


def load_cast_weight(nc, K, stage, dst, dst_b, src_rows, ncols, gain, gain_b, engines):
    stg, stg_b, itr = stage
    NS = len(stg)
    CH = 2048
    dq = [K.sp, K.act]
    it = itr[0]
    for r, src in enumerate(src_rows):
        for c0 in range(0, ncols, CH):
            cw = min(CH, ncols - c0)
            i = it % NS
            E = engines[it % len(engines)]
            Q = dq[it % 2]
            it += 1
            K.dma(Q, stg[i][:, 0:cw], src[:, c0:c0 + cw], writes=[stg_b[i]])
            if gain is None:
                if E is K.act:
                    K.op(E, lambda i=i, r=r, c0=c0, cw=cw: nc.scalar.copy(out=dst[:, r, c0:c0 + cw], in_=stg[i][:, 0:cw]),
                         reads=[stg_b[i]], writes=[dst_b])
                else:
                    K.op(E, lambda i=i, r=r, c0=c0, cw=cw, E=E: E.eng.tensor_copy(out=dst[:, r, c0:c0 + cw], in_=stg[i][:, 0:cw]),
                         reads=[stg_b[i]], writes=[dst_b])
            else:
                if E is K.act:
                    K.op(E, lambda i=i, r=r, c0=c0, cw=cw: nc.scalar.activation(
                        out=dst[:, r, c0:c0 + cw], in_=stg[i][:, 0:cw], func=AF.Copy, scale=gain[:, r:r + 1]),
                        reads=[stg_b[i], gain_b], writes=[dst_b])
                else:
                    K.op(E, lambda i=i, r=r, c0=c0, cw=cw, E=E: E.eng.tensor_scalar(
                        out=dst[:, r, c0:c0 + cw], in0=stg[i][:, 0:cw], scalar1=gain[:, r:r + 1], scalar2=None,
                        op0=ALU.mult), reads=[stg_b[i], gain_b], writes=[dst_b])
    itr[0] = it


def load_cols(nc, K, psum, ps_b, next_ps, ident, cb, ld, ld_b, dst, dst_b, src1d, n):
    K.dma(K.sp, ld[0:n, :], src1d.rearrange("(r p) -> r p", p=128), writes=[ld_b])
    b = next_ps()
    K.op(K.pe, lambda: nc.tensor.transpose(psum[:, b, 0:n], ld[0:n, :], ident[0:n, 0:n]), reads=[ld_b, cb], writes=[ps_b[b]])
    K.op(K.dve, lambda: nc.vector.tensor_copy(out=dst, in_=psum[:, b, 0:n]), reads=[ps_b[b]], writes=[dst_b])


def rstd_from_ssq(nc, K, ssq, rs, n, width, epsb, bufs_r, bufs_w):
    K.op(K.act, lambda: nc.scalar.activation(out=rs[:, 0:n], in_=ssq[:, 0:n], func=AF.Sqrt, bias=epsb[:, 0:1],
                                             scale=1.0 / width), reads=bufs_r, writes=bufs_w)
    K.op(K.dve, lambda: nc.vector.reciprocal(out=rs[:, 0:n], in_=rs[:, 0:n]), reads=bufs_w, writes=bufs_w)


def s5_prologue(nc, K, es, sb, psum, ps_b, next_ps, P, l, ident, c_extab, c_tmask,
                aL, aL_b, matsA, matsB, matsA_b, matsB_b, dbg):
    pe, act, dve, pool, sp = K.pe, K.act, K.dve, K.pool, K.sp
    V = nc.vector
    G = NG
    tb = Buf("s5tab")

    def t(name, shape, dt=F32):
        return sb(es, name, shape, dt)

    lamld = t("lamld", (32, 2, 128))
    lam = t("lam", (128, 2, 32))
    ls = t("ls", (128, 32))
    braw = t("braw", (128, 2, 32, 16))
    cld = t("cld", (128, 2, 4, 128))
    craw = t("craw", (128, 2, 32, 16))
    dvec = t("dvec", (128, 32))
    extab = t("extab_sb", (128, NEC))
    tmask = t("tmask_sb", (128, 2, 128))
    K.dma(sp, extab[:], c_extab, writes=[tb])
    K.dma(sp, tmask[:], c_tmask, writes=[tb])
    with nc.allow_non_contiguous_dma(reason="tiny param loads"):
        for ri, nm in enumerate(("lam_re", "lam_im")):
            K.dma(sp, lamld[:, ri, :].rearrange("g (d n) -> g d n", d=2), P[nm][l].rearrange("d g n -> g d n"), writes=[tb])
        for d in range(2):
            K.dma(sp, ls[d * 64:(d + 1) * 64, :], P["log_step"][l, d].partition_broadcast(64), writes=[tb])
            for ri, nm in enumerate(("b_re", "b_im")):
                K.dma(sp, braw[d * 64:(d + 1) * 64, ri, :, :], P[nm][l, d].rearrange("g n q -> n g q"), writes=[tb])
        for ri, nm in enumerate(("c_re", "c_im")):
            for d in range(2):
                K.dma(sp, cld[:, ri, :, d * 64:(d + 1) * 64],
                      P[nm][l, d].rearrange("(t gl) p n -> (gl p) t n", t=4), writes=[tb])
        for s in range(8):
            K.dma(sp, dvec[s * 16:(s + 1) * 16, :], P["ssm_d"][l].rearrange("(g q) -> q g", q=16), writes=[tb])
    for ri in range(2):
        b = next_ps()
        K.op(pe, lambda ri=ri, b=b: nc.tensor.transpose(psum[:, b, 0:32], lamld[:, ri, :], ident[0:32, 0:32]),
             reads=[tb], writes=[ps_b[b]])
        K.op(dve, lambda ri=ri, b=b: V.tensor_copy(out=lam[:, ri, :], in_=psum[:, b, 0:32]), reads=[ps_b[b]], writes=[tb])
        b = next_ps()
        for tt in range(4):
            K.op(pe, lambda ri=ri, b=b, tt=tt: nc.tensor.transpose(psum[:, b, tt * 128:(tt + 1) * 128], cld[:, ri, tt, :], ident[:]),
                 reads=[tb], writes=[ps_b[b]], inc=(tt == 3))
        K.op(dve, lambda ri=ri, b=b: V.tensor_copy(out=craw[:, ri, :, :].rearrange("p g q -> p (g q)"), in_=psum[:, b, :]),
             reads=[ps_b[b]], writes=[tb])

    dt_ = t("dt", (128, 32))
    ar = t("ar", (128, 32))
    ang = t("ang", (128, 32))
    lb = t("lb", (128, 2, 32))
    coef = t("coef", (128, 2, 32))
    tmp = t("tmpa", (128, 4, 32))

    def dv(fn, inc=True):
        K.op(dve, fn, reads=[tb], writes=[tb], inc=inc)

    def ac(fn):
        K.op(act, fn, reads=[tb], writes=[tb])

    ac(lambda: nc.scalar.activation(out=dt_[:], in_=ls[:], func=AF.Exp))
    dv(lambda: V.tensor_tensor(out=ar[:], in0=lam[:, 0, :], in1=dt_[:], op=ALU.mult))
    dv(lambda: V.tensor_tensor(out=ang[:], in0=lam[:, 1, :], in1=dt_[:], op=ALU.mult))

    PR = t("PR", (128, 32, NEC))
    PI = t("PI", (128, 32, NEC))
    A1 = t("A1", (128, 32, NEC))
    A2 = t("A2", (128, 32, NEC))
    A3i = t("A3i", (128, 32, NEC), I32)
    A4 = t("A4", (128, 32, NEC))
    ex_b = extab[:].unsqueeze(1).to_broadcast([128, 32, NEC])

    def bc_g(x):
        return x.unsqueeze(2).to_broadcast([128, 32, NEC])

    dv(lambda: V.tensor_tensor(out=A1[:], in0=ex_b, in1=bc_g(ar[:]), op=ALU.mult))
    ac(lambda: nc.scalar.activation(out=A4[:], in_=A1[:], func=AF.Exp))
    dv(lambda: V.tensor_tensor(out=A1[:], in0=ex_b, in1=bc_g(ang[:]), op=ALU.mult))

    def sin_of(dst, shift):
        dv(lambda: V.tensor_scalar(out=A2[:], in0=A1[:], scalar1=shift + ANG_SHIFT * TWO_PI, scalar2=1.0 / TWO_PI,
                                   op0=ALU.add, op1=ALU.mult))
        dv(lambda: V.tensor_copy(out=A3i[:], in_=A2[:]))
        dv(lambda: V.tensor_copy(out=dst[:], in_=A3i[:]))
        dv(lambda: V.tensor_tensor(out=A2[:], in0=A2[:], in1=dst[:], op=ALU.subtract))
        dv(lambda: V.tensor_scalar(out=dst[:], in0=A2[:], scalar1=0.5, scalar2=None, op0=ALU.is_gt))
        dv(lambda: V.tensor_tensor(out=A2[:], in0=A2[:], in1=dst[:], op=ALU.subtract))
        dv(lambda: V.tensor_scalar(out=dst[:], in0=A2[:], scalar1=-0.5, scalar2=None, op0=ALU.is_lt))
        dv(lambda: V.tensor_tensor(out=A2[:], in0=A2[:], in1=dst[:], op=ALU.add))
        dv(lambda: V.tensor_scalar(out=A2[:], in0=A2[:], scalar1=TWO_PI, scalar2=3.14159, op0=ALU.mult, op1=ALU.min))
        dv(lambda: V.tensor_scalar(out=A2[:], in0=A2[:], scalar1=-3.14159, scalar2=None, op0=ALU.max))
        ac(lambda: nc.scalar.activation(out=dst[:], in_=A2[:], func=AF.Sin))

    sin_of(PI, 0.0)
    sin_of(PR, math.pi / 2)
    dv(lambda: V.tensor_tensor(out=PR[:], in0=PR[:], in1=A4[:], op=ALU.mult))
    dv(lambda: V.tensor_tensor(out=PI[:], in0=PI[:], in1=A4[:], op=ALU.mult))

    ac(lambda: nc.scalar.activation(out=tmp[:, 0, :], in_=ar[:], func=AF.Exp))
    dv(lambda: V.tensor_copy(out=lb[0:64, 0, :], in_=PR[0:64, :, EC_TC + 1]))
    dv(lambda: V.tensor_copy(out=lb[0:64, 1, :], in_=PI[0:64, :, EC_TC + 1]))
    dv(lambda: V.tensor_copy(out=lb[64:128, 0, :], in_=PR[64:128, :, EC_TB + 1]))
    dv(lambda: V.tensor_copy(out=lb[64:128, 1, :], in_=PI[64:128, :, EC_TB + 1]))
    dv(lambda: V.tensor_tensor(out=tmp[:, 0, :], in0=lam[:, 0, :], in1=lam[:, 0, :], op=ALU.mult))
    dv(lambda: V.tensor_tensor(out=tmp[:, 1, :], in0=lam[:, 1, :], in1=lam[:, 1, :], op=ALU.mult))
    dv(lambda: V.tensor_tensor(out=tmp[:, 0, :], in0=tmp[:, 0, :], in1=tmp[:, 1, :], op=ALU.add))
    dv(lambda: V.reciprocal(out=tmp[:, 0, :], in_=tmp[:, 0, :]))
    dv(lambda: V.tensor_scalar(out=tmp[:, 1, :], in0=lb[:, 0, :], scalar1=-1.0, scalar2=None, op0=ALU.add))
    dv(lambda: V.tensor_tensor(out=tmp[:, 2, :], in0=tmp[:, 1, :], in1=lam[:, 0, :], op=ALU.mult))
    dv(lambda: V.tensor_tensor(out=tmp[:, 3, :], in0=lb[:, 1, :], in1=lam[:, 1, :], op=ALU.mult))
    dv(lambda: V.tensor_tensor(out=tmp[:, 2, :], in0=tmp[:, 2, :], in1=tmp[:, 3, :], op=ALU.add))
    dv(lambda: V.tensor_tensor(out=coef[:, 0, :], in0=tmp[:, 2, :], in1=tmp[:, 0, :], op=ALU.mult))
    dv(lambda: V.tensor_tensor(out=tmp[:, 2, :], in0=lb[:, 1, :], in1=lam[:, 0, :], op=ALU.mult))
    dv(lambda: V.tensor_tensor(out=tmp[:, 3, :], in0=tmp[:, 1, :], in1=lam[:, 1, :], op=ALU.mult))
    dv(lambda: V.tensor_tensor(out=tmp[:, 2, :], in0=tmp[:, 2, :], in1=tmp[:, 3, :], op=ALU.subtract))
    dv(lambda: V.tensor_tensor(out=coef[:, 1, :], in0=tmp[:, 2, :], in1=tmp[:, 0, :], op=ALU.mult))
    bb = t("bb", (128, 2, 32, 16))
    t16 = t("t16", (128, 32, 16))

    def bq(x):
        return x.unsqueeze(2).to_broadcast([128, 32, 16])

    dv(lambda: V.tensor_tensor(out=bb[:, 0], in0=braw[:, 0], in1=bq(coef[:, 0, :]), op=ALU.mult))
    dv(lambda: V.tensor_tensor(out=t16[:], in0=braw[:, 1], in1=bq(coef[:, 1, :]), op=ALU.mult))
    dv(lambda: V.tensor_tensor(out=bb[:, 0], in0=bb[:, 0], in1=t16[:], op=ALU.subtract))
    dv(lambda: V.tensor_tensor(out=bb[:, 1], in0=braw[:, 1], in1=bq(coef[:, 0, :]), op=ALU.mult))
    dv(lambda: V.tensor_tensor(out=t16[:], in0=braw[:, 0], in1=bq(coef[:, 1, :]), op=ALU.mult))
    dv(lambda: V.tensor_tensor(out=bb[:, 1], in0=bb[:, 1], in1=t16[:], op=ALU.add))

    if "s5tab" in dbg:
        K.dma(sp, dbg["s5tab"][:, 0:NEC], PR[:, 0, :], reads=[tb])
        K.dma(sp, dbg["s5tab"][:, NEC:2 * NEC], PI[:, 0, :], reads=[tb])
        K.dma(sp, dbg["s5tab"][:, 2 * NEC:2 * NEC + 32], bb[:, 0, 0:2, :].rearrange("p g q -> p (g q)"), reads=[tb])
        K.dma(sp, dbg["s5tab"][:, 2 * NEC + 32:2 * NEC + 64], bb[:, 1, 0:2, :].rearrange("p g q -> p (g q)"), reads=[tb])

    K.op(dve, lambda: V.tensor_copy(out=aL[:, 0, 0, :], in_=PR[:, :, EC_L]), reads=[tb], writes=[aL_b])
    K.op(dve, lambda: V.tensor_copy(out=aL[:, 0, 1, :], in_=PR[:, :, EC_L]), reads=[tb], writes=[aL_b])
    K.op(dve, lambda: V.tensor_scalar(out=aL[:, 1, 0, :], in0=PI[:, :, EC_L], scalar1=-1.0, scalar2=None, op0=ALU.mult), reads=[tb], writes=[aL_b])
    K.op(dve, lambda: V.tensor_copy(out=aL[:, 1, 1, :], in_=PI[:, :, EC_L]), reads=[tb], writes=[aL_b])

    NB = 8 + 8 * JC
    NC8 = 8 * JC
    GB = 2
    xb = [t(f"xb{i}", (128, 2, GB, NB, 16)) for i in range(2)]
    xc = [t(f"xc{i}", (128, 2, GB, NC8, 16)) for i in range(2)]
    xcc = [t(f"xcc{i}", (128, 2, GB, NC8, 16)) for i in range(2)]
    tq = [t(f"tq{i}", (128, GB, NB, 16)) for i in range(2)]
    mA = [t(f"mA{i}", (128, MA_COLS), BF16) for i in range(2)]
    mB = [t(f"mB{i}", (128, MB_COLS), BF16) for i in range(2)]
    dmat = [t(f"dmat{i}", (128, 128)) for i in range(2)]
    t0m = [t(f"t0m{i}", (128, 128)) for i in range(2)]
    xb_b = [Buf() for _ in range(2)]
    xc_b = [Buf() for _ in range(2)]
    xcc_b = [Buf() for _ in range(2)]
    tq_b = [Buf() for _ in range(2)]
    mA_b = [Buf() for _ in range(2)]
    mB_b = [Buf() for _ in range(2)]
    dm_b = [Buf() for _ in range(2)]

    def pwB(x, g0, c0, n):
        return x[:, g0:g0 + GB, c0:c0 + n].unsqueeze(3).to_broadcast([128, GB, n, 16])

    def bbB(ri, g0, n):
        return bb[:, ri, g0:g0 + GB, :].unsqueeze(2).to_broadcast([128, GB, n, 16])

    def ccB(ri, g0, n):
        return craw[:, ri, g0:g0 + GB, :].unsqueeze(2).to_broadcast([128, GB, n, 16])

    ncraw = t("ncraw", (128, 2, 32, 16))
    dv(lambda: V.tensor_scalar(out=ncraw[:], in0=craw[:], scalar1=-1.0, scalar2=None, op0=ALU.mult))

    def nccB(ri, g0, n):
        return ncraw[:, ri, g0:g0 + GB, :].unsqueeze(2).to_broadcast([128, GB, n, 16])

    for gb in range(G // GB):
        g0 = gb * GB
        i = gb % 2
        E2 = pool if (gb % 3 == 2) else dve
        EN = E2.eng

        def tt_(out, in0, in1, op, reads, writes):
            K.op(E2, lambda: EN.tensor_tensor(out=out, in0=in0, in1=in1, op=op), reads=reads, writes=writes)

        for (dst0, c0, n) in ((0, EC_TB, 8), (8, EC_SB, NC8)):
            o_re = xb[i][:, 0, :, dst0:dst0 + n, :]
            o_im = xb[i][:, 1, :, dst0:dst0 + n, :]
            t_ = tq[i][:, :, dst0:dst0 + n, :]
            tt_(o_re, pwB(PR, g0, c0, n), bbB(0, g0, n), ALU.mult, [tb], [xb_b[i]])
            tt_(t_, pwB(PI, g0, c0, n), bbB(1, g0, n), ALU.mult, [tb], [tq_b[i]])
            tt_(o_re, o_re, t_, ALU.subtract, [tq_b[i], xb_b[i]], [xb_b[i]])
            tt_(o_im, pwB(PR, g0, c0, n), bbB(1, g0, n), ALU.mult, [tb], [xb_b[i]])
            tt_(t_, pwB(PI, g0, c0, n), bbB(0, g0, n), ALU.mult, [tb, xb_b[i]], [tq_b[i]])
            tt_(o_im, o_im, t_, ALU.add, [tq_b[i], xb_b[i]], [xb_b[i]])
        for (dstt, dstb, c0) in ((xc, xc_b, EC_TC), (xcc, xcc_b, EC_CC)):
            o_re = dstt[i][:, 0]
            o_im = dstt[i][:, 1]
            t_ = tq[i][:, :, 0:NC8, :]
            tt_(o_re, pwB(PR, g0, c0, NC8), ccB(0, g0, NC8), ALU.mult, [tb], [dstb[i]])
            tt_(t_, pwB(PI, g0, c0, NC8), ccB(1, g0, NC8), ALU.mult, [tb], [tq_b[i]])
            tt_(o_re, o_re, t_, ALU.subtract, [tq_b[i], dstb[i]], [dstb[i]])
            tt_(o_im, pwB(PI, g0, c0, NC8), nccB(0, g0, NC8), ALU.mult, [tb], [dstb[i]])
            tt_(t_, pwB(PR, g0, c0, NC8), nccB(1, g0, NC8), ALU.mult, [tb, dstb[i]], [tq_b[i]])
            tt_(o_im, o_im, t_, ALU.add, [tq_b[i], dstb[i]], [dstb[i]])

        for gl in range(GB):
            g = g0 + gl
            j2 = g % 2
            for d in range(2):
                b = next_ps()
                rows = slice(d * 64, (d + 1) * 64)
                for ri in range(2):
                    K.op(pe, lambda: nc.tensor.matmul(
                        psum[:, b, 0:JC * 128],
                        xb[i][rows, ri, gl, 0:8, :].rearrange("p s q -> p (s q)"),
                        xc[i][rows, ri, gl, :, :].rearrange("p c q -> p (c q)"),
                        start=(ri == 0), stop=(ri == 1)),
                        reads=[xb_b[i], xc_b[i]], writes=[ps_b[b]], inc=(ri == 1))
                base = d * JC * 128
                if d == 0:
                    K.op(dve, lambda: V.tensor_scalar(out=dmat[j2][:], in0=ident[:], scalar1=dvec[:, g:g + 1],
                                                      scalar2=None, op0=ALU.mult), reads=[tb], writes=[dm_b[j2]])
                    K.op(dve, lambda: V.tensor_tensor(out=t0m[j2][:], in0=psum[:, b, 0:128], in1=tmask[:, 0, :], op=ALU.mult),
                         reads=[ps_b[b], tb, dm_b[j2]], writes=[dm_b[j2]])
                    K.op(dve, lambda: V.tensor_tensor(out=mB[j2][:, base:base + 128], in0=t0m[j2][:], in1=dmat[j2][:], op=ALU.add),
                         reads=[dm_b[j2]], writes=[mB_b[j2]])
                else:
                    K.op(dve, lambda: V.tensor_tensor(out=mB[j2][:, base:base + 128], in0=psum[:, b, 0:128],
                                                      in1=tmask[:, 1, :], op=ALU.mult),
                         reads=[ps_b[b], tb], writes=[mB_b[j2]])
                if JC > 1:
                    K.op(act, lambda: nc.scalar.copy(out=mB[j2][:, base + 128:base + JC * 128], in_=psum[:, b, 128:JC * 128]),
                         reads=[ps_b[b]], writes=[mB_b[j2]])
            K.op(act, lambda: nc.scalar.copy(out=mB[j2][:, 2 * JC * 128:3 * JC * 128],
                                             in_=xcc[i][:, 0, gl].rearrange("p c q -> p (c q)")),
                 reads=[xcc_b[i]], writes=[mB_b[j2]])
            K.op(act, lambda: nc.scalar.copy(out=mB[j2][:, 3 * JC * 128:4 * JC * 128],
                                             in_=xcc[i][:, 1, gl].rearrange("p c q -> p (c q)")),
                 reads=[xcc_b[i]], writes=[mB_b[j2]])
            K.dma(sp, matsB[g], mB[j2][:], reads=[mB_b[j2]], writes=[matsB_b[g]])
            for ri in range(2):
                b = next_ps()
                for j in range(JC):
                    K.op(pe, lambda: nc.tensor.transpose(
                        psum[:, b, j * 128:(j + 1) * 128],
                        xb[i][:, ri, gl, 8 + 8 * j:16 + 8 * j, :].rearrange("p s q -> p (s q)"), ident[:]),
                        reads=[xb_b[i]], writes=[ps_b[b]], inc=(j == JC - 1))
                K.op(act, lambda: nc.scalar.copy(out=mA[j2][:, ri * JC * 128:(ri + 1) * JC * 128], in_=psum[:, b, 0:JC * 128]),
                     reads=[ps_b[b]], writes=[mA_b[j2]])
            K.dma(sp, matsA[g], mA[j2][:], reads=[mA_b[j2]], writes=[matsA_b[g]])
    if "matsB0" in dbg:
        pass


def mixer_phase(nc, K, es, sb, psum, ps_b, next_ps, P, l, nseq, x_src, out, out_b, ident, identb, ones_bf,
                eb_scr, ebB, epsb, cb, aL, aL_b, matsA, matsB, matsA_b, matsB_b, u_scr, z_scr, u_scr_b, z_scr_b, dbg):
    pe, act, dve, pool, sp = K.pe, K.act, K.dve, K.pool, K.sp
    V = nc.vector
    G = NG
    wi = sb(es, "wi", (128, 8, IN_W), BF16)
    wg = sb(es, "wg", (128, 4, 512), BF16)
    wo = sb(es, "wo", (128, 8, 1024), BF16)
    gpm = sb(es, "gpm", (128, 8))
    gso = sb(es, "gso", (128, 8))
    bglu = sb(es, "bglu", (128, 4))
    esink = sb(es, "esink", (128, 8))
    gpost = sb(es, "gpost", (128, 1024))
    wi_b, wg_b, wo_b, pb = Buf(), Buf(), Buf(), Buf()
    eb = sb(es, "eb", (128, 3, 2, 4, 128), BF16)
    K.dma(sp, eb[:].rearrange("p o k c q -> p (o k c q)"), eb_scr, reads=[ebB], writes=[pb])
    ebB = pb
    with ExitStack() as es1:
        ld = sb(es1, "ldm", (8, 128))
        ld_b = Buf()
        LC = lambda dst, src, n: load_cols(nc, K, psum, ps_b, next_ps, ident, cb, ld, ld_b, dst, pb, src, n)
        LC(gpm[:], P["pre_mix_norm"][l], 8)
        LC(gso[:, 0:4], P["ssm_out_norm"][l], 4)
        LC(gso[:, 4:8], P["attn_out_norm"][l], 4)
        LC(bglu[:], P["b_glu"][l], 4)
        K.barrier()
    K.dma(sp, esink[:], P["attn_sink"][l].partition_broadcast(128), writes=[pb])
    K.dma(sp, gpost[:], P["post_mix_norm"][l].partition_broadcast(128), writes=[pb])
    K.op(act, lambda: nc.scalar.activation(out=esink[:], in_=esink[:], func=AF.Exp), reads=[pb], writes=[pb])
    with ExitStack() as es2:
        engs = [act, dve]
        stage = make_staging(es2, sb, "m")
        load_cast_weight(nc, K, stage, wi, wi_b, [P["w_in"][l, k * 128:(k + 1) * 128, :] for k in range(8)], IN_W,
                         gpm, pb, engs)
        load_cast_weight(nc, K, stage, wg, wg_b, [P["w_glu"][l, k * 128:(k + 1) * 128, :] for k in range(4)], 512,
                         None, pb, engs)
        load_cast_weight(nc, K, stage, wo, wo_b, [P["w_out"][l, k * 128:(k + 1) * 128, :] for k in range(8)], 1024,
                         gso, pb, engs)
        K.barrier()

    X1 = sb(es, "X1", (128, 4, SEQ), BF16)
    hTlo = sb(es, "hTlo", (128, 4, SEQ), BF16)
    hThi = sb(es, "hThi", (128, 4, SEQ), BF16)
    qT = sb(es, "qT", (128, 4, SEQ), BF16)
    UY = sb(es, "UY", (128, 4, SEQ), BF16)
    kT = sb(es, "kT", (128, SEQ), BF16)
    vaug = sb(es, "vaug", (128, NT, 2, 66), BF16)
    xt = [sb(es, f"xt{i}", (128, 1024)) for i in range(3)]
    hs = xt
    xo = xt
    junk = sb(es, "junk", (128, 1024), BF16)
    ssq = sb(es, "ssq", (128, 4, NT))
    rs = sb(es, "rs", (128, 4, NT))
    B = lambda n=1: [Buf() for _ in range(n)]
    X1_b, hTlo_b, hThi_b, qT_b, UY_b, kT_b, va_b, H_b, Ein_b = (Buf() for _ in range(9))
    Hf_b, Hb_b = Buf(), Buf()
    xt_b, mAb_b, mBb_b, et_b, pt_b, oa_b, den_b, gate_b, mt_b = B(3), B(2), B(2), B(3), B(6), B(2), B(2), B(2), B(2)
    hs_b = xt_b
    xo_b = xt_b
    junk_b, ssq_b, rs_b, st_b = Buf(), [Buf() for _ in range(4)], [Buf() for _ in range(4)], [Buf(), Buf()]
    m1s_b = [(Buf(), Buf()) for _ in range(4)]
    K.op(dve, lambda: V.memset(vaug[:, :, :, 64:66], 1.0), writes=[va_b])
    hT_b = [hTlo_b, hThi_b]
    hT = [hTlo, hThi]
    Uf = UY[:].rearrange("p c t -> p (c t)").rearrange("p (g s) -> p g s", g=NG)
    Zf = hTlo[:].rearrange("p c t -> p (c t)").rearrange("p (g s) -> p g s", g=NG)
    ysT, ysT_b = qT, qT_b
    yaT, yaT_b = hThi, hThi_b
    ysq, ysq_b = UY, UY_b

    for s in range(nseq):
        tok0 = s * SEQ
        def m1_a(tt):
            i = tt % 3
            r0 = tok0 + tt * 128
            rd = [out_b[s][tt]] if x_src is out else []
            K.dma(sp, xt[i][:], x_src[r0:r0 + 128, :], reads=rd, writes=[xt_b[i]])
            sq_b, r_b = m1s_b[tt % 4]
            K.op(act, lambda: nc.scalar.activation(out=junk[:], in_=xt[i][:], func=AF.Square, accum_out=ssq[:, 0, tt:tt + 1]),
                 reads=[xt_b[i]], writes=[junk_b, sq_b])
            K.op(act, lambda: nc.scalar.activation(out=rs[:, 0, tt:tt + 1], in_=ssq[:, 0, tt:tt + 1], func=AF.Sqrt,
                                                   bias=epsb[:, 0:1], scale=1.0 / D_MODEL), reads=[sq_b, cb], writes=[r_b])
            K.op(dve, lambda: V.reciprocal(out=rs[:, 0, tt:tt + 1], in_=rs[:, 0, tt:tt + 1]), reads=[r_b], writes=[r_b])
            K.op(dve, lambda: V.tensor_scalar(out=hs[i][:], in0=xt[i][:], scalar1=rs[:, 0, tt:tt + 1], scalar2=None, op0=ALU.mult),
                 reads=[xt_b[i], r_b], writes=[hs_b[i]])

        def m1_b(tt):
            i = tt % 3
            b = next_ps(2)
            for c in range(8):
                K.op(pe, lambda: nc.tensor.transpose(psum[:, b + c // 4, (c % 4) * 128:(c % 4 + 1) * 128],
                                                     hs[i][:, c * 128:(c + 1) * 128], ident[:]),
                     reads=[hs_b[i], cb], writes=[ps_b[b], ps_b[b + 1]], inc=(c == 7))
            K.op(dve, lambda: V.tensor_copy(out=hTlo[:, :, tt * 128:(tt + 1) * 128],
                                            in_=psum[:, b, :].rearrange("p (c t) -> p c t", c=4)),
                 reads=[ps_b[b]], writes=[hTlo_b])
            K.op(act, lambda: nc.scalar.copy(out=hThi[:, :, tt * 128:(tt + 1) * 128],
                                             in_=psum[:, b + 1, :].rearrange("p (c t) -> p c t", c=4)),
                 reads=[ps_b[b + 1]], writes=[hThi_b])

        m1_a(0)
        for tt in range(NT):
            if tt + 1 < NT:
                m1_a(tt + 1)
            m1_b(tt)
        for m in range(9):
            if m == 4:
                K.dma(sp, u_scr.rearrange("(c r) t -> r c t", c=4), X1[:], reads=[X1_b], writes=[u_scr_b])
                for s8 in range(8):
                    K.dma(sp, Uf[s8 * 16:(s8 + 1) * 16, :, :],
                          u_scr.rearrange("(g q) (s sub) -> s q g sub", q=16, s=8)[s8], reads=[u_scr_b], writes=[UY_b])
            for tg in range(4):
                b = next_ps()
                for k in range(8):
                    K.op(pe, lambda: nc.tensor.matmul(psum[:, b, :], wi[:, k, m * 128:(m + 1) * 128],
                                                      hT[k // 4][:, k % 4, tg * 512:(tg + 1) * 512],
                                                      start=(k == 0), stop=(k == 7)),
                         reads=[wi_b, hT_b[k // 4]], writes=[ps_b[b]], inc=(k == 7))
                if m < 4:
                    K.op(act, lambda: nc.scalar.copy(
                        out=X1[:, m, :].rearrange("p (s sub) -> p sub s", s=8)[:, tg * 64:(tg + 1) * 64, :],
                        in_=psum[:, b, :].rearrange("p (sub s) -> p sub s", s=8)), reads=[ps_b[b]], writes=[X1_b])
                elif m < 8:
                    K.op(dve, lambda: V.tensor_copy(out=qT[:, m - 4, tg * 512:(tg + 1) * 512], in_=psum[:, b, :]),
                         reads=[ps_b[b]], writes=[qT_b])
                else:
                    K.op(act, lambda: nc.scalar.copy(out=kT[:, tg * 512:(tg + 1) * 512], in_=psum[:, b, :]),
                         reads=[ps_b[b]], writes=[kT_b])
        for tt in range(NT):
            b = next_ps()
            for k in range(8):
                K.op(pe, lambda: nc.tensor.matmul(psum[:, b, 0:128], hT[k // 4][:, k % 4, tt * 128:(tt + 1) * 128],
                                                  wi[:, k, 1152:1280], start=(k == 0), stop=(k == 7)),
                     reads=[wi_b, hT_b[k // 4]], writes=[ps_b[b]], inc=(k == 7))
            K.op(dve, lambda: V.tensor_copy(out=vaug[:, tt, :, 0:64], in_=psum[:, b, 0:128].rearrange("p (k d) -> p k d", k=2)),
                 reads=[ps_b[b]], writes=[va_b])
        if s == 0 and "uT" in dbg:
            K.dma(sp, dbg["uT"], X1[:], reads=[X1_b])
            K.dma(sp, dbg["qT"], qT[:], reads=[qT_b])
            K.dma(sp, dbg["kT"], kT[:], reads=[kT_b])
            K.dma(sp, dbg["vaug"], vaug[:], reads=[va_b])

        e_s5 = ExitStack()
        H = sb(e_s5, "H", (128, 2, NG, NCH + 1))
        Ein = sb(e_s5, "Ein", (128, 2, NG, NCH), BF16)
        st1 = sb(e_s5, "st1", (128, 2, NG))
        st2 = sb(e_s5, "st2", (128, 2, NG))
        mAb = [sb(e_s5, f"mAb{i}", (128, MA_COLS), BF16) for i in range(2)]
        mBb = [sb(e_s5, f"mBb{i}", (128, MB_COLS), BF16) for i in range(2)]
        e_att = ExitStack()
        et = [sb(e_att, f"et{i}", (128, 512), BF16) for i in range(3)]
        pt = [sb(e_att, f"pt{i}", (128, 512), BF16) for i in range(6)]
        oa = [sb(e_att, f"oa{i}", (128, 512)) for i in range(2)]
        den = [sb(e_att, f"den{i}", (128, 4)) for i in range(2)]

        if l == 0 and s == 0:
            print("mixer sbuf bytes remaining", nc.sbuf_bytes_remaining)

        def att_S(qb, kv):
            kbs = [kb for kb in (qb - 1, qb, qb + 1) if 0 <= kb < NT]
            rows = slice(kv * 64, (kv + 1) * 64)
            pts = []
            for kb in kbs:
                ie = K_rr(K, "et", 3)
                ip = K_rr(K, "pt", 6)
                b = next_ps()
                K.op(pe, lambda: nc.tensor.matmul(psum[:, b, :], kT[rows, kb * 128:(kb + 1) * 128],
                                                  qT[rows, :, qb * 128:(qb + 1) * 128], start=True, stop=True),
                     reads=[kT_b, qT_b], writes=[ps_b[b]])
                K.op(act, lambda: nc.scalar.activation(out=et[ie][:], in_=psum[:, b, :], func=AF.Exp, scale=0.125),
                     reads=[ps_b[b]], writes=[et_b[ie]])
                EM = dve
                K.op(EM, lambda: EM.eng.tensor_tensor(out=pt[ip][:], in0=et[ie][:],
                                                      in1=eb[:, kb - qb + 1, kv].rearrange("p c q -> p (c q)"), op=ALU.mult),
                     reads=[et_b[ie], ebB], writes=[pt_b[ip]])
                pts.append((ip, kb))
            return pts

        def att_P(qb, kv, pts):
            b = next_ps()
            for c in range(4):
                for n_, (ip, kb) in enumerate(pts):
                    K.op(pe, lambda: nc.tensor.matmul(psum[:, b, c * 65:(c + 1) * 65], pt[ip][:, c * 128:(c + 1) * 128],
                                                      vaug[:, kb, kv, 0:65], start=(n_ == 0), stop=(n_ == len(pts) - 1)),
                         reads=[pt_b[ip], va_b], writes=[ps_b[b]], inc=(c == 3 and n_ == len(pts) - 1))
            io = qb % 2
            ov = psum[:, b, 0:260].rearrange("p (c d) -> p c d", c=4)
            K.op(dve, lambda: V.tensor_tensor(out=den[kv][:], in0=ov[:, :, 64], in1=esink[:, kv * 4:(kv + 1) * 4], op=ALU.add),
                 reads=[ps_b[b], pb], writes=[den_b[kv]])
            K.op(dve, lambda: V.reciprocal(out=den[kv][:], in_=den[kv][:]), reads=[den_b[kv]], writes=[den_b[kv]])
            K.op(dve, lambda: V.tensor_tensor(out=oa[io][:, kv * 256:(kv + 1) * 256].rearrange("p (c d) -> p c d", c=4),
                                              in0=ov[:, :, 0:64], in1=den[kv][:].unsqueeze(2).to_broadcast([128, 4, 64]),
                                              op=ALU.mult), reads=[ps_b[b], den_b[kv]], writes=[oa_b[io]])

        def att_F(qb):
            io = qb % 2
            K.op(act, lambda: nc.scalar.activation(out=junk[:, 0:512], in_=oa[io][:], func=AF.Square, accum_out=ssq[:, 2, qb:qb + 1]),
                 reads=[oa_b[io]], writes=[junk_b, ssq_b[2]])
            b = next_ps()
            for c in range(4):
                K.op(pe, lambda: nc.tensor.transpose(psum[:, b, c * 128:(c + 1) * 128], oa[io][:, c * 128:(c + 1) * 128], ident[:]),
                     reads=[oa_b[io], cb], writes=[ps_b[b]], inc=(c == 3))
            K.op(act, lambda: nc.scalar.copy(out=yaT[:, :, qb * 128:(qb + 1) * 128], in_=psum[:, b, :].rearrange("p (c t) -> p c t", c=4)),
                 reads=[ps_b[b]], writes=[yaT_b])

        def scan_step(c):
            hs_ = (slice(0, NG // 2), slice(NG // 2, NG))
            hb_ = (Hf_b, Hb_b)
            for (in_sw, ai) in ((False, 0), (True, 1)):
                for h2 in range(2):
                    gs = hs_[h2]
                    src = H[:, ::-1, gs, c] if in_sw else H[:, :, gs, c]
                    K.op(dve, lambda: V.tensor_tensor(out=st1[:, :, gs], in0=src, in1=aL[:, ai, :, gs], op=ALU.mult),
                         reads=[H_b, aL_b, hb_[h2]], writes=[hb_[h2]])
                for h2 in range(2):
                    gs = hs_[h2]
                    K.op(dve, lambda: V.tensor_tensor(out=H[:, :, gs, c + 1], in0=H[:, :, gs, c + 1], in1=st1[:, :, gs], op=ALU.add),
                         reads=[H_b, aL_b, hb_[h2]], writes=[hb_[h2]])

        K.op(dve, lambda: V.memset(H[:, :, :, 0:1], 0.0), writes=[H_b, Hf_b, Hb_b])

        def mA_load(g):
            K.dma(sp, mAb[g % 2][:], matsA[g], reads=[matsA_b[g]], writes=[mAb_b[g % 2]])

        def passA_batch(g0):
            b = next_ps()
            for gl in range(4):
                g = g0 + gl
                im = g % 2
                for ri in range(2):
                    for j in range(JC):
                        K.op(pe, lambda: nc.tensor.matmul(
                            psum[:, b, (ri * 4 + gl) * NCH:(ri * 4 + gl + 1) * NCH],
                            mAb[im][:, (ri * JC + j) * 128:(ri * JC + j + 1) * 128],
                            Uf[:, g, :].rearrange("p (c j) -> p c j", j=JC)[:, :, j],
                            start=(j == 0), stop=(j == JC - 1)),
                            reads=[mAb_b[im], UY_b], writes=[ps_b[b]], inc=(ri == 1 and j == JC - 1))
                if g + 2 < G:
                    mA_load(g + 2)
            pv = psum[:, b, :].rearrange("p (r g c) -> p r g c", r=2, g=4)
            K.op(act, lambda: nc.scalar.copy(out=H[0:64, :, g0:g0 + 4, 1:NCH + 1], in_=pv[0:64]), reads=[ps_b[b]], writes=[H_b, Hf_b, Hb_b])
            K.op(act, lambda: nc.scalar.copy(out=H[64:128, :, g0:g0 + 4, 1:NCH + 1][:, :, :, ::-1], in_=pv[64:128]), reads=[ps_b[b]], writes=[H_b, Hf_b, Hb_b])

        assert NCH % (NT // 2) == 0 and G // 4 == NT // 2
        mA_load(0)
        mA_load(1)
        tasks = [(qb, kv) for qb in range(NT) for kv in range(2)]
        spq = NCH // (NT // 2)
        cur = att_S(*tasks[0])
        fin = None
        for ti, (qb, kv) in enumerate(tasks):
            nxt = att_S(*tasks[ti + 1]) if ti + 1 < len(tasks) else None
            att_P(qb, kv, cur)
            cur = nxt
            if fin is not None:
                att_F(fin)
                fin = None
            if kv == 1:
                fin = qb
                if qb < NT // 2:
                    passA_batch(4 * qb)
                else:
                    for c in range((qb - NT // 2) * spq, (qb - NT // 2 + 1) * spq):
                        scan_step(c)
        att_F(fin)
        K.op(dve, lambda: V.tensor_copy(out=Ein[0:64], in_=H[0:64, :, :, 0:NCH]), reads=[H_b, Hf_b, Hb_b], writes=[Ein_b])
        K.op(act, lambda: nc.scalar.copy(out=Ein[64:128], in_=H[64:128, :, :, 0:NCH][:, :, :, ::-1]), reads=[H_b, Hf_b, Hb_b], writes=[Ein_b])
        def mB_load(g):
            K.dma(sp, mBb[g % 2][:], matsB[g], reads=[matsB_b[g]], writes=[mBb_b[g % 2]])
        mB_load(0)
        mB_load(1)
        for g0 in range(0, G, 2):
            b = next_ps()
            K.op(dve, lambda: V.memset(psum[:, b, :], 0.0), writes=[ps_b[b]])
            for gl in range(2):
                g = g0 + gl
                im = g % 2
                Ug = Uf[:, g, :].rearrange("p (c j) -> p c j", j=JC)
                Yg = psum[:, b, gl * 256:(gl + 1) * 256].rearrange("p (c j) -> p c j", j=JC)
                mm = []
                for d in range(JC):
                    mm.append((Yg[:, :, d:JC], mBb[im][:, d * 128:(d + 1) * 128], Ug[:, :, 0:JC - d], [UY_b]))
                    mm.append((Yg[:, :, 0:JC - d], mBb[im][:, (JC + d) * 128:(JC + d + 1) * 128], Ug[:, :, d:JC], [UY_b]))
                for j in range(JC):
                    mm.append((Yg[:, :, j], mBb[im][:, (2 * JC + j) * 128:(2 * JC + j + 1) * 128], Ein[:, 0, g, :], [Ein_b]))
                    mm.append((Yg[:, :, j], mBb[im][:, (3 * JC + j) * 128:(3 * JC + j + 1) * 128], Ein[:, 1, g, :], [Ein_b]))
                for n_, (o_, l_, r_, rd_) in enumerate(mm):
                    K.op(pe, lambda: nc.tensor.matmul(o_, l_, r_, start=False, stop=(n_ == len(mm) - 1), skip_group_check=True),
                         reads=[mBb_b[im]] + rd_, writes=[ps_b[b]], inc=(n_ == len(mm) - 1))
                if g + 2 < G:
                    mB_load(g + 2)
            K.op(act, lambda: nc.scalar.activation(out=Zf[:, g0:g0 + 2, :], in_=psum[:, b, :].rearrange("p (g s) -> p g s", g=2),
                                                   func=AF.Gelu_apprx_tanh), reads=[ps_b[b]], writes=[hTlo_b])
        for s8 in range(8):
            K.dma(sp, z_scr.rearrange("(g q) (s sub) -> s q g sub", q=16, s=8)[s8], Zf[s8 * 16:(s8 + 1) * 16, :, :],
                  reads=[hTlo_b], writes=[z_scr_b])
        K.dma(sp, X1[:], z_scr.rearrange("(c r) t -> r c t", c=4), reads=[z_scr_b], writes=[X1_b])
        K.barrier()
        e_att.close()
        gate = [sb(e_s5, f"gate{i}", (128, 512)) for i in range(2)]
        for m in range(4):
            for tg in range(4):
                b = next_ps()
                ig = (m * 4 + tg) % 2
                for k in range(4):
                    K.op(pe, lambda: nc.tensor.matmul(psum[:, b, :], wg[:, k, m * 128:(m + 1) * 128], X1[:, k, tg * 512:(tg + 1) * 512],
                                                      start=(k == 0), stop=(k == 3)),
                         reads=[wg_b, X1_b], writes=[ps_b[b]], inc=(k == 3))
                K.op(act, lambda: nc.scalar.activation(out=gate[ig][:], in_=psum[:, b, :], func=AF.Sigmoid, bias=bglu[:, m:m + 1]),
                     reads=[ps_b[b], pb], writes=[gate_b[ig]])
                nat = lambda tns: tns[:, m, :].rearrange("p (sub s) -> p s sub", s=8)[:, 2 * tg:2 * tg + 2, :]
                K.op(dve, lambda: V.tensor_tensor(out=nat(ysT), in0=X1[:, m, tg * 512:(tg + 1) * 512].rearrange("p (s sub) -> p s sub", s=2),
                                                  in1=gate[ig][:].rearrange("p (s sub) -> p s sub", s=2), op=ALU.mult),
                     reads=[X1_b, gate_b[ig]], writes=[ysT_b])
        K.op(act, lambda: nc.scalar.activation(out=ysq[:], in_=ysT[:], func=AF.Square), reads=[ysT_b], writes=[ysq_b])

        K.barrier()
        e_s5.close()
        if s == 0 and "ysT" in dbg:
            K.dma(sp, dbg["ysT"], ysT[:], reads=[ysT_b])
            K.dma(sp, dbg["yaT"], yaT[:], reads=[yaT_b])
            K.dma(sp, dbg["zT"], X1[:], reads=[X1_b])
        e_m4 = ExitStack()
        mt = [sb(e_m4, f"mt{i}", (128, 1024)) for i in range(2)]
        b = next_ps()
        for tt in range(NT):
            for k in range(4):
                K.op(pe, lambda: nc.tensor.matmul(psum[:, b, tt:tt + 1], ysq[:, k, tt * 128:(tt + 1) * 128], ones_bf[:, 0:1],
                                                  start=(k == 0), stop=(k == 3)),
                     reads=[ysq_b, cb], writes=[ps_b[b]], inc=(k == 3 and tt == NT - 1))
        K.op(act, lambda: nc.scalar.activation(out=rs[:, 1, :], in_=psum[:, b, 0:NT], func=AF.Sqrt, bias=epsb[:, 0:1], scale=1.0 / SSM_W),
             reads=[ps_b[b], cb], writes=[rs_b[1]])
        K.op(dve, lambda: V.reciprocal(out=rs[:, 1, :], in_=rs[:, 1, :]), reads=[rs_b[1]], writes=[rs_b[1]])
        K.op(act, lambda: nc.scalar.activation(out=rs[:, 2, :], in_=ssq[:, 2, :], func=AF.Sqrt, bias=epsb[:, 0:1], scale=1.0 / SSM_W),
             reads=[ssq_b[2], cb], writes=[rs_b[2]])
        K.op(dve, lambda: V.reciprocal(out=rs[:, 2, :], in_=rs[:, 2, :]), reads=[rs_b[2]], writes=[rs_b[2]])
        for tt in range(NT):
            i = tt % 2
            r0 = tok0 + tt * 128
            ba = next_ps(2)
            for hf in range(2):
                for k in range(4):
                    K.op(pe, lambda: nc.tensor.matmul(psum[:, ba + hf, :], ysT[:, k, tt * 128:(tt + 1) * 128], wo[:, k, hf * 512:(hf + 1) * 512],
                                                      start=(k == 0), stop=(k == 3)),
                         reads=[ysT_b, wo_b], writes=[ps_b[ba + hf]], inc=(k == 3))
            bb_ = next_ps(2)
            for hf in range(2):
                for k in range(4):
                    K.op(pe, lambda: nc.tensor.matmul(psum[:, bb_ + hf, :], yaT[:, k, tt * 128:(tt + 1) * 128], wo[:, 4 + k, hf * 512:(hf + 1) * 512],
                                                      start=(k == 0), stop=(k == 3)),
                         reads=[yaT_b, wo_b], writes=[ps_b[bb_ + hf]], inc=(k == 3))
            K.op(act, lambda: nc.scalar.activation(out=mt[i][:], in_=psum[:, ba:ba + 2, :].rearrange("p a n -> p (a n)"), func=AF.Copy,
                                                   scale=rs[:, 1, tt:tt + 1]), reads=[ps_b[ba], ps_b[ba + 1], rs_b[1]], writes=[mt_b[i]])
            K.op(dve, lambda: V.scalar_tensor_tensor(out=mt[i][:], in0=psum[:, bb_:bb_ + 2, :].rearrange("p a n -> p (a n)"),
                                                     scalar=rs[:, 2, tt:tt + 1], in1=mt[i][:], op0=ALU.mult, op1=ALU.add),
                 reads=[ps_b[bb_], ps_b[bb_ + 1], rs_b[2], mt_b[i]], writes=[mt_b[i]])
            K.op(act, lambda: nc.scalar.activation(out=junk[:], in_=mt[i][:], func=AF.Square, accum_out=ssq[:, 3, tt:tt + 1]),
                 reads=[mt_b[i]], writes=[junk_b, ssq_b[3]])
            K.op(act, lambda: nc.scalar.activation(out=rs[:, 3, tt:tt + 1], in_=ssq[:, 3, tt:tt + 1], func=AF.Sqrt,
                                                   bias=epsb[:, 0:1], scale=1.0 / D_MODEL), reads=[ssq_b[3], cb], writes=[rs_b[3]])
            K.op(dve, lambda: V.reciprocal(out=rs[:, 3, tt:tt + 1], in_=rs[:, 3, tt:tt + 1]), reads=[rs_b[3]], writes=[rs_b[3]])
            K.op(dve, lambda: V.tensor_tensor(out=mt[i][:], in0=mt[i][:], in1=gpost[:], op=ALU.mult),
                 reads=[mt_b[i], pb], writes=[mt_b[i]])
            rd = [out_b[s][tt]] if x_src is out else []
            K.dma(sp, xt[i][:], x_src[r0:r0 + 128, :], reads=rd, writes=[xt_b[i]])
            K.op(dve, lambda: V.scalar_tensor_tensor(out=xo[i][:], in0=mt[i][:], scalar=rs[:, 3, tt:tt + 1], in1=xt[i][:],
                                                     op0=ALU.mult, op1=ALU.add),
                 reads=[mt_b[i], rs_b[3], xt_b[i]], writes=[xo_b[i]])
            K.dma(sp, out[r0:r0 + 128, :], xo[i][:], reads=[xo_b[i]], writes=[out_b[s][tt]])
        K.barrier()
        e_m4.close()


_RR = {}


def K_rr(K, key, n):
    v = _RR.get(key, 0)
    _RR[key] = (v + 1) % n
    return v


def ffn_phase(nc, K, es, sb, psum, ps_b, next_ps, P, l, nseq, out, out_b, ident, identb, epsb, cb, dbg):
    pe, act, dve, pool, sp = K.pe, K.act, K.dve, K.pool, K.sp
    V = nc.vector
    wu = sb(es, "wu", (128, 8, 2 * D_FF), BF16)
    wd = sb(es, "wd", (128, NFB, 1024), BF16)
    gpf = sb(es, "gpf", (128, 8))
    cw = sb(es, "cw", (128, 4, 2 * NFB))
    gpost = sb(es, "gpost2", (128, 1024))
    wu_b, wd_b, pb = Buf(), Buf(), Buf()
    with ExitStack() as es1:
        ld = sb(es1, "ldf", (2 * NFB, 128))
        ld_b = Buf()
        LC = lambda dst, src, n: load_cols(nc, K, psum, ps_b, next_ps, ident, cb, ld, ld_b, dst, pb, src, n)
        LC(gpf[:], P["pre_ffn_norm"][l], 8)
        for j in range(3):
            LC(cw[:, j, :], P["conv_w"][l, j], 2 * NFB)
        LC(cw[:, 3, :], P["conv_b"][l], 2 * NFB)
        K.barrier()
    K.dma(sp, gpost[:], P["post_ffn_norm"][l].partition_broadcast(128), writes=[pb])
    with ExitStack() as es2:
        engs = [act, dve]
        stage = make_staging(es2, sb, "f")
        load_cast_weight(nc, K, stage, wu, wu_b, [P["w_up"][l, k * 128:(k + 1) * 128, :] for k in range(8)], 2 * D_FF,
                         gpf, pb, engs)
        load_cast_weight(nc, K, stage, wd, wd_b, [P["w_down"][l, k * 128:(k + 1) * 128, :] for k in range(NFB)], 1024,
                         None, pb, engs)
        K.barrier()

    HW = 1026
    h2T = sb(es, "h2T", (128, 8, HW), BF16)
    NPRE = 4
    gbuf = sb(es, "gbuf", (128, NFB - NPRE, 384), BF16)
    gpre = [sb(es, f"gpre{i}", (128, NPRE, 384), BF16) for i in range(2)]
    gbuf_b = [Buf() for _ in range(NFB - NPRE)]
    gpre_b = [[Buf() for _ in range(NPRE)] for _ in range(2)]

    def gslot(kpar, fb):
        if fb < NPRE:
            return gpre[kpar], fb, gpre_b[kpar][fb]
        return gbuf, fb - NPRE, gbuf_b[fb - NPRE]
    pending = []
    tgk = [0]
    halo = sb(es, "halo", (128, 8, 2), BF16)
    xt = [sb(es, f"fxt{i}", (128, 1024)) for i in range(3)]
    hs = xt
    junk = sb(es, "fjunk", (128, 1024), BF16)
    cv = [sb(es, f"cv{i}", (128, 384)) for i in range(3)]
    cg = [sb(es, f"cg{i}", (128, 384)) for i in range(3)]
    gg = [sb(es, f"gg{i}", (128, 384)) for i in range(3)]
    yt = [sb(es, f"yt{i}", (128, 1024)) for i in range(1)] * 2
    xo = xt
    st = sb(es, "fst", (128, 4))
    stf = sb(es, "fstf", (128, 2, 4))
    stf_b = [Buf() for _ in range(4)]
    std = sb(es, "fstd", (128, 2, 2))
    std_b = [Buf() for _ in range(2)]
    B = lambda n=1: [Buf() for _ in range(n)]
    h2T_b, g_b, junk_b, st_b = Buf(), Buf(), Buf(), Buf()
    xt_b, cv_b, cg_b, gg_b = B(3), B(3), B(3), B(3)
    yt_b = B(1) * 2
    hs_b = xt_b
    xo_b = xt_b
    def emit_down(s, tok0, hs0, t0, ln, kpar):
        for ti in range(ln // 128):
            tt = (hs0 + t0) // 128 + ti
            r0 = tok0 + tt * 128
            i = tt % 2
            b = next_ps(2)
            for hf in range(2):
                for fb in range(NFB):
                    gt_, gi_, gb_ = gslot(kpar, fb)
                    K.op(pe, lambda: nc.tensor.matmul(psum[:, b + hf, :], gt_[:, gi_, ti * 128:(ti + 1) * 128],
                                                      wd[:, fb, hf * 512:(hf + 1) * 512], start=(fb == 0), stop=(fb == NFB - 1)),
                         reads=[gb_, wd_b], writes=[ps_b[b + hf]], inc=(fb == NFB - 1))
            pv = psum[:, b:b + 2, :].rearrange("p a n -> p (a n)")
            K.op(act, lambda: nc.scalar.activation(out=junk[:], in_=pv, func=AF.Square, accum_out=std[:, 0, i:i + 1]),
                 reads=[ps_b[b], ps_b[b + 1]], writes=[junk_b, std_b[i]])
            K.op(act, lambda: nc.scalar.activation(out=std[:, 1, i:i + 1], in_=std[:, 0, i:i + 1], func=AF.Sqrt, bias=epsb[:, 0:1], scale=1.0 / D_MODEL),
                 reads=[std_b[i], cb], writes=[std_b[i]])
            K.op(dve, lambda: V.reciprocal(out=std[:, 1, i:i + 1], in_=std[:, 1, i:i + 1]), reads=[std_b[i]], writes=[std_b[i]])
            K.op(dve, lambda: V.tensor_tensor(out=yt[i][:], in0=pv, in1=gpost[:], op=ALU.mult),
                 reads=[ps_b[b], ps_b[b + 1], pb], writes=[yt_b[i]])
            K.dma(sp, xt[i][:], out[r0:r0 + 128, :], reads=[out_b[s][tt]], writes=[xt_b[i]])
            K.op(dve, lambda: V.scalar_tensor_tensor(out=xo[i][:], in0=yt[i][:], scalar=std[:, 1, i:i + 1], in1=xt[i][:],
                                                     op0=ALU.mult, op1=ALU.add),
                 reads=[yt_b[i], std_b[i], xt_b[i]], writes=[xo_b[i]])
            K.dma(sp, out[r0:r0 + 128, :], xo[i][:], reads=[xo_b[i]], writes=[out_b[s][tt]])

    it = [0]
    if l == 0:
        print("ffn sbuf bytes remaining", nc.sbuf_bytes_remaining)

    for s in range(nseq):
        tok0 = s * SEQ
        for half in range(2):
            hs0 = half * 1024
            t_lo = hs0 // 128 - 1
            if half == 1:
                K.op(dve, lambda: V.tensor_copy(out=h2T[:, :, 0:1], in_=halo[:, :, 0:1]), reads=[h2T_b], writes=[h2T_b])
            tiles = [tt for tt in range(t_lo, t_lo + 10) if not (tt < 0 or tt >= NT or (half == 1 and tt == t_lo))]

            def fill_a(tt):
                i = it[0] % 3
                js = it[0] % 4
                it[0] += 1
                r0 = tok0 + tt * 128
                K.dma(sp, xt[i][:], out[r0:r0 + 128, :], reads=[out_b[s][tt]], writes=[xt_b[i]])
                K.op(act, lambda: nc.scalar.activation(out=junk[:], in_=xt[i][:], func=AF.Square, accum_out=stf[:, 0, js:js + 1]),
                     reads=[xt_b[i]], writes=[junk_b, stf_b[js]])
                K.op(act, lambda: nc.scalar.activation(out=stf[:, 1, js:js + 1], in_=stf[:, 0, js:js + 1], func=AF.Sqrt, bias=epsb[:, 0:1], scale=1.0 / D_MODEL),
                     reads=[stf_b[js], cb], writes=[stf_b[js]])
                K.op(dve, lambda: V.reciprocal(out=stf[:, 1, js:js + 1], in_=stf[:, 1, js:js + 1]), reads=[stf_b[js]], writes=[stf_b[js]])
                K.op(dve, lambda: V.tensor_scalar(out=hs[i][:], in0=xt[i][:], scalar1=stf[:, 1, js:js + 1], scalar2=None, op0=ALU.mult),
                     reads=[xt_b[i], stf_b[js]], writes=[hs_b[i]])
                return i

            def fill_b(tt, i):
                b = next_ps(2)
                for c in range(8):
                    K.op(pe, lambda: nc.tensor.transpose(psum[:, b + c // 4, (c % 4) * 128:(c % 4 + 1) * 128],
                                                         hs[i][:, c * 128:(c + 1) * 128], ident[:]),
                         reads=[hs_b[i], cb], writes=[ps_b[b], ps_b[b + 1]], inc=(c == 7))
                j0 = tt * 128 - hs0 + 1
                lo, hi = max(j0, 0), min(j0 + 128, HW)
                for hh in range(2):
                    src = psum[:, b + hh, :].rearrange("p (c t) -> p c t", c=4)[:, :, lo - j0:hi - j0]
                    dst = h2T[:, hh * 4:(hh + 1) * 4, lo:hi]
                    if hh == 0:
                        K.op(dve, lambda: V.tensor_copy(out=dst, in_=src), reads=[ps_b[b + hh]], writes=[h2T_b])
                    else:
                        K.op(act, lambda: nc.scalar.copy(out=dst, in_=src), reads=[ps_b[b + hh]], writes=[h2T_b])

            ia = fill_a(tiles[0])
            for n_, tt in enumerate(tiles):
                ia_next = fill_a(tiles[n_ + 1]) if n_ + 1 < len(tiles) else None
                fill_b(tt, ia)
                ia = ia_next
            if half == 0:
                K.op(dve, lambda: V.tensor_copy(out=halo[:, :, 0:1], in_=h2T[:, :, 1024:1025]), reads=[h2T_b], writes=[h2T_b])
            for (t0, ln) in ((0, 384), (384, 384), (768, 256)):
                g_first = (hs0 + t0 == 0)
                g_last = (hs0 + t0 + ln == SEQ)
                lo = 1 if g_first else 0
                hi = 1 if g_last else 0
                w0c, w1c = t0 + lo, t0 + ln + 2 - hi
                nW = w1c - w0c
                ctr = 1 - lo
                kpar = tgk[0] % 2
                tgk[0] += 1
                for fb in range(NFB):
                    i = fb % 3
                    if fb == NPRE:
                        for fn in pending:
                            fn()
                        pending.clear()
                    for vg in range(2):
                        rb = vg * NFB + fb
                        b = next_ps()
                        for k in range(8):
                            K.op(pe, lambda: nc.tensor.matmul(psum[:, b, 0:nW], wu[:, k, rb * 128:(rb + 1) * 128],
                                                              h2T[:, k, w0c:w1c], start=(k == 0), stop=(k == 7)),
                                 reads=[wu_b, h2T_b], writes=[ps_b[b]], inc=(k == 7))
                        dst, dst_b = (cv[i], cv_b[i]) if vg == 0 else (cg[i], cg_b[i])
                        K.op(act, lambda: nc.scalar.activation(out=dst[:, 0:ln], in_=psum[:, b, ctr:ctr + ln], func=AF.Identity,
                                                               bias=cw[:, 3, rb:rb + 1], scale=cw[:, 1, rb:rb + 1]),
                             reads=[ps_b[b], pb], writes=[dst_b])
                        K.op(dve, lambda: V.scalar_tensor_tensor(out=dst[:, lo:ln], in0=psum[:, b, ctr + lo - 1:ctr + ln - 1],
                                                                 scalar=cw[:, 0, rb:rb + 1], in1=dst[:, lo:ln], op0=ALU.mult, op1=ALU.add),
                             reads=[ps_b[b], pb, dst_b], writes=[dst_b])
                        K.op(dve, lambda: V.scalar_tensor_tensor(out=dst[:, 0:ln - hi], in0=psum[:, b, ctr + 1:ctr + ln - hi + 1],
                                                                 scalar=cw[:, 2, rb:rb + 1], in1=dst[:, 0:ln - hi], op0=ALU.mult, op1=ALU.add),
                             reads=[ps_b[b], pb, dst_b], writes=[dst_b])
                    K.op(act, lambda: nc.scalar.activation(out=gg[i][:, 0:ln], in_=cg[i][:, 0:ln], func=AF.Gelu_apprx_tanh),
                         reads=[cg_b[i]], writes=[gg_b[i]])
                    gt_, gi_, gb_ = gslot(kpar, fb)
                    K.op(pool, lambda: nc.gpsimd.tensor_tensor(out=gt_[:, gi_, 0:ln], in0=gg[i][:, 0:ln], in1=cv[i][:, 0:ln], op=ALU.mult),
                         reads=[gg_b[i], cv_b[i]], writes=[gb_])
                pending.append(lambda s=s, tok0=tok0, hs0=hs0, t0=t0, ln=ln, kpar=kpar: emit_down(s, tok0, hs0, t0, ln, kpar))
    for fn in pending:
        fn()
    pending.clear()
_CACHE = {}


def _q_perm():
    cols = list(range(512))
    for c in range(4):
        for h in (c, 4 + c):
            cols.extend(range(512 + h * 64, 512 + (h + 1) * 64))
    cols.extend(range(1024, 1280))
    return np.asarray(cols)


def kernel(**inputs):
    n_cores = 8
    x = np.ascontiguousarray(np.asarray(inputs["x"], dtype=np.float32))
    nseq = x.shape[0] // n_cores
    if "nc" not in _CACHE:
        _CACHE["nc"] = build(nseq=nseq)[0]
        _CACHE["consts"] = _host_consts()
    nc = _CACHE["nc"]
    consts = _CACHE["consts"]
    shared = {}
    for k, v in inputs.items():
        if k == "x":
            continue
        a = np.ascontiguousarray(np.asarray(v, dtype=np.float32))
        if k == "w_in":
            a = np.ascontiguousarray(a[:, :, _q_perm()])
        shared[k] = a
    shared.update(consts)
    in_maps = []
    for i in range(n_cores):
        m = dict(shared)
        m["x"] = x[i * nseq:(i + 1) * nseq].reshape(nseq * SEQ, D_MODEL)
        in_maps.append(m)
    res = run_bass_kernel_spmd(nc, in_maps, core_ids=list(range(n_cores)))
    outs = [np.asarray(r["out"]).reshape(nseq, SEQ, D_MODEL) for r in res.results]
    return np.concatenate(outs, axis=0).astype(np.float32)
```

```python
import math
from contextlib import ExitStack
import numpy as np
import jax
import jax.numpy as jnp
import concourse.bass as bass
import concourse.mybir as mybir
from concourse.bass_utils import run_bass_kernel_spmd

F32 = mybir.dt.float32
BF16 = mybir.dt.bfloat16
I32 = mybir.dt.int32
AF = mybir.ActivationFunctionType
ALU = mybir.AluOpType

D_MODEL = 1024
SEQ = 2048
DEPTH = 4
SSM_W = 512
NG = 32
NST = 64
D_FF = 2816
NFB = 22
IN_W = 1280
EPS = 1e-6
JC = 4
LCH = 8 * JC
NSUB = SEQ // 8
NCH = NSUB // JC
NT = SEQ // 128
TWO_PI = 2.0 * math.pi

EC_TB = 0
EC_TC = EC_TB + 8
EC_SB = EC_TC + 8 * JC
EC_CC = EC_SB + 8 * JC
EC_L = EC_CC + 8 * JC
NEC = EC_L + 1
ANG_SHIFT = 64

MA_COLS = 2 * JC * 128
MB_COLS = 4 * JC * 128


class Buf:
    __slots__ = ("w", "r", "name")

    def __init__(self, name=""):
        self.w = None
        self.r = {}
        self.name = name


class Eng:
    def __init__(self, K, name, eng, is_pe=False):
        self.K = K
        self.name = name
        self.eng = eng
        self.is_pe = is_pe
        self.sems = []
        self.ep = -1
        self.cnt = 0
        self.seen = {}
        self.pending = []
        self._new_epoch()

    def _new_epoch(self):
        self.ep += 1
        self.cnt = 0
        self.sems.append(self.K.nc.alloc_semaphore(f"s_{self.name}_{self.ep}"))

class Kern:
    def __init__(self, nc):
        self.nc = nc
        self.pe = Eng(self, "pe", nc.tensor, True)
        self.act = Eng(self, "act", nc.scalar)
        self.dve = Eng(self, "dve", nc.vector)
        self.pool = Eng(self, "pool", nc.gpsimd)
        self.sp = Eng(self, "sp", nc.sync)
        self.engs = [self.pe, self.act, self.dve, self.pool, self.sp]
        self.slots = {}
        for e, n in ((self.sp, 20), (self.pool, 6), (self.act, 6)):
            self.slots[e.name] = [[nc.alloc_semaphore(f"d_{e.name}_{i}"), 0, ("dma", e.name, i)] for i in range(n)]
        self.slot_i = {k: 0 for k in self.slots}
        self.n_ins = 0

    def _wait(self, E, tick):
        sem, val, key = tick
        if key[0] == E.name and E.is_pe:
            return
        if E.seen.get(key, 0) >= val:
            return
        if key[0] != "dma":
            for (k2, v2) in E.seen.items():
                if k2[0] == key[0] and k2[0] != "dma" and k2[1] > key[1]:
                    return
        E.eng.wait_ge(sem, val)
        E.seen[key] = val
        self.n_ins += 1

    def _deps(self, E, reads, writes):
        need = []
        for b in reads:
            if b.w is not None:
                need.append(b.w)
        for b in writes:
            if b.w is not None:
                need.append(b.w)
            need.extend(b.r.values())
        for t in need:
            self._wait(E, t)

    def op(self, E, fn, reads=(), writes=(), inc=True):
        if E.cnt >= 28000 and not E.pending:
            E._new_epoch()
        self._deps(E, reads, writes)
        tick = (E.sems[E.ep], E.cnt + 1, (E.name, E.ep))
        ins = fn()
        self.n_ins += 1
        if inc:
            ins.then_inc(E.sems[E.ep], 1)
            E.cnt += 1
            E.pending = []
        else:
            E.pending.append(1)
        for b in reads:
            b.r[E.name] = tick
        for b in writes:
            b.w = tick
            b.r = {}
        return ins

    def dma(self, E, out, in_, reads=(), writes=(), **kw):
        slots = self.slots[E.name]
        i = self.slot_i[E.name]
        self.slot_i[E.name] = (i + 1) % len(slots)
        s = slots[i]
        if s[1] > 0:
            self._wait(E, (s[0], s[1], s[2]))
        self._deps(E, reads, writes)
        ins = E.eng.dma_start(out=out, in_=in_, **kw)
        s[1] += 16
        ins.then_inc(s[0], 16)
        self.n_ins += 1
        tick = (s[0], s[1], s[2])
        for b in reads:
            b.r[("dma", E.name, i)] = tick
        for b in writes:
            b.w = tick
            b.r = {}
        return ins

    def barrier(self):
        sp = self.sp
        for E in self.engs:
            if E is sp:
                continue
            if E.cnt > 0:
                self._wait(sp, (E.sems[E.ep], E.cnt, (E.name, E.ep)))
        for lst in self.slots.values():
            for s in lst:
                if s[1] > 0:
                    self._wait(sp, (s[0], s[1], s[2]))
        self.op(sp, lambda: sp.eng.nop())
        t = (sp.sems[sp.ep], sp.cnt, (sp.name, sp.ep))
        for E in self.engs:
            if E is not sp:
                self._wait(E, t)


def _t5_bucket(rel):
    n_buckets, max_distance = 32, 128
    half = n_buckets // 2
    max_exact = half // 2
    ret = jnp.where(rel > 0, half, 0)
    n = jnp.abs(rel)
    nf = jnp.maximum(n, 1).astype(jnp.float32)
    large = max_exact + (jnp.log(nf / max_exact) / math.log(max_distance / max_exact)
                         * (half - max_exact)).astype(jnp.int32)
    large = jnp.minimum(large, half - 1)
    return ret + jnp.where(n < max_exact, n, large)


def _host_consts():
    import ml_dtypes
    c = {}
    c["ident_f"] = np.eye(128, dtype=np.float32)
    ex = np.zeros((2, NEC), np.float32)
    for s in range(8):
        ex[0, EC_TB + s] = -s
        ex[1, EC_TB + s] = s
    for d in range(JC):
        for s in range(8):
            ex[0, EC_TC + d * 8 + s] = 8 * d + s
            ex[1, EC_TC + d * 8 + s] = 8 * d - s
            ex[0, EC_SB + d * 8 + s] = LCH - 1 - 8 * d - s
            ex[1, EC_SB + d * 8 + s] = 8 * d + s
            ex[0, EC_CC + d * 8 + s] = 8 * d + s + 1
            ex[1, EC_CC + d * 8 + s] = LCH - 8 * d - s
    ex[:, EC_L] = LCH
    c["extab"] = np.repeat(ex, 64, axis=0).astype(np.float32)
    sp = np.arange(128)[:, None] // 16
    s = np.arange(128)[None, :] // 16
    tm = np.stack([(s >= sp), (sp >= s)], axis=1).astype(np.float32)
    c["tmask"] = np.ascontiguousarray(tm)
    k = np.arange(128)[:, None, None]
    off = (np.arange(3) - 1)[None, :, None]
    q = np.arange(128)[None, None, :]
    rel = (k + 128 * off - q).astype(np.int32)
    with jax.default_device(jax.devices("cpu")[0]):
        bk = np.asarray(_t5_bucket(jnp.asarray(rel)))
    oh = (bk[:, None, :, :] == np.arange(32)[None, :, None, None]).astype(np.float32)
    c["onehot"] = oh.reshape(128, 32, 384).astype(ml_dtypes.bfloat16)
    c["vmask"] = (np.abs(rel) <= 128).astype(np.float32).reshape(128, 384)
    return c


def build(nseq=4, nlayers=DEPTH, debug=None):
    nc = bass.Bass("TRN2", target_bir_lowering=False)
    K = Kern(nc)
    pe, act, dve, pool, sp = K.pe, K.act, K.dve, K.pool, K.sp
    NTOK = nseq * SEQ

    def din(name, shape, dt=F32):
        return nc.dram_tensor(name, list(shape), dt, kind="ExternalInput").ap()

    x_in = din("x", (NTOK, D_MODEL))
    P = {}
    P["rel_bias"] = din("rel_bias", (32, 8))
    P["pre_mix_norm"] = din("pre_mix_norm", (DEPTH, D_MODEL))
    P["w_in"] = din("w_in", (DEPTH, D_MODEL, IN_W))
    P["lam_re"] = din("lam_re", (DEPTH, 2, NG, NST))
    P["lam_im"] = din("lam_im", (DEPTH, 2, NG, NST))
    P["log_step"] = din("log_step", (DEPTH, 2, NG))
    P["b_re"] = din("b_re", (DEPTH, 2, NG, NST, 16))
    P["b_im"] = din("b_im", (DEPTH, 2, NG, NST, 16))
    P["c_re"] = din("c_re", (DEPTH, 2, NG, 16, NST))
    P["c_im"] = din("c_im", (DEPTH, 2, NG, 16, NST))
    P["ssm_d"] = din("ssm_d", (DEPTH, SSM_W))
    P["w_glu"] = din("w_glu", (DEPTH, SSM_W, SSM_W))
    P["b_glu"] = din("b_glu", (DEPTH, SSM_W))
    P["attn_sink"] = din("attn_sink", (DEPTH, 8))
    P["ssm_out_norm"] = din("ssm_out_norm", (DEPTH, SSM_W))
    P["attn_out_norm"] = din("attn_out_norm", (DEPTH, SSM_W))
    P["w_out"] = din("w_out", (DEPTH, D_MODEL, D_MODEL))
    P["post_mix_norm"] = din("post_mix_norm", (DEPTH, D_MODEL))
    P["pre_ffn_norm"] = din("pre_ffn_norm", (DEPTH, D_MODEL))
    P["w_up"] = din("w_up", (DEPTH, D_MODEL, 2 * D_FF))
    P["conv_w"] = din("conv_w", (DEPTH, 3, 2 * D_FF))
    P["conv_b"] = din("conv_b", (DEPTH, 2 * D_FF))
    P["w_down"] = din("w_down", (DEPTH, D_FF, D_MODEL))
    P["post_ffn_norm"] = din("post_ffn_norm", (DEPTH, D_MODEL))
    c_ident = din("ident_f", (128, 128))
    c_extab = din("extab", (128, NEC))
    c_tmask = din("tmask", (128, 2, 128))
    c_onehot = din("onehot", (128, 32, 384), BF16)
    c_vmask = din("vmask", (128, 384))

    out = nc.dram_tensor("out", [NTOK, D_MODEL], F32, kind="ExternalOutput").ap()
    matsA = nc.dram_tensor("matsA", [NG, 128, MA_COLS], BF16).ap()
    matsB = nc.dram_tensor("matsB", [NG, 128, MB_COLS], BF16).ap()
    u_scr = nc.dram_tensor("u_scr", [NG * 16, SEQ], BF16).ap()
    z_scr = nc.dram_tensor("z_scr", [NG * 16, SEQ], BF16).ap()
    eb_scr = nc.dram_tensor("eb_scr", [128, 3 * 2 * 4 * 128], BF16).ap()
    matsA_b = [Buf() for _ in range(NG)]
    matsB_b = [Buf() for _ in range(NG)]
    u_scr_b, z_scr_b = Buf(), Buf()
    out_b = [[Buf() for _ in range(NT)] for _ in range(nseq)]
    dbg = {}
    if debug:
        for nm, (shp, dt_) in debug.items():
          if shp is not None:
            dbg[nm] = nc.dram_tensor("dbg_" + nm, list(shp), dt_, kind="ExternalOutput").ap()

    top = ExitStack()
    with top:
        uid = [0]

        def sb(es, name, shape, dt=F32):
            uid[0] += 1
            return es.enter_context(nc.sbuf_tensor(f"{name}_{uid[0]}", list(shape), dt))

        psum = top.enter_context(nc.psum_tensor("psum", [128, 8, 512], F32))
        ps_b = [Buf(f"ps{i}") for i in range(8)]
        ps_rr = [0]

        def next_ps(n=1):
            if n == 1:
                i = ps_rr[0] % 8
                ps_rr[0] += 1
                return i
            i = ps_rr[0] % 8
            if i % 2:
                i = (i + 1) % 8
            ps_rr[0] = i + 2
            return i

        ident = sb(top, "ident", (128, 128))
        identb = sb(top, "identb", (128, 128), BF16)
        ones_bf = sb(top, "ones_bf", (128, 1), BF16)
        epsb = sb(top, "epsb", (128, 1))
        aL = sb(top, "aL", (128, 2, 2, 32))
        aL_b = Buf("aL")
        cb = Buf("consts")
        ebB = Buf("eb")
        K.dma(sp, ident[:], c_ident, writes=[cb])
        K.op(dve, lambda: nc.vector.tensor_copy(out=identb[:], in_=ident[:]), reads=[cb], writes=[cb])
        K.op(dve, lambda: nc.vector.memset(ones_bf[:], 1.0), writes=[cb])
        K.op(dve, lambda: nc.vector.memset(epsb[:], EPS), writes=[cb])

        with ExitStack() as es:
            oh = sb(es, "oh", (128, 32, 384), BF16)
            eb = sb(es, "eb0", (128, 3, 2, 4, 128), BF16)
            rb = sb(es, "rb", (128, 256))
            vm = sb(es, "vm", (128, 384))
            acc = sb(es, "acc", (128, 8, 384))
            tb = Buf()
            K.dma(sp, oh[:], c_onehot, writes=[tb])
            K.dma(sp, vm[:], c_vmask, writes=[tb])
            K.dma(sp, rb[:], P["rel_bias"].rearrange("b h -> (b h)").partition_broadcast(128), writes=[tb])
            accb = [Buf() for _ in range(8)]
            for h in range(8):
                E = dve
                K.op(E, lambda h=h: nc.vector.tensor_scalar(out=acc[:, h, :], in0=oh[:, 0, :], scalar1=rb[:, h:h + 1],
                                                            scalar2=None, op0=ALU.mult), reads=[tb], writes=[accb[h]])
                for b in range(1, 32):
                    K.op(E, lambda h=h, b=b: nc.vector.scalar_tensor_tensor(
                        out=acc[:, h, :], in0=oh[:, b, :], scalar=rb[:, b * 8 + h:b * 8 + h + 1], in1=acc[:, h, :],
                        op0=ALU.mult, op1=ALU.add), reads=[tb, accb[h]], writes=[accb[h]])
                K.op(act, lambda h=h: nc.scalar.activation(out=acc[:, h, :], in_=acc[:, h, :], func=AF.Exp),
                     reads=[accb[h]], writes=[accb[h]])
                kv, c = h // 4, h % 4
                K.op(dve, lambda h=h, kv=kv, c=c: nc.vector.tensor_tensor(
                    out=eb[:, :, kv, c, :], in0=acc[:, h, :].rearrange("p (o q) -> p o q", o=3),
                    in1=vm[:].rearrange("p (o q) -> p o q", o=3), op=ALU.mult), reads=[accb[h], tb], writes=[ebB])
            K.dma(sp, eb_scr, eb[:].rearrange("p o k c q -> p (o k c q)"), reads=[ebB], writes=[ebB])
            K.barrier()

        for l in range(nlayers):
            x_src = x_in if l == 0 else out
            with ExitStack() as es:
                s5_prologue(nc, K, es, sb, psum, ps_b, next_ps, P, l, ident, c_extab, c_tmask,
                            aL, aL_b, matsA, matsB, matsA_b, matsB_b, dbg)
                K.barrier()
            with ExitStack() as es:
                mixer_phase(nc, K, es, sb, psum, ps_b, next_ps, P, l, nseq, x_src, out, out_b, ident, identb, ones_bf,
                            eb_scr, ebB, epsb, cb, aL, aL_b, matsA, matsB, matsA_b, matsB_b, u_scr, z_scr, u_scr_b, z_scr_b, dbg)
                K.barrier()
            with ExitStack() as es:
              if not (debug and "skip_ffn" in debug):
                ffn_phase(nc, K, es, sb, psum, ps_b, next_ps, P, l, nseq, out, out_b, ident, identb, epsb, cb, dbg)
                K.barrier()
        K.barrier()
    return nc, K


def make_staging(es, sb, tag, n=6, ch=2048):
    return ([sb(es, f"stg_{tag}{i}", (128, ch)) for i in range(n)], [Buf() for _ in range(n)], [0])


def load_cast_weight(nc, K, stage, dst, dst_b, src_rows, ncols, gain, gain_b, engines):
    stg, stg_b, itr = stage
    NS = len(stg)
    CH = 2048
    dq = [K.sp, K.act]
    it = itr[0]
    for r, src in enumerate(src_rows):
        for c0 in range(0, ncols, CH):
            cw = min(CH, ncols - c0)
            i = it % NS
            E = engines[it % len(engines)]
            Q = dq[it % 2]
            it += 1
            K.dma(Q, stg[i][:, 0:cw], src[:, c0:c0 + cw], writes=[stg_b[i]])
            if gain is None:
                if E is K.act:
                    K.op(E, lambda i=i, r=r, c0=c0, cw=cw: nc.scalar.copy(out=dst[:, r, c0:c0 + cw], in_=stg[i][:, 0:cw]),
                         reads=[stg_b[i]], writes=[dst_b])
                else:
                    K.op(E, lambda i=i, r=r, c0=c0, cw=cw, E=E: E.eng.tensor_copy(out=dst[:, r, c0:c0 + cw], in_=stg[i][:, 0:cw]),
                         reads=[stg_b[i]], writes=[dst_b])
            else:
                if E is K.act:
                    K.op(E, lambda i=i, r=r, c0=c0, cw=cw: nc.scalar.activation(
                        out=dst[:, r, c0:c0 + cw], in_=stg[i][:, 0:cw], func=AF.Copy, scale=gain[:, r:r + 1]),
                        reads=[stg_b[i], gain_b], writes=[dst_b])
                else:
                    K.op(E, lambda i=i, r=r, c0=c0, cw=cw, E=E: E.eng.tensor_scalar(
                        out=dst[:, r, c0:c0 + cw], in0=stg[i][:, 0:cw], scalar1=gain[:, r:r + 1], scalar2=None,
                        op0=ALU.mult), reads=[stg_b[i], gain_b], writes=[dst_b])
    itr[0] = it


def load_cols(nc, K, psum, ps_b, next_ps, ident, cb, ld, ld_b, dst, dst_b, src1d, n):
    K.dma(K.sp, ld[0:n, :], src1d.rearrange("(r p) -> r p", p=128), writes=[ld_b])
    b = next_ps()
    K.op(K.pe, lambda: nc.tensor.transpose(psum[:, b, 0:n], ld[0:n, :], ident[0:n, 0:n]), reads=[ld_b, cb], writes=[ps_b[b]])
    K.op(K.dve, lambda: nc.vector.tensor_copy(out=dst, in_=psum[:, b, 0:n]), reads=[ps_b[b]], writes=[dst_b])


def rstd_from_ssq(nc, K, ssq, rs, n, width, epsb, bufs_r, bufs_w):
    K.op(K.act, lambda: nc.scalar.activation(out=rs[:, 0:n], in_=ssq[:, 0:n], func=AF.Sqrt, bias=epsb[:, 0:1],
                                             scale=1.0 / width), reads=bufs_r, writes=bufs_w)
    K.op(K.dve, lambda: nc.vector.reciprocal(out=rs[:, 0:n], in_=rs[:, 0:n]), reads=bufs_w, writes=bufs_w)


def s5_prologue(nc, K, es, sb, psum, ps_b, next_ps, P, l, ident, c_extab, c_tmask,
                aL, aL_b, matsA, matsB, matsA_b, matsB_b, dbg):
    pe, act, dve, pool, sp = K.pe, K.act, K.dve, K.pool, K.sp
    V = nc.vector
    G = NG
    tb = Buf("s5tab")

    def t(name, shape, dt=F32):
        return sb(es, name, shape, dt)

    lamld = t("lamld", (32, 2, 128))
    lam = t("lam", (128, 2, 32))
    ls = t("ls", (128, 32))
    braw = t("braw", (128, 2, 32, 16))
    cld = t("cld", (128, 2, 4, 128))
    craw = t("craw", (128, 2, 32, 16))
    dvec = t("dvec", (128, 32))
    extab = t("extab_sb", (128, NEC))
    tmask = t("tmask_sb", (128, 2, 128))
    K.dma(sp, extab[:], c_extab, writes=[tb])
    K.dma(sp, tmask[:], c_tmask, writes=[tb])
    with nc.allow_non_contiguous_dma(reason="tiny param loads"):
        for ri, nm in enumerate(("lam_re", "lam_im")):
            K.dma(sp, lamld[:, ri, :].rearrange("g (d n) -> g d n", d=2), P[nm][l].rearrange("d g n -> g d n"), writes=[tb])
        for d in range(2):
            K.dma(sp, ls[d * 64:(d + 1) * 64, :], P["log_step"][l, d].partition_broadcast(64), writes=[tb])
            for ri, nm in enumerate(("b_re", "b_im")):
                K.dma(sp, braw[d * 64:(d + 1) * 64, ri, :, :], P[nm][l, d].rearrange("g n q -> n g q"), writes=[tb])
        for ri, nm in enumerate(("c_re", "c_im")):
            for d in range(2):
                K.dma(sp, cld[:, ri, :, d * 64:(d + 1) * 64],
                      P[nm][l, d].rearrange("(t gl) p n -> (gl p) t n", t=4), writes=[tb])
        for s in range(8):
            K.dma(sp, dvec[s * 16:(s + 1) * 16, :], P["ssm_d"][l].rearrange("(g q) -> q g", q=16), writes=[tb])
    for ri in range(2):
        b = next_ps()
        K.op(pe, lambda ri=ri, b=b: nc.tensor.transpose(psum[:, b, 0:32], lamld[:, ri, :], ident[0:32, 0:32]),
             reads=[tb], writes=[ps_b[b]])
        K.op(dve, lambda ri=ri, b=b: V.tensor_copy(out=lam[:, ri, :], in_=psum[:, b, 0:32]), reads=[ps_b[b]], writes=[tb])
        b = next_ps()
        for tt in range(4):
            K.op(pe, lambda ri=ri, b=b, tt=tt: nc.tensor.transpose(psum[:, b, tt * 128:(tt + 1) * 128], cld[:, ri, tt, :], ident[:]),
                 reads=[tb], writes=[ps_b[b]], inc=(tt == 3))
        K.op(dve, lambda ri=ri, b=b: V.tensor_copy(out=craw[:, ri, :, :].rearrange("p g q -> p (g q)"), in_=psum[:, b, :]),
             reads=[ps_b[b]], writes=[tb])

    dt_ = t("dt", (128, 32))
    ar = t("ar", (128, 32))
    ang = t("ang", (128, 32))
    lb = t("lb", (128, 2, 32))
    coef = t("coef", (128, 2, 32))
    tmp = t("tmpa", (128, 4, 32))

    def dv(fn, inc=True):
        K.op(dve, fn, reads=[tb], writes=[tb], inc=inc)

    def ac(fn):
        K.op(act, fn, reads=[tb], writes=[tb])

    ac(lambda: nc.scalar.activation(out=dt_[:], in_=ls[:], func=AF.Exp))
    dv(lambda: V.tensor_tensor(out=ar[:], in0=lam[:, 0, :], in1=dt_[:], op=ALU.mult))
    dv(lambda: V.tensor_tensor(out=ang[:], in0=lam[:, 1, :], in1=dt_[:], op=ALU.mult))

    PR = t("PR", (128, 32, NEC))
    PI = t("PI", (128, 32, NEC))
    A1 = t("A1", (128, 32, NEC))
    A2 = t("A2", (128, 32, NEC))
    A3i = t("A3i", (128, 32, NEC), I32)
    A4 = t("A4", (128, 32, NEC))
    ex_b = extab[:].unsqueeze(1).to_broadcast([128, 32, NEC])

    def bc_g(x):
        return x.unsqueeze(2).to_broadcast([128, 32, NEC])

    dv(lambda: V.tensor_tensor(out=A1[:], in0=ex_b, in1=bc_g(ar[:]), op=ALU.mult))
    ac(lambda: nc.scalar.activation(out=A4[:], in_=A1[:], func=AF.Exp))
    dv(lambda: V.tensor_tensor(out=A1[:], in0=ex_b, in1=bc_g(ang[:]), op=ALU.mult))

    def sin_of(dst, shift):
        dv(lambda: V.tensor_scalar(out=A2[:], in0=A1[:], scalar1=shift + ANG_SHIFT * TWO_PI, scalar2=1.0 / TWO_PI,
                                   op0=ALU.add, op1=ALU.mult))
        dv(lambda: V.tensor_copy(out=A3i[:], in_=A2[:]))
        dv(lambda: V.tensor_copy(out=dst[:], in_=A3i[:]))
        dv(lambda: V.tensor_tensor(out=A2[:], in0=A2[:], in1=dst[:], op=ALU.subtract))
        dv(lambda: V.tensor_scalar(out=dst[:], in0=A2[:], scalar1=0.5, scalar2=None, op0=ALU.is_gt))
        dv(lambda: V.tensor_tensor(out=A2[:], in0=A2[:], in1=dst[:], op=ALU.subtract))
        dv(lambda: V.tensor_scalar(out=dst[:], in0=A2[:], scalar1=-0.5, scalar2=None, op0=ALU.is_lt))
        dv(lambda: V.tensor_tensor(out=A2[:], in0=A2[:], in1=dst[:], op=ALU.add))
        dv(lambda: V.tensor_scalar(out=A2[:], in0=A2[:], scalar1=TWO_PI, scalar2=3.14159, op0=ALU.mult, op1=ALU.min))
        dv(lambda: V.tensor_scalar(out=A2[:], in0=A2[:], scalar1=-3.14159, scalar2=None, op0=ALU.max))
        ac(lambda: nc.scalar.activation(out=dst[:], in_=A2[:], func=AF.Sin))

    sin_of(PI, 0.0)
    sin_of(PR, math.pi / 2)
    dv(lambda: V.tensor_tensor(out=PR[:], in0=PR[:], in1=A4[:], op=ALU.mult))
    dv(lambda: V.tensor_tensor(out=PI[:], in0=PI[:], in1=A4[:], op=ALU.mult))

    ac(lambda: nc.scalar.activation(out=tmp[:, 0, :], in_=ar[:], func=AF.Exp))
    dv(lambda: V.tensor_copy(out=lb[0:64, 0, :], in_=PR[0:64, :, EC_TC + 1]))
    dv(lambda: V.tensor_copy(out=lb[0:64, 1, :], in_=PI[0:64, :, EC_TC + 1]))
    dv(lambda: V.tensor_copy(out=lb[64:128, 0, :], in_=PR[64:128, :, EC_TB + 1]))
    dv(lambda: V.tensor_copy(out=lb[64:128, 1, :], in_=PI[64:128, :, EC_TB + 1]))
    dv(lambda: V.tensor_tensor(out=tmp[:, 0, :], in0=lam[:, 0, :], in1=lam[:, 0, :], op=ALU.mult))
    dv(lambda: V.tensor_tensor(out=tmp[:, 1, :], in0=lam[:, 1, :], in1=lam[:, 1, :], op=ALU.mult))
    dv(lambda: V.tensor_tensor(out=tmp[:, 0, :], in0=tmp[:, 0, :], in1=tmp[:, 1, :], op=ALU.add))
    dv(lambda: V.reciprocal(out=tmp[:, 0, :], in_=tmp[:, 0, :]))
    dv(lambda: V.tensor_scalar(out=tmp[:, 1, :], in0=lb[:, 0, :], scalar1=-1.0, scalar2=None, op0=ALU.add))
    dv(lambda: V.tensor_tensor(out=tmp[:, 2, :], in0=tmp[:, 1, :], in1=lam[:, 0, :], op=ALU.mult))
    dv(lambda: V.tensor_tensor(out=tmp[:, 3, :], in0=lb[:, 1, :], in1=lam[:, 1, :], op=ALU.mult))
    dv(lambda: V.tensor_tensor(out=tmp[:, 2, :], in0=tmp[:, 2, :], in1=tmp[:, 3, :], op=ALU.add))
    dv(lambda: V.tensor_tensor(out=coef[:, 0, :], in0=tmp[:, 2, :], in1=tmp[:, 0, :], op=ALU.mult))
    dv(lambda: V.tensor_tensor(out=tmp[:, 2, :], in0=lb[:, 1, :], in1=lam[:, 0, :], op=ALU.mult))
    dv(lambda: V.tensor_tensor(out=tmp[:, 3, :], in0=tmp[:, 1, :], in1=lam[:, 1, :], op=ALU.mult))
    dv(lambda: V.tensor_tensor(out=tmp[:, 2, :], in0=tmp[:, 2, :], in1=tmp[:, 3, :], op=ALU.subtract))
    dv(lambda: V.tensor_tensor(out=coef[:, 1, :], in0=tmp[:, 2, :], in1=tmp[:, 0, :], op=ALU.mult))
    bb = t("bb", (128, 2, 32, 16))
    t16 = t("t16", (128, 32, 16))

    def bq(x):
        return x.unsqueeze(2).to_broadcast([128, 32, 16])

    dv(lambda: V.tensor_tensor(out=bb[:, 0], in0=braw[:, 0], in1=bq(coef[:, 0, :]), op=ALU.mult))
    dv(lambda: V.tensor_tensor(out=t16[:], in0=braw[:, 1], in1=bq(coef[:, 1, :]), op=ALU.mult))
    dv(lambda: V.tensor_tensor(out=bb[:, 0], in0=bb[:, 0], in1=t16[:], op=ALU.subtract))
    dv(lambda: V.tensor_tensor(out=bb[:, 1], in0=braw[:, 1], in1=bq(coef[:, 0, :]), op=ALU.mult))
    dv(lambda: V.tensor_tensor(out=t16[:], in0=braw[:, 0], in1=bq(coef[:, 1, :]), op=ALU.mult))
    dv(lambda: V.tensor_tensor(out=bb[:, 1], in0=bb[:, 1], in1=t16[:], op=ALU.add))

    if "s5tab" in dbg:
        K.dma(sp, dbg["s5tab"][:, 0:NEC], PR[:, 0, :], reads=[tb])
        K.dma(sp, dbg["s5tab"][:, NEC:2 * NEC], PI[:, 0, :], reads=[tb])
        K.dma(sp, dbg["s5tab"][:, 2 * NEC:2 * NEC + 32], bb[:, 0, 0:2, :].rearrange("p g q -> p (g q)"), reads=[tb])
        K.dma(sp, dbg["s5tab"][:, 2 * NEC + 32:2 * NEC + 64], bb[:, 1, 0:2, :].rearrange("p g q -> p (g q)"), reads=[tb])

    K.op(dve, lambda: V.tensor_copy(out=aL[:, 0, 0, :], in_=PR[:, :, EC_L]), reads=[tb], writes=[aL_b])
    K.op(dve, lambda: V.tensor_copy(out=aL[:, 0, 1, :], in_=PR[:, :, EC_L]), reads=[tb], writes=[aL_b])
    K.op(dve, lambda: V.tensor_scalar(out=aL[:, 1, 0, :], in0=PI[:, :, EC_L], scalar1=-1.0, scalar2=None, op0=ALU.mult), reads=[tb], writes=[aL_b])
    K.op(dve, lambda: V.tensor_copy(out=aL[:, 1, 1, :], in_=PI[:, :, EC_L]), reads=[tb], writes=[aL_b])

    NB = 8 + 8 * JC
    NC8 = 8 * JC
    GB = 2
    xb = [t(f"xb{i}", (128, 2, GB, NB, 16)) for i in range(2)]
    xc = [t(f"xc{i}", (128, 2, GB, NC8, 16)) for i in range(2)]
    xcc = [t(f"xcc{i}", (128, 2, GB, NC8, 16)) for i in range(2)]
    tq = [t(f"tq{i}", (128, GB, NB, 16)) for i in range(2)]
    mA = [t(f"mA{i}", (128, MA_COLS), BF16) for i in range(2)]
    mB = [t(f"mB{i}", (128, MB_COLS), BF16) for i in range(2)]
    dmat = [t(f"dmat{i}", (128, 128)) for i in range(2)]
    t0m = [t(f"t0m{i}", (128, 128)) for i in range(2)]
    xb_b = [Buf() for _ in range(2)]
    xc_b = [Buf() for _ in range(2)]
    xcc_b = [Buf() for _ in range(2)]
    tq_b = [Buf() for _ in range(2)]
    mA_b = [Buf() for _ in range(2)]
    mB_b = [Buf() for _ in range(2)]
    dm_b = [Buf() for _ in range(2)]

    def pwB(x, g0, c0, n):
        return x[:, g0:g0 + GB, c0:c0 + n].unsqueeze(3).to_broadcast([128, GB, n, 16])

    def bbB(ri, g0, n):
        return bb[:, ri, g0:g0 + GB, :].unsqueeze(2).to_broadcast([128, GB, n, 16])

    def ccB(ri, g0, n):
        return craw[:, ri, g0:g0 + GB, :].unsqueeze(2).to_broadcast([128, GB, n, 16])

    ncraw = t("ncraw", (128, 2, 32, 16))
    dv(lambda: V.tensor_scalar(out=ncraw[:], in0=craw[:], scalar1=-1.0, scalar2=None, op0=ALU.mult))

    def nccB(ri, g0, n):
        return ncraw[:, ri, g0:g0 + GB, :].unsqueeze(2).to_broadcast([128, GB, n, 16])

    for gb in range(G // GB):
        g0 = gb * GB
        i = gb % 2
        E2 = pool if (gb % 3 == 2) else dve
        EN = E2.eng

        def tt_(out, in0, in1, op, reads, writes):
            K.op(E2, lambda: EN.tensor_tensor(out=out, in0=in0, in1=in1, op=op), reads=reads, writes=writes)

        for (dst0, c0, n) in ((0, EC_TB, 8), (8, EC_SB, NC8)):
            o_re = xb[i][:, 0, :, dst0:dst0 + n, :]
            o_im = xb[i][:, 1, :, dst0:dst0 + n, :]
            t_ = tq[i][:, :, dst0:dst0 + n, :]
            tt_(o_re, pwB(PR, g0, c0, n), bbB(0, g0, n), ALU.mult, [tb], [xb_b[i]])
            tt_(t_, pwB(PI, g0, c0, n), bbB(1, g0, n), ALU.mult, [tb], [tq_b[i]])
            tt_(o_re, o_re, t_, ALU.subtract, [tq_b[i], xb_b[i]], [xb_b[i]])
            tt_(o_im, pwB(PR, g0, c0, n), bbB(1, g0, n), ALU.mult, [tb], [xb_b[i]])
            tt_(t_, pwB(PI, g0, c0, n), bbB(0, g0, n), ALU.mult, [tb, xb_b[i]], [tq_b[i]])
            tt_(o_im, o_im, t_, ALU.add, [tq_b[i], xb_b[i]], [xb_b[i]])
        for (dstt, dstb, c0) in ((xc, xc_b, EC_TC), (xcc, xcc_b, EC_CC)):
            o_re = dstt[i][:, 0]
            o_im = dstt[i][:, 1]
            t_ = tq[i][:, :, 0:NC8, :]
            tt_(o_re, pwB(PR, g0, c0, NC8), ccB(0, g0, NC8), ALU.mult, [tb], [dstb[i]])
            tt_(t_, pwB(PI, g0, c0, NC8), ccB(1, g0, NC8), ALU.mult, [tb], [tq_b[i]])
            tt_(o_re, o_re, t_, ALU.subtract, [tq_b[i], dstb[i]], [dstb[i]])
            tt_(o_im, pwB(PI, g0, c0, NC8), nccB(0, g0, NC8), ALU.mult, [tb], [dstb[i]])
            tt_(t_, pwB(PR, g0, c0, NC8), nccB(1, g0, NC8), ALU.mult, [tb, dstb[i]], [tq_b[i]])
            tt_(o_im, o_im, t_, ALU.add, [tq_b[i], dstb[i]], [dstb[i]])

        for gl in range(GB):
            g = g0 + gl
            j2 = g % 2
            for d in range(2):
                b = next_ps()
                rows = slice(d * 64, (d + 1) * 64)
                for ri in range(2):
                    K.op(pe, lambda: nc.tensor.matmul(
                        psum[:, b, 0:JC * 128],
                        xb[i][rows, ri, gl, 0:8, :].rearrange("p s q -> p (s q)"),
                        xc[i][rows, ri, gl, :, :].rearrange("p c q -> p (c q)"),
                        start=(ri == 0), stop=(ri == 1)),
                        reads=[xb_b[i], xc_b[i]], writes=[ps_b[b]], inc=(ri == 1))
                base = d * JC * 128
                if d == 0:
                    K.op(dve, lambda: V.tensor_scalar(out=dmat[j2][:], in0=ident[:], scalar1=dvec[:, g:g + 1],
                                                      scalar2=None, op0=ALU.mult), reads=[tb], writes=[dm_b[j2]])
                    K.op(dve, lambda: V.tensor_tensor(out=t0m[j2][:], in0=psum[:, b, 0:128], in1=tmask[:, 0, :], op=ALU.mult),
                         reads=[ps_b[b], tb, dm_b[j2]], writes=[dm_b[j2]])
                    K.op(dve, lambda: V.tensor_tensor(out=mB[j2][:, base:base + 128], in0=t0m[j2][:], in1=dmat[j2][:], op=ALU.add),
                         reads=[dm_b[j2]], writes=[mB_b[j2]])
                else:
                    K.op(dve, lambda: V.tensor_tensor(out=mB[j2][:, base:base + 128], in0=psum[:, b, 0:128],
                                                      in1=tmask[:, 1, :], op=ALU.mult),
                         reads=[ps_b[b], tb], writes=[mB_b[j2]])
                if JC > 1:
                    K.op(act, lambda: nc.scalar.copy(out=mB[j2][:, base + 128:base + JC * 128], in_=psum[:, b, 128:JC * 128]),
                         reads=[ps_b[b]], writes=[mB_b[j2]])
            K.op(act, lambda: nc.scalar.copy(out=mB[j2][:, 2 * JC * 128:3 * JC * 128],
                                             in_=xcc[i][:, 0, gl].rearrange("p c q -> p (c q)")),
                 reads=[xcc_b[i]], writes=[mB_b[j2]])
            K.op(act, lambda: nc.scalar.copy(out=mB[j2][:, 3 * JC * 128:4 * JC * 128],
                                             in_=xcc[i][:, 1, gl].rearrange("p c q -> p (c q)")),
                 reads=[xcc_b[i]], writes=[mB_b[j2]])
            K.dma(sp, matsB[g], mB[j2][:], reads=[mB_b[j2]], writes=[matsB_b[g]])
            for ri in range(2):
                b = next_ps()
                for j in range(JC):
                    K.op(pe, lambda: nc.tensor.transpose(
                        psum[:, b, j * 128:(j + 1) * 128],
                        xb[i][:, ri, gl, 8 + 8 * j:16 + 8 * j, :].rearrange("p s q -> p (s q)"), ident[:]),
                        reads=[xb_b[i]], writes=[ps_b[b]], inc=(j == JC - 1))
                K.op(act, lambda: nc.scalar.copy(out=mA[j2][:, ri * JC * 128:(ri + 1) * JC * 128], in_=psum[:, b, 0:JC * 128]),
                     reads=[ps_b[b]], writes=[mA_b[j2]])
            K.dma(sp, matsA[g], mA[j2][:], reads=[mA_b[j2]], writes=[matsA_b[g]])
    if "matsB0" in dbg:
        pass


def mixer_phase(nc, K, es, sb, psum, ps_b, next_ps, P, l, nseq, x_src, out, out_b, ident, identb, ones_bf,
                eb_scr, ebB, epsb, cb, aL, aL_b, matsA, matsB, matsA_b, matsB_b, u_scr, z_scr, u_scr_b, z_scr_b, dbg):
    pe, act, dve, pool, sp = K.pe, K.act, K.dve, K.pool, K.sp
    V = nc.vector
    G = NG
    wi = sb(es, "wi", (128, 8, IN_W), BF16)
    wg = sb(es, "wg", (128, 4, 512), BF16)
    wo = sb(es, "wo", (128, 8, 1024), BF16)
    gpm = sb(es, "gpm", (128, 8))
    gso = sb(es, "gso", (128, 8))
    bglu = sb(es, "bglu", (128, 4))
    esink = sb(es, "esink", (128, 8))
    gpost = sb(es, "gpost", (128, 1024))
    wi_b, wg_b, wo_b, pb = Buf(), Buf(), Buf(), Buf()
    eb = sb(es, "eb", (128, 3, 2, 4, 128), BF16)
    K.dma(sp, eb[:].rearrange("p o k c q -> p (o k c q)"), eb_scr, reads=[ebB], writes=[pb])
    ebB = pb
    with ExitStack() as es1:
        ld = sb(es1, "ldm", (8, 128))
        ld_b = Buf()
        LC = lambda dst, src, n: load_cols(nc, K, psum, ps_b, next_ps, ident, cb, ld, ld_b, dst, pb, src, n)
        LC(gpm[:], P["pre_mix_norm"][l], 8)
        LC(gso[:, 0:4], P["ssm_out_norm"][l], 4)
        LC(gso[:, 4:8], P["attn_out_norm"][l], 4)
        LC(bglu[:], P["b_glu"][l], 4)
        K.barrier()
    K.dma(sp, esink[:], P["attn_sink"][l].partition_broadcast(128), writes=[pb])
    K.dma(sp, gpost[:], P["post_mix_norm"][l].partition_broadcast(128), writes=[pb])
    K.op(act, lambda: nc.scalar.activation(out=esink[:], in_=esink[:], func=AF.Exp), reads=[pb], writes=[pb])
    with ExitStack() as es2:
        engs = [act, dve]
        stage = make_staging(es2, sb, "m")
        load_cast_weight(nc, K, stage, wi, wi_b, [P["w_in"][l, k * 128:(k + 1) * 128, :] for k in range(8)], IN_W,
                         gpm, pb, engs)
        load_cast_weight(nc, K, stage, wg, wg_b, [P["w_glu"][l, k * 128:(k + 1) * 128, :] for k in range(4)], 512,
                         None, pb, engs)
        load_cast_weight(nc, K, stage, wo, wo_b, [P["w_out"][l, k * 128:(k + 1) * 128, :] for k in range(8)], 1024,
                         gso, pb, engs)
        K.barrier()

    X1 = sb(es, "X1", (128, 4, SEQ), BF16)
    hTlo = sb(es, "hTlo", (128, 4, SEQ), BF16)
    hThi = sb(es, "hThi", (128, 4, SEQ), BF16)
    qT = sb(es, "qT", (128, 4, SEQ), BF16)
    UY = sb(es, "UY", (128, 4, SEQ), BF16)
    kT = sb(es, "kT", (128, SEQ), BF16)
    vaug = sb(es, "vaug", (128, NT, 2, 66), BF16)
    xt = [sb(es, f"xt{i}", (128, 1024)) for i in range(3)]
    hs = xt
    xo = xt
    junk = sb(es, "junk", (128, 1024), BF16)
    ssq = sb(es, "ssq", (128, 4, NT))
    rs = sb(es, "rs", (128, 4, NT))
    B = lambda n=1: [Buf() for _ in range(n)]
    X1_b, hTlo_b, hThi_b, qT_b, UY_b, kT_b, va_b, H_b, Ein_b = (Buf() for _ in range(9))
    Hf_b, Hb_b = Buf(), Buf()
    xt_b, mAb_b, mBb_b, et_b, pt_b, oa_b, den_b, gate_b, mt_b = B(3), B(2), B(2), B(3), B(6), B(2), B(2), B(2), B(2)
    hs_b = xt_b
    xo_b = xt_b
    junk_b, ssq_b, rs_b, st_b = Buf(), [Buf() for _ in range(4)], [Buf() for _ in range(4)], [Buf(), Buf()]
    m1s_b = [(Buf(), Buf()) for _ in range(4)]
    K.op(dve, lambda: V.memset(vaug[:, :, :, 64:66], 1.0), writes=[va_b])
    hT_b = [hTlo_b, hThi_b]
    hT = [hTlo, hThi]
    Uf = UY[:].rearrange("p c t -> p (c t)").rearrange("p (g s) -> p g s", g=NG)
    Zf = hTlo[:].rearrange("p c t -> p (c t)").rearrange("p (g s) -> p g s", g=NG)
    ysT, ysT_b = qT, qT_b
    yaT, yaT_b = hThi, hThi_b
    ysq, ysq_b = UY, UY_b

    for s in range(nseq):
        tok0 = s * SEQ
        def m1_a(tt):
            i = tt % 3
            r0 = tok0 + tt * 128
            rd = [out_b[s][tt]] if x_src is out else []
            K.dma(sp, xt[i][:], x_src[r0:r0 + 128, :], reads=rd, writes=[xt_b[i]])
            sq_b, r_b = m1s_b[tt % 4]
            K.op(act, lambda: nc.scalar.activation(out=junk[:], in_=xt[i][:], func=AF.Square, accum_out=ssq[:, 0, tt:tt + 1]),
                 reads=[xt_b[i]], writes=[junk_b, sq_b])
            K.op(act, lambda: nc.scalar.activation(out=rs[:, 0, tt:tt + 1], in_=ssq[:, 0, tt:tt + 1], func=AF.Sqrt,
                                                   bias=epsb[:, 0:1], scale=1.0 / D_MODEL), reads=[sq_b, cb], writes=[r_b])
            K.op(dve, lambda: V.reciprocal(out=rs[:, 0, tt:tt + 1], in_=rs[:, 0, tt:tt + 1]), reads=[r_b], writes=[r_b])
            K.op(dve, lambda: V.tensor_scalar(out=hs[i][:], in0=xt[i][:], scalar1=rs[:, 0, tt:tt + 1], scalar2=None, op0=ALU.mult),
                 reads=[xt_b[i], r_b], writes=[hs_b[i]])

        def m1_b(tt):
            i = tt % 3
            b = next_ps(2)
            for c in range(8):
                K.op(pe, lambda: nc.tensor.transpose(psum[:, b + c // 4, (c % 4) * 128:(c % 4 + 1) * 128],
                                                     hs[i][:, c * 128:(c + 1) * 128], ident[:]),
                     reads=[hs_b[i], cb], writes=[ps_b[b], ps_b[b + 1]], inc=(c == 7))
            K.op(dve, lambda: V.tensor_copy(out=hTlo[:, :, tt * 128:(tt + 1) * 128],
                                            in_=psum[:, b, :].rearrange("p (c t) -> p c t", c=4)),
                 reads=[ps_b[b]], writes=[hTlo_b])
            K.op(act, lambda: nc.scalar.copy(out=hThi[:, :, tt * 128:(tt + 1) * 128],
                                             in_=psum[:, b + 1, :].rearrange("p (c t) -> p c t", c=4)),
                 reads=[ps_b[b + 1]], writes=[hThi_b])

        m1_a(0)
        for tt in range(NT):
            if tt + 1 < NT:
                m1_a(tt + 1)
            m1_b(tt)
        for m in range(9):
            if m == 4:
                K.dma(sp, u_scr.rearrange("(c r) t -> r c t", c=4), X1[:], reads=[X1_b], writes=[u_scr_b])
                for s8 in range(8):
                    K.dma(sp, Uf[s8 * 16:(s8 + 1) * 16, :, :],
                          u_scr.rearrange("(g q) (s sub) -> s q g sub", q=16, s=8)[s8], reads=[u_scr_b], writes=[UY_b])
            for tg in range(4):
                b = next_ps()
                for k in range(8):
                    K.op(pe, lambda: nc.tensor.matmul(psum[:, b, :], wi[:, k, m * 128:(m + 1) * 128],
                                                      hT[k // 4][:, k % 4, tg * 512:(tg + 1) * 512],
                                                      start=(k == 0), stop=(k == 7)),
                         reads=[wi_b, hT_b[k // 4]], writes=[ps_b[b]], inc=(k == 7))
                if m < 4:
                    K.op(act, lambda: nc.scalar.copy(
                        out=X1[:, m, :].rearrange("p (s sub) -> p sub s", s=8)[:, tg * 64:(tg + 1) * 64, :],
                        in_=psum[:, b, :].rearrange("p (sub s) -> p sub s", s=8)), reads=[ps_b[b]], writes=[X1_b])
                elif m < 8:
                    K.op(dve, lambda: V.tensor_copy(out=qT[:, m - 4, tg * 512:(tg + 1) * 512], in_=psum[:, b, :]),
                         reads=[ps_b[b]], writes=[qT_b])
                else:
                    K.op(act, lambda: nc.scalar.copy(out=kT[:, tg * 512:(tg + 1) * 512], in_=psum[:, b, :]),
                         reads=[ps_b[b]], writes=[kT_b])
        for tt in range(NT):
            b = next_ps()
            for k in range(8):
                K.op(pe, lambda: nc.tensor.matmul(psum[:, b, 0:128], hT[k // 4][:, k % 4, tt * 128:(tt + 1) * 128],
                                                  wi[:, k, 1152:1280], start=(k == 0), stop=(k == 7)),
                     reads=[wi_b, hT_b[k // 4]], writes=[ps_b[b]], inc=(k == 7))
            K.op(dve, lambda: V.tensor_copy(out=vaug[:, tt, :, 0:64], in_=psum[:, b, 0:128].rearrange("p (k d) -> p k d", k=2)),
                 reads=[ps_b[b]], writes=[va_b])
        if s == 0 and "uT" in dbg:
            K.dma(sp, dbg["uT"], X1[:], reads=[X1_b])
            K.dma(sp, dbg["qT"], qT[:], reads=[qT_b])
            K.dma(sp, dbg["kT"], kT[:], reads=[kT_b])
            K.dma(sp, dbg["vaug"], vaug[:], reads=[va_b])

        e_s5 = ExitStack()
        H = sb(e_s5, "H", (128, 2, NG, NCH + 1))
        Ein = sb(e_s5, "Ein", (128, 2, NG, NCH), BF16)
        st1 = sb(e_s5, "st1", (128, 2, NG))
        st2 = sb(e_s5, "st2", (128, 2, NG))
        mAb = [sb(e_s5, f"mAb{i}", (128, MA_COLS), BF16) for i in range(2)]
        mBb = [sb(e_s5, f"mBb{i}", (128, MB_COLS), BF16) for i in range(2)]
        e_att = ExitStack()
        et = [sb(e_att, f"et{i}", (128, 512), BF16) for i in range(3)]
        pt = [sb(e_att, f"pt{i}", (128, 512), BF16) for i in range(6)]
        oa = [sb(e_att, f"oa{i}", (128, 512)) for i in range(2)]
        den = [sb(e_att, f"den{i}", (128, 4)) for i in range(2)]

        if l == 0 and s == 0:
            print("mixer sbuf bytes remaining", nc.sbuf_bytes_remaining)

        def att_S(qb, kv):
            kbs = [kb for kb in (qb - 1, qb, qb + 1) if 0 <= kb < NT]
            rows = slice(kv * 64, (kv + 1) * 64)
            pts = []
            for kb in kbs:
                ie = K_rr(K, "et", 3)
                ip = K_rr(K, "pt", 6)
                b = next_ps()
                K.op(pe, lambda: nc.tensor.matmul(psum[:, b, :], kT[rows, kb * 128:(kb + 1) * 128],
                                                  qT[rows, :, qb * 128:(qb + 1) * 128], start=True, stop=True),
                     reads=[kT_b, qT_b], writes=[ps_b[b]])
                K.op(act, lambda: nc.scalar.activation(out=et[ie][:], in_=psum[:, b, :], func=AF.Exp, scale=0.125),
                     reads=[ps_b[b]], writes=[et_b[ie]])
                EM = dve
                K.op(EM, lambda: EM.eng.tensor_tensor(out=pt[ip][:], in0=et[ie][:],
                                                      in1=eb[:, kb - qb + 1, kv].rearrange("p c q -> p (c q)"), op=ALU.mult),
                     reads=[et_b[ie], ebB], writes=[pt_b[ip]])
                pts.append((ip, kb))
            return pts

        def att_P(qb, kv, pts):
            b = next_ps()
            for c in range(4):
                for n_, (ip, kb) in enumerate(pts):
                    K.op(pe, lambda: nc.tensor.matmul(psum[:, b, c * 65:(c + 1) * 65], pt[ip][:, c * 128:(c + 1) * 128],
                                                      vaug[:, kb, kv, 0:65], start=(n_ == 0), stop=(n_ == len(pts) - 1)),
                         reads=[pt_b[ip], va_b], writes=[ps_b[b]], inc=(c == 3 and n_ == len(pts) - 1))
            io = qb % 2
            ov = psum[:, b, 0:260].rearrange("p (c d) -> p c d", c=4)
            K.op(dve, lambda: V.tensor_tensor(out=den[kv][:], in0=ov[:, :, 64], in1=esink[:, kv * 4:(kv + 1) * 4], op=ALU.add),
                 reads=[ps_b[b], pb], writes=[den_b[kv]])
            K.op(dve, lambda: V.reciprocal(out=den[kv][:], in_=den[kv][:]), reads=[den_b[kv]], writes=[den_b[kv]])
            K.op(dve, lambda: V.tensor_tensor(out=oa[io][:, kv * 256:(kv + 1) * 256].rearrange("p (c d) -> p c d", c=4),
                                              in0=ov[:, :, 0:64], in1=den[kv][:].unsqueeze(2).to_broadcast([128, 4, 64]),
                                              op=ALU.mult), reads=[ps_b[b], den_b[kv]], writes=[oa_b[io]])

        def att_F(qb):
            io = qb % 2
            K.op(act, lambda: nc.scalar.activation(out=junk[:, 0:512], in_=oa[io][:], func=AF.Square, accum_out=ssq[:, 2, qb:qb + 1]),
                 reads=[oa_b[io]], writes=[junk_b, ssq_b[2]])
            b = next_ps()
            for c in range(4):
                K.op(pe, lambda: nc.tensor.transpose(psum[:, b, c * 128:(c + 1) * 128], oa[io][:, c * 128:(c + 1) * 128], ident[:]),
                     reads=[oa_b[io], cb], writes=[ps_b[b]], inc=(c == 3))
            K.op(act, lambda: nc.scalar.copy(out=yaT[:, :, qb * 128:(qb + 1) * 128], in_=psum[:, b, :].rearrange("p (c t) -> p c t", c=4)),
                 reads=[ps_b[b]], writes=[yaT_b])

        def scan_step(c):
            hs_ = (slice(0, NG // 2), slice(NG // 2, NG))
            hb_ = (Hf_b, Hb_b)
            for (in_sw, ai) in ((False, 0), (True, 1)):
                for h2 in range(2):
                    gs = hs_[h2]
                    src = H[:, ::-1, gs, c] if in_sw else H[:, :, gs, c]
                    K.op(dve, lambda: V.tensor_tensor(out=st1[:, :, gs], in0=src, in1=aL[:, ai, :, gs], op=ALU.mult),
                         reads=[H_b, aL_b, hb_[h2]], writes=[hb_[h2]])
                for h2 in range(2):
                    gs = hs_[h2]
                    K.op(dve, lambda: V.tensor_tensor(out=H[:, :, gs, c + 1], in0=H[:, :, gs, c + 1], in1=st1[:, :, gs], op=ALU.add),
                         reads=[H_b, aL_b, hb_[h2]], writes=[hb_[h2]])

        K.op(dve, lambda: V.memset(H[:, :, :, 0:1], 0.0), writes=[H_b, Hf_b, Hb_b])

        def mA_load(g):
            K.dma(sp, mAb[g % 2][:], matsA[g], reads=[matsA_b[g]], writes=[mAb_b[g % 2]])

        def passA_batch(g0):
            b = next_ps()
            for gl in range(4):
                g = g0 + gl
                im = g % 2
                for ri in range(2):
                    for j in range(JC):
                        K.op(pe, lambda: nc.tensor.matmul(
                            psum[:, b, (ri * 4 + gl) * NCH:(ri * 4 + gl + 1) * NCH],
                            mAb[im][:, (ri * JC + j) * 128:(ri * JC + j + 1) * 128],
                            Uf[:, g, :].rearrange("p (c j) -> p c j", j=JC)[:, :, j],
                            start=(j == 0), stop=(j == JC - 1)),
                            reads=[mAb_b[im], UY_b], writes=[ps_b[b]], inc=(ri == 1 and j == JC - 1))
                if g + 2 < G:
                    mA_load(g + 2)
            pv = psum[:, b, :].rearrange("p (r g c) -> p r g c", r=2, g=4)
            K.op(act, lambda: nc.scalar.copy(out=H[0:64, :, g0:g0 + 4, 1:NCH + 1], in_=pv[0:64]), reads=[ps_b[b]], writes=[H_b, Hf_b, Hb_b])
            K.op(act, lambda: nc.scalar.copy(out=H[64:128, :, g0:g0 + 4, 1:NCH + 1][:, :, :, ::-1], in_=pv[64:128]), reads=[ps_b[b]], writes=[H_b, Hf_b, Hb_b])

        assert NCH % (NT // 2) == 0 and G // 4 == NT // 2
        mA_load(0)
        mA_load(1)
        tasks = [(qb, kv) for qb in range(NT) for kv in range(2)]
        spq = NCH // (NT // 2)
        cur = att_S(*tasks[0])
        fin = None
        for ti, (qb, kv) in enumerate(tasks):
            nxt = att_S(*tasks[ti + 1]) if ti + 1 < len(tasks) else None
            att_P(qb, kv, cur)
            cur = nxt
            if fin is not None:
                att_F(fin)
                fin = None
            if kv == 1:
                fin = qb
                if qb < NT // 2:
                    passA_batch(4 * qb)
                else:
                    for c in range((qb - NT // 2) * spq, (qb - NT // 2 + 1) * spq):
                        scan_step(c)
        att_F(fin)
        K.op(dve, lambda: V.tensor_copy(out=Ein[0:64], in_=H[0:64, :, :, 0:NCH]), reads=[H_b, Hf_b, Hb_b], writes=[Ein_b])
        K.op(act, lambda: nc.scalar.copy(out=Ein[64:128], in_=H[64:128, :, :, 0:NCH][:, :, :, ::-1]), reads=[H_b, Hf_b, Hb_b], writes=[Ein_b])
        def mB_load(g):
            K.dma(sp, mBb[g % 2][:], matsB[g], reads=[matsB_b[g]], writes=[mBb_b[g % 2]])
        mB_load(0)
        mB_load(1)
        for g0 in range(0, G, 2):
            b = next_ps()
            K.op(dve, lambda: V.memset(psum[:, b, :], 0.0), writes=[ps_b[b]])
            for gl in range(2):
                g = g0 + gl
                im = g % 2
                Ug = Uf[:, g, :].rearrange("p (c j) -> p c j", j=JC)
                Yg = psum[:, b, gl * 256:(gl + 1) * 256].rearrange("p (c j) -> p c j", j=JC)
                mm = []
                for d in range(JC):
                    mm.append((Yg[:, :, d:JC], mBb[im][:, d * 128:(d + 1) * 128], Ug[:, :, 0:JC - d], [UY_b]))
                    mm.append((Yg[:, :, 0:JC - d], mBb[im][:, (JC + d) * 128:(JC + d + 1) * 128], Ug[:, :, d:JC], [UY_b]))
                for j in range(JC):
                    mm.append((Yg[:, :, j], mBb[im][:, (2 * JC + j) * 128:(2 * JC + j + 1) * 128], Ein[:, 0, g, :], [Ein_b]))
                    mm.append((Yg[:, :, j], mBb[im][:, (3 * JC + j) * 128:(3 * JC + j + 1) * 128], Ein[:, 1, g, :], [Ein_b]))
                for n_, (o_, l_, r_, rd_) in enumerate(mm):
                    K.op(pe, lambda: nc.tensor.matmul(o_, l_, r_, start=False, stop=(n_ == len(mm) - 1), skip_group_check=True),
                         reads=[mBb_b[im]] + rd_, writes=[ps_b[b]], inc=(n_ == len(mm) - 1))
                if g + 2 < G:
                    mB_load(g + 2)
            K.op(act, lambda: nc.scalar.activation(out=Zf[:, g0:g0 + 2, :], in_=psum[:, b, :].rearrange("p (g s) -> p g s", g=2),
                                                   func=AF.Gelu_apprx_tanh), reads=[ps_b[b]], writes=[hTlo_b])
        for s8 in range(8):
            K.dma(sp, z_scr.rearrange("(g q) (s sub) -> s q g sub", q=16, s=8)[s8], Zf[s8 * 16:(s8 + 1) * 16, :, :],
                  reads=[hTlo_b], writes=[z_scr_b])
        K.dma(sp, X1[:], z_scr.rearrange("(c r) t -> r c t", c=4), reads=[z_scr_b], writes=[X1_b])
        K.barrier()
        e_att.close()
        gate = [sb(e_s5, f"gate{i}", (128, 512)) for i in range(2)]
        for m in range(4):
            for tg in range(4):
                b = next_ps()
                ig = (m * 4 + tg) % 2
                for k in range(4):
                    K.op(pe, lambda: nc.tensor.matmul(psum[:, b, :], wg[:, k, m * 128:(m + 1) * 128], X1[:, k, tg * 512:(tg + 1) * 512],
                                                      start=(k == 0), stop=(k == 3)),
                         reads=[wg_b, X1_b], writes=[ps_b[b]], inc=(k == 3))
                K.op(act, lambda: nc.scalar.activation(out=gate[ig][:], in_=psum[:, b, :], func=AF.Sigmoid, bias=bglu[:, m:m + 1]),
                     reads=[ps_b[b], pb], writes=[gate_b[ig]])
                nat = lambda tns: tns[:, m, :].rearrange("p (sub s) -> p s sub", s=8)[:, 2 * tg:2 * tg + 2, :]
                K.op(dve, lambda: V.tensor_tensor(out=nat(ysT), in0=X1[:, m, tg * 512:(tg + 1) * 512].rearrange("p (s sub) -> p s sub", s=2),
                                                  in1=gate[ig][:].rearrange("p (s sub) -> p s sub", s=2), op=ALU.mult),
                     reads=[X1_b, gate_b[ig]], writes=[ysT_b])
        K.op(act, lambda: nc.scalar.activation(out=ysq[:], in_=ysT[:], func=AF.Square), reads=[ysT_b], writes=[ysq_b])

        K.barrier()
        e_s5.close()
        if s == 0 and "ysT" in dbg:
            K.dma(sp, dbg["ysT"], ysT[:], reads=[ysT_b])
            K.dma(sp, dbg["yaT"], yaT[:], reads=[yaT_b])
            K.dma(sp, dbg["zT"], X1[:], reads=[X1_b])
        e_m4 = ExitStack()
        mt = [sb(e_m4, f"mt{i}", (128, 1024)) for i in range(2)]
        b = next_ps()
        for tt in range(NT):
            for k in range(4):
                K.op(pe, lambda: nc.tensor.matmul(psum[:, b, tt:tt + 1], ysq[:, k, tt * 128:(tt + 1) * 128], ones_bf[:, 0:1],
                                                  start=(k == 0), stop=(k == 3)),
                     reads=[ysq_b, cb], writes=[ps_b[b]], inc=(k == 3 and tt == NT - 1))
        K.op(act, lambda: nc.scalar.activation(out=rs[:, 1, :], in_=psum[:, b, 0:NT], func=AF.Sqrt, bias=epsb[:, 0:1], scale=1.0 / SSM_W),
             reads=[ps_b[b], cb], writes=[rs_b[1]])
        K.op(dve, lambda: V.reciprocal(out=rs[:, 1, :], in_=rs[:, 1, :]), reads=[rs_b[1]], writes=[rs_b[1]])
        K.op(act, lambda: nc.scalar.activation(out=rs[:, 2, :], in_=ssq[:, 2, :], func=AF.Sqrt, bias=epsb[:, 0:1], scale=1.0 / SSM_W),
             reads=[ssq_b[2], cb], writes=[rs_b[2]])
        K.op(dve, lambda: V.reciprocal(out=rs[:, 2, :], in_=rs[:, 2, :]), reads=[rs_b[2]], writes=[rs_b[2]])
        for tt in range(NT):
            i = tt % 2
            r0 = tok0 + tt * 128
            ba = next_ps(2)
            for hf in range(2):
                for k in range(4):
                    K.op(pe, lambda: nc.tensor.matmul(psum[:, ba + hf, :], ysT[:, k, tt * 128:(tt + 1) * 128], wo[:, k, hf * 512:(hf + 1) * 512],
                                                      start=(k == 0), stop=(k == 3)),
                         reads=[ysT_b, wo_b], writes=[ps_b[ba + hf]], inc=(k == 3))
            bb_ = next_ps(2)
            for hf in range(2):
                for k in range(4):
                    K.op(pe, lambda: nc.tensor.matmul(psum[:, bb_ + hf, :], yaT[:, k, tt * 128:(tt + 1) * 128], wo[:, 4 + k, hf * 512:(hf + 1) * 512],
                                                      start=(k == 0), stop=(k == 3)),
                         reads=[yaT_b, wo_b], writes=[ps_b[bb_ + hf]], inc=(k == 3))
            K.op(act, lambda: nc.scalar.activation(out=mt[i][:], in_=psum[:, ba:ba + 2, :].rearrange("p a n -> p (a n)"), func=AF.Copy,
                                                   scale=rs[:, 1, tt:tt + 1]), reads=[ps_b[ba], ps_b[ba + 1], rs_b[1]], writes=[mt_b[i]])
            K.op(dve, lambda: V.scalar_tensor_tensor(out=mt[i][:], in0=psum[:, bb_:bb_ + 2, :].rearrange("p a n -> p (a n)"),
                                                     scalar=rs[:, 2, tt:tt + 1], in1=mt[i][:], op0=ALU.mult, op1=ALU.add),
                 reads=[ps_b[bb_], ps_b[bb_ + 1], rs_b[2], mt_b[i]], writes=[mt_b[i]])
            K.op(act, lambda: nc.scalar.activation(out=junk[:], in_=mt[i][:], func=AF.Square, accum_out=ssq[:, 3, tt:tt + 1]),
                 reads=[mt_b[i]], writes=[junk_b, ssq_b[3]])
            K.op(act, lambda: nc.scalar.activation(out=rs[:, 3, tt:tt + 1], in_=ssq[:, 3, tt:tt + 1], func=AF.Sqrt,
                                                   bias=epsb[:, 0:1], scale=1.0 / D_MODEL), reads=[ssq_b[3], cb], writes=[rs_b[3]])
            K.op(dve, lambda: V.reciprocal(out=rs[:, 3, tt:tt + 1], in_=rs[:, 3, tt:tt + 1]), reads=[rs_b[3]], writes=[rs_b[3]])
            K.op(dve, lambda: V.tensor_tensor(out=mt[i][:], in0=mt[i][:], in1=gpost[:], op=ALU.mult),
                 reads=[mt_b[i], pb], writes=[mt_b[i]])
            rd = [out_b[s][tt]] if x_src is out else []
            K.dma(sp, xt[i][:], x_src[r0:r0 + 128, :], reads=rd, writes=[xt_b[i]])
            K.op(dve, lambda: V.scalar_tensor_tensor(out=xo[i][:], in0=mt[i][:], scalar=rs[:, 3, tt:tt + 1], in1=xt[i][:],
                                                     op0=ALU.mult, op1=ALU.add),
                 reads=[mt_b[i], rs_b[3], xt_b[i]], writes=[xo_b[i]])
            K.dma(sp, out[r0:r0 + 128, :], xo[i][:], reads=[xo_b[i]], writes=[out_b[s][tt]])
        K.barrier()
        e_m4.close()


_RR = {}


def K_rr(K, key, n):
    v = _RR.get(key, 0)
    _RR[key] = (v + 1) % n
    return v


def ffn_phase(nc, K, es, sb, psum, ps_b, next_ps, P, l, nseq, out, out_b, ident, identb, epsb, cb, dbg):
    pe, act, dve, pool, sp = K.pe, K.act, K.dve, K.pool, K.sp
    V = nc.vector
    wu = sb(es, "wu", (128, 8, 2 * D_FF), BF16)
    wd = sb(es, "wd", (128, NFB, 1024), BF16)
    gpf = sb(es, "gpf", (128, 8))
    cw = sb(es, "cw", (128, 4, 2 * NFB))
    gpost = sb(es, "gpost2", (128, 1024))
    wu_b, wd_b, pb = Buf(), Buf(), Buf()
    with ExitStack() as es1:
        ld = sb(es1, "ldf", (2 * NFB, 128))
        ld_b = Buf()
        LC = lambda dst, src, n: load_cols(nc, K, psum, ps_b, next_ps, ident, cb, ld, ld_b, dst, pb, src, n)
        LC(gpf[:], P["pre_ffn_norm"][l], 8)
        for j in range(3):
            LC(cw[:, j, :], P["conv_w"][l, j], 2 * NFB)
        LC(cw[:, 3, :], P["conv_b"][l], 2 * NFB)
        K.barrier()
    K.dma(sp, gpost[:], P["post_ffn_norm"][l].partition_broadcast(128), writes=[pb])
    with ExitStack() as es2:
        engs = [act, dve]
        stage = make_staging(es2, sb, "f")
        load_cast_weight(nc, K, stage, wu, wu_b, [P["w_up"][l, k * 128:(k + 1) * 128, :] for k in range(8)], 2 * D_FF,
                         gpf, pb, engs)
        load_cast_weight(nc, K, stage, wd, wd_b, [P["w_down"][l, k * 128:(k + 1) * 128, :] for k in range(NFB)], 1024,
                         None, pb, engs)
        K.barrier()

    HW = 1026
    h2T = sb(es, "h2T", (128, 8, HW), BF16)
    NPRE = 6
    gbuf = sb(es, "gbuf", (128, NFB - NPRE, 384), BF16)
    gpre = [sb(es, f"gpre{i}", (128, NPRE, 384), BF16) for i in range(2)]
    gbuf_b = [Buf() for _ in range(NFB - NPRE)]
    gpre_b = [[Buf() for _ in range(NPRE)] for _ in range(2)]

    def gslot(kpar, fb):
        if fb < NPRE:
            return gpre[kpar], fb, gpre_b[kpar][fb]
        return gbuf, fb - NPRE, gbuf_b[fb - NPRE]
    pending = []
    tgk = [0]
    halo = sb(es, "halo", (128, 8, 2), BF16)
    xt = [sb(es, f"fxt{i}", (128, 1024)) for i in range(3)]
    hs = xt
    junk = sb(es, "fjunk", (128, 1024), BF16)
    cv = [sb(es, f"cv{i}", (128, 384)) for i in range(3)]
    cg = [sb(es, f"cg{i}", (128, 384)) for i in range(3)]
    gg = [sb(es, f"gg{i}", (128, 384)) for i in range(3)]
    yt = [sb(es, f"yt{i}", (128, 1024)) for i in range(1)] * 2
    xo = xt
    st = sb(es, "fst", (128, 4))
    stf = sb(es, "fstf", (128, 2, 4))
    stf_b = [Buf() for _ in range(4)]
    std = sb(es, "fstd", (128, 2, 2))
    std_b = [Buf() for _ in range(2)]
    B = lambda n=1: [Buf() for _ in range(n)]
    h2T_b, g_b, junk_b, st_b = Buf(), Buf(), Buf(), Buf()
    xt_b, cv_b, cg_b, gg_b = B(3), B(3), B(3), B(3)
    yt_b = B(1) * 2
    hs_b = xt_b
    xo_b = xt_b
    def emit_down(s, tok0, hs0, t0, ln, kpar):
        for ti in range(ln // 128):
            tt = (hs0 + t0) // 128 + ti
            r0 = tok0 + tt * 128
            i = tt % 2
            b = next_ps(2)
            for hf in range(2):
                for fb in range(NFB):
                    gt_, gi_, gb_ = gslot(kpar, fb)
                    K.op(pe, lambda: nc.tensor.matmul(psum[:, b + hf, :], gt_[:, gi_, ti * 128:(ti + 1) * 128],
                                                      wd[:, fb, hf * 512:(hf + 1) * 512], start=(fb == 0), stop=(fb == NFB - 1)),
                         reads=[gb_, wd_b], writes=[ps_b[b + hf]], inc=(fb == NFB - 1))
            pv = psum[:, b:b + 2, :].rearrange("p a n -> p (a n)")
            K.op(act, lambda: nc.scalar.activation(out=junk[:], in_=pv, func=AF.Square, accum_out=std[:, 0, i:i + 1]),
                 reads=[ps_b[b], ps_b[b + 1]], writes=[junk_b, std_b[i]])
            K.op(act, lambda: nc.scalar.activation(out=std[:, 1, i:i + 1], in_=std[:, 0, i:i + 1], func=AF.Sqrt, bias=epsb[:, 0:1], scale=1.0 / D_MODEL),
                 reads=[std_b[i], cb], writes=[std_b[i]])
            K.op(dve, lambda: V.reciprocal(out=std[:, 1, i:i + 1], in_=std[:, 1, i:i + 1]), reads=[std_b[i]], writes=[std_b[i]])
            K.op(dve, lambda: V.tensor_tensor(out=yt[i][:], in0=pv, in1=gpost[:], op=ALU.mult),
                 reads=[ps_b[b], ps_b[b + 1], pb], writes=[yt_b[i]])
            K.dma(sp, xt[i][:], out[r0:r0 + 128, :], reads=[out_b[s][tt]], writes=[xt_b[i]])
            K.op(dve, lambda: V.scalar_tensor_tensor(out=xo[i][:], in0=yt[i][:], scalar=std[:, 1, i:i + 1], in1=xt[i][:],
                                                     op0=ALU.mult, op1=ALU.add),
                 reads=[yt_b[i], std_b[i], xt_b[i]], writes=[xo_b[i]])
            K.dma(sp, out[r0:r0 + 128, :], xo[i][:], reads=[xo_b[i]], writes=[out_b[s][tt]])

    it = [0]
    if l == 0:
        print("ffn sbuf bytes remaining", nc.sbuf_bytes_remaining)

    for s in range(nseq):
        tok0 = s * SEQ
        for half in range(2):
            hs0 = half * 1024
            t_lo = hs0 // 128 - 1
            if half == 1:
                K.op(dve, lambda: V.tensor_copy(out=h2T[:, :, 0:1], in_=halo[:, :, 0:1]), reads=[h2T_b], writes=[h2T_b])
            tiles = [tt for tt in range(t_lo, t_lo + 10) if not (tt < 0 or tt >= NT or (half == 1 and tt == t_lo))]

            def fill_a(tt):
                i = it[0] % 3
                js = it[0] % 4
                it[0] += 1
                r0 = tok0 + tt * 128
                K.dma(sp, xt[i][:], out[r0:r0 + 128, :], reads=[out_b[s][tt]], writes=[xt_b[i]])
                K.op(act, lambda: nc.scalar.activation(out=junk[:], in_=xt[i][:], func=AF.Square, accum_out=stf[:, 0, js:js + 1]),
                     reads=[xt_b[i]], writes=[junk_b, stf_b[js]])
                K.op(act, lambda: nc.scalar.activation(out=stf[:, 1, js:js + 1], in_=stf[:, 0, js:js + 1], func=AF.Sqrt, bias=epsb[:, 0:1], scale=1.0 / D_MODEL),
                     reads=[stf_b[js], cb], writes=[stf_b[js]])
                K.op(dve, lambda: V.reciprocal(out=stf[:, 1, js:js + 1], in_=stf[:, 1, js:js + 1]), reads=[stf_b[js]], writes=[stf_b[js]])
                K.op(dve, lambda: V.tensor_scalar(out=hs[i][:], in0=xt[i][:], scalar1=stf[:, 1, js:js + 1], scalar2=None, op0=ALU.mult),
                     reads=[xt_b[i], stf_b[js]], writes=[hs_b[i]])
                return i

            def fill_b(tt, i):
                b = next_ps(2)
                for c in range(8):
                    K.op(pe, lambda: nc.tensor.transpose(psum[:, b + c // 4, (c % 4) * 128:(c % 4 + 1) * 128],
                                                         hs[i][:, c * 128:(c + 1) * 128], ident[:]),
                         reads=[hs_b[i], cb], writes=[ps_b[b], ps_b[b + 1]], inc=(c == 7))
                j0 = tt * 128 - hs0 + 1
                lo, hi = max(j0, 0), min(j0 + 128, HW)
                for hh in range(2):
                    src = psum[:, b + hh, :].rearrange("p (c t) -> p c t", c=4)[:, :, lo - j0:hi - j0]
                    dst = h2T[:, hh * 4:(hh + 1) * 4, lo:hi]
                    if hh == 0:
                        K.op(dve, lambda: V.tensor_copy(out=dst, in_=src), reads=[ps_b[b + hh]], writes=[h2T_b])
                    else:
                        K.op(act, lambda: nc.scalar.copy(out=dst, in_=src), reads=[ps_b[b + hh]], writes=[h2T_b])

            ia = fill_a(tiles[0])
            for n_, tt in enumerate(tiles):
                ia_next = fill_a(tiles[n_ + 1]) if n_ + 1 < len(tiles) else None
                fill_b(tt, ia)
                ia = ia_next
            if half == 0:
                K.op(dve, lambda: V.tensor_copy(out=halo[:, :, 0:1], in_=h2T[:, :, 1024:1025]), reads=[h2T_b], writes=[h2T_b])
            for (t0, ln) in ((0, 384), (384, 384), (768, 256)):
                g_first = (hs0 + t0 == 0)
                g_last = (hs0 + t0 + ln == SEQ)
                lo = 1 if g_first else 0
                hi = 1 if g_last else 0
                w0c, w1c = t0 + lo, t0 + ln + 2 - hi
                nW = w1c - w0c
                ctr = 1 - lo
                kpar = tgk[0] % 2
                tgk[0] += 1
                for fb in range(NFB):
                    i = fb % 3
                    if fb == NPRE:
                        for fn in pending:
                            fn()
                        pending.clear()
                    for vg in range(2):
                        rb = vg * NFB + fb
                        b = next_ps()
                        for k in range(8):
                            K.op(pe, lambda: nc.tensor.matmul(psum[:, b, 0:nW], wu[:, k, rb * 128:(rb + 1) * 128],
                                                              h2T[:, k, w0c:w1c], start=(k == 0), stop=(k == 7)),
                                 reads=[wu_b, h2T_b], writes=[ps_b[b]], inc=(k == 7))
                        dst, dst_b = (cv[i], cv_b[i]) if vg == 0 else (cg[i], cg_b[i])
                        K.op(act, lambda: nc.scalar.activation(out=dst[:, 0:ln], in_=psum[:, b, ctr:ctr + ln], func=AF.Identity,
                                                               bias=cw[:, 3, rb:rb + 1], scale=cw[:, 1, rb:rb + 1]),
                             reads=[ps_b[b], pb], writes=[dst_b])
                        K.op(dve, lambda: V.scalar_tensor_tensor(out=dst[:, lo:ln], in0=psum[:, b, ctr + lo - 1:ctr + ln - 1],
                                                                 scalar=cw[:, 0, rb:rb + 1], in1=dst[:, lo:ln], op0=ALU.mult, op1=ALU.add),
                             reads=[ps_b[b], pb, dst_b], writes=[dst_b])
                        K.op(dve, lambda: V.scalar_tensor_tensor(out=dst[:, 0:ln - hi], in0=psum[:, b, ctr + 1:ctr + ln - hi + 1],
                                                                 scalar=cw[:, 2, rb:rb + 1], in1=dst[:, 0:ln - hi], op0=ALU.mult, op1=ALU.add),
                             reads=[ps_b[b], pb, dst_b], writes=[dst_b])
                    K.op(act, lambda: nc.scalar.activation(out=gg[i][:, 0:ln], in_=cg[i][:, 0:ln], func=AF.Gelu_apprx_tanh),
                         reads=[cg_b[i]], writes=[gg_b[i]])
                    gt_, gi_, gb_ = gslot(kpar, fb)
                    K.op(pool, lambda: nc.gpsimd.tensor_tensor(out=gt_[:, gi_, 0:ln], in0=gg[i][:, 0:ln], in1=cv[i][:, 0:ln], op=ALU.mult),
                         reads=[gg_b[i], cv_b[i]], writes=[gb_])
                pending.append(lambda s=s, tok0=tok0, hs0=hs0, t0=t0, ln=ln, kpar=kpar: emit_down(s, tok0, hs0, t0, ln, kpar))
    for fn in pending:
        fn()
    pending.clear()
_CACHE = {}


def _q_perm():
    cols = list(range(512))
    for c in range(4):
        for h in (c, 4 + c):
            cols.extend(range(512 + h * 64, 512 + (h + 1) * 64))
    cols.extend(range(1024, 1280))
    return np.asarray(cols)


def kernel(**inputs):
    n_cores = 8
    x = np.ascontiguousarray(np.asarray(inputs["x"], dtype=np.float32))
    nseq = x.shape[0] // n_cores
    if "nc" not in _CACHE:
        _CACHE["nc"] = build(nseq=nseq)[0]
        _CACHE["consts"] = _host_consts()
    nc = _CACHE["nc"]
    consts = _CACHE["consts"]
    shared = {}
    for k, v in inputs.items():
        if k == "x":
            continue
        a = np.ascontiguousarray(np.asarray(v, dtype=np.float32))
        if k == "w_in":
            a = np.ascontiguousarray(a[:, :, _q_perm()])
        shared[k] = a
    shared.update(consts)
    in_maps = []
    for i in range(n_cores):
        m = dict(shared)
        m["x"] = x[i * nseq:(i + 1) * nseq].reshape(nseq * SEQ, D_MODEL)
        in_maps.append(m)
    res = run_bass_kernel_spmd(nc, in_maps, core_ids=list(range(n_cores)))
    outs = [np.asarray(r["out"]).reshape(nseq, SEQ, D_MODEL) for r in res.results]
    return np.concatenate(outs, axis=0).astype(np.float32)
```
